# Optimizing a Trainium2 kernel written in Bass

```python
import math
import jax, jax.numpy as jnp
from jax import lax
import numpy as np

D_MODEL = 1024
BATCH = 2
SEQ = 8192
DEPTH = 2

PLE_DIM = 256
ROPE_THETA = 10000.0
EPS = 1e-6
NEG_INF = -1e30
HEAD_DIM = 64

A_HEADS = 8
A_KV_GROUPS = 2
A_CMP_LEN = 32
A_CMP_STRIDE = 16
A_CMP_HIDDEN = 256
A_SEL_LEN = 64
A_TOPK = 16
A_WINDOW = 512
A_Q_CHUNK = 128
A_FORCE_BONUS = 1e4

B_HEADS = 8
B_KV_HEADS = 2
B_WINDOW = 128
B_BLOCK = 128

C_HEADS = 8
C_Q_RANK = 256
C_KV_RANK = 256
C_NOPE = 64
C_ROPE = 32
C_V = 64
C_Q_BLOCK = 128

D_FF = int(math.ceil(8 * D_MODEL / 3 / 256)) * 256

A_Q = A_HEADS * HEAD_DIM
A_KV = A_KV_GROUPS * HEAD_DIM
A_GATES = A_HEADS * 3
B_Q = B_HEADS * HEAD_DIM
B_KV = B_KV_HEADS * HEAD_DIM
IN_SIZES = (A_Q, A_KV, A_KV, A_KV, A_KV, A_KV, A_KV, A_GATES, B_Q, B_KV, B_KV, C_Q_RANK, C_KV_RANK, C_ROPE)
IN_COLS = sum(IN_SIZES)
A_OUT = A_HEADS * HEAD_DIM
B_OUT = B_HEADS * HEAD_DIM
C_OUT = C_HEADS * C_V

kernel_name = "hybrid_nsa_swa_sink_mla_gated_block"


def rmsnorm(x, g):
    xf = x.astype(jnp.float32)
    y = xf * lax.rsqrt(jnp.mean(xf * xf, axis=-1, keepdims=True) + EPS)
    return (y * g.astype(jnp.float32)).astype(x.dtype)


def rope_tables(positions, dim):
    inv_freq = jnp.power(jnp.float32(ROPE_THETA), -jnp.arange(0, dim, 2, dtype=jnp.float32) / dim)
    ang = positions.astype(jnp.float32)[..., None] * inv_freq
    return jnp.cos(ang), jnp.sin(ang)


def apply_rope(x, cos, sin):
    half = x.shape[-1] // 2
    xf = x.astype(jnp.float32)
    x1, x2 = xf[..., :half], xf[..., half:]
    c, s = cos[:, :, None, :], sin[:, :, None, :]
    return jnp.concatenate([x1 * c - x2 * s, x2 * c + x1 * s], axis=-1).astype(x.dtype)


def masked_softmax(scores, mask):
    p = jax.nn.softmax(jnp.where(mask, scores, NEG_INF), axis=-1)
    return jnp.where(mask, p, 0.0)


def compress_blocks(t, tok, pos, w1, w2):
    B, _, G, dh = t.shape
    n_cmp, L = tok.shape
    blocks = t[:, tok] + pos[:, None, :].astype(t.dtype)
    flat = blocks.transpose(0, 1, 3, 2, 4).reshape(B, n_cmp, G, L * dh)
    return jax.nn.gelu(flat @ w1) @ w2


def nsa_attention(q, kc, vc, ks, vs, kw, vw, gate_logits, pos_k, w1_k, w2_k, pos_v, w1_v, w2_v):
    B, S = q.shape[0], q.shape[1]
    G, R, dh = A_KV_GROUPS, A_HEADS // A_KV_GROUPS, HEAD_DIM
    QC = A_Q_CHUNK
    scale = dh ** -0.5
    dt = q.dtype
    qg = q.reshape(B, S, G, R, dh)
    gates = jax.nn.sigmoid(gate_logits.astype(jnp.float32)).reshape(B, S, G, R, 3).astype(dt)
    n_cmp = (S - A_CMP_LEN) // A_CMP_STRIDE + 1
    tok = np.arange(n_cmp)[:, None] * A_CMP_STRIDE + np.arange(A_CMP_LEN)[None, :]
    k_cmp = compress_blocks(kc, tok, pos_k, w1_k, w2_k)
    v_cmp = compress_blocks(vc, tok, pos_v, w1_v, w2_v)
    cmp_end = jnp.asarray(tok[:, -1], jnp.int32)
    n_sel = S // A_SEL_LEN
    sel_map = np.zeros((n_cmp, n_sel), np.float32)
    np.add.at(sel_map, (np.repeat(np.arange(n_cmp), A_CMP_LEN), (tok // A_SEL_LEN).reshape(-1)), 1.0 / A_CMP_LEN)
    sel_map = jnp.asarray(sel_map)
    top_k = min(A_TOPK, n_sel)
    ks_blk = ks.transpose(0, 2, 1, 3).reshape(B, G, n_sel, A_SEL_LEN * dh)
    vs_blk = vs.transpose(0, 2, 1, 3).reshape(B, G, n_sel, A_SEL_LEN * dh)
    gather_blocks = jax.vmap(jax.vmap(lambda t, i: t[i]))
    kw_pad = jnp.pad(kw, ((0, 0), (A_WINDOW, 0), (0, 0), (0, 0)))
    vw_pad = jnp.pad(vw, ((0, 0), (A_WINDOW, 0), (0, 0), (0, 0)))
    blk_ids = jnp.arange(n_sel)

    def chunk(c):
        t0 = c * QC
        tq = t0 + jnp.arange(QC)
        qc = lax.dynamic_slice_in_dim(qg, t0, QC, axis=1)
        gc = lax.dynamic_slice_in_dim(gates, t0, QC, axis=1)
        s = jnp.einsum('bqgrd,bngd->bgrqn', qc, k_cmp).astype(jnp.float32) * scale
        p_cmp = masked_softmax(s, cmp_end[None, :] <= tq[:, None])
        o_cmp = jnp.einsum('bgrqn,bngd->bqgrd', p_cmp.astype(dt), v_cmp)
        imp = jnp.einsum('bgrqn,nj->bgqj', p_cmp, sel_map)
        cur = tq // A_SEL_LEN
        valid = blk_ids[None, :] <= cur[:, None]
        forced = (blk_ids[None, :] == 0) | (blk_ids[None, :] == cur[:, None]) | (blk_ids[None, :] == cur[:, None] - 1)
        imp = jnp.where(valid, imp + jnp.where(forced, A_FORCE_BONUS, 0.0), NEG_INF)
        _, top_idx = lax.top_k(imp, top_k)
        idx_flat = top_idx.reshape(B, G, QC * top_k)
        kb = gather_blocks(ks_blk, idx_flat).reshape(B, G, QC, top_k * A_SEL_LEN, dh)
        vb = gather_blocks(vs_blk, idx_flat).reshape(B, G, QC, top_k * A_SEL_LEN, dh)
        key_pos = (top_idx[..., None] * A_SEL_LEN + jnp.arange(A_SEL_LEN)).reshape(B, G, QC, top_k * A_SEL_LEN)
        m_sel = key_pos <= tq[None, None, :, None]
        s = jnp.einsum('bqgrd,bgqkd->bgrqk', qc, kb).astype(jnp.float32) * scale
        p_sel = masked_softmax(s, m_sel[:, :, None])
        o_sel = jnp.einsum('bgrqk,bgqkd->bqgrd', p_sel.astype(dt), vb)
        kwc = lax.dynamic_slice_in_dim(kw_pad, t0, QC + A_WINDOW, axis=1)
        vwc = lax.dynamic_slice_in_dim(vw_pad, t0, QC + A_WINDOW, axis=1)
        kp = t0 - A_WINDOW + jnp.arange(QC + A_WINDOW)
        m_win = (kp[None, :] <= tq[:, None]) & (kp[None, :] > tq[:, None] - A_WINDOW) & (kp[None, :] >= 0)
        s = jnp.einsum('bqgrd,bkgd->bgrqk', qc, kwc).astype(jnp.float32) * scale
        p_win = masked_softmax(s, m_win)
        o_win = jnp.einsum('bgrqk,bkgd->bqgrd', p_win.astype(dt), vwc)
        return gc[..., 0:1] * o_cmp + gc[..., 1:2] * o_sel + gc[..., 2:3] * o_win

    out = lax.map(chunk, jnp.arange(S // QC))
    return out.transpose(1, 0, 2, 3, 4, 5).reshape(B, S, A_HEADS * dh)


def swa_sink_attention(q, k, v, sinks):
    B, S = q.shape[0], q.shape[1]
    G, R, dh, BLK = B_KV_HEADS, B_HEADS // B_KV_HEADS, HEAD_DIM, B_BLOCK
    nb = S // BLK
    qb = q.reshape(B, nb, BLK, G, R, dh)

    def band(t):
        tb = t.reshape(B, nb, BLK, G, dh)
        prev = jnp.pad(tb, ((0, 0), (1, 0), (0, 0), (0, 0), (0, 0)))[:, :-1]
        return jnp.concatenate([prev, tb], axis=2)

    kb, vb = band(k), band(v)
    s = jnp.einsum('bnqgrd,bnkgd->bngrqk', qb, kb).astype(jnp.float32) * (dh ** -0.5)
    qi = jnp.arange(BLK)[:, None]
    kj = jnp.arange(2 * BLK)[None, :] - BLK
    rel = (kj <= qi) & (kj > qi - B_WINDOW)
    mask = rel[None] & ((jnp.arange(nb)[:, None, None] > 0) | (kj[None] >= 0))
    s = jnp.where(mask[None, :, None, None], s, NEG_INF)
    sink = jnp.broadcast_to(sinks.astype(jnp.float32).reshape(G, R)[None, None, :, :, None, None], s.shape[:-1] + (1,))
    p = jax.nn.softmax(jnp.concatenate([s, sink], axis=-1), axis=-1)[..., :-1]
    o = jnp.einsum('bngrqk,bnkgd->bnqgrd', p.astype(v.dtype), vb)
    return o.reshape(B, S, B_HEADS * dh)


def causal_block_attention(q, k, v, scale):
    B, S, H, _ = q.shape
    dv = v.shape[-1]
    key_pos = jnp.arange(S)

    def body(c):
        t0 = c * C_Q_BLOCK
        qb = lax.dynamic_slice_in_dim(q, t0, C_Q_BLOCK, axis=1)
        s = jnp.einsum('bqhd,bkhd->bhqk', qb, k).astype(jnp.float32) * scale
        mask = key_pos[None, :] <= (t0 + jnp.arange(C_Q_BLOCK))[:, None]
        p = jax.nn.softmax(jnp.where(mask, s, NEG_INF), axis=-1)
        return jnp.einsum('bhqk,bkhd->bqhd', p.astype(v.dtype), v)

    out = lax.map(body, jnp.arange(S // C_Q_BLOCK))
    return out.transpose(1, 0, 2, 3, 4).reshape(B, S, H * dv)


def mla_attention(cq, ckv, k_pe_raw, q_norm, w_q_up, kv_norm, w_kv_up, cos32, sin32):
    B, S = cq.shape[0], cq.shape[1]
    q = (rmsnorm(cq, q_norm) @ w_q_up).reshape(B, S, C_HEADS, C_NOPE + C_ROPE)
    q_nope, q_pe = q[..., :C_NOPE], apply_rope(q[..., C_NOPE:], cos32, sin32)
    kv = (rmsnorm(ckv, kv_norm) @ w_kv_up).reshape(B, S, C_HEADS, C_NOPE + C_V)
    k_nope, v = kv[..., :C_NOPE], kv[..., C_NOPE:]
    k_pe = apply_rope(k_pe_raw[:, :, None, :], cos32, sin32)
    qf = jnp.concatenate([q_nope, q_pe], axis=-1)
    kf = jnp.concatenate([k_nope, jnp.broadcast_to(k_pe, (B, S, C_HEADS, C_ROPE))], axis=-1)
    return causal_block_attention(qf, kf, v, (C_NOPE + C_ROPE) ** -0.5)


def hybrid_layer(x, p_i, rope64, rope32, mix_norm, w_in, a_cmp_pos_k, a_cmp_w1_k, a_cmp_w2_k,
                 a_cmp_pos_v, a_cmp_w1_v, a_cmp_w2_v, b_sinks, c_q_norm, c_w_q_up, c_kv_norm, c_w_kv_up,
                 w_branch_gate, w_branch_a, w_branch_b, w_branch_c, w_out, ffn_norm, w_ffn_gate, w_ffn_up,
                 w_ffn_down, ple_norm, w_ple_proj, w_ple_gate):
    B, S, _ = x.shape
    cos64, sin64 = rope64
    cos32, sin32 = rope32
    h = rmsnorm(x, mix_norm)
    z = h @ w_in
    split_points = [int(v) for v in np.cumsum(IN_SIZES)[:-1]]
    (a_q, a_kc, a_vc, a_ks, a_vs, a_kw, a_vw, a_g, b_q, b_k, b_v, c_cq, c_ckv, c_kpe) = jnp.split(z, split_points, axis=-1)

    def heads(t, n):
        return t.reshape(B, S, n, -1)

    def rot(t, n):
        return apply_rope(heads(t, n), cos64, sin64)

    G = A_KV_GROUPS
    o_a = nsa_attention(rot(a_q, A_HEADS), rot(a_kc, G), heads(a_vc, G), rot(a_ks, G), heads(a_vs, G),
                        rot(a_kw, G), heads(a_vw, G), a_g, a_cmp_pos_k, a_cmp_w1_k, a_cmp_w2_k,
                        a_cmp_pos_v, a_cmp_w1_v, a_cmp_w2_v)
    o_b = swa_sink_attention(rot(b_q, B_HEADS), rot(b_k, B_KV_HEADS), heads(b_v, B_KV_HEADS), b_sinks)
    o_c = mla_attention(c_cq, c_ckv, c_kpe, c_q_norm, c_w_q_up, c_kv_norm, c_w_kv_up, cos32, sin32)

    g = jax.nn.sigmoid((h @ w_branch_gate).astype(jnp.float32)).astype(x.dtype)
    g_a, g_b, g_c = jnp.split(g, 3, axis=-1)
    merged = g_a * (o_a @ w_branch_a) + g_b * (o_b @ w_branch_b) + g_c * (o_c @ w_branch_c)
    x = x + merged @ w_out

    h2 = rmsnorm(x, ffn_norm)
    x = x + (jax.nn.silu(h2 @ w_ffn_gate) * (h2 @ w_ffn_up)) @ w_ffn_down

    gate = jax.nn.sigmoid((rmsnorm(x, ple_norm) @ w_ple_gate).astype(jnp.float32)).astype(x.dtype)
    return x + gate * (p_i.astype(x.dtype) @ w_ple_proj)


def setup_inputs(seed: int = 0) -> dict:
    key = jax.random.key(seed)
    ks = jax.random.split(key, 32)

    def dense(k, shape, fan_in):
        return jax.random.normal(k, shape, jnp.float32) * (fan_in ** -0.5)

    def gain(k, shape):
        return 1.0 + 0.05 * jax.random.normal(k, shape, jnp.float32)

    L, dh = A_CMP_LEN, HEAD_DIM
    return {
        "x": jax.random.normal(ks[0], (BATCH, SEQ, D_MODEL), jnp.float32),
        "p": jax.random.normal(ks[1], (DEPTH, BATCH, SEQ, PLE_DIM), jnp.float32),
        "positions": jnp.broadcast_to(jnp.arange(SEQ, dtype=jnp.int32)[None, :], (BATCH, SEQ)),
        "mix_norm": gain(ks[2], (DEPTH, D_MODEL)),
        "w_in": dense(ks[3], (DEPTH, D_MODEL, IN_COLS), D_MODEL),
        "a_cmp_pos_k": 0.2 * jax.random.normal(ks[4], (DEPTH, L, dh), jnp.float32),
        "a_cmp_w1_k": dense(ks[5], (DEPTH, L * dh, A_CMP_HIDDEN), L * dh),
        "a_cmp_w2_k": dense(ks[6], (DEPTH, A_CMP_HIDDEN, dh), A_CMP_HIDDEN),
        "a_cmp_pos_v": 0.2 * jax.random.normal(ks[7], (DEPTH, L, dh), jnp.float32),
        "a_cmp_w1_v": dense(ks[8], (DEPTH, L * dh, A_CMP_HIDDEN), L * dh),
        "a_cmp_w2_v": dense(ks[9], (DEPTH, A_CMP_HIDDEN, dh), A_CMP_HIDDEN),
        "b_sinks": 0.5 * jax.random.normal(ks[10], (DEPTH, B_HEADS), jnp.float32),
        "c_q_norm": gain(ks[11], (DEPTH, C_Q_RANK)),
        "c_w_q_up": dense(ks[12], (DEPTH, C_Q_RANK, C_HEADS * (C_NOPE + C_ROPE)), C_Q_RANK),
        "c_kv_norm": gain(ks[13], (DEPTH, C_KV_RANK)),
        "c_w_kv_up": dense(ks[14], (DEPTH, C_KV_RANK, C_HEADS * (C_NOPE + C_V)), C_KV_RANK),
        "w_branch_gate": dense(ks[15], (DEPTH, D_MODEL, 3 * D_MODEL), D_MODEL),
        "w_branch_a": dense(ks[16], (DEPTH, A_OUT, D_MODEL), A_OUT),
        "w_branch_b": dense(ks[17], (DEPTH, B_OUT, D_MODEL), B_OUT),
        "w_branch_c": dense(ks[18], (DEPTH, C_OUT, D_MODEL), C_OUT),
        "w_out": dense(ks[19], (DEPTH, D_MODEL, D_MODEL), D_MODEL),
        "ffn_norm": gain(ks[20], (DEPTH, D_MODEL)),
        "w_ffn_gate": dense(ks[21], (DEPTH, D_MODEL, D_FF), D_MODEL),
        "w_ffn_up": dense(ks[22], (DEPTH, D_MODEL, D_FF), D_MODEL),
        "w_ffn_down": dense(ks[23], (DEPTH, D_FF, D_MODEL), D_FF),
        "ple_norm": gain(ks[24], (DEPTH, D_MODEL)),
        "w_ple_proj": dense(ks[25], (DEPTH, PLE_DIM, D_MODEL), PLE_DIM),
        "w_ple_gate": dense(ks[26], (DEPTH, D_MODEL, D_MODEL), D_MODEL),
        "final_norm": gain(ks[27], (D_MODEL,)),
    }


def reference(x, p, positions, mix_norm, w_in, a_cmp_pos_k, a_cmp_w1_k, a_cmp_w2_k, a_cmp_pos_v,
              a_cmp_w1_v, a_cmp_w2_v, b_sinks, c_q_norm, c_w_q_up, c_kv_norm, c_w_kv_up, w_branch_gate,
              w_branch_a, w_branch_b, w_branch_c, w_out, ffn_norm, w_ffn_gate, w_ffn_up, w_ffn_down,
              ple_norm, w_ple_proj, w_ple_gate, final_norm):
    rope64 = rope_tables(positions, HEAD_DIM)
    rope32 = rope_tables(positions, C_ROPE)
    for i in range(DEPTH):
        x = hybrid_layer(x, p[i], rope64, rope32, mix_norm[i], w_in[i], a_cmp_pos_k[i], a_cmp_w1_k[i],
                         a_cmp_w2_k[i], a_cmp_pos_v[i], a_cmp_w1_v[i], a_cmp_w2_v[i], b_sinks[i],
                         c_q_norm[i], c_w_q_up[i], c_kv_norm[i], c_w_kv_up[i], w_branch_gate[i],
                         w_branch_a[i], w_branch_b[i], w_branch_c[i], w_out[i], ffn_norm[i],
                         w_ffn_gate[i], w_ffn_up[i], w_ffn_down[i], ple_norm[i], w_ple_proj[i],
                         w_ple_gate[i])
    return rmsnorm(x, final_norm)
```

```python
import numpy as np
import ml_dtypes
import concourse.bass as bass
import concourse.mybir as mybir
from concourse.bass_utils import run_bass_kernel_spmd

F32 = mybir.dt.float32
BF16 = mybir.dt.bfloat16
I32 = mybir.dt.int32
AF = mybir.ActivationFunctionType
ALU = mybir.AluOpType
AX = mybir.AxisListType

D = 1024
S = 8192
NTOK = 2048
NTILE = 16
NSUP = 4
EPS = 1e-6
DFF = 2816
NEG = -30000.0


class Res:
    __slots__ = ("name", "w", "r", "wdma")

    def __init__(self, name):
        self.name = name
        self.w = None
        self.r = {}


class Prog:
    ENG = ("pe", "act", "dve", "pool", "sp")

    def __init__(self):
        self.nc = bass.Bass("TRN2", target_bir_lowering=False)
        nc = self.nc
        self.cnt = {k: 0 for k in self.ENG}
        self.ops = {k: [] for k in self.ENG}
        self.seen = {k: {} for k in self.ENG}
        self.semobj = {}
        for k in self.ENG:
            self.semobj["s_" + k] = nc.alloc_semaphore("s_" + k)
        self.dsem = {}
        self.free_dsems = []
        self.nres = 0
        self.nname = 0
        self.ncc = 0
        self._rank = None

    def sb(self, shape, dt, name=None):
        self.nname += 1
        return self.nc.alloc_sbuf_tensor(name or f"t{self.nname}", list(shape), dt)

    def ps(self, shape, dt=F32, name=None):
        self.nname += 1
        return self.nc.alloc_psum_tensor(name or f"p{self.nname}", list(shape), dt)

    def res(self, name=None):
        self.nres += 1
        return Res(name or f"r{self.nres}")

    def dram(self, name, shape, dt, kind):
        return self.nc.dram_tensor(name, list(shape), dt, kind=kind)

    def _waits(self, eng, reads, writes, dma_key=None):
        waits = {}

        def need(tok):
            if tok is None:
                return
            s, v = tok
            if waits.get(s, 0) < v:
                waits[s] = v

        for r in reads:
            need(r.w)
        for w in writes:
            if not (dma_key is not None and w.w is not None and w.w[0] == dma_key):
                need(w.w)
            for tok in w.r.values():
                need(tok)
        wl = []
        for s, v in waits.items():
            if eng == "pe" and s == "s_pe":
                continue
            if self.seen[eng].get(s, 0) >= v:
                continue
            self.seen[eng][s] = v
            wl.append((s, v))
        return wl

    def op(self, eng, fn, reads=(), writes=()):
        wl = self._waits(eng, reads, writes)
        self.cnt[eng] += 1
        sname = "s_" + eng
        tok = (sname, self.cnt[eng])
        self.ops[eng].append((wl, fn, (sname, 1)))
        for r in reads:
            r.r[eng] = tok
        for w in writes:
            w.w = tok
            w.r = {}

    def dma(self, q, out, in_, reads=(), writes=(), chan=None, **kw):
        key = (list(writes) + list(reads))[0]
        if key.name not in self.dsem:
            if self.free_dsems:
                self.dsem[key.name] = self.free_dsems.pop()
            else:
                sname = "d_" + key.name
                self.semobj[sname] = self.nc.alloc_semaphore(sname)
                self.dsem[key.name] = [sname, 0]
        d = self.dsem[key.name]
        wl = self._waits(q, reads, writes, dma_key=d[0])
        d[1] += 16
        tok = (d[0], d[1])
        self.ops[q].append((wl, (lambda e: e.dma_start(out=out, in_=(in_() if callable(in_) else in_), **kw)), (d[0], 16)))
        for r in reads:
            r.r["dma:" + key.name] = tok
        for w in writes:
            w.w = tok
            w.wdma = True
            w.r = {}

    def barrier(self):
        allw = [(d[0], d[1]) for d in self.dsem.values() if d[1] > 0]
        for k in self.ENG:
            if self.cnt[k] > 0:
                allw.append(("s_" + k, self.cnt[k]))
        for k in self.ENG:
            wl = []
            for s, v in allw:
                if self.seen[k].get(s, 0) >= v:
                    continue
                self.seen[k][s] = v
                wl.append((s, v))
            if wl:
                self.ops[k].append((wl, None, None))
        self.free_dsems.extend(self.dsem.values())
        self.dsem = {}

    def freg(self, e, val):
        if not hasattr(self, "_fregs"):
            self._fregs = {}
        if val not in self._fregs:
            self._fregs[val] = e.to_reg(float(val))
        return self._fregs[val]

    def rank(self, eng="pool"):
        if self._rank is None:
            self._rank = {}
        if eng not in self._rank:
            et = {"pool": mybir.EngineType.Pool, "sp": mybir.EngineType.SP}[eng]
            self._rank[eng] = self.nc.partition_id([et]) % 4
        return self._rank[eng]

    def allgather(self, ins_ap, outs_ap, rres, reads=(), sem="cc"):
        sname = "s_" + sem
        if sname not in self.semobj:
            self.semobj[sname] = self.nc.alloc_semaphore(sname)
            self.cccnt = getattr(self, "cccnt", {})
            self.cccnt[sname] = 0
        self.cccnt[sname] += 1
        wl = self._waits("pool", list(reads), []) if reads else []
        self.ops["pool"].append((wl, (lambda e: e.collective_compute("AllGather", ALU.bypass, replica_groups=[[0, 1, 2, 3], [4, 5, 6, 7]],
                                                                    ins=[ins_ap], outs=[outs_ap])), (sname, 1)))
        rres.w = (sname, self.cccnt[sname])
        rres.r = {}

    def build(self):
        nc = self.nc
        self.barrier()
        with nc.Block() as block:
            def emit(k):
                def body(e):
                    for wl, fn, inc in self.ops[k]:
                        for s, v in wl:
                            e.wait_ge(self.semobj[s], v)
                        if fn is None:
                            continue
                        ins = fn(e)
                        ins.then_inc(self.semobj[inc[0]], inc[1])
                return body
            block.tensor(emit("pe"))
            block.scalar(emit("act"))
            block.vector(emit("dve"))
            block.gpsimd(emit("pool"))
            block.sync(emit("sp"))
        return nc


class Rot:
    def __init__(self, P, n, shape, dt, psum=False):
        self.t = [(P.ps(shape, dt) if psum else P.sb(shape, dt)) for _ in range(n)]
        self.r = [P.res() for _ in range(n)]
        self.i = -1
        self.n = n

    def next(self):
        self.i = (self.i + 1) % self.n
        return self.t[self.i], self.r[self.i]

    def cur(self):
        return self.t[self.i], self.r[self.i]


W_IN_PERM = np.concatenate([
    np.arange(0, 512), np.arange(512, 640), np.arange(768, 896), np.arange(1024, 1152),
    np.arange(1304, 1816), np.arange(1816, 1944),
    np.arange(640, 768), np.arange(896, 1024), np.arange(1152, 1280), np.arange(1944, 2072),
    np.arange(2072, 2328), np.arange(2328, 2584), np.arange(2584, 2616), np.arange(1280, 1304)])


def const_inputs():
    c = {}
    c["ident"] = np.eye(128, dtype=np.float32)
    f64 = (10000.0 ** (-np.arange(0, 64, 2, dtype=np.float32) / 64)).astype(np.float32)
    f32 = (10000.0 ** (-np.arange(0, 32, 2, dtype=np.float32) / 32)).astype(np.float32)
    c["invf"] = np.ascontiguousarray(np.broadcast_to(np.concatenate([f64, f32])[None, :], (128, 48))).astype(np.float32)
    return c


P1_X = {
    "xqa": ([2, 64, NTOK], BF16), "xqg": ([2, 64, NTOK], BF16), "xka": ([4, 64, NTOK], BF16),
    "xva": ([2, NTOK, 64], BF16), "xqb": ([2, 64, NTOK], BF16), "xkb": ([64, NTOK], BF16),
    "xvb": ([NTOK, 64], BF16), "xqc": ([2, 96, NTOK], BF16), "xkc": ([2, 96, NTOK], BF16),
    "xvc": ([2, NTOK, 64], BF16), "xg": ([NTOK, 6], F32),
}


class Common:
    def __init__(self, P, x_in, pos_in, ident_in, invf_in):
        self.P = P
        nc = P.nc
        self.x = P.sb([128, NTILE, D], F32, "xres")
        self.rx = [P.res() for _ in range(NTILE)]
        for t in range(NTILE):
            P.dma("sp", self.x[:, t, :], x_in[t * 128:(t + 1) * 128, :], writes=[self.rx[t]], chan="xin")
        self.ident = P.sb([128, 128], BF16)
        self.rid = P.res()
        self.cos = P.sb([128, NTILE, 48], F32)
        self.sin = P.sb([128, NTILE, 48], F32)
        self.rcs = P.res()
        mark = nc.sbuf_base
        self.identf = P.sb([128, 128], F32)
        P.dma("sp", self.identf[:], ident_in, writes=[self.rid], chan="c0")
        P.op("dve", lambda e: e.tensor_copy(out=self.ident[:], in_=self.identf[:]), reads=[self.rid], writes=[self.rid])
        pos_i = P.sb([128, NTILE], I32)
        pos_f = P.sb([128, NTILE], F32)
        invf = P.sb([128, 48], F32)
        rp = P.res()
        P.dma("sp", pos_i[:], pos_in, writes=[rp], chan="c0")
        P.dma("sp", invf[:], invf_in, writes=[rp], chan="c0")
        P.op("dve", lambda e: e.tensor_copy(out=pos_f[:], in_=pos_i[:]), reads=[rp], writes=[rp])
        ang = P.sb([128, NTILE, 48], F32)
        ra = P.res()
        for t in range(NTILE):
            P.op("dve", (lambda e, t=t: e.tensor_scalar(out=ang[:, t, :], in0=invf[:], scalar1=pos_f[:, t:t + 1], scalar2=None, op0=ALU.mult)),
                 reads=[rp], writes=[ra])
        tmp = P.sb([128, NTILE, 48], F32)
        ni = P.sb([128, NTILE, 48], I32)
        nf = P.sb([128, NTILE, 48], F32)
        msk = P.sb([128, NTILE, 48], F32)
        C1 = 6.28125
        C2 = 2.0 * np.pi - 6.28125
        PI = float(np.pi)
        TS = lambda **kw: (lambda e: e.tensor_scalar(**kw))
        STT = lambda **kw: (lambda e: e.scalar_tensor_tensor(**kw))
        A2 = lambda ap: ap.rearrange("p t c -> p (t c)")
        P.op("dve", TS(out=A2(ni[:]), in0=A2(ang[:]), scalar1=float(1.0 / (2.0 * np.pi)), scalar2=None, op0=ALU.mult), reads=[ra], writes=[ra])
        P.op("dve", lambda e: e.tensor_copy(out=A2(nf[:]), in_=A2(ni[:])), reads=[ra], writes=[ra])
        P.op("dve", STT(out=A2(tmp[:]), in0=A2(nf[:]), scalar=-C1, in1=A2(ang[:]), op0=ALU.mult, op1=ALU.add), reads=[ra], writes=[ra])
        P.op("dve", STT(out=A2(tmp[:]), in0=A2(nf[:]), scalar=-C2, in1=A2(tmp[:]), op0=ALU.mult, op1=ALU.add), reads=[ra], writes=[ra])
        P.op("dve", TS(out=A2(msk[:]), in0=A2(tmp[:]), scalar1=PI, scalar2=None, op0=ALU.is_gt), reads=[ra], writes=[ra])
        P.op("dve", STT(out=A2(tmp[:]), in0=A2(msk[:]), scalar=-2.0 * PI, in1=A2(tmp[:]), op0=ALU.mult, op1=ALU.add), reads=[ra], writes=[ra])
        P.op("dve", TS(out=A2(msk[:]), in0=A2(tmp[:]), scalar1=-PI, scalar2=None, op0=ALU.is_lt), reads=[ra], writes=[ra])
        P.op("dve", STT(out=A2(tmp[:]), in0=A2(msk[:]), scalar=2.0 * PI, in1=A2(tmp[:]), op0=ALU.mult, op1=ALU.add), reads=[ra], writes=[ra])
        P.op("act", lambda e: e.activation(out=self.sin[:], in_=tmp[:], func=AF.Sin), reads=[ra], writes=[ra, self.rcs])
        P.op("dve", TS(out=A2(tmp[:]), in0=A2(tmp[:]), scalar1=PI / 2.0, scalar2=None, op0=ALU.add), reads=[ra], writes=[ra])
        P.op("dve", TS(out=A2(msk[:]), in0=A2(tmp[:]), scalar1=PI, scalar2=None, op0=ALU.is_gt), reads=[ra], writes=[ra])
        P.op("dve", STT(out=A2(tmp[:]), in0=A2(msk[:]), scalar=-2.0 * PI, in1=A2(tmp[:]), op0=ALU.mult, op1=ALU.add), reads=[ra], writes=[ra])
        P.op("act", lambda e: e.activation(out=self.cos[:], in_=tmp[:], func=AF.Sin), reads=[ra], writes=[ra, self.rcs])
        P.barrier()
        nc.sbuf_base = mark

    def rms_to_T(self, src_fn, rsrc, g_tile, rg, hT, rhT, col0, ncols, scratch):
        pass


def load_w_bf16(P, dst, rdst, w_dram, rows, cols, chan):
    k = rows // 128
    for i in range(k):
        P.dma("pool", dst[:, i, :], w_dram[i * 128:(i + 1) * 128, :], writes=[rdst], chan=chan)


def load_bcast(P, dst, rdst, v_dram, n, chan):
    P.dma("sp", dst[:], v_dram.partition_broadcast(128), writes=[rdst], chan=chan)


def rmsnorm_tile(P, C, src, rsrc, n, g_tile, rg, out_bf, rout, tmp):
    junk, ss, rt = tmp
    P.op("pool", lambda e: e.memset(ss[:, 0:1], 0.0), writes=[rt])
    P.op("act", lambda e: e.activation(out=junk[:, 0:n], in_=src, func=AF.Square, accum_out=ss[:, 0:1]), reads=[rsrc, rt], writes=[rt])
    P.op("dve", lambda e: e.tensor_scalar(out=ss[:, 1:2], in0=ss[:, 0:1], scalar1=1.0 / n, scalar2=EPS, op0=ALU.mult, op1=ALU.add), reads=[rt], writes=[rt])
    P.op("act", lambda e: e.activation(out=ss[:, 2:3], in_=ss[:, 1:2], func=AF.Sqrt), reads=[rt], writes=[rt])
    P.op("dve", lambda e: e.reciprocal(out=ss[:, 3:4], in_=ss[:, 2:3]), reads=[rt], writes=[rt])
    P.op("dve", lambda e: e.scalar_tensor_tensor(out=out_bf, in0=src, scalar=ss[:, 3:4], in1=g_tile, op0=ALU.mult, op1=ALU.mult),
         reads=[rsrc, rt, rg], writes=[rout])


def transpose_chunks(P, C, src_bf, rsrc, nch, pt, rpt, width=128):
    for c in range(nch):
        P.op("pe", (lambda e, c=c: e.transpose(out=pt[0:width, c * 128:(c + 1) * 128], in_=src_bf[:, c * width:(c + 1) * width], identity=C.ident[:])),
             reads=[rsrc, C.rid], writes=[rpt])


def TT(P, eng, out, in0, in1, op, reads, writes):
    P.op(eng, lambda e: e.tensor_tensor(out=out, in0=in0, in1=in1, op=op), reads=reads, writes=writes)


def TS(P, eng, out, in0, s1, s2, op0, op1, reads, writes):
    if op1 is None:
        P.op(eng, lambda e: e.tensor_scalar(out=out, in0=in0, scalar1=s1, scalar2=None, op0=op0), reads=reads, writes=writes)
    else:
        P.op(eng, lambda e: e.tensor_scalar(out=out, in0=in0, scalar1=s1, scalar2=s2, op0=op0, op1=op1), reads=reads, writes=writes)


def STT(P, out, in0, scalar, in1, op0, op1, reads, writes):
    P.op("dve", lambda e: e.scalar_tensor_tensor(out=out, in0=in0, scalar=scalar, in1=in1, op0=op0, op1=op1), reads=reads, writes=writes)


def ACT(P, out, in_, func, reads, writes, **kw):
    P.op("act", lambda e: e.activation(out=out, in_=in_, func=func, **kw), reads=reads, writes=writes)


def MM(P, out, lhsT, rhs, start, stop, reads, writes, skip=False):
    if skip:
        P.op("pe", lambda e: e.matmul(out, lhsT=lhsT, rhs=rhs, start=start, stop=stop, skip_group_check=True), reads=reads, writes=writes)
    else:
        P.op("pe", lambda e: e.matmul(out, lhsT=lhsT, rhs=rhs, start=start, stop=stop), reads=reads, writes=writes)


def TR(P, out, in_, ident, reads, writes):
    P.op("pe", lambda e: e.transpose(out=out, in_=in_, identity=ident), reads=reads, writes=writes)


def MEMSET(P, eng, ap, val, reads, writes):
    P.op(eng, lambda e: e.memset(ap, val), reads=reads, writes=writes)


def cp(P, eng, out, in_, reads, writes):
    if eng == "act":
        P.op("act", lambda e: e.copy(out=out, in_=in_), reads=reads, writes=writes)
    else:
        P.op(eng, lambda e: e.tensor_copy(out=out, in_=in_), reads=reads, writes=writes)


def emit_p1(P, C, io):
    nc = P.nc
    w_in = P.sb([128, 8, 2616], BF16); rw = P.res()
    load_w_bf16(P, w_in, rw, io["w_in"], 1024, 2616, "w1")
    w_qu = P.sb([128, 2, 768], BF16); w_kvu = P.sb([128, 2, 1024], BF16); rwm = P.res()
    load_w_bf16(P, w_qu, rwm, io["w_q_up"], 256, 768, "w1")
    load_w_bf16(P, w_kvu, rwm, io["w_kv_up"], 256, 1024, "w1")
    g_mix = P.sb([128, D], F32); g_q = P.sb([128, 256], F32); g_kv = P.sb([128, 256], F32); rg = P.res()
    load_bcast(P, g_mix, rg, io["mix_norm"], D, "w2")
    load_bcast(P, g_q, rg, io["q_norm"], 256, "w2")
    load_bcast(P, g_kv, rg, io["kv_norm"], 256, "w2")

    junk = P.sb([128, D], BF16)
    ssR = Rot(P, 2, [128, 4], F32)
    hbR = Rot(P, 1, [128, D], BF16)
    hTR = Rot(P, 2, [128, 8, 128], BF16)
    ptR = Rot(P, 2, [128, 1024], BF16, psum=True)
    pzR = Rot(P, 3, [128, 512], F32, psum=True)
    zsR = Rot(P, 1, [128, 2616], F32)
    zs_rc = [[P.res() for _ in range(6)] for _ in range(zsR.n)]
    rqR = Rot(P, 2, [128, 26, 64], BF16)
    tmpA = P.sb([128, 12, 32], F32); tmpB = P.sb([128, 12, 32], F32); rtA = P.res()
    tmpC = P.sb([128, 12, 32], F32); tmpD = P.sb([128, 12, 32], F32); rtC = P.res()
    stq = P.sb([128, 13, 512], BF16); rstq = P.res()
    stv = P.sb([128, 4, 8, 64], BF16); rstv = P.res()
    stqc = P.sb([128, 8, 512], BF16); rstqc = P.res()
    stkc = P.sb([128, 8, 512], BF16); rstkc = P.res()
    stvc = P.sb([128, 4, 8, 64], BF16); rstvc = P.res()
    stg = P.sb([128, 4, 24], F32); rstg = P.res()
    cnR = Rot(P, 1, [128, 512], BF16)
    cnTR = Rot(P, 1, [128, 4, 128], BF16)
    qfR = Rot(P, 1, [128, 8, 96], BF16)
    kfR = Rot(P, 1, [128, 8, 96], BF16)
    kpe = P.sb([128, 32], F32); rkpe = P.res()
    qsb = P.sb([128, 768], F32); rqsb = P.res()
    t16 = [P.sb([128, 8, 16], F32) for _ in range(4)]; rt16 = P.res()
    ktmp = P.sb([128, 4, 16], F32)

    for t in range(NTILE):
        st, tt = t // 4, t % 4
        s0 = st * 512
        xt = C.x[:, t, :]
        ss, rss = ssR.next()
        hb, rhb = hbR.next()
        rmsnorm_tile(P, C, xt, C.rx[t], D, g_mix[:], rg, hb[:], rhb, (junk, ss, rss))
        pt, rpt = ptR.next()
        transpose_chunks(P, C, hb, rhb, 8, pt, rpt)
        hT, rhT = hTR.next()
        P.op("act", lambda e, hT=hT, pt=pt: e.copy(out=hT[:].rearrange("p k t -> p (k t)"), in_=pt[:]), reads=[rpt], writes=[rhT])
        zs, rzs = zsR.next()
        rzc = zs_rc[zsR.i]
        for c in range(6):
            c0 = c * 512
            n = min(512, 2616 - c0)
            pz, rpz = pzR.next()
            for k in range(8):
                P.op("pe", (lambda e, pz=pz, hT=hT, k=k, c0=c0, n=n: e.matmul(pz[:, 0:n], lhsT=hT[:, k, :], rhs=w_in[:, k, c0:c0 + n], start=(k == 0), stop=(k == 7))),
                     reads=[rhT, rw], writes=[rpz])
            P.op("act", (lambda e, pz=pz, zs=zs, c0=c0, n=n: e.copy(out=zs[:, c0:c0 + n], in_=pz[:, 0:n])), reads=[rpz], writes=[rzc[c]])
        if t == 0 and "dbg_zs" in io:
            P.dma("sp", io["dbg_zs"], zs[:], reads=rzc, chan="dbg")
            P.dma("sp", io["dbg_hb"], hb[:], reads=[rhb], chan="dbg")
            P.dma("sp", io["dbg_ss"], ss[:], reads=[rss], chan="dbg")
            P.dma("sp", io["dbg_hT"], hT[:].rearrange("p k t -> p (k t)"), reads=[rhT], chan="dbg")
        rq, rrq = rqR.next()
        zv = zs[:, 0:1536].rearrange("p (h two d) -> p h two d", h=24, two=2)
        rqv = rq[:].rearrange("p h (two d) -> p h two d", two=2)
        cb = C.cos[:, t, 0:32].unsqueeze(1).broadcast_to([128, 12, 32])
        sb_ = C.sin[:, t, 0:32].unsqueeze(1).broadcast_to([128, 12, 32])
        zr = rzc[0:3]
        for hh in range(2):
            hs = slice(hh * 12, hh * 12 + 12)
            x1, x2 = zv[:, hs, 0, :], zv[:, hs, 1, :]
            o1, o2 = rqv[:, hs, 0, :], rqv[:, hs, 1, :]
            P.op("dve", lambda e, x1=x1, cb=cb: e.tensor_tensor(out=tmpA[:], in0=x1, in1=cb, op=ALU.mult), reads=zr + [C.rcs], writes=[rtA])
            P.op("dve", lambda e, x2=x2, sb_=sb_: e.tensor_tensor(out=tmpB[:], in0=x2, in1=sb_, op=ALU.mult), reads=zr + [C.rcs], writes=[rtA])
            P.op("dve", lambda e, o1=o1: e.tensor_tensor(out=o1, in0=tmpA[:], in1=tmpB[:], op=ALU.subtract), reads=[rtA], writes=[rrq])
            P.op("pool", lambda e, x2=x2, cb=cb: e.tensor_tensor(out=tmpC[:], in0=x2, in1=cb, op=ALU.mult), reads=zr + [C.rcs], writes=[rtC])
            P.op("pool", lambda e, x1=x1, sb_=sb_: e.tensor_tensor(out=tmpD[:], in0=x1, in1=sb_, op=ALU.mult), reads=zr + [C.rcs], writes=[rtC])
            P.op("pool", lambda e, o2=o2: e.tensor_tensor(out=o2, in0=tmpC[:], in1=tmpD[:], op=ALU.add), reads=[rtC], writes=[rrq])
        if t == 0 and "dbg_rq" in io:
            P.dma("sp", io["dbg_rq"], rq[:].rearrange("p h d -> p (h d)"), reads=[rrq], chan="dbg")
            P.dma("sp", io["dbg_cs"], C.cos[:, 0, :], reads=[C.rcs], chan="dbg")
            P.dma("sp", io["dbg_sn"], C.sin[:, 0, :], reads=[C.rcs], chan="dbg")
        cp(P, "pool", rq[:, 24:26, :].rearrange("p h d -> p (h d)"), zs[:, 1536:1664], [rzc[3]], [rrq])
        rqf = rq[:].rearrange("p h d -> p (h d)")
        for half, (cs_, ce_) in enumerate(((0, 8), (8, 13))):
            pt2, rpt2 = ptR.next()
            for c in range(cs_, ce_):
                P.op("pe", (lambda e, c=c, pt2=pt2, cs_=cs_, rqf=rqf: e.transpose(out=pt2[:, (c - cs_) * 128:(c - cs_ + 1) * 128], in_=rqf[:, c * 128:(c + 1) * 128], identity=C.ident[:])),
                     reads=[rrq, C.rid], writes=[rpt2])
            nn = ce_ - cs_
            cp(P, "act" if half == 0 else "dve", stq[:, cs_:cs_ + nn, tt * 128:(tt + 1) * 128],
               pt2[:, 0:nn * 128].rearrange("p (c t) -> p c t", c=nn), [rpt2], [rstq])
        P.op("pool", lambda e, zs=zs, tt=tt: e.tensor_copy(out=stv[:, tt, :, :].rearrange("p s d -> p (s d)"), in_=zs[:, 1536:2048]), reads=[rzc[3]], writes=[rstv])
        P.op("act", lambda e, zs=zs, tt=tt: e.activation(out=stg[:, tt, :], in_=zs[:, 2592:2616], func=AF.Sigmoid), reads=[rzc[5]], writes=[rstg])
        cn, rcn = cnR.next()
        ss2, rss2 = ssR.next()
        rmsnorm_tile(P, C, zs[:, 2048:2304], rzc[4], 256, g_q[:], rg, cn[:, 0:256], rcn, (junk, ss2, rss2))
        ss3, rss3 = ssR.next()
        rmsnorm_tile(P, C, zs[:, 2304:2560], rzc[4], 256, g_kv[:], rg, cn[:, 256:512], rcn, (junk, ss3, rss3))
        pt3, rpt3 = ptR.next()
        transpose_chunks(P, C, cn, rcn, 4, pt3, rpt3)
        cnT, rcnT = cnTR.next()
        P.op("dve", lambda e, cnT=cnT, pt3=pt3: e.tensor_copy(out=cnT[:].rearrange("p k t -> p (k t)"), in_=pt3[:, 0:512]), reads=[rpt3], writes=[rcnT])
        qf, rqf_ = qfR.next()
        kf, rkf = kfR.next()
        for (c0, n) in ((0, 512), (512, 256)):
            pz, rpz = pzR.next()
            for k in range(2):
                P.op("pe", (lambda e, pz=pz, cnT=cnT, k=k, c0=c0, n=n: e.matmul(pz[:, 0:n], lhsT=cnT[:, k, :], rhs=w_qu[:, k, c0:c0 + n], start=(k == 0), stop=(k == 1))),
                     reads=[rcnT, rwm], writes=[rpz])
            P.op("act", (lambda e, pz=pz, c0=c0, n=n: e.copy(out=qsb[:, c0:c0 + n], in_=pz[:, 0:n])), reads=[rpz], writes=[rqsb])
        qv = qsb[:].rearrange("p (h d) -> p h d", h=8)
        P.op("pool", lambda e, qf=qf, qv=qv: e.tensor_copy(out=qf[:, :, 0:64], in_=qv[:, :, 0:64]), reads=[rqsb], writes=[rqf_])
        c32 = C.cos[:, t, 32:48].unsqueeze(1).broadcast_to([128, 8, 16])
        s32 = C.sin[:, t, 32:48].unsqueeze(1).broadcast_to([128, 8, 16])
        qx1, qx2 = qv[:, :, 64:80], qv[:, :, 80:96]
        P.op("dve", lambda e, qx1=qx1, c32=c32: e.tensor_tensor(out=t16[0][:], in0=qx1, in1=c32, op=ALU.mult), reads=[rqsb, C.rcs], writes=[rt16])
        P.op("dve", lambda e, qx2=qx2, s32=s32: e.tensor_tensor(out=t16[1][:], in0=qx2, in1=s32, op=ALU.mult), reads=[rqsb, C.rcs], writes=[rt16])
        P.op("dve", lambda e, qx2=qx2, c32=c32: e.tensor_tensor(out=t16[2][:], in0=qx2, in1=c32, op=ALU.mult), reads=[rqsb, C.rcs], writes=[rt16])
        P.op("dve", lambda e, qx1=qx1, s32=s32: e.tensor_tensor(out=t16[3][:], in0=qx1, in1=s32, op=ALU.mult), reads=[rqsb, C.rcs], writes=[rt16])
        P.op("dve", lambda e, qf=qf: e.tensor_tensor(out=qf[:, :, 64:80], in0=t16[0][:], in1=t16[1][:], op=ALU.subtract), reads=[rt16], writes=[rqf_])
        P.op("dve", lambda e, qf=qf: e.tensor_tensor(out=qf[:, :, 80:96], in0=t16[2][:], in1=t16[3][:], op=ALU.add), reads=[rt16], writes=[rqf_])
        kx1, kx2 = zs[:, 2560:2576], zs[:, 2576:2592]
        c16, s16 = C.cos[:, t, 32:48], C.sin[:, t, 32:48]
        P.op("pool", lambda e, kx1=kx1, c16=c16: e.tensor_tensor(out=ktmp[:, 0, :], in0=kx1, in1=c16, op=ALU.mult), reads=[rzc[5], C.rcs], writes=[rkpe])
        P.op("pool", lambda e, kx2=kx2, s16=s16: e.tensor_tensor(out=ktmp[:, 1, :], in0=kx2, in1=s16, op=ALU.mult), reads=[rzc[5], C.rcs], writes=[rkpe])
        P.op("pool", lambda e, kx2=kx2, c16=c16: e.tensor_tensor(out=ktmp[:, 2, :], in0=kx2, in1=c16, op=ALU.mult), reads=[rzc[5], C.rcs], writes=[rkpe])
        P.op("pool", lambda e, kx1=kx1, s16=s16: e.tensor_tensor(out=ktmp[:, 3, :], in0=kx1, in1=s16, op=ALU.mult), reads=[rzc[5], C.rcs], writes=[rkpe])
        P.op("pool", lambda e: e.tensor_tensor(out=kpe[:, 0:16], in0=ktmp[:, 0, :], in1=ktmp[:, 1, :], op=ALU.subtract), reads=[rkpe], writes=[rkpe])
        P.op("pool", lambda e: e.tensor_tensor(out=kpe[:, 16:32], in0=ktmp[:, 2, :], in1=ktmp[:, 3, :], op=ALU.add), reads=[rkpe], writes=[rkpe])
        P.op("pool", lambda e, kf=kf: e.tensor_copy(out=kf[:, :, 64:96], in_=kpe[:].unsqueeze(1).broadcast_to([128, 8, 32])), reads=[rkpe], writes=[rkf])
        for ci, c0 in enumerate((0, 512)):
            pz, rpz = pzR.next()
            for k in range(2):
                P.op("pe", (lambda e, pz=pz, cnT=cnT, k=k, c0=c0: e.matmul(pz[:, 0:512], lhsT=cnT[:, 2 + k, :], rhs=w_kvu[:, k, c0:c0 + 512], start=(k == 0), stop=(k == 1))),
                     reads=[rcnT, rwm], writes=[rpz])
            pv = pz[:, 0:512].rearrange("p (h d) -> p h d", h=4)
            P.op("act", (lambda e, pv=pv, kf=kf, ci=ci: e.copy(out=kf[:, ci * 4:(ci + 1) * 4, 0:64], in_=pv[:, :, 0:64])), reads=[rpz], writes=[rkf])
            P.op("dve", (lambda e, pv=pv, ci=ci, tt=tt: e.tensor_copy(out=stvc[:, tt, ci * 4:(ci + 1) * 4, :], in_=pv[:, :, 64:128])), reads=[rpz], writes=[rstvc])
        for src, rsrc, dst, rdst, eng in ((qf, rqf_, stqc, rstqc, "act"), (kf, rkf, stkc, rstkc, "dve")):
            pt4, rpt4 = ptR.next()
            for h in range(8):
                P.op("pe", (lambda e, h=h, pt4=pt4, src=src: e.transpose(out=pt4[0:96, h * 128:(h + 1) * 128], in_=src[:, h, :], identity=C.ident[:])),
                     reads=[rsrc, C.rid], writes=[rpt4])
            if eng == "act":
                P.op("act", (lambda e, pt4=pt4, dst=dst, tt=tt: e.copy(out=dst[0:96, :, tt * 128:(tt + 1) * 128], in_=pt4[0:96, :].rearrange("p (h t) -> p h t", h=8))), reads=[rpt4], writes=[rdst])
            else:
                P.op("dve", (lambda e, pt4=pt4, dst=dst, tt=tt: e.tensor_copy(out=dst[0:96, :, tt * 128:(tt + 1) * 128], in_=pt4[0:96, :].rearrange("p (h t) -> p h t", h=8))), reads=[rpt4], writes=[rdst])

        if tt == 3:
            sl = slice(s0, s0 + 512)
            for c in range(4):
                P.dma("sp", io["xqa"][c].rearrange("h d t -> (h d) t")[:, sl], stq[:, c, :], reads=[rstq])
                P.dma("sp", io["xqg"][c ^ 1].rearrange("h d t -> (h d) t")[:, sl], stq[:, c, :], reads=[rstq])
                P.dma("sp", io["xqb"][c].rearrange("h d t -> (h d) t")[:, sl], stq[:, 7 + c, :], reads=[rstq])
            for g in range(2):
                for dest in (2 * g, 2 * g + 1):
                    for ty, ch in enumerate((4, 5, 6, 12)):
                        P.dma("sp", io["xka"][dest, ty][:, sl], stq[g * 64:(g + 1) * 64, ch, :], reads=[rstq])
                    P.dma("sp", io["xkb"][dest][:, sl], stq[g * 64:(g + 1) * 64, 11, :], reads=[rstq])
                    for ty in range(2):
                        P.dma("sp", io["xva"][dest, ty][sl, :].rearrange("(tt p) d -> p tt d", p=128), stv[:, :, (ty + 1) * 2 + g, :], reads=[rstv])
                    P.dma("sp", io["xvb"][dest][sl, :].rearrange("(tt p) d -> p tt d", p=128), stv[:, :, 6 + g, :], reads=[rstv])
            for dest in range(4):
                P.dma("sp", io["xqc"][dest].rearrange("h d t -> d h t")[:, :, sl], stqc[0:96, 2 * dest:2 * dest + 2, :], reads=[rstqc], chan="x3")
                P.dma("sp", io["xkc"][dest].rearrange("h d t -> d h t")[:, :, sl], stkc[0:96, 2 * dest:2 * dest + 2, :], reads=[rstkc], chan="x3")
                for hh in range(2):
                    P.dma("sp", io["xvc"][dest, hh][sl, :].rearrange("(tt p) d -> p tt d", p=128), stvc[:, :, 2 * dest + hh, :], reads=[rstvc], chan="x4")
                P.dma("sp", io["xg"][dest][sl, :].rearrange("(tt p) c -> p tt c", p=128), stg[:, :, 6 * dest:6 * dest + 6], reads=[rstg], chan="x4")


P2_IN = {
    "qa": ([4, 2, 64, NTOK], BF16), "qg": ([4, 2, 64, NTOK], BF16), "ka": ([4, 4, 64, NTOK], BF16),
    "va": ([4, 2, NTOK, 64], BF16), "qb": ([4, 2, 64, NTOK], BF16), "kb": ([4, 64, NTOK], BF16),
    "vb": ([4, NTOK, 64], BF16), "qc": ([4, 2, 96, NTOK], BF16), "kc": ([4, 2, 96, NTOK], BF16),
    "vc": ([4, 2, NTOK, 64], BF16), "g": ([4, NTOK, 6], F32),
}
P2_W = {"posk": [128, 16], "w1k": [2048, 256], "w2k": [256, 64], "posv": [128, 16], "w1v": [2048, 256],
        "w2v": [256, 64], "sinks": [1, 2], "selmap": [128, 4, 128]}


def selmap_const():
    n_cmp = 511
    tok = np.arange(n_cmp)[:, None] * 16 + np.arange(32)[None, :]
    sm = np.zeros((512, 128), np.float32)
    np.add.at(sm, (np.repeat(np.arange(n_cmp), 32), (tok // 64).reshape(-1)), 1.0 / 32)
    return np.ascontiguousarray(sm.reshape(4, 128, 128).transpose(1, 0, 2))


class AttnCtx:
    def __init__(self, P, ident, rid):
        self.P = P
        self.ident = ident
        self.rid = rid
        self.S = Rot(P, 3, [128, 512], F32, psum=True)
        self.pT = Rot(P, 3, [128, 512], BF16)
        self.acc = Rot(P, 2, [128, 4, 128], F32, psum=True)


def attn_qgroup(P, A, kT, rkT, Vt, rV, nv, qT, rqT, kbs, scale, accv, racc):
    cover = {qb: [i for i, e in enumerate(kbs) if e[1] <= qb <= e[2]] for qb in range(4)}
    n_kb = len(kbs)
    tiles = [None] * n_kb

    def scores(i):
        kb, lo, hi, segs = kbs[i]
        ps, rps = A.S.next()
        for (q0, q1, extra) in segs:
            c0, c1 = q0 * 128, (q1 + 1) * 128
            n = len(extra)
            MM(P, ps[:, c0:c1], kT[:, kb * 128:(kb + 1) * 128], qT[:, c0:c1], True, n == 0, [rkT, rqT], [rps])
            for j, (l_, r_, rd) in enumerate(extra):
                MM(P, ps[:, c0:c1], l_, r_, False, j == n - 1, rd, [rps])
        tiles[i] = (ps, rps)

    def rest(i):
        kb, lo, hi, segs = kbs[i]
        ps, rps = tiles[i]
        pT, rpT = A.pT.next()
        c0, c1 = lo * 128, (hi + 1) * 128
        ACT(P, pT[:, c0:c1], ps[:, c0:c1], AF.Exp, [rps], [rpT], scale=scale)
        for qb in range(lo, hi + 1):
            MM(P, accv(qb), pT[:, qb * 128:(qb + 1) * 128], Vt(kb), i == 0 and qb == lo, cover[qb][-1] == i, [rpT, rV], [racc], skip=True)

    LOOK = 1
    for i in range(n_kb + LOOK):
        if i < n_kb:
            scores(i)
        if i - LOOK >= 0:
            rest(i - LOOK)


def emit_p2(P, io, ident, rid):
    nc = P.nc
    deps = io.get("deps", {"nsa": [], "swa": [], "mla": []})
    dA, dB, dC = deps["nsa"], deps["swa"], deps["mla"]
    A = AttnCtx(P, ident, rid)
    rc = P.res()
    zero_bf = P.sb([128, 512], BF16)
    ones_bf = P.sb([128, 128], BF16)
    MEMSET(P, "pool", zero_bf[:], 0.0, [], [rc])
    MEMSET(P, "pool", ones_bf[:], 1.0, [], [rc])
    pen_diag = P.sb([128, 128], BF16)
    pen_far = P.sb([128, 128], BF16)
    P.op("pool", lambda e: e.affine_select(out=pen_diag[:], in_=zero_bf[:, 0:128], pattern=[[1, 128]], compare_op=ALU.is_ge, fill=P.freg(e, NEG), base=0, channel_multiplier=-1), reads=[rc], writes=[rc])
    P.op("pool", lambda e: e.affine_select(out=pen_far[:], in_=zero_bf[:, 0:128], pattern=[[-1, 128]], compare_op=ALU.is_gt, fill=P.freg(e, NEG), base=0, channel_multiplier=1), reads=[rc], writes=[rc])
    E = P.sb([128, 64, 128], BF16)
    for j in range(64):
        P.op("pool", (lambda e, j=j: e.affine_select(out=E[:, j, :].rearrange("p (a b) -> p a b", a=2), in_=ones_bf[:].rearrange("p (a b) -> p a b", a=2),
                                                     pattern=[[-1, 2], [0, 64]], compare_op=ALU.is_equal, fill=P.freg(e, 0.0), base=-2 * j, channel_multiplier=1)), reads=[rc], writes=[rc])
    vcmp = P.sb([128, 4, 200], BF16); rvcmp = P.res()
    MEMSET(P, "pool", vcmp[:], 0.0, [], [rvcmp])
    MEMSET(P, "pool", vcmp[:, :, 64:65], 1.0, [], [rvcmp])
    P.dma("pool", vcmp[:, :, 65:193], io["selmap"], writes=[rvcmp])
    kcmpT = P.sb([64, 512], BF16); rkcmp = P.res()
    MEMSET(P, "pool", kcmpT[:], 0.0, [], [rkcmp])
    esink = P.sb([128, 2], F32); resink = P.res()
    P.dma("sp", esink[:], io["sinks"].partition_broadcast(128), writes=[resink])
    ACT(P, esink[:], esink[:], AF.Exp, [resink], [resink])
    mark = nc.sbuf_base
    kT2 = P.sb([128, S], BF16); rkT2 = P.res()
    w1 = P.sb([128, 16, 256], BF16); w2 = P.sb([128, 2, 64], BF16); posT = P.sb([128, 16], BF16); rwc = P.res()
    gT = P.sb([128, 2, 512], BF16); rgT = P.res()
    cb = P.sb([128, 2], F32); rcb = P.res()
    for which, ty in (("k", 0), ("v", 3)):
        for s in range(4):
            P.dma("sp", kT2[0:64, s * NTOK:(s + 1) * NTOK], io["ka"][s, ty], writes=[rkT2], reads=list(dA))
            P.dma("sp", kT2[64:128, s * NTOK:(s + 1) * NTOK - 1], io["ka"][s, ty][:, 1:NTOK], writes=[rkT2], reads=list(dA))
            if s < 3:
                P.dma("sp", kT2[64:128, (s + 1) * NTOK - 1:(s + 1) * NTOK], io["ka"][s + 1, ty][:, 0:1], writes=[rkT2], reads=list(dA), allow_slow_non_contiguous=True)
        load_w_bf16(P, w1, rwc, io["w1" + which], 2048, 256, None)
        load_w_bf16(P, w2, rwc, io["w2" + which], 256, 64, None)
        P.dma("pool", posT[:], io["pos" + which], writes=[rwc])
        kviews = [kT2[:, b0:b0 + 8176].rearrange("p (n s) -> p n s", s=16) for b0 in (0, 16)]
        for hc in range(2):
            ps, rps = A.S.next()
            for lp in range(16):
                MM(P, ps[:, 0:511], w1[:, lp, hc * 128:(hc + 1) * 128], kviews[(2 * lp) // 16][:, :, (2 * lp) % 16], lp == 0, lp == 15, [rwc, rkT2], [rps])
            pb, rpb = A.acc.next()
            for lp in range(16):
                MM(P, pb[:, 0, 0:1], w1[:, lp, hc * 128:(hc + 1) * 128], posT[:, lp:lp + 1], lp == 0, lp == 15, [rwc], [rpb])
            cp(P, "dve", cb[:, hc:hc + 1], pb[:, 0, 0:1], [rpb], [rcb])
            ACT(P, gT[:, hc, 0:511], ps[:, 0:511], AF.Gelu_apprx_tanh, [rps, rcb], [rgT], bias=cb[:, hc:hc + 1])
        if which == "k":
            ps, rps = A.S.next()
            for hc in range(2):
                MM(P, ps[0:64, 0:511], w2[:, hc, :], gT[:, hc, 0:511], hc == 0, hc == 1, [rwc, rgT], [rps])
            cp(P, "dve", kcmpT[:, 0:511], ps[0:64, 0:511], [rps], [rkcmp])
        else:
            for c in range(4):
                nn = 128 if c < 3 else 127
                ps, rps = A.S.next()
                for hc in range(2):
                    MM(P, ps[0:nn, 0:64], gT[:, hc, c * 128:c * 128 + nn], w2[:, hc, :], hc == 0, hc == 1, [rwc, rgT], [rps])
                cp(P, "dve", vcmp[0:nn, c, 0:64], ps[0:nn, 0:64], [rps], [rvcmp])
    P.barrier()
    nc.sbuf_base = mark

    kTa = P.sb([128, S], BF16); rkTa = P.res()
    kTb = P.sb([128, S], BF16); rkTb = P.res()
    Va = P.sb([128, 64, 65], BF16); rVa = P.res()
    Vb = P.sb([128, 64, 65], BF16); rVb = P.res()
    MEMSET(P, "pool", Va[:, :, 64:65], 1.0, [], [rVa])
    MEMSET(P, "pool", Vb[:, :, 64:65], 1.0, [], [rVb])

    def load_kT(dst, rdst, src_fn, dk, dep=()):
        for s in range(4):
            P.dma("sp", dst[0:dk, s * NTOK:(s + 1) * NTOK], src_fn(s), writes=[rdst], reads=list(dep))

    def load_V(dst, rdst, src_fn, dep=()):
        for s in range(4):
            P.dma("sp", dst[:, s * 16:(s + 1) * 16, 0:64], src_fn(s).rearrange("(blk p) d -> p blk d", p=128), writes=[rdst], reads=list(dep))

    qR = Rot(P, 2, [128, 4, 512], BF16)
    gR = Rot(P, 2, [128, 4, 6], F32)
    ostR = Rot(P, 2, [128, 4, 128], BF16)
    oacc = P.sb([128, 2, 4, 64], F32); roacc = [P.res(), P.res()]
    imp = P.sb([128, 4, 128], F32); rimp = P.res()
    rcp = P.sb([128, 8], F32); rrcp = P.res()
    fac = P.sb([128, 8], F32)
    tmpo = P.sb([128, 4, 64], F32); rtmpo = P.res()
    penR = Rot(P, 2, [128, 512], BF16)
    biasR = Rot(P, 2, [128, 128], F32)
    val = P.sb([128, 128], F32); rval = P.res()
    wk = P.sb([128, 128], F32)
    m16 = P.sb([128, 16], F32)
    penq = P.sb([128, 128], BF16); rpenq = P.res()
    penT = P.sb([128, 512], BF16); rpenT = P.res()
    cmpacc = [P.ps([128, 2, 256], F32), P.ps([128, 2, 256], F32)]; rcmpacc = P.res()
    trp = P.ps([128, 512], BF16); rtrp = P.res()

    def out_dma(ost, rost, Gq, col0, ncol):
        src, off = Gq // 4, (Gq % 4) * 512
        for dest in range(1):
            pass
        d = Gq // 4
        if "o_mix" in io:
            m, c0 = col0 // 128, col0 % 128
            P.dma("sp", io["o_mix"](m)[d][off:off + 512, c0:c0 + ncol].rearrange("(qb p) c -> p qb c", p=128), ost[:, :, 0:ncol],
                  reads=[rost], writes=[io["ro"][m]])
        else:
            P.dma("sp", io["o"][d][off:off + 512, col0:col0 + ncol].rearrange("(qb p) c -> p qb c", p=128), ost[:, :, 0:ncol], reads=[rost])

    def finish_branch(accv_t, racc_, h, gcol, g, rg, first, extra_den=None):
        if extra_den is None:
            P.op("dve", lambda e: e.reciprocal(out=rcp[:, 0:4], in_=accv_t[:, :, 64]), reads=[racc_], writes=[rrcp])
        else:
            TS(P, "dve", rcp[:, 4:8], accv_t[:, :, 64], extra_den, None, ALU.add, None, [racc_, resink], [rrcp])
            P.op("dve", lambda e: e.reciprocal(out=rcp[:, 0:4], in_=rcp[:, 4:8]), reads=[rrcp], writes=[rrcp])
        if gcol is not None:
            TT(P, "dve", fac[:, 0:4], rcp[:, 0:4], g[:, :, gcol], ALU.mult, [rrcp, rg], [rrcp])
            f = fac[:, 0:4]
        else:
            f = rcp[:, 0:4]
        fb = f.unsqueeze(2).broadcast_to([128, 4, 64])
        if first:
            TT(P, "dve", oacc[:, h, :, :], accv_t[:, :, 0:64], fb, ALU.mult, [racc_, rrcp], [roacc[h]])
        else:
            TT(P, "dve", tmpo[:], accv_t[:, :, 0:64], fb, ALU.mult, [racc_, rrcp], [rtmpo])
            TT(P, "dve", oacc[:, h, :, :], oacc[:, h, :, :], tmpo[:], ALU.add, [rtmpo], [roacc[h]])

    load_kT(kTa, rkTa, lambda s: io["ka"][s, 1], 64, dA)
    load_kT(kTb, rkTb, lambda s: io["ka"][s, 2], 64, dA)
    load_V(Va, rVa, lambda s: io["va"][s, 0], dA)
    load_V(Vb, rVb, lambda s: io["va"][s, 1], dA)
    for Gq in range(16):
        src, off = Gq // 4, (Gq % 4) * 512
        q4, rq4 = qR.next()
        P.dma("sp", q4[0:64, 0:2, :], io["qa"][src].rearrange("h d t -> d h t")[:, :, off:off + 512], writes=[rq4], reads=list(dA))
        P.dma("sp", q4[0:64, 2:4, :], io["qg"][src].rearrange("h d t -> d h t")[:, :, off:off + 512], writes=[rq4], reads=list(dA))
        g, rg = gR.next()
        P.dma("sp", g[:], io["g"][src][off:off + 512, :].rearrange("(qb p) c -> p qb c", p=128), writes=[rg], reads=list(dA))
        cmax = (32 * Gq + 30) // 128
        pens = {}
        for c in range(cmax + 1):
            if Gq >= 4 * c + 5:
                continue
            pn, rpn = penR.next()
            P.op("pool", (lambda e, pn=pn, c=c, Gq=Gq: e.affine_select(out=pn[:], in_=zero_bf[:], pattern=[[1, 512]], compare_op=ALU.is_ge, fill=P.freg(e, NEG),
                                                                      base=512 * Gq - 2048 * c - 31, channel_multiplier=-16)), reads=[rc], writes=[rpn])
            pens[c] = (pn, rpn)
        for r4 in range(4):
            ctiles = {}

            def cscores(c, r4=r4):
                ps, rps = A.S.next()
                if c in pens:
                    MM(P, ps[:, :], kcmpT[:, c * 128:(c + 1) * 128], q4[0:64, r4, :], True, False, [rkcmp, rq4], [rps])
                    MM(P, ps[:, :], ident[:], pens[c][0][:], False, True, [rid, pens[c][1]], [rps])
                else:
                    MM(P, ps[:, :], kcmpT[:, c * 128:(c + 1) * 128], q4[0:64, r4, :], True, True, [rkcmp, rq4], [rps])
                ctiles[c] = (ps, rps)

            def crest(c):
                ps, rps = ctiles[c]
                pT, rpT = A.pT.next()
                ACT(P, pT[:, :], ps[:, :], AF.Exp, [rps], [rpT], scale=0.125)
                for qb in range(4):
                    MM(P, cmpacc[qb // 2][:, qb % 2, 0:193], pT[:, qb * 128:(qb + 1) * 128], vcmp[:, c, 0:193], c == 0 and qb % 2 == 0, c == cmax, [rpT, rvcmp], [rcmpacc], skip=True)

            for c in range(cmax + 2):
                if c <= cmax:
                    cscores(c)
                if c >= 1:
                    crest(c - 1)
            for half in range(2):
                TS(P, "dve", rcp[:, 4 + 2 * half:6 + 2 * half], cmpacc[half][:, :, 64], 1e-30, None, ALU.max, None, [rcmpacc], [rrcp])
            P.op("dve", lambda e: e.reciprocal(out=rcp[:, 0:4], in_=rcp[:, 4:8]), reads=[rrcp], writes=[rrcp])
            for qb in range(4):
                src_imp = cmpacc[qb // 2][:, qb % 2, 65:193]
                if r4 == 0:
                    TS(P, "dve", imp[:, qb, :], src_imp, rcp[:, qb:qb + 1], None, ALU.mult, None, [rcmpacc, rrcp], [rimp])
                else:
                    STT(P, imp[:, qb, :], src_imp, rcp[:, qb:qb + 1], imp[:, qb, :], ALU.mult, ALU.add, [rcmpacc, rrcp], [rimp])
            if r4 < 2:
                TT(P, "dve", fac[:, 0:4], rcp[:, 0:4], g[:, :, 3 * r4 + 0], ALU.mult, [rrcp, rg], [rrcp])
                for half in range(2):
                    fb = fac[:, 2 * half:2 * half + 2].unsqueeze(2).broadcast_to([128, 2, 64])
                    TT(P, "dve", oacc[:, r4, 2 * half:2 * half + 2, :], cmpacc[half][:, :, 0:64], fb, ALU.mult, [rcmpacc, rrcp], [roacc[r4]])
        for qb in range(4):
            j = 4 * Gq + qb
            bt, rbt = biasR.next()
            MEMSET(P, "pool", bt[:], 0.0, [], [rbt])
            MEMSET(P, "pool", bt[:, 0:1], 1e4, [], [rbt])
            if j >= 1:
                MEMSET(P, "pool", bt[0:64, 2 * j - 1:2 * j + 1], 1e4, [], [rbt])
            MEMSET(P, "pool", bt[64:128, 2 * j:2 * j + 2], 1e4, [], [rbt])
            if 2 * j + 1 < 128:
                MEMSET(P, "pool", bt[0:64, 2 * j + 1:128], -1e30, [], [rbt])
            if 2 * j + 2 < 128:
                MEMSET(P, "pool", bt[64:128, 2 * j + 2:128], -1e30, [], [rbt])
            TT(P, "dve", val[:], imp[:, qb, :], bt[:], ALU.add, [rimp, rbt], [rval])
            P.op("dve", lambda e: e.max(out=m16[:, 0:8], in_=val[:]), reads=[rval], writes=[rval])
            P.op("dve", lambda e: e.match_replace(out=wk[:], in_to_replace=m16[:, 0:8], in_values=val[:], imm_value=-3e38), reads=[rval], writes=[rval])
            P.op("dve", lambda e: e.max(out=m16[:, 8:16], in_=wk[:]), reads=[rval], writes=[rval])
            TS(P, "dve", penq[:], val[:], m16[:, 15:16], NEG, ALU.is_lt, ALU.mult, [rval], [rpenq])
            TR(P, trp[:, qb * 128:(qb + 1) * 128], penq[:], ident[:], [rpenq, rid], [rtrp])
        cp(P, "dve", penT[:], trp[:], [rtrp], [rpenT])
        for h in range(2):
            acc, racc = A.acc.next()
            kbs = []
            for kb in range(4 * Gq + 4):
                ex_sel = lambda q0, q1, kb=kb: (E[:, kb // 1, :], penT[:, q0 * 128:(q1 + 1) * 128], [rc, rpenT])
                if kb < 4 * Gq:
                    kbs.append((kb, 0, 3, [(0, 3, [ex_sel(0, 3)])]))
                else:
                    i = kb - 4 * Gq
                    segs = [(i, i, [ex_sel(i, i), (ident[:], pen_diag[:], [rid, rc])])]
                    if i < 3:
                        segs.append((i + 1, 3, [ex_sel(i + 1, 3)]))
                    kbs.append((kb, i, 3, segs))
            attn_qgroup(P, A, kTa[0:64, :], rkTa, lambda kb: Va[:, kb, :], rVa, 65, q4[0:64, h, :], rq4, kbs, 0.125, lambda qb, acc=acc: acc[:, qb, 0:65], racc)
            finish_branch(acc, racc, h, 3 * h + 1, g, rg, False)
            acc, racc = A.acc.next()
            kbs = []
            for i in range(8):
                kb = 4 * Gq - 4 + i
                if kb < 0:
                    continue
                lo, hi = max(0, i - 4), min(3, i)
                segs = []
                if i <= 3:
                    if lo < i:
                        segs.append((lo, i - 1, []))
                    segs.append((i, i, [(ident[:], pen_far[:], [rid, rc])]))
                else:
                    segs.append((i - 4, i - 4, [(ident[:], pen_diag[:], [rid, rc])]))
                    if i - 4 < hi:
                        segs.append((i - 3, hi, []))
                kbs.append((kb, lo, hi, segs))
            attn_qgroup(P, A, kTb[0:64, :], rkTb, lambda kb: Vb[:, kb, :], rVb, 65, q4[0:64, h, :], rq4, kbs, 0.125, lambda qb, acc=acc: acc[:, qb, 0:65], racc)
            finish_branch(acc, racc, h, 3 * h + 2, g, rg, False)
        ost, rost = ostR.next()
        cp(P, "act", ost[:, :, 0:128].rearrange("p q (h d) -> p h q d", h=2), oacc[:], roacc, [rost])
        out_dma(ost, rost, Gq, 0, 128)

    if "post_mix" in io:
        io["post_mix"](0)
    if "pre_swa" in io:
        io["pre_swa"]()
    load_kT(kTa, rkTa, lambda s: io["kb"][s], 64, dB)
    load_V(Va, rVa, lambda s: io["vb"][s], dB)
    for Gq in range(16):
        src, off = Gq // 4, (Gq % 4) * 512
        q4, rq4 = qR.next()
        P.dma("sp", q4[0:64, 0:2, :], io["qb"][src].rearrange("h d t -> d h t")[:, :, off:off + 512], writes=[rq4], reads=list(dB))
        for h in range(2):
            acc, racc = A.acc.next()
            kbs = []
            for i in range(5):
                kb = 4 * Gq - 1 + i
                if kb < 0:
                    continue
                segs = []
                lo, hi = max(0, i - 1), min(3, i)
                if i <= 3:
                    segs.append((i, i, [(ident[:], pen_far[:], [rid, rc])]))
                if i >= 1:
                    segs.append((i - 1, i - 1, [(ident[:], pen_diag[:], [rid, rc])]))
                segs.sort()
                kbs.append((kb, lo, hi, segs))
            attn_qgroup(P, A, kTa[0:64, :], rkTa, lambda kb: Va[:, kb, :], rVa, 65, q4[0:64, h, :], rq4, kbs, 0.125, lambda qb, acc=acc: acc[:, qb, 0:65], racc)
            finish_branch(acc, racc, h, None, None, None, True, extra_den=esink[:, h:h + 1])
        ost, rost = ostR.next()
        cp(P, "act", ost[:, :, 0:128].rearrange("p q (h d) -> p h q d", h=2), oacc[:], roacc, [rost])
        out_dma(ost, rost, Gq, 128, 128)

    if "post_mix" in io:
        io["post_mix"](1)
    if "pre_mla" in io:
        io["pre_mla"]()
    for h in range(2):
        kT, rkT = (kTa, rkTa) if h == 0 else (kTb, rkTb)
        Vx, rVx = (Va, rVa) if h == 0 else (Vb, rVb)
        load_kT(kT, rkT, lambda s, h=h: io["kc"][s, h], 96, dC)
        load_V(Vx, rVx, lambda s, h=h: io["vc"][s, h], dC)
        for Gq in range(16):
            src, off = Gq // 4, (Gq % 4) * 512
            q4, rq4 = qR.next()
            P.dma("sp", q4[0:96, 0, :], io["qc"][src, h][:, off:off + 512], writes=[rq4], reads=list(dC))
            acc, racc = A.acc.next()
            kbs = []
            for kb in range(4 * Gq + 4):
                if kb < 4 * Gq:
                    kbs.append((kb, 0, 3, [(0, 3, [])]))
                else:
                    i = kb - 4 * Gq
                    segs = [(i, i, [(ident[:], pen_diag[:], [rid, rc])])]
                    if i < 3:
                        segs.append((i + 1, 3, []))
                    kbs.append((kb, i, 3, segs))
            attn_qgroup(P, A, kT[0:96, :], rkT, lambda kb, Vx=Vx: Vx[:, kb, :], rVx, 65, q4[0:96, 0, :], rq4, kbs, 96 ** -0.5, lambda qb, acc=acc: acc[:, qb, 0:65], racc)
            finish_branch(acc, racc, 0, None, None, None, True)
            ost, rost = ostR.next()
            cp(P, "act", ost[:, :, 0:64], oacc[:, 0, :, :], roacc, [rost])
            out_dma(ost, rost, Gq, 256 + 64 * h, 64)
    if "post_mix" in io:
        io["post_mix"](2)


def emit_p3(P, C, io, last):
    nc = P.nc
    base_mark = nc.sbuf_base
    pbase = nc.psum_base
    junk = P.sb([128, D], BF16)
    ssR = Rot(P, 2, [128, 4], F32)
    ptR = Rot(P, 2, [128, 1024], BF16, psum=True)
    pzR = Rot(P, 4, [128, 512], F32, psum=True)
    g_n = P.sb([128, D], F32); rgn = P.res()

    def norm_T(t, hb, rhb, dstT, col0, rdst, eng="act"):
        ss, rss = ssR.next()
        rmsnorm_tile(P, C, C.x[:, t, :], C.rx[t], D, g_n[:], rgn, hb[:], rhb, (junk, ss, rss))
        pt, rpt = ptR.next()
        transpose_chunks(P, C, hb, rhb, 8, pt, rpt)
        cp(P, eng, dstT[:, :, col0:col0 + 128], pt[:].rearrange("p (k t) -> p k t", k=8), [rpt], [rdst])

    markA = nc.sbuf_base
    load_bcast(P, g_n, rgn, io["mix_norm"], D, None)
    wbg = P.sb([128, 8, 3072], BF16); rwA = P.res(); rwAp = P.res(); rwAo = P.res()
    load_w_bf16(P, wbg, rwA, io["w_bg"], 1024, 3072, None)
    wp = P.sb([128, 12, 1024], BF16)
    for i, nm in enumerate(("w_pa", "w_pb", "w_pc")):
        for k in range(4):
            P.dma("pool", wp[:, 4 * i + k, :], io[nm][k * 128:(k + 1) * 128, :], writes=[rwAp])
    wo = P.sb([128, 8, 1024], BF16)
    load_w_bf16(P, wo, rwAo, io["w_out"], 1024, 1024, None)
    hbR = Rot(P, 1, [128, D], BF16)
    hTR = Rot(P, 2, [128, 8, 128], BF16)
    gsb = P.sb([128, 3072], F32); rgsb = P.res()
    otR = Rot(P, 1, [128, 4, 384], BF16)
    oTR = Rot(P, 1, [128, 12, 128], BF16)
    mrg = P.sb([128, D], F32); rmrg = P.res()
    tmpm = P.sb([128, 512], F32); rtmpm = P.res()
    mbR = Rot(P, 1, [128, D], BF16)
    mTR = Rot(P, 1, [128, 8, 128], BF16)
    for t in range(NTILE):
        hb, rhb = hbR.next()
        hT, rhT = hTR.next()
        norm_T(t, hb, rhb, hT, 0, rhT)
        for c in range(6):
            pz, rpz = pzR.next()
            for k in range(8):
                MM(P, pz[:, :], hT[:, k, :], wbg[:, k, c * 512:(c + 1) * 512], k == 0, k == 7, [rhT, rwA], [rpz])
            ACT(P, gsb[:, c * 512:(c + 1) * 512], pz[:, :], AF.Sigmoid, [rpz], [rgsb])
        ot, rot = otR.next()
        if "o_tile3" in io:
            for m in range(3):
                P.dma("sp", ot[:, :, m * 128:(m + 1) * 128], io["o_tile3"](t, m).rearrange("s p c -> p s c"), writes=[rot], reads=list(io["o_dep"]))
        else:
            o_src = io["o_tile"](t) if "o_tile" in io else io["o"][:, t * 128:(t + 1) * 128, :]
            P.dma("sp", ot[:], o_src.rearrange("s p c -> p s c"), writes=[rot])
        oT, roT = oTR.next()
        for half, (a0, a1) in enumerate(((0, 8), (8, 12))):
            pt, rpt = ptR.next()
            for j in range(a0, a1):
                i, s = j // 4, j % 4
                TR(P, pt[:, (j - a0) * 128:(j - a0 + 1) * 128], ot[:, s, i * 128:(i + 1) * 128], C.ident[:], [rot, C.rid], [rpt])
            cp(P, "act" if half == 0 else "dve", oT[:, a0:a1, :], pt[:, 0:(a1 - a0) * 128].rearrange("p (k t) -> p k t", k=a1 - a0), [rpt], [roT])
        for c in range(2):
            cs = slice(c * 512, (c + 1) * 512)
            for i in range(3):
                pz, rpz = pzR.next()
                for k in range(4):
                    MM(P, pz[:, :], oT[:, 4 * i + k, :], wp[:, 4 * i + k, cs], k == 0, k == 3, [roT, rwAp], [rpz])
                gs = gsb[:, i * 1024 + c * 512:i * 1024 + (c + 1) * 512]
                if i == 0:
                    TT(P, "dve", mrg[:, cs], pz[:, :], gs, ALU.mult, [rpz, rgsb], [rmrg])
                else:
                    TT(P, "dve", tmpm[:], pz[:, :], gs, ALU.mult, [rpz, rgsb], [rtmpm])
                    TT(P, "dve", mrg[:, cs], mrg[:, cs], tmpm[:], ALU.add, [rtmpm], [rmrg])
        mb, rmb = mbR.next()
        cp(P, "act", mb[:], mrg[:], [rmrg], [rmb])
        pt, rpt = ptR.next()
        transpose_chunks(P, C, mb, rmb, 8, pt, rpt)
        mT, rmT = mTR.next()
        cp(P, "act", mT[:].rearrange("p k t -> p (k t)"), pt[:], [rpt], [rmT])
        for c in range(2):
            cs = slice(c * 512, (c + 1) * 512)
            pz, rpz = pzR.next()
            for k in range(8):
                MM(P, pz[:, :], mT[:, k, :], wo[:, k, cs], k == 0, k == 7, [rmT, rwAo], [rpz])
            TT(P, "dve", C.x[:, t, cs], C.x[:, t, cs], pz[:, :], ALU.add, [rpz], [C.rx[t]])
    P.barrier()
    nc.sbuf_base = markA

    load_bcast(P, g_n, rgn, io["ffn_norm"], D, None)
    h2T = P.sb([128, 8, NTOK], BF16); rh2T = [P.res() for _ in range(NSUP)]
    hbR = Rot(P, 2, [128, D], BF16)
    for t in range(NTILE):
        hb, rhb = hbR.next()
        norm_T(t, hb, rhb, h2T, t * 128, rh2T[t // 4], eng="act" if t % 2 == 0 else "dve")
    NF = 11
    wg = P.sb([128, 8, NF * 128], BF16); wu = P.sb([128, 8, NF * 128], BF16); wd = P.sb([128, NF, D], BF16); rwB = P.res()
    actT = P.sb([128, NF, 512], BF16); ractT = P.res()
    sgR = Rot(P, 2, [128, 512], F32)
    for grp in range(2):
        f0 = grp * NF * 128
        for k in range(8):
            P.dma("pool", wg[:, k, :], io["w_fg"][k * 128:(k + 1) * 128, f0:f0 + NF * 128], writes=[rwB])
            P.dma("pool", wu[:, k, :], io["w_fu"][k * 128:(k + 1) * 128, f0:f0 + NF * 128], writes=[rwB])
        for f in range(NF):
            P.dma("pool", wd[:, f, :], io["w_fd"][f0 + f * 128:f0 + (f + 1) * 128, :], writes=[rwB])
        for st in range(NSUP):
            ts_ = slice(st * 512, (st + 1) * 512)
            for f in range(NF):
                pg, rpg = pzR.next()
                for k in range(8):
                    MM(P, pg[:, :], wg[:, k, f * 128:(f + 1) * 128], h2T[:, k, ts_], k == 0, k == 7, [rwB, rh2T[st]], [rpg])
                pu, rpu = pzR.next()
                for k in range(8):
                    MM(P, pu[:, :], wu[:, k, f * 128:(f + 1) * 128], h2T[:, k, ts_], k == 0, k == 7, [rwB, rh2T[st]], [rpu])
                sg, rsg = sgR.next()
                ACT(P, sg[:], pg[:, :], AF.Silu, [rpg], [rsg])
                TT(P, "dve", actT[:, f, :], sg[:], pu[:, :], ALU.mult, [rsg, rpu], [ractT])
            for tt in range(4):
                t = st * 4 + tt
                for c in range(2):
                    cs = slice(c * 512, (c + 1) * 512)
                    pz, rpz = pzR.next()
                    for f in range(NF):
                        MM(P, pz[:, :], actT[:, f, tt * 128:(tt + 1) * 128], wd[:, f, cs], f == 0, f == NF - 1, [ractT, rwB], [rpz])
                    TT(P, "dve", C.x[:, t, cs], C.x[:, t, cs], pz[:, :], ALU.add, [rpz], [C.rx[t]])
    P.barrier()
    nc.sbuf_base = markA

    load_bcast(P, g_n, rgn, io["ple_norm"], D, None)
    wpg = P.sb([128, 8, D], BF16); wpp = P.sb([128, 2, D], BF16); rwC = P.res()
    load_w_bf16(P, wpg, rwC, io["w_pg"], 1024, 1024, None)
    load_w_bf16(P, wpp, rwC, io["w_pp"], 256, 1024, None)
    hbR = Rot(P, 2, [128, D], BF16)
    hTR = Rot(P, 2, [128, 8, 128], BF16)
    pfR = Rot(P, 2, [128, 256], BF16)
    pTR = Rot(P, 2, [128, 2, 128], BF16)
    sgR = Rot(P, 2, [128, 512], F32)
    tmpm = P.sb([128, 512], F32); rtmpm = P.res()
    if last:
        g_f = P.sb([128, D], F32); rgf = P.res()
        load_bcast(P, g_f, rgf, io["final_norm"], D, None)
        yR = Rot(P, 2, [128, D], F32)
    for t in range(NTILE):
        hb, rhb = hbR.next()
        hT, rhT = hTR.next()
        norm_T(t, hb, rhb, hT, 0, rhT)
        pf, rpf = pfR.next()
        P.dma("pool", pf[:], io["p"][t * 128:(t + 1) * 128, :], writes=[rpf])
        pt, rpt = ptR.next()
        transpose_chunks(P, C, pf, rpf, 2, pt, rpt)
        pT, rpT = pTR.next()
        cp(P, "dve", pT[:].rearrange("p k t -> p (k t)"), pt[:, 0:256], [rpt], [rpT])
        for c in range(2):
            cs = slice(c * 512, (c + 1) * 512)
            pz, rpz = pzR.next()
            for k in range(8):
                MM(P, pz[:, :], hT[:, k, :], wpg[:, k, cs], k == 0, k == 7, [rhT, rwC], [rpz])
            sg, rsg = sgR.next()
            ACT(P, sg[:], pz[:, :], AF.Sigmoid, [rpz], [rsg])
            pp, rpp = pzR.next()
            for k in range(2):
                MM(P, pp[:, :], pT[:, k, :], wpp[:, k, cs], k == 0, k == 1, [rpT, rwC], [rpp])
            TT(P, "dve", tmpm[:], sg[:], pp[:, :], ALU.mult, [rsg, rpp], [rtmpm])
            TT(P, "dve", C.x[:, t, cs], C.x[:, t, cs], tmpm[:], ALU.add, [rtmpm], [C.rx[t]])
        if last:
            ss, rss = ssR.next()
            y, ry = yR.next()
            MEMSET(P, "pool", ss[:, 0:1], 0.0, [], [rss])
            ACT(P, junk[:], C.x[:, t, :], AF.Square, [C.rx[t], rss], [rss], accum_out=ss[:, 0:1])
            TS(P, "dve", ss[:, 1:2], ss[:, 0:1], 1.0 / D, EPS, ALU.mult, ALU.add, [rss], [rss])
            ACT(P, ss[:, 2:3], ss[:, 1:2], AF.Sqrt, [rss], [rss])
            P.op("dve", lambda e, ss=ss: e.reciprocal(out=ss[:, 3:4], in_=ss[:, 2:3]), reads=[rss], writes=[rss])
            STT(P, y[:], C.x[:, t, :], ss[:, 3:4], g_f[:], ALU.mult, ALU.mult, [C.rx[t], rss, rgf], [ry])
            P.dma("sp", io["y"][t * 128:(t + 1) * 128, :], y[:], reads=[ry])
    P.barrier()
    nc.sbuf_base = base_mark
    nc.psum_base = pbase


P1_WNAMES = {"w_in": [D, 2616], "mix_norm": [1, D], "q_norm": [1, 256], "kv_norm": [1, 256], "w_q_up": [256, 768], "w_kv_up": [256, 1024]}
P3_WNAMES = {"mix_norm": [1, D], "w_bg": [D, 3072], "w_pa": [512, D], "w_pb": [512, D], "w_pc": [512, D], "w_out": [D, D],
             "ffn_norm": [1, D], "w_fg": [D, DFF], "w_fu": [D, DFF], "w_fd": [DFF, D], "ple_norm": [1, D], "w_pg": [D, D],
             "w_pp": [256, D], "p": [NTOK, 256]}


def build_tok_program(do_p3, do_p1, last):
    P = Prog()
    x_in = P.dram("x_in", [NTOK, D], F32, "ExternalInput").ap()
    pos_in = P.dram("pos_in", [128, NTILE], I32, "ExternalInput").ap()
    ident = P.dram("ident", [128, 128], F32, "ExternalInput").ap()
    invf = P.dram("invf", [128, 48], F32, "ExternalInput").ap()
    C = Common(P, x_in, pos_in, ident, invf)
    if do_p3:
        io = {}
        for k, shp in P3_WNAMES.items():
            io[k] = P.dram("p3_" + k, shp, F32, "ExternalInput").ap()
        io["o"] = P.dram("p3_o", [4, NTOK, 384], BF16, "ExternalInput").ap()
        if last:
            io["final_norm"] = P.dram("p3_final_norm", [1, D], F32, "ExternalInput").ap()
            io["y"] = P.dram("y", [NTOK, D], F32, "ExternalOutput").ap()
        emit_p3(P, C, io, last)
        if not last:
            x_out = P.dram("x_out", [NTOK, D], F32, "ExternalOutput").ap()
            for t in range(NTILE):
                P.dma("sp", x_out[t * 128:(t + 1) * 128, :], C.x[:, t, :], reads=[C.rx[t]])
    if do_p1:
        io = {}
        for k, shp in P1_WNAMES.items():
            io[k] = P.dram("p1_" + k, shp, F32, "ExternalInput").ap()
        for k, (shp, dt) in P1_X.items():
            io[k] = P.dram(k, [4] + shp, dt, "ExternalOutput").ap()
        emit_p1(P, C, io)
    return P.build()


def build_p2_program():
    P = Prog()
    io = {}
    for k, (shp, dt) in P2_IN.items():
        io[k] = P.dram(k, shp, dt, "ExternalInput").ap()
    for k, shp in P2_W.items():
        io[k] = P.dram(k, shp, F32, "ExternalInput").ap()
    identd = P.dram("ident", [128, 128], F32, "ExternalInput").ap()
    io["o"] = P.dram("o", [4, NTOK, 384], BF16, "ExternalOutput").ap()
    identf = P.sb([128, 128], F32)
    ident = P.sb([128, 128], BF16)
    rid = P.res()
    P.dma("sp", identf[:], identd, writes=[rid])
    cp(P, "dve", ident[:], identf[:], [rid], [rid])
    emit_p2(P, io, ident, rid)
    return P.build()


X1_TO_P2 = {"xqa": "qa", "xqg": "qg", "xka": "ka", "xva": "va", "xqb": "qb", "xkb": "kb", "xvb": "vb",
            "xqc": "qc", "xkc": "kc", "xvc": "vc", "xg": "g"}


def p1_weights(inp, l):
    m = {}
    m["p1_w_in"] = np.ascontiguousarray(inp["w_in"][l][:, W_IN_PERM])
    m["p1_mix_norm"] = np.ascontiguousarray(inp["mix_norm"][l][None, :])
    m["p1_q_norm"] = np.ascontiguousarray(inp["c_q_norm"][l][None, :])
    m["p1_kv_norm"] = np.ascontiguousarray(inp["c_kv_norm"][l][None, :])
    m["p1_w_q_up"] = np.ascontiguousarray(inp["c_w_q_up"][l])
    m["p1_w_kv_up"] = np.ascontiguousarray(inp["c_w_kv_up"][l])
    return m


def p2_weights(inp, l, r):
    m = {}
    for w in ("k", "v"):
        m["pos" + w] = np.ascontiguousarray(inp[f"a_cmp_pos_{w}"][l].reshape(16, 128).T)
        m["w1" + w] = np.ascontiguousarray(inp[f"a_cmp_w1_{w}"][l])
        m["w2" + w] = np.ascontiguousarray(inp[f"a_cmp_w2_{w}"][l])
    m["sinks"] = np.ascontiguousarray(inp["b_sinks"][l][2 * r:2 * r + 2][None, :])
    m["selmap"] = selmap_const()
    m["ident"] = np.eye(128, dtype=np.float32)
    return m


def p3_weights(inp, l, b, r, last):
    m = {}
    src = {"mix_norm": "mix_norm", "w_bg": "w_branch_gate", "w_pa": "w_branch_a", "w_pb": "w_branch_b", "w_pc": "w_branch_c",
           "w_out": "w_out", "ffn_norm": "ffn_norm", "w_fg": "w_ffn_gate", "w_fu": "w_ffn_up", "w_fd": "w_ffn_down",
           "ple_norm": "ple_norm", "w_pg": "w_ple_gate", "w_pp": "w_ple_proj"}
    for k, s in src.items():
        a = inp[s][l]
        m["p3_" + k] = np.ascontiguousarray(a[None, :] if a.ndim == 1 else a)
    m["p3_p"] = np.ascontiguousarray(inp["p"][l, b, r * NTOK:(r + 1) * NTOK])
    if last:
        m["p3_final_norm"] = np.ascontiguousarray(inp["final_norm"][None, :])
    return m


def all_to_all(outs, names):
    res = []
    for core in range(8):
        b, r = divmod(core, 4)
        res.append({nm: np.ascontiguousarray(np.stack([outs[4 * b + s][nm][r] for s in range(4)], axis=0)) for nm in names})
    return res


X1_LAYOUT = [("xqa", [2, 64, NTOK], 0, 0), ("xqg", [2, 64, NTOK], 0, 128), ("xka", [4, 64, NTOK], 1, 0),
             ("xva", [2, NTOK, 64], 2, 0), ("xqb", [2, 64, NTOK], 2, 128), ("xkb", [64, NTOK], 3, 0),
             ("xvb", [NTOK, 64], 3, 64), ("xqc", [2, 96, NTOK], 4, 0), ("xkc", [2, 96, NTOK], 5, 0),
             ("xvc", [2, NTOK, 64], 6, 0)]
X1_K = 7
X1_CR = 256


def x1_views(rows):
    views = {}
    for nm, shp, k, r0 in X1_LAYOUT:
        n = int(np.prod(shp)) // 2048
        v = rows(k, r0, n)
        if nm in ("xqa", "xqg", "xka", "xqb", "xqc", "xkc"):
            v = v.rearrange("e (h d) t -> e h d t", h=shp[0])
        elif nm in ("xva", "xvc"):
            v = v.rearrange("e r c -> e (r c)").rearrange("e (h t d) -> e h t d", h=shp[0], d=64)
        elif nm == "xvb":
            v = v.rearrange("e r c -> e (r c)").rearrange("e (t d) -> e t d", d=64)
        views[nm] = v
    return views


def build_fused_program():
    P = Prog()
    nc = P.nc
    x_in = P.dram("x_in", [NTOK, D], F32, "ExternalInput").ap()
    pos_in = P.dram("pos_in", [128, NTILE], I32, "ExternalInput").ap()
    ident = P.dram("ident", [128, 128], F32, "ExternalInput").ap()
    invf = P.dram("invf", [128, 48], F32, "ExternalInput").ap()
    y_out = P.dram("y", [NTOK, D], F32, "ExternalOutput").ap()
    RD = X1_K * X1_CR
    X1 = P.dram("ex_x1", [4 * RD, 2048], BF16, "Internal").ap()
    G1 = P.dram("ex_g1", [16 * RD, 2048], BF16, "Internal").ap()
    M1 = P.dram("ex_m1", [4 * RD, 2048], BF16, "Internal").ap()
    XG = P.dram("ex_xg", [4 * 16, 768], F32, "Internal").ap()
    GG = P.dram("ex_gg", [16 * 16, 768], F32, "Internal").ap()
    MG = P.dram("ex_mg", [4 * 16, 768], F32, "Internal").ap()
    O2 = P.dram("ex_o2", [12 * NTOK, 128], BF16, "Internal").ap()
    GO = P.dram("ex_go", [48 * NTOK, 128], BF16, "Internal").ap()
    MO = P.dram("ex_mo", [12 * NTOK, 128], BF16, "Internal").ap()
    C = Common(P, x_in, pos_in, ident, invf)
    mark, pmark = nc.sbuf_base, nc.psum_base

    def phase_end():
        P.barrier()
        nc.sbuf_base = mark
        nc.psum_base = pmark

    def exchange_chunked(src, gath, mine, nchunk, cr):
        P.barrier()
        rc_ = P.res()
        for j in range(4 * nchunk):
            P.allgather(src[j * cr:(j + 1) * cr, :], gath[j * 4 * cr:(j + 1) * 4 * cr, :], rc_)
        rm_ = P.res()
        g3 = gath.rearrange("(d x) c -> d x c", d=4)
        P.dma("pool", mine, (lambda: g3[bass.ds(P.rank(), 1), :, :]), reads=[rc_], writes=[rm_])
        P.barrier()

    def exchange_small(src, gath, mine):
        P.barrier()
        rc_ = P.res()
        P.allgather(src, gath, rc_)
        rm_ = P.res()
        g4 = gath.rearrange("(s d r) c -> s d r c", s=4, d=4)
        m3 = mine.rearrange("(s r) c -> s r c", s=4)
        P.dma("pool", m3, (lambda: g4[:, bass.ds(P.rank(), 1), :, :]), reads=[rc_], writes=[rm_])
        P.barrier()

    X1v = X1.rearrange("(e r) c -> e r c", e=4)
    M1v = M1.rearrange("(k s i) c -> k s i c", k=X1_K, s=4)
    MOv = MO.rearrange("(k s i) c -> k s i c", k=2, s=4)
    for l in range(2):
        last = l == 1
        io = {}
        for k, shp in P1_WNAMES.items():
            io[k] = P.dram(f"l{l}_p1_{k}", shp, F32, "ExternalInput").ap()
        io.update(x1_views(lambda k, r0, n: X1v[:, k * X1_CR + r0:k * X1_CR + r0 + n, :]))
        io["xg"] = XG.rearrange("(e a) (p c) -> e (a p) c", e=4, c=6)
        emit_p1(P, C, io)
        P.barrier()
        g3 = G1.rearrange("(d x) c -> d x c", d=4)
        groups = {"nsa": (0, 3), "swa": (3, 4), "mla": (4, 7)}
        rcg, rmg = {}, {}
        for gname, (k0, k1) in groups.items():
            rcg[gname] = P.res()
            rmg[gname] = P.res()
            for d in range(4):
                for k in range(k0, k1):
                    j = d * X1_K + k
                    P.allgather(X1[j * X1_CR:(j + 1) * X1_CR, :], G1[j * 4 * X1_CR:(j + 1) * 4 * X1_CR, :], rcg[gname], sem="cc_" + gname)
            if gname == "nsa":
                rcg["g"] = P.res()
                rmg["g"] = P.res()
                P.allgather(XG, GG, rcg["g"], sem="cc_g")

        def select(gname):
            k0, k1 = groups[gname]
            r0, r1 = k0 * 4 * X1_CR, k1 * 4 * X1_CR
            P.dma("sp", M1[r0:r1, :], (lambda: g3[bass.ds(P.rank("sp"), 1), r0:r1, :]), reads=[rcg[gname]], writes=[rmg[gname]])

        select("nsa")
        gg4 = GG.rearrange("(s d r) c -> s d r c", s=4, d=4)
        P.dma("sp", MG.rearrange("(s r) c -> s r c", s=4), (lambda: gg4[:, bass.ds(P.rank("sp"), 1), :, :]), reads=[rcg["g"]], writes=[rmg["g"]])
        nc.sbuf_base = mark
        nc.psum_base = pmark
        io = {}
        v = x1_views(lambda k, r0, n: M1v[k][:, r0:r0 + n, :])
        for k, vv in v.items():
            io[X1_TO_P2[k]] = vv
        io["g"] = MG.rearrange("(e a) (p c) -> e (a p) c", e=4, c=6)
        io["deps"] = {"nsa": [rmg["nsa"], rmg["g"]], "swa": [rmg["swa"]], "mla": [rmg["mla"]]}
        io["pre_swa"] = lambda: select("swa")
        io["pre_mla"] = lambda: select("mla")
        for k, shp in P2_W.items():
            if k == "selmap":
                if l == 0:
                    selmap_ap = P.dram("selmap", shp, F32, "ExternalInput").ap()
                io[k] = selmap_ap
            else:
                io[k] = P.dram(f"l{l}_p2_{k}", shp, F32, "ExternalInput").ap()
        O2v = O2.rearrange("(m e t) c -> m e t c", m=3, e=4)
        ro = [P.res(), P.res(), P.res()]
        rco = P.res()
        io["o_mix"] = lambda m: O2v[m]
        io["ro"] = ro

        def post_mix(m, ro=ro, rco=rco):
            for d in range(4):
                P.allgather(O2[(m * 4 + d) * NTOK:(m * 4 + d + 1) * NTOK, :], GO[((d * 3 + m) * 4) * NTOK:((d * 3 + m) * 4 + 4) * NTOK, :], rco,
                            reads=[ro[m]] if d == 0 else (), sem="cc_o")

        io["post_mix"] = post_mix
        emit_p2(P, io, C.ident, C.rid)
        P.barrier()
        nc.sbuf_base = mark
        nc.psum_base = pmark
        rmo = P.res()
        go3 = GO.rearrange("(d x) c -> d x c", d=4)
        P.dma("sp", MO, (lambda: go3[bass.ds(P.rank("sp"), 1), :, :]), reads=[rco], writes=[rmo])
        MOv = MO.rearrange("(m s t) c -> m s t c", m=3, s=4)
        io = {}
        for k, shp in P3_WNAMES.items():
            io[k] = P.dram(f"l{l}_p3_{k}", shp, F32, "ExternalInput").ap()
        io["o_tile3"] = lambda t, m: MOv[m][:, t * 128:(t + 1) * 128, :]
        io["o_dep"] = [rmo]
        if last:
            io["final_norm"] = P.dram("final_norm", [1, D], F32, "ExternalInput").ap()
            io["y"] = y_out
        emit_p3(P, C, io, last)
        phase_end()
    return P.build()


_PROGS = {}


def _prog(key):
    if key not in _PROGS:
        if key == "fused":
            _PROGS[key] = build_fused_program()
        elif key == "p2":
            _PROGS[key] = build_p2_program()
        else:
            _PROGS[key] = build_tok_program(*key)
    return _PROGS[key]


def kernel(**inp):
    inp = {k: np.asarray(v) for k, v in inp.items()}
    cst = const_inputs()
    cores = list(range(8))
    maps = []
    for c in cores:
        b, r = divmod(c, 4)
        m = dict(cst)
        m["x_in"] = np.ascontiguousarray(inp["x"][b, r * NTOK:(r + 1) * NTOK]).astype(np.float32)
        m["pos_in"] = np.ascontiguousarray(inp["positions"][b, r * NTOK:(r + 1) * NTOK].reshape(NTILE, 128).T.astype(np.int32))
        m["selmap"] = selmap_const()
        m["final_norm"] = np.ascontiguousarray(inp["final_norm"][None, :])
        for l in range(2):
            for k, v in p1_weights(inp, l).items():
                m[f"l{l}_{k}"] = v
            for k, v in p2_weights(inp, l, r).items():
                if k not in ("selmap", "ident"):
                    m[f"l{l}_p2_{k}"] = v
            for k, v in p3_weights(inp, l, b, r, False).items():
                m[f"l{l}_{k}"] = v
        maps.append(m)
    res = run_bass_kernel_spmd(_prog("fused"), maps, core_ids=cores).results
    y = np.stack([np.concatenate([np.asarray(res[4 * b + r]["y"]) for r in range(4)], axis=0) for b in range(2)], axis=0)
    return y.astype(np.float32)
```

```python
import numpy as np
import ml_dtypes
import concourse.bass as bass
import concourse.mybir as mybir
from concourse.bass_utils import run_bass_kernel_spmd

F32 = mybir.dt.float32
BF16 = mybir.dt.bfloat16
I32 = mybir.dt.int32
AF = mybir.ActivationFunctionType
ALU = mybir.AluOpType
AX = mybir.AxisListType

D = 1024
S = 8192
NTOK = 2048
NTILE = 16
NSUP = 4
EPS = 1e-6
DFF = 2816
NEG = -30000.0


class Res:
    __slots__ = ("name", "w", "r", "wdma")

    def __init__(self, name):
        self.name = name
        self.w = None
        self.r = {}


class Prog:
    ENG = ("pe", "act", "dve", "pool", "sp")

    def __init__(self):
        self.nc = bass.Bass("TRN2", target_bir_lowering=False)
        nc = self.nc
        self.cnt = {k: 0 for k in self.ENG}
        self.ops = {k: [] for k in self.ENG}
        self.seen = {k: {} for k in self.ENG}
        self.semobj = {}
        for k in self.ENG:
            self.semobj["s_" + k] = nc.alloc_semaphore("s_" + k)
        self.dsem = {}
        self.free_dsems = []
        self.nres = 0
        self.nname = 0
        self.ncc = 0
        self._rank = None

    def sb(self, shape, dt, name=None):
        self.nname += 1
        return self.nc.alloc_sbuf_tensor(name or f"t{self.nname}", list(shape), dt)

    def ps(self, shape, dt=F32, name=None):
        self.nname += 1
        return self.nc.alloc_psum_tensor(name or f"p{self.nname}", list(shape), dt)

    def res(self, name=None):
        self.nres += 1
        return Res(name or f"r{self.nres}")

    def dram(self, name, shape, dt, kind):
        return self.nc.dram_tensor(name, list(shape), dt, kind=kind)

    def _waits(self, eng, reads, writes, dma_key=None):
        waits = {}

        def need(tok):
            if tok is None:
                return
            s, v = tok
            if waits.get(s, 0) < v:
                waits[s] = v

        for r in reads:
            need(r.w)
        for w in writes:
            if not (dma_key is not None and w.w is not None and w.w[0] == dma_key):
                need(w.w)
            for tok in w.r.values():
                need(tok)
        wl = []
        for s, v in waits.items():
            if eng == "pe" and s == "s_pe":
                continue
            if self.seen[eng].get(s, 0) >= v:
                continue
            self.seen[eng][s] = v
            wl.append((s, v))
        return wl

    def op(self, eng, fn, reads=(), writes=()):
        wl = self._waits(eng, reads, writes)
        self.cnt[eng] += 1
        sname = "s_" + eng
        tok = (sname, self.cnt[eng])
        self.ops[eng].append((wl, fn, (sname, 1)))
        for r in reads:
            r.r[eng] = tok
        for w in writes:
            w.w = tok
            w.r = {}

    def dma(self, q, out, in_, reads=(), writes=(), chan=None, **kw):
        key = (list(writes) + list(reads))[0]
        if key.name not in self.dsem:
            if self.free_dsems:
                self.dsem[key.name] = self.free_dsems.pop()
            else:
                sname = "d_" + key.name
                self.semobj[sname] = self.nc.alloc_semaphore(sname)
                self.dsem[key.name] = [sname, 0]
        d = self.dsem[key.name]
        wl = self._waits(q, reads, writes, dma_key=d[0])
        d[1] += 16
        tok = (d[0], d[1])
        self.ops[q].append((wl, (lambda e: e.dma_start(out=out, in_=(in_() if callable(in_) else in_), **kw)), (d[0], 16)))
        for r in reads:
            r.r["dma:" + key.name] = tok
        for w in writes:
            w.w = tok
            w.wdma = True
            w.r = {}

    def barrier(self):
        allw = [(d[0], d[1]) for d in self.dsem.values() if d[1] > 0]
        for k in self.ENG:
            if self.cnt[k] > 0:
                allw.append(("s_" + k, self.cnt[k]))
        for k in self.ENG:
            wl = []
            for s, v in allw:
                if self.seen[k].get(s, 0) >= v:
                    continue
                self.seen[k][s] = v
                wl.append((s, v))
            if wl:
                self.ops[k].append((wl, None, None))
        self.free_dsems.extend(self.dsem.values())
        self.dsem = {}

    def freg(self, e, val):
        if not hasattr(self, "_fregs"):
            self._fregs = {}
        if val not in self._fregs:
            self._fregs[val] = e.to_reg(float(val))
        return self._fregs[val]

    def rank(self, eng="pool"):
        if self._rank is None:
            self._rank = {}
        if eng not in self._rank:
            et = {"pool": mybir.EngineType.Pool, "sp": mybir.EngineType.SP}[eng]
            self._rank[eng] = self.nc.partition_id([et]) % 4
        return self._rank[eng]

    def allgather(self, ins_ap, outs_ap, rres, reads=(), sem="cc"):
        sname = "s_" + sem
        if sname not in self.semobj:
            self.semobj[sname] = self.nc.alloc_semaphore(sname)
            self.cccnt = getattr(self, "cccnt", {})
            self.cccnt[sname] = 0
        self.cccnt[sname] += 1
        wl = self._waits("pool", list(reads), []) if reads else []
        self.ops["pool"].append((wl, (lambda e: e.collective_compute("AllGather", ALU.bypass, replica_groups=[[0, 1, 2, 3], [4, 5, 6, 7]],
                                                                    ins=[ins_ap], outs=[outs_ap])), (sname, 1)))
        rres.w = (sname, self.cccnt[sname])
        rres.r = {}

    def build(self):
        nc = self.nc
        self.barrier()
        with nc.Block() as block:
            def emit(k):
                def body(e):
                    for wl, fn, inc in self.ops[k]:
                        for s, v in wl:
                            e.wait_ge(self.semobj[s], v)
                        if fn is None:
                            continue
                        ins = fn(e)
                        ins.then_inc(self.semobj[inc[0]], inc[1])
                return body
            block.tensor(emit("pe"))
            block.scalar(emit("act"))
            block.vector(emit("dve"))
            block.gpsimd(emit("pool"))
            block.sync(emit("sp"))
        return nc


class Rot:
    def __init__(self, P, n, shape, dt, psum=False):
        self.t = [(P.ps(shape, dt) if psum else P.sb(shape, dt)) for _ in range(n)]
        self.r = [P.res() for _ in range(n)]
        self.i = -1
        self.n = n

    def next(self):
        self.i = (self.i + 1) % self.n
        return self.t[self.i], self.r[self.i]

    def cur(self):
        return self.t[self.i], self.r[self.i]


W_IN_PERM = np.concatenate([
    np.arange(0, 512), np.arange(512, 640), np.arange(768, 896), np.arange(1024, 1152),
    np.arange(1304, 1816), np.arange(1816, 1944),
    np.arange(640, 768), np.arange(896, 1024), np.arange(1152, 1280), np.arange(1944, 2072),
    np.arange(2072, 2328), np.arange(2328, 2584), np.arange(2584, 2616), np.arange(1280, 1304)])


def const_inputs():
    c = {}
    c["ident"] = np.eye(128, dtype=np.float32)
    f64 = (10000.0 ** (-np.arange(0, 64, 2, dtype=np.float32) / 64)).astype(np.float32)
    f32 = (10000.0 ** (-np.arange(0, 32, 2, dtype=np.float32) / 32)).astype(np.float32)
    c["invf"] = np.ascontiguousarray(np.broadcast_to(np.concatenate([f64, f32])[None, :], (128, 48))).astype(np.float32)
    return c


P1_X = {
    "xqa": ([2, 64, NTOK], BF16), "xqg": ([2, 64, NTOK], BF16), "xka": ([4, 64, NTOK], BF16),
    "xva": ([2, NTOK, 64], BF16), "xqb": ([2, 64, NTOK], BF16), "xkb": ([64, NTOK], BF16),
    "xvb": ([NTOK, 64], BF16), "xqc": ([2, 96, NTOK], BF16), "xkc": ([2, 96, NTOK], BF16),
    "xvc": ([2, NTOK, 64], BF16), "xg": ([NTOK, 6], F32),
}


class Common:
    def __init__(self, P, x_in, pos_in, ident_in, invf_in):
        self.P = P
        nc = P.nc
        self.x = P.sb([128, NTILE, D], F32, "xres")
        self.rx = [P.res() for _ in range(NTILE)]
        for t in range(NTILE):
            P.dma("sp", self.x[:, t, :], x_in[t * 128:(t + 1) * 128, :], writes=[self.rx[t]], chan="xin")
        self.ident = P.sb([128, 128], BF16)
        self.rid = P.res()
        self.cos = P.sb([128, NTILE, 48], F32)
        self.sin = P.sb([128, NTILE, 48], F32)
        self.rcs = P.res()
        mark = nc.sbuf_base
        self.identf = P.sb([128, 128], F32)
        P.dma("sp", self.identf[:], ident_in, writes=[self.rid], chan="c0")
        P.op("dve", lambda e: e.tensor_copy(out=self.ident[:], in_=self.identf[:]), reads=[self.rid], writes=[self.rid])
        pos_i = P.sb([128, NTILE], I32)
        pos_f = P.sb([128, NTILE], F32)
        invf = P.sb([128, 48], F32)
        rp = P.res()
        P.dma("sp", pos_i[:], pos_in, writes=[rp], chan="c0")
        P.dma("sp", invf[:], invf_in, writes=[rp], chan="c0")
        P.op("dve", lambda e: e.tensor_copy(out=pos_f[:], in_=pos_i[:]), reads=[rp], writes=[rp])
        ang = P.sb([128, NTILE, 48], F32)
        ra = P.res()
        for t in range(NTILE):
            P.op("dve", (lambda e, t=t: e.tensor_scalar(out=ang[:, t, :], in0=invf[:], scalar1=pos_f[:, t:t + 1], scalar2=None, op0=ALU.mult)),
                 reads=[rp], writes=[ra])
        tmp = P.sb([128, NTILE, 48], F32)
        ni = P.sb([128, NTILE, 48], I32)
        nf = P.sb([128, NTILE, 48], F32)
        msk = P.sb([128, NTILE, 48], F32)
        C1 = 6.28125
        C2 = 2.0 * np.pi - 6.28125
        PI = float(np.pi)
        TS = lambda **kw: (lambda e: e.tensor_scalar(**kw))
        STT = lambda **kw: (lambda e: e.scalar_tensor_tensor(**kw))
        A2 = lambda ap: ap.rearrange("p t c -> p (t c)")
        P.op("dve", TS(out=A2(ni[:]), in0=A2(ang[:]), scalar1=float(1.0 / (2.0 * np.pi)), scalar2=None, op0=ALU.mult), reads=[ra], writes=[ra])
        P.op("dve", lambda e: e.tensor_copy(out=A2(nf[:]), in_=A2(ni[:])), reads=[ra], writes=[ra])
        P.op("dve", STT(out=A2(tmp[:]), in0=A2(nf[:]), scalar=-C1, in1=A2(ang[:]), op0=ALU.mult, op1=ALU.add), reads=[ra], writes=[ra])
        P.op("dve", STT(out=A2(tmp[:]), in0=A2(nf[:]), scalar=-C2, in1=A2(tmp[:]), op0=ALU.mult, op1=ALU.add), reads=[ra], writes=[ra])
        P.op("dve", TS(out=A2(msk[:]), in0=A2(tmp[:]), scalar1=PI, scalar2=None, op0=ALU.is_gt), reads=[ra], writes=[ra])
        P.op("dve", STT(out=A2(tmp[:]), in0=A2(msk[:]), scalar=-2.0 * PI, in1=A2(tmp[:]), op0=ALU.mult, op1=ALU.add), reads=[ra], writes=[ra])
        P.op("dve", TS(out=A2(msk[:]), in0=A2(tmp[:]), scalar1=-PI, scalar2=None, op0=ALU.is_lt), reads=[ra], writes=[ra])
        P.op("dve", STT(out=A2(tmp[:]), in0=A2(msk[:]), scalar=2.0 * PI, in1=A2(tmp[:]), op0=ALU.mult, op1=ALU.add), reads=[ra], writes=[ra])
        P.op("act", lambda e: e.activation(out=self.sin[:], in_=tmp[:], func=AF.Sin), reads=[ra], writes=[ra, self.rcs])
        P.op("dve", TS(out=A2(tmp[:]), in0=A2(tmp[:]), scalar1=PI / 2.0, scalar2=None, op0=ALU.add), reads=[ra], writes=[ra])
        P.op("dve", TS(out=A2(msk[:]), in0=A2(tmp[:]), scalar1=PI, scalar2=None, op0=ALU.is_gt), reads=[ra], writes=[ra])
        P.op("dve", STT(out=A2(tmp[:]), in0=A2(msk[:]), scalar=-2.0 * PI, in1=A2(tmp[:]), op0=ALU.mult, op1=ALU.add), reads=[ra], writes=[ra])
        P.op("act", lambda e: e.activation(out=self.cos[:], in_=tmp[:], func=AF.Sin), reads=[ra], writes=[ra, self.rcs])
        P.barrier()
        nc.sbuf_base = mark

    def rms_to_T(self, src_fn, rsrc, g_tile, rg, hT, rhT, col0, ncols, scratch):
        pass


def load_w_bf16(P, dst, rdst, w_dram, rows, cols, chan):
    k = rows // 128
    for i in range(k):
        P.dma("pool", dst[:, i, :], w_dram[i * 128:(i + 1) * 128, :], writes=[rdst], chan=chan)


def load_bcast(P, dst, rdst, v_dram, n, chan):
    P.dma("sp", dst[:], v_dram.partition_broadcast(128), writes=[rdst], chan=chan)


def rmsnorm_tile(P, C, src, rsrc, n, g_tile, rg, out_bf, rout, tmp):
    junk, ss, rt = tmp
    P.op("pool", lambda e: e.memset(ss[:, 0:1], 0.0), writes=[rt])
    P.op("act", lambda e: e.activation(out=junk[:, 0:n], in_=src, func=AF.Square, accum_out=ss[:, 0:1]), reads=[rsrc, rt], writes=[rt])
    P.op("dve", lambda e: e.tensor_scalar(out=ss[:, 1:2], in0=ss[:, 0:1], scalar1=1.0 / n, scalar2=EPS, op0=ALU.mult, op1=ALU.add), reads=[rt], writes=[rt])
    P.op("act", lambda e: e.activation(out=ss[:, 2:3], in_=ss[:, 1:2], func=AF.Sqrt), reads=[rt], writes=[rt])
    P.op("dve", lambda e: e.reciprocal(out=ss[:, 3:4], in_=ss[:, 2:3]), reads=[rt], writes=[rt])
    P.op("dve", lambda e: e.scalar_tensor_tensor(out=out_bf, in0=src, scalar=ss[:, 3:4], in1=g_tile, op0=ALU.mult, op1=ALU.mult),
         reads=[rsrc, rt, rg], writes=[rout])


def transpose_chunks(P, C, src_bf, rsrc, nch, pt, rpt, width=128):
    for c in range(nch):
        P.op("pe", (lambda e, c=c: e.transpose(out=pt[0:width, c * 128:(c + 1) * 128], in_=src_bf[:, c * width:(c + 1) * width], identity=C.ident[:])),
             reads=[rsrc, C.rid], writes=[rpt])


def TT(P, eng, out, in0, in1, op, reads, writes):
    P.op(eng, lambda e: e.tensor_tensor(out=out, in0=in0, in1=in1, op=op), reads=reads, writes=writes)


def TS(P, eng, out, in0, s1, s2, op0, op1, reads, writes):
    if op1 is None:
        P.op(eng, lambda e: e.tensor_scalar(out=out, in0=in0, scalar1=s1, scalar2=None, op0=op0), reads=reads, writes=writes)
    else:
        P.op(eng, lambda e: e.tensor_scalar(out=out, in0=in0, scalar1=s1, scalar2=s2, op0=op0, op1=op1), reads=reads, writes=writes)


def STT(P, out, in0, scalar, in1, op0, op1, reads, writes):
    P.op("dve", lambda e: e.scalar_tensor_tensor(out=out, in0=in0, scalar=scalar, in1=in1, op0=op0, op1=op1), reads=reads, writes=writes)


def ACT(P, out, in_, func, reads, writes, **kw):
    P.op("act", lambda e: e.activation(out=out, in_=in_, func=func, **kw), reads=reads, writes=writes)


def MM(P, out, lhsT, rhs, start, stop, reads, writes, skip=False):
    if skip:
        P.op("pe", lambda e: e.matmul(out, lhsT=lhsT, rhs=rhs, start=start, stop=stop, skip_group_check=True), reads=reads, writes=writes)
    else:
        P.op("pe", lambda e: e.matmul(out, lhsT=lhsT, rhs=rhs, start=start, stop=stop), reads=reads, writes=writes)


def TR(P, out, in_, ident, reads, writes):
    P.op("pe", lambda e: e.transpose(out=out, in_=in_, identity=ident), reads=reads, writes=writes)


def MEMSET(P, eng, ap, val, reads, writes):
    P.op(eng, lambda e: e.memset(ap, val), reads=reads, writes=writes)


def cp(P, eng, out, in_, reads, writes):
    if eng == "act":
        P.op("act", lambda e: e.copy(out=out, in_=in_), reads=reads, writes=writes)
    else:
        P.op(eng, lambda e: e.tensor_copy(out=out, in_=in_), reads=reads, writes=writes)


def emit_p1(P, C, io):
    nc = P.nc
    w_in = P.sb([128, 8, 2616], BF16); rw = P.res()
    load_w_bf16(P, w_in, rw, io["w_in"], 1024, 2616, "w1")
    w_qu = P.sb([128, 2, 768], BF16); w_kvu = P.sb([128, 2, 1024], BF16); rwm = P.res()
    load_w_bf16(P, w_qu, rwm, io["w_q_up"], 256, 768, "w1")
    load_w_bf16(P, w_kvu, rwm, io["w_kv_up"], 256, 1024, "w1")
    g_mix = P.sb([128, D], F32); g_q = P.sb([128, 256], F32); g_kv = P.sb([128, 256], F32); rg = P.res()
    load_bcast(P, g_mix, rg, io["mix_norm"], D, "w2")
    load_bcast(P, g_q, rg, io["q_norm"], 256, "w2")
    load_bcast(P, g_kv, rg, io["kv_norm"], 256, "w2")

    junk = P.sb([128, D], BF16)
    ssR = Rot(P, 2, [128, 4], F32)
    hbR = Rot(P, 1, [128, D], BF16)
    hTR = Rot(P, 2, [128, 8, 128], BF16)
    ptR = Rot(P, 2, [128, 1024], BF16, psum=True)
    pzR = Rot(P, 3, [128, 512], F32, psum=True)
    zsR = Rot(P, 1, [128, 2616], F32)
    zs_rc = [[P.res() for _ in range(6)] for _ in range(zsR.n)]
    rqR = Rot(P, 2, [128, 26, 64], BF16)
    tmpA = P.sb([128, 12, 32], F32); tmpB = P.sb([128, 12, 32], F32); rtA = P.res()
    tmpC = P.sb([128, 12, 32], F32); tmpD = P.sb([128, 12, 32], F32); rtC = P.res()
    stq = P.sb([128, 13, 512], BF16); rstq = P.res()
    stv = P.sb([128, 4, 8, 64], BF16); rstv = P.res()
    stqc = P.sb([128, 8, 512], BF16); rstqc = P.res()
    stkc = P.sb([128, 8, 512], BF16); rstkc = P.res()
    stvc = P.sb([128, 4, 8, 64], BF16); rstvc = P.res()
    stg = P.sb([128, 4, 24], F32); rstg = P.res()
    cnR = Rot(P, 1, [128, 512], BF16)
    cnTR = Rot(P, 1, [128, 4, 128], BF16)
    qfR = Rot(P, 1, [128, 8, 96], BF16)
    kfR = Rot(P, 1, [128, 8, 96], BF16)
    kpe = P.sb([128, 32], F32); rkpe = P.res()
    qsb = P.sb([128, 768], F32); rqsb = P.res()
    t16 = [P.sb([128, 8, 16], F32) for _ in range(4)]; rt16 = P.res()
    ktmp = P.sb([128, 4, 16], F32)

    for t in range(NTILE):
        st, tt = t // 4, t % 4
        s0 = st * 512
        xt = C.x[:, t, :]
        ss, rss = ssR.next()
        hb, rhb = hbR.next()
        rmsnorm_tile(P, C, xt, C.rx[t], D, g_mix[:], rg, hb[:], rhb, (junk, ss, rss))
        pt, rpt = ptR.next()
        transpose_chunks(P, C, hb, rhb, 8, pt, rpt)
        hT, rhT = hTR.next()
        P.op("act", lambda e, hT=hT, pt=pt: e.copy(out=hT[:].rearrange("p k t -> p (k t)"), in_=pt[:]), reads=[rpt], writes=[rhT])
        zs, rzs = zsR.next()
        rzc = zs_rc[zsR.i]
        for c in range(6):
            c0 = c * 512
            n = min(512, 2616 - c0)
            pz, rpz = pzR.next()
            for k in range(8):
                P.op("pe", (lambda e, pz=pz, hT=hT, k=k, c0=c0, n=n: e.matmul(pz[:, 0:n], lhsT=hT[:, k, :], rhs=w_in[:, k, c0:c0 + n], start=(k == 0), stop=(k == 7))),
                     reads=[rhT, rw], writes=[rpz])
            P.op("act", (lambda e, pz=pz, zs=zs, c0=c0, n=n: e.copy(out=zs[:, c0:c0 + n], in_=pz[:, 0:n])), reads=[rpz], writes=[rzc[c]])
        if t == 0 and "dbg_zs" in io:
            P.dma("sp", io["dbg_zs"], zs[:], reads=rzc, chan="dbg")
            P.dma("sp", io["dbg_hb"], hb[:], reads=[rhb], chan="dbg")
            P.dma("sp", io["dbg_ss"], ss[:], reads=[rss], chan="dbg")
            P.dma("sp", io["dbg_hT"], hT[:].rearrange("p k t -> p (k t)"), reads=[rhT], chan="dbg")
        rq, rrq = rqR.next()
        zv = zs[:, 0:1536].rearrange("p (h two d) -> p h two d", h=24, two=2)
        rqv = rq[:].rearrange("p h (two d) -> p h two d", two=2)
        cb = C.cos[:, t, 0:32].unsqueeze(1).broadcast_to([128, 12, 32])
        sb_ = C.sin[:, t, 0:32].unsqueeze(1).broadcast_to([128, 12, 32])
        zr = rzc[0:3]
        for hh in range(2):
            hs = slice(hh * 12, hh * 12 + 12)
            x1, x2 = zv[:, hs, 0, :], zv[:, hs, 1, :]
            o1, o2 = rqv[:, hs, 0, :], rqv[:, hs, 1, :]
            P.op("dve", lambda e, x1=x1, cb=cb: e.tensor_tensor(out=tmpA[:], in0=x1, in1=cb, op=ALU.mult), reads=zr + [C.rcs], writes=[rtA])
            P.op("dve", lambda e, x2=x2, sb_=sb_: e.tensor_tensor(out=tmpB[:], in0=x2, in1=sb_, op=ALU.mult), reads=zr + [C.rcs], writes=[rtA])
            P.op("dve", lambda e, o1=o1: e.tensor_tensor(out=o1, in0=tmpA[:], in1=tmpB[:], op=ALU.subtract), reads=[rtA], writes=[rrq])
            P.op("pool", lambda e, x2=x2, cb=cb: e.tensor_tensor(out=tmpC[:], in0=x2, in1=cb, op=ALU.mult), reads=zr + [C.rcs], writes=[rtC])
            P.op("pool", lambda e, x1=x1, sb_=sb_: e.tensor_tensor(out=tmpD[:], in0=x1, in1=sb_, op=ALU.mult), reads=zr + [C.rcs], writes=[rtC])
            P.op("pool", lambda e, o2=o2: e.tensor_tensor(out=o2, in0=tmpC[:], in1=tmpD[:], op=ALU.add), reads=[rtC], writes=[rrq])
        if t == 0 and "dbg_rq" in io:
            P.dma("sp", io["dbg_rq"], rq[:].rearrange("p h d -> p (h d)"), reads=[rrq], chan="dbg")
            P.dma("sp", io["dbg_cs"], C.cos[:, 0, :], reads=[C.rcs], chan="dbg")
            P.dma("sp", io["dbg_sn"], C.sin[:, 0, :], reads=[C.rcs], chan="dbg")
        cp(P, "pool", rq[:, 24:26, :].rearrange("p h d -> p (h d)"), zs[:, 1536:1664], [rzc[3]], [rrq])
        rqf = rq[:].rearrange("p h d -> p (h d)")
        for half, (cs_, ce_) in enumerate(((0, 8), (8, 13))):
            pt2, rpt2 = ptR.next()
            for c in range(cs_, ce_):
                P.op("pe", (lambda e, c=c, pt2=pt2, cs_=cs_, rqf=rqf: e.transpose(out=pt2[:, (c - cs_) * 128:(c - cs_ + 1) * 128], in_=rqf[:, c * 128:(c + 1) * 128], identity=C.ident[:])),
                     reads=[rrq, C.rid], writes=[rpt2])
            nn = ce_ - cs_
            cp(P, "act" if half == 0 else "dve", stq[:, cs_:cs_ + nn, tt * 128:(tt + 1) * 128],
               pt2[:, 0:nn * 128].rearrange("p (c t) -> p c t", c=nn), [rpt2], [rstq])
        P.op("pool", lambda e, zs=zs, tt=tt: e.tensor_copy(out=stv[:, tt, :, :].rearrange("p s d -> p (s d)"), in_=zs[:, 1536:2048]), reads=[rzc[3]], writes=[rstv])
        P.op("act", lambda e, zs=zs, tt=tt: e.activation(out=stg[:, tt, :], in_=zs[:, 2592:2616], func=AF.Sigmoid), reads=[rzc[5]], writes=[rstg])
        cn, rcn = cnR.next()
        ss2, rss2 = ssR.next()
        rmsnorm_tile(P, C, zs[:, 2048:2304], rzc[4], 256, g_q[:], rg, cn[:, 0:256], rcn, (junk, ss2, rss2))
        ss3, rss3 = ssR.next()
        rmsnorm_tile(P, C, zs[:, 2304:2560], rzc[4], 256, g_kv[:], rg, cn[:, 256:512], rcn, (junk, ss3, rss3))
        pt3, rpt3 = ptR.next()
        transpose_chunks(P, C, cn, rcn, 4, pt3, rpt3)
        cnT, rcnT = cnTR.next()
        P.op("dve", lambda e, cnT=cnT, pt3=pt3: e.tensor_copy(out=cnT[:].rearrange("p k t -> p (k t)"), in_=pt3[:, 0:512]), reads=[rpt3], writes=[rcnT])
        qf, rqf_ = qfR.next()
        kf, rkf = kfR.next()
        for (c0, n) in ((0, 512), (512, 256)):
            pz, rpz = pzR.next()
            for k in range(2):
                P.op("pe", (lambda e, pz=pz, cnT=cnT, k=k, c0=c0, n=n: e.matmul(pz[:, 0:n], lhsT=cnT[:, k, :], rhs=w_qu[:, k, c0:c0 + n], start=(k == 0), stop=(k == 1))),
                     reads=[rcnT, rwm], writes=[rpz])
            P.op("act", (lambda e, pz=pz, c0=c0, n=n: e.copy(out=qsb[:, c0:c0 + n], in_=pz[:, 0:n])), reads=[rpz], writes=[rqsb])
        qv = qsb[:].rearrange("p (h d) -> p h d", h=8)
        P.op("pool", lambda e, qf=qf, qv=qv: e.tensor_copy(out=qf[:, :, 0:64], in_=qv[:, :, 0:64]), reads=[rqsb], writes=[rqf_])
        c32 = C.cos[:, t, 32:48].unsqueeze(1).broadcast_to([128, 8, 16])
        s32 = C.sin[:, t, 32:48].unsqueeze(1).broadcast_to([128, 8, 16])
        qx1, qx2 = qv[:, :, 64:80], qv[:, :, 80:96]
        P.op("dve", lambda e, qx1=qx1, c32=c32: e.tensor_tensor(out=t16[0][:], in0=qx1, in1=c32, op=ALU.mult), reads=[rqsb, C.rcs], writes=[rt16])
        P.op("dve", lambda e, qx2=qx2, s32=s32: e.tensor_tensor(out=t16[1][:], in0=qx2, in1=s32, op=ALU.mult), reads=[rqsb, C.rcs], writes=[rt16])
        P.op("dve", lambda e, qx2=qx2, c32=c32: e.tensor_tensor(out=t16[2][:], in0=qx2, in1=c32, op=ALU.mult), reads=[rqsb, C.rcs], writes=[rt16])
        P.op("dve", lambda e, qx1=qx1, s32=s32: e.tensor_tensor(out=t16[3][:], in0=qx1, in1=s32, op=ALU.mult), reads=[rqsb, C.rcs], writes=[rt16])
        P.op("dve", lambda e, qf=qf: e.tensor_tensor(out=qf[:, :, 64:80], in0=t16[0][:], in1=t16[1][:], op=ALU.subtract), reads=[rt16], writes=[rqf_])
        P.op("dve", lambda e, qf=qf: e.tensor_tensor(out=qf[:, :, 80:96], in0=t16[2][:], in1=t16[3][:], op=ALU.add), reads=[rt16], writes=[rqf_])
        kx1, kx2 = zs[:, 2560:2576], zs[:, 2576:2592]
        c16, s16 = C.cos[:, t, 32:48], C.sin[:, t, 32:48]
        P.op("pool", lambda e, kx1=kx1, c16=c16: e.tensor_tensor(out=ktmp[:, 0, :], in0=kx1, in1=c16, op=ALU.mult), reads=[rzc[5], C.rcs], writes=[rkpe])
        P.op("pool", lambda e, kx2=kx2, s16=s16: e.tensor_tensor(out=ktmp[:, 1, :], in0=kx2, in1=s16, op=ALU.mult), reads=[rzc[5], C.rcs], writes=[rkpe])
        P.op("pool", lambda e, kx2=kx2, c16=c16: e.tensor_tensor(out=ktmp[:, 2, :], in0=kx2, in1=c16, op=ALU.mult), reads=[rzc[5], C.rcs], writes=[rkpe])
        P.op("pool", lambda e, kx1=kx1, s16=s16: e.tensor_tensor(out=ktmp[:, 3, :], in0=kx1, in1=s16, op=ALU.mult), reads=[rzc[5], C.rcs], writes=[rkpe])
        P.op("pool", lambda e: e.tensor_tensor(out=kpe[:, 0:16], in0=ktmp[:, 0, :], in1=ktmp[:, 1, :], op=ALU.subtract), reads=[rkpe], writes=[rkpe])
        P.op("pool", lambda e: e.tensor_tensor(out=kpe[:, 16:32], in0=ktmp[:, 2, :], in1=ktmp[:, 3, :], op=ALU.add), reads=[rkpe], writes=[rkpe])
        P.op("pool", lambda e, kf=kf: e.tensor_copy(out=kf[:, :, 64:96], in_=kpe[:].unsqueeze(1).broadcast_to([128, 8, 32])), reads=[rkpe], writes=[rkf])
        for ci, c0 in enumerate((0, 512)):
            pz, rpz = pzR.next()
            for k in range(2):
                P.op("pe", (lambda e, pz=pz, cnT=cnT, k=k, c0=c0: e.matmul(pz[:, 0:512], lhsT=cnT[:, 2 + k, :], rhs=w_kvu[:, k, c0:c0 + 512], start=(k == 0), stop=(k == 1))),
                     reads=[rcnT, rwm], writes=[rpz])
            pv = pz[:, 0:512].rearrange("p (h d) -> p h d", h=4)
            P.op("act", (lambda e, pv=pv, kf=kf, ci=ci: e.copy(out=kf[:, ci * 4:(ci + 1) * 4, 0:64], in_=pv[:, :, 0:64])), reads=[rpz], writes=[rkf])
            P.op("dve", (lambda e, pv=pv, ci=ci, tt=tt: e.tensor_copy(out=stvc[:, tt, ci * 4:(ci + 1) * 4, :], in_=pv[:, :, 64:128])), reads=[rpz], writes=[rstvc])
        for src, rsrc, dst, rdst, eng in ((qf, rqf_, stqc, rstqc, "act"), (kf, rkf, stkc, rstkc, "dve")):
            pt4, rpt4 = ptR.next()
            for h in range(8):
                P.op("pe", (lambda e, h=h, pt4=pt4, src=src: e.transpose(out=pt4[0:96, h * 128:(h + 1) * 128], in_=src[:, h, :], identity=C.ident[:])),
                     reads=[rsrc, C.rid], writes=[rpt4])
            if eng == "act":
                P.op("act", (lambda e, pt4=pt4, dst=dst, tt=tt: e.copy(out=dst[0:96, :, tt * 128:(tt + 1) * 128], in_=pt4[0:96, :].rearrange("p (h t) -> p h t", h=8))), reads=[rpt4], writes=[rdst])
            else:
                P.op("dve", (lambda e, pt4=pt4, dst=dst, tt=tt: e.tensor_copy(out=dst[0:96, :, tt * 128:(tt + 1) * 128], in_=pt4[0:96, :].rearrange("p (h t) -> p h t", h=8))), reads=[rpt4], writes=[rdst])

        if tt == 3:
            sl = slice(s0, s0 + 512)
            for c in range(4):
                P.dma("sp", io["xqa"][c].rearrange("h d t -> (h d) t")[:, sl], stq[:, c, :], reads=[rstq])
                P.dma("sp", io["xqg"][c ^ 1].rearrange("h d t -> (h d) t")[:, sl], stq[:, c, :], reads=[rstq])
                P.dma("sp", io["xqb"][c].rearrange("h d t -> (h d) t")[:, sl], stq[:, 7 + c, :], reads=[rstq])
            for g in range(2):
                for dest in (2 * g, 2 * g + 1):
                    for ty, ch in enumerate((4, 5, 6, 12)):
                        P.dma("sp", io["xka"][dest, ty][:, sl], stq[g * 64:(g + 1) * 64, ch, :], reads=[rstq])
                    P.dma("sp", io["xkb"][dest][:, sl], stq[g * 64:(g + 1) * 64, 11, :], reads=[rstq])
                    for ty in range(2):
                        P.dma("sp", io["xva"][dest, ty][sl, :].rearrange("(tt p) d -> p tt d", p=128), stv[:, :, (ty + 1) * 2 + g, :], reads=[rstv])
                    P.dma("sp", io["xvb"][dest][sl, :].rearrange("(tt p) d -> p tt d", p=128), stv[:, :, 6 + g, :], reads=[rstv])
            for dest in range(4):
                P.dma("sp", io["xqc"][dest].rearrange("h d t -> d h t")[:, :, sl], stqc[0:96, 2 * dest:2 * dest + 2, :], reads=[rstqc], chan="x3")
                P.dma("sp", io["xkc"][dest].rearrange("h d t -> d h t")[:, :, sl], stkc[0:96, 2 * dest:2 * dest + 2, :], reads=[rstkc], chan="x3")
                for hh in range(2):
                    P.dma("sp", io["xvc"][dest, hh][sl, :].rearrange("(tt p) d -> p tt d", p=128), stvc[:, :, 2 * dest + hh, :], reads=[rstvc], chan="x4")
                P.dma("sp", io["xg"][dest][sl, :].rearrange("(tt p) c -> p tt c", p=128), stg[:, :, 6 * dest:6 * dest + 6], reads=[rstg], chan="x4")


P2_IN = {
    "qa": ([4, 2, 64, NTOK], BF16), "qg": ([4, 2, 64, NTOK], BF16), "ka": ([4, 4, 64, NTOK], BF16),
    "va": ([4, 2, NTOK, 64], BF16), "qb": ([4, 2, 64, NTOK], BF16), "kb": ([4, 64, NTOK], BF16),
    "vb": ([4, NTOK, 64], BF16), "qc": ([4, 2, 96, NTOK], BF16), "kc": ([4, 2, 96, NTOK], BF16),
    "vc": ([4, 2, NTOK, 64], BF16), "g": ([4, NTOK, 6], F32),
}
P2_W = {"posk": [128, 16], "w1k": [2048, 256], "w2k": [256, 64], "posv": [128, 16], "w1v": [2048, 256],
        "w2v": [256, 64], "sinks": [1, 2], "selmap": [128, 4, 128]}


def selmap_const():
    n_cmp = 511
    tok = np.arange(n_cmp)[:, None] * 16 + np.arange(32)[None, :]
    sm = np.zeros((512, 128), np.float32)
    np.add.at(sm, (np.repeat(np.arange(n_cmp), 32), (tok // 64).reshape(-1)), 1.0 / 32)
    return np.ascontiguousarray(sm.reshape(4, 128, 128).transpose(1, 0, 2))


class AttnCtx:
    def __init__(self, P, ident, rid):
        self.P = P
        self.ident = ident
        self.rid = rid
        self.S = Rot(P, 3, [128, 512], F32, psum=True)
        self.pT = Rot(P, 3, [128, 512], BF16)
        self.acc = Rot(P, 2, [128, 4, 128], F32, psum=True)


def attn_qgroup(P, A, kT, rkT, Vt, rV, nv, qT, rqT, kbs, scale, accv, racc, look=1):
    cover = {qb: [i for i, e in enumerate(kbs) if e[1] <= qb <= e[2]] for qb in range(4)}
    n_kb = len(kbs)
    tiles = [None] * n_kb

    def scores(i):
        kb, lo, hi, segs = kbs[i]
        ps, rps = A.S.next()
        for (q0, q1, extra) in segs:
            c0, c1 = q0 * 128, (q1 + 1) * 128
            n = len(extra)
            MM(P, ps[:, c0:c1], kT[:, kb * 128:(kb + 1) * 128], qT[:, c0:c1], True, n == 0, [rkT, rqT], [rps])
            for j, (l_, r_, rd) in enumerate(extra):
                MM(P, ps[:, c0:c1], l_, r_, False, j == n - 1, rd, [rps])
        tiles[i] = (ps, rps)

    def rest(i):
        kb, lo, hi, segs = kbs[i]
        ps, rps = tiles[i]
        pT, rpT = A.pT.next()
        c0, c1 = lo * 128, (hi + 1) * 128
        ACT(P, pT[:, c0:c1], ps[:, c0:c1], AF.Exp, [rps], [rpT], scale=scale)
        for qb in range(lo, hi + 1):
            MM(P, accv(qb), pT[:, qb * 128:(qb + 1) * 128], Vt(kb), i == 0 and qb == lo, cover[qb][-1] == i, [rpT, rV], [racc], skip=True)

    LOOK = look
    for i in range(n_kb + LOOK):
        if i < n_kb:
            scores(i)
        if i - LOOK >= 0:
            rest(i - LOOK)


def emit_p2(P, io, ident, rid):
    nc = P.nc
    deps = io.get("deps", {"nsa": [], "swa": [], "mla": []})
    dA, dB, dC = deps["nsa"], deps["swa"], deps["mla"]
    A = AttnCtx(P, ident, rid)
    rc = P.res()
    zero_bf = P.sb([128, 512], BF16)
    ones_bf = P.sb([128, 128], BF16)
    MEMSET(P, "pool", zero_bf[:], 0.0, [], [rc])
    MEMSET(P, "pool", ones_bf[:], 1.0, [], [rc])
    pen_diag = P.sb([128, 128], BF16)
    pen_far = P.sb([128, 128], BF16)
    P.op("pool", lambda e: e.affine_select(out=pen_diag[:], in_=zero_bf[:, 0:128], pattern=[[1, 128]], compare_op=ALU.is_ge, fill=P.freg(e, NEG), base=0, channel_multiplier=-1), reads=[rc], writes=[rc])
    P.op("pool", lambda e: e.affine_select(out=pen_far[:], in_=zero_bf[:, 0:128], pattern=[[-1, 128]], compare_op=ALU.is_gt, fill=P.freg(e, NEG), base=0, channel_multiplier=1), reads=[rc], writes=[rc])
    E = P.sb([128, 64, 128], BF16)
    for j in range(64):
        P.op("pool", (lambda e, j=j: e.affine_select(out=E[:, j, :].rearrange("p (a b) -> p a b", a=2), in_=ones_bf[:].rearrange("p (a b) -> p a b", a=2),
                                                     pattern=[[-1, 2], [0, 64]], compare_op=ALU.is_equal, fill=P.freg(e, 0.0), base=-2 * j, channel_multiplier=1)), reads=[rc], writes=[rc])
    vcmp = P.sb([128, 4, 200], BF16); rvcmp = P.res()
    MEMSET(P, "pool", vcmp[:], 0.0, [], [rvcmp])
    MEMSET(P, "pool", vcmp[:, :, 64:65], 1.0, [], [rvcmp])
    P.dma("pool", vcmp[:, :, 65:193], io["selmap"], writes=[rvcmp])
    kcmpT = P.sb([64, 512], BF16); rkcmp = P.res()
    MEMSET(P, "pool", kcmpT[:], 0.0, [], [rkcmp])
    esink = P.sb([128, 2], F32); resink = P.res()
    P.dma("sp", esink[:], io["sinks"].partition_broadcast(128), writes=[resink])
    ACT(P, esink[:], esink[:], AF.Exp, [resink], [resink])
    mark = nc.sbuf_base
    kT2 = P.sb([128, S], BF16); rkT2 = P.res()
    w1 = P.sb([128, 16, 256], BF16); w2 = P.sb([128, 2, 64], BF16); posT = P.sb([128, 16], BF16); rwc = P.res()
    gT = P.sb([128, 2, 512], BF16); rgT = P.res()
    cb = P.sb([128, 2], F32); rcb = P.res()
    for which, ty in (("k", 0), ("v", 3)):
        for s in range(4):
            P.dma("sp", kT2[0:64, s * NTOK:(s + 1) * NTOK], io["ka"][s, ty], writes=[rkT2], reads=list(dA))
            P.dma("sp", kT2[64:128, s * NTOK:(s + 1) * NTOK - 1], io["ka"][s, ty][:, 1:NTOK], writes=[rkT2], reads=list(dA))
            if s < 3:
                P.dma("sp", kT2[64:128, (s + 1) * NTOK - 1:(s + 1) * NTOK], io["ka"][s + 1, ty][:, 0:1], writes=[rkT2], reads=list(dA), allow_slow_non_contiguous=True)
        load_w_bf16(P, w1, rwc, io["w1" + which], 2048, 256, None)
        load_w_bf16(P, w2, rwc, io["w2" + which], 256, 64, None)
        P.dma("pool", posT[:], io["pos" + which], writes=[rwc])
        kviews = [kT2[:, b0:b0 + 8176].rearrange("p (n s) -> p n s", s=16) for b0 in (0, 16)]
        for hc in range(2):
            ps, rps = A.S.next()
            for lp in range(16):
                MM(P, ps[:, 0:511], w1[:, lp, hc * 128:(hc + 1) * 128], kviews[(2 * lp) // 16][:, :, (2 * lp) % 16], lp == 0, lp == 15, [rwc, rkT2], [rps])
            pb, rpb = A.acc.next()
            for lp in range(16):
                MM(P, pb[:, 0, 0:1], w1[:, lp, hc * 128:(hc + 1) * 128], posT[:, lp:lp + 1], lp == 0, lp == 15, [rwc], [rpb])
            cp(P, "dve", cb[:, hc:hc + 1], pb[:, 0, 0:1], [rpb], [rcb])
            ACT(P, gT[:, hc, 0:511], ps[:, 0:511], AF.Gelu_apprx_tanh, [rps, rcb], [rgT], bias=cb[:, hc:hc + 1])
        if which == "k":
            ps, rps = A.S.next()
            for hc in range(2):
                MM(P, ps[0:64, 0:511], w2[:, hc, :], gT[:, hc, 0:511], hc == 0, hc == 1, [rwc, rgT], [rps])
            cp(P, "dve", kcmpT[:, 0:511], ps[0:64, 0:511], [rps], [rkcmp])
        else:
            for c in range(4):
                nn = 128 if c < 3 else 127
                ps, rps = A.S.next()
                for hc in range(2):
                    MM(P, ps[0:nn, 0:64], gT[:, hc, c * 128:c * 128 + nn], w2[:, hc, :], hc == 0, hc == 1, [rwc, rgT], [rps])
                cp(P, "dve", vcmp[0:nn, c, 0:64], ps[0:nn, 0:64], [rps], [rvcmp])
    P.barrier()
    nc.sbuf_base = mark

    kTa = P.sb([128, S], BF16); rkTa = P.res()
    kTb = P.sb([128, S], BF16); rkTb = P.res()
    Va = P.sb([128, 64, 65], BF16); rVa = P.res()
    Vb = P.sb([128, 64, 65], BF16); rVb = P.res()
    MEMSET(P, "pool", Va[:, :, 64:65], 1.0, [], [rVa])
    MEMSET(P, "pool", Vb[:, :, 64:65], 1.0, [], [rVb])

    def load_kT(dst, rdst, src_fn, dk, dep=()):
        for s in range(4):
            P.dma("sp", dst[0:dk, s * NTOK:(s + 1) * NTOK], src_fn(s), writes=[rdst], reads=list(dep))

    def load_V(dst, rdst, src_fn, dep=()):
        for s in range(4):
            P.dma("sp", dst[:, s * 16:(s + 1) * 16, 0:64], src_fn(s).rearrange("(blk p) d -> p blk d", p=128), writes=[rdst], reads=list(dep))

    qR = Rot(P, 2, [128, 4, 512], BF16)
    gR = Rot(P, 2, [128, 4, 6], F32)
    ostR = Rot(P, 2, [128, 4, 128], BF16)
    oacc = P.sb([128, 2, 4, 64], F32); roacc = [P.res(), P.res()]
    imp = P.sb([128, 4, 128], F32); rimp = P.res()
    rcp = P.sb([128, 8], F32); rrcp = P.res()
    fac = P.sb([128, 8], F32)
    tmpo = P.sb([128, 4, 64], F32); rtmpo = P.res()
    penR = Rot(P, 2, [128, 512], BF16)
    biasR = Rot(P, 2, [128, 128], F32)
    val = P.sb([128, 128], F32); rval = P.res()
    wk = P.sb([128, 128], F32)
    m16 = P.sb([128, 16], F32)
    penq = P.sb([128, 128], BF16); rpenq = P.res()
    penT = P.sb([128, 512], BF16); rpenT = P.res()
    cmpacc = [P.ps([128, 2, 256], F32), P.ps([128, 2, 256], F32)]; rcmpacc = P.res()
    trp = P.ps([128, 512], BF16); rtrp = P.res()

    def out_dma(ost, rost, Gq, col0, ncol):
        src, off = Gq // 4, (Gq % 4) * 512
        for dest in range(1):
            pass
        d = Gq // 4
        if "o_mix" in io:
            m, c0 = col0 // 128, col0 % 128
            P.dma("sp", io["o_mix"](m)[d][off:off + 512, c0:c0 + ncol].rearrange("(qb p) c -> p qb c", p=128), ost[:, :, 0:ncol],
                  reads=[rost], writes=[io["ro"][m]])
        else:
            P.dma("sp", io["o"][d][off:off + 512, col0:col0 + ncol].rearrange("(qb p) c -> p qb c", p=128), ost[:, :, 0:ncol], reads=[rost])

    def finish_branch(accv_t, racc_, h, gcol, g, rg, first, extra_den=None):
        if extra_den is None:
            P.op("dve", lambda e: e.reciprocal(out=rcp[:, 0:4], in_=accv_t[:, :, 64]), reads=[racc_], writes=[rrcp])
        else:
            TS(P, "dve", rcp[:, 4:8], accv_t[:, :, 64], extra_den, None, ALU.add, None, [racc_, resink], [rrcp])
            P.op("dve", lambda e: e.reciprocal(out=rcp[:, 0:4], in_=rcp[:, 4:8]), reads=[rrcp], writes=[rrcp])
        if gcol is not None:
            TT(P, "dve", fac[:, 0:4], rcp[:, 0:4], g[:, :, gcol], ALU.mult, [rrcp, rg], [rrcp])
            f = fac[:, 0:4]
        else:
            f = rcp[:, 0:4]
        fb = f.unsqueeze(2).broadcast_to([128, 4, 64])
        if first:
            TT(P, "dve", oacc[:, h, :, :], accv_t[:, :, 0:64], fb, ALU.mult, [racc_, rrcp], [roacc[h]])
        else:
            TT(P, "dve", tmpo[:], accv_t[:, :, 0:64], fb, ALU.mult, [racc_, rrcp], [rtmpo])
            TT(P, "dve", oacc[:, h, :, :], oacc[:, h, :, :], tmpo[:], ALU.add, [rtmpo], [roacc[h]])

    load_kT(kTa, rkTa, lambda s: io["ka"][s, 1], 64, dA)
    load_kT(kTb, rkTb, lambda s: io["ka"][s, 2], 64, dA)
    load_V(Va, rVa, lambda s: io["va"][s, 0], dA)
    load_V(Vb, rVb, lambda s: io["va"][s, 1], dA)
    for Gq in range(16):
        src, off = Gq // 4, (Gq % 4) * 512
        q4, rq4 = qR.next()
        P.dma("sp", q4[0:64, 0:2, :], io["qa"][src].rearrange("h d t -> d h t")[:, :, off:off + 512], writes=[rq4], reads=list(dA))
        P.dma("sp", q4[0:64, 2:4, :], io["qg"][src].rearrange("h d t -> d h t")[:, :, off:off + 512], writes=[rq4], reads=list(dA))
        g, rg = gR.next()
        P.dma("sp", g[:], io["g"][src][off:off + 512, :].rearrange("(qb p) c -> p qb c", p=128), writes=[rg], reads=list(dA))
        cmax = (32 * Gq + 30) // 128
        pens = {}
        for c in range(cmax + 1):
            if Gq >= 4 * c + 5:
                continue
            pn, rpn = penR.next()
            P.op("pool", (lambda e, pn=pn, c=c, Gq=Gq: e.affine_select(out=pn[:], in_=zero_bf[:], pattern=[[1, 512]], compare_op=ALU.is_ge, fill=P.freg(e, NEG),
                                                                      base=512 * Gq - 2048 * c - 31, channel_multiplier=-16)), reads=[rc], writes=[rpn])
            pens[c] = (pn, rpn)
        for r4 in range(4):
            ctiles = {}

            def cscores(c, r4=r4):
                ps, rps = A.S.next()
                if c in pens:
                    MM(P, ps[:, :], kcmpT[:, c * 128:(c + 1) * 128], q4[0:64, r4, :], True, False, [rkcmp, rq4], [rps])
                    MM(P, ps[:, :], ident[:], pens[c][0][:], False, True, [rid, pens[c][1]], [rps])
                else:
                    MM(P, ps[:, :], kcmpT[:, c * 128:(c + 1) * 128], q4[0:64, r4, :], True, True, [rkcmp, rq4], [rps])
                ctiles[c] = (ps, rps)

            def crest(c):
                ps, rps = ctiles[c]
                pT, rpT = A.pT.next()
                ACT(P, pT[:, :], ps[:, :], AF.Exp, [rps], [rpT], scale=0.125)
                for qb in range(4):
                    MM(P, cmpacc[qb // 2][:, qb % 2, 0:193], pT[:, qb * 128:(qb + 1) * 128], vcmp[:, c, 0:193], c == 0 and qb % 2 == 0, c == cmax, [rpT, rvcmp], [rcmpacc], skip=True)

            for c in range(cmax + 2):
                if c <= cmax:
                    cscores(c)
                if c >= 1:
                    crest(c - 1)
            for half in range(2):
                TS(P, "dve", rcp[:, 4 + 2 * half:6 + 2 * half], cmpacc[half][:, :, 64], 1e-30, None, ALU.max, None, [rcmpacc], [rrcp])
            P.op("dve", lambda e: e.reciprocal(out=rcp[:, 0:4], in_=rcp[:, 4:8]), reads=[rrcp], writes=[rrcp])
            for qb in range(4):
                src_imp = cmpacc[qb // 2][:, qb % 2, 65:193]
                if r4 == 0:
                    TS(P, "dve", imp[:, qb, :], src_imp, rcp[:, qb:qb + 1], None, ALU.mult, None, [rcmpacc, rrcp], [rimp])
                else:
                    STT(P, imp[:, qb, :], src_imp, rcp[:, qb:qb + 1], imp[:, qb, :], ALU.mult, ALU.add, [rcmpacc, rrcp], [rimp])
            if r4 < 2:
                TT(P, "dve", fac[:, 0:4], rcp[:, 0:4], g[:, :, 3 * r4 + 0], ALU.mult, [rrcp, rg], [rrcp])
                for half in range(2):
                    fb = fac[:, 2 * half:2 * half + 2].unsqueeze(2).broadcast_to([128, 2, 64])
                    TT(P, "dve", oacc[:, r4, 2 * half:2 * half + 2, :], cmpacc[half][:, :, 0:64], fb, ALU.mult, [rcmpacc, rrcp], [roacc[r4]])
        for qb in range(4):
            j = 4 * Gq + qb
            bt, rbt = biasR.next()
            MEMSET(P, "pool", bt[:], 0.0, [], [rbt])
            MEMSET(P, "pool", bt[:, 0:1], 1e4, [], [rbt])
            if j >= 1:
                MEMSET(P, "pool", bt[0:64, 2 * j - 1:2 * j + 1], 1e4, [], [rbt])
            MEMSET(P, "pool", bt[64:128, 2 * j:2 * j + 2], 1e4, [], [rbt])
            if 2 * j + 1 < 128:
                MEMSET(P, "pool", bt[0:64, 2 * j + 1:128], -1e30, [], [rbt])
            if 2 * j + 2 < 128:
                MEMSET(P, "pool", bt[64:128, 2 * j + 2:128], -1e30, [], [rbt])
            TT(P, "dve", val[:], imp[:, qb, :], bt[:], ALU.add, [rimp, rbt], [rval])
            P.op("dve", lambda e: e.max(out=m16[:, 0:8], in_=val[:]), reads=[rval], writes=[rval])
            P.op("dve", lambda e: e.match_replace(out=wk[:], in_to_replace=m16[:, 0:8], in_values=val[:], imm_value=-3e38), reads=[rval], writes=[rval])
            P.op("dve", lambda e: e.max(out=m16[:, 8:16], in_=wk[:]), reads=[rval], writes=[rval])
            TS(P, "dve", penq[:], val[:], m16[:, 15:16], NEG, ALU.is_lt, ALU.mult, [rval], [rpenq])
            TR(P, trp[:, qb * 128:(qb + 1) * 128], penq[:], ident[:], [rpenq, rid], [rtrp])
        cp(P, "dve", penT[:], trp[:], [rtrp], [rpenT])
        for h in range(2):
            acc, racc = A.acc.next()
            kbs = []
            for kb in range(4 * Gq + 4):
                ex_sel = lambda q0, q1, kb=kb: (E[:, kb // 1, :], penT[:, q0 * 128:(q1 + 1) * 128], [rc, rpenT])
                if kb < 4 * Gq:
                    kbs.append((kb, 0, 3, [(0, 3, [ex_sel(0, 3)])]))
                else:
                    i = kb - 4 * Gq
                    segs = [(i, i, [ex_sel(i, i), (ident[:], pen_diag[:], [rid, rc])])]
                    if i < 3:
                        segs.append((i + 1, 3, [ex_sel(i + 1, 3)]))
                    kbs.append((kb, i, 3, segs))
            attn_qgroup(P, A, kTa[0:64, :], rkTa, lambda kb: Va[:, kb, :], rVa, 65, q4[0:64, h, :], rq4, kbs, 0.125, lambda qb, acc=acc: acc[:, qb, 0:65], racc)
            finish_branch(acc, racc, h, 3 * h + 1, g, rg, False)
            acc, racc = A.acc.next()
            kbs = []
            for i in range(8):
                kb = 4 * Gq - 4 + i
                if kb < 0:
                    continue
                lo, hi = max(0, i - 4), min(3, i)
                segs = []
                if i <= 3:
                    if lo < i:
                        segs.append((lo, i - 1, []))
                    segs.append((i, i, [(ident[:], pen_far[:], [rid, rc])]))
                else:
                    segs.append((i - 4, i - 4, [(ident[:], pen_diag[:], [rid, rc])]))
                    if i - 4 < hi:
                        segs.append((i - 3, hi, []))
                kbs.append((kb, lo, hi, segs))
            attn_qgroup(P, A, kTb[0:64, :], rkTb, lambda kb: Vb[:, kb, :], rVb, 65, q4[0:64, h, :], rq4, kbs, 0.125, lambda qb, acc=acc: acc[:, qb, 0:65], racc)
            finish_branch(acc, racc, h, 3 * h + 2, g, rg, False)
        ost, rost = ostR.next()
        cp(P, "act", ost[:, :, 0:128].rearrange("p q (h d) -> p h q d", h=2), oacc[:], roacc, [rost])
        out_dma(ost, rost, Gq, 0, 128)

    if "post_mix" in io:
        io["post_mix"](0)
    P.barrier()
    S5 = Rot.__new__(Rot)
    S5.t = list(A.S.t) + [cmpacc[0][:].rearrange("p a b -> p (a b)"), cmpacc[1][:].rearrange("p a b -> p (a b)")]
    S5.r = list(A.S.r) + [P.res(), P.res()]
    S5.i = -1
    S5.n = 5
    A.S = S5
    if "pre_swa" in io:
        io["pre_swa"]()
    load_kT(kTa, rkTa, lambda s: io["kb"][s], 64, dB)
    load_V(Va, rVa, lambda s: io["vb"][s], dB)
    for Gq in range(16):
        src, off = Gq // 4, (Gq % 4) * 512
        q4, rq4 = qR.next()
        P.dma("sp", q4[0:64, 0:2, :], io["qb"][src].rearrange("h d t -> d h t")[:, :, off:off + 512], writes=[rq4], reads=list(dB))
        for h in range(2):
            acc, racc = A.acc.next()
            kbs = []
            for i in range(5):
                kb = 4 * Gq - 1 + i
                if kb < 0:
                    continue
                segs = []
                lo, hi = max(0, i - 1), min(3, i)
                if i <= 3:
                    segs.append((i, i, [(ident[:], pen_far[:], [rid, rc])]))
                if i >= 1:
                    segs.append((i - 1, i - 1, [(ident[:], pen_diag[:], [rid, rc])]))
                segs.sort()
                kbs.append((kb, lo, hi, segs))
            attn_qgroup(P, A, kTa[0:64, :], rkTa, lambda kb: Va[:, kb, :], rVa, 65, q4[0:64, h, :], rq4, kbs, 0.125, lambda qb, acc=acc: acc[:, qb, 0:65], racc, look=2)
            finish_branch(acc, racc, h, None, None, None, True, extra_den=esink[:, h:h + 1])
        ost, rost = ostR.next()
        cp(P, "act", ost[:, :, 0:128].rearrange("p q (h d) -> p h q d", h=2), oacc[:], roacc, [rost])
        out_dma(ost, rost, Gq, 128, 128)

    if "post_mix" in io:
        io["post_mix"](1)
    if "pre_mla" in io:
        io["pre_mla"]()
    for h in range(2):
        kT, rkT = (kTa, rkTa) if h == 0 else (kTb, rkTb)
        Vx, rVx = (Va, rVa) if h == 0 else (Vb, rVb)
        load_kT(kT, rkT, lambda s, h=h: io["kc"][s, h], 96, dC)
        load_V(Vx, rVx, lambda s, h=h: io["vc"][s, h], dC)
        for Gq in range(16):
            src, off = Gq // 4, (Gq % 4) * 512
            q4, rq4 = qR.next()
            P.dma("sp", q4[0:96, 0, :], io["qc"][src, h][:, off:off + 512], writes=[rq4], reads=list(dC))
            acc, racc = A.acc.next()
            kbs = []
            for kb in range(4 * Gq + 4):
                if kb < 4 * Gq:
                    kbs.append((kb, 0, 3, [(0, 3, [])]))
                else:
                    i = kb - 4 * Gq
                    segs = [(i, i, [(ident[:], pen_diag[:], [rid, rc])])]
                    if i < 3:
                        segs.append((i + 1, 3, []))
                    kbs.append((kb, i, 3, segs))
            attn_qgroup(P, A, kT[0:96, :], rkT, lambda kb, Vx=Vx: Vx[:, kb, :], rVx, 65, q4[0:96, 0, :], rq4, kbs, 96 ** -0.5, lambda qb, acc=acc: acc[:, qb, 0:65], racc, look=2)
            finish_branch(acc, racc, 0, None, None, None, True)
            ost, rost = ostR.next()
            cp(P, "act", ost[:, :, 0:64], oacc[:, 0, :, :], roacc, [rost])
            out_dma(ost, rost, Gq, 256 + 64 * h, 64)
    if "post_mix" in io:
        io["post_mix"](2)


def emit_p3(P, C, io, last):
    nc = P.nc
    base_mark = nc.sbuf_base
    pbase = nc.psum_base
    junk = P.sb([128, D], BF16)
    ssR = Rot(P, 2, [128, 4], F32)
    ptR = Rot(P, 2, [128, 1024], BF16, psum=True)
    pzR = Rot(P, 4, [128, 512], F32, psum=True)
    g_n = P.sb([128, D], F32); rgn = P.res()

    def norm_T(t, hb, rhb, dstT, col0, rdst, eng="act"):
        ss, rss = ssR.next()
        rmsnorm_tile(P, C, C.x[:, t, :], C.rx[t], D, g_n[:], rgn, hb[:], rhb, (junk, ss, rss))
        pt, rpt = ptR.next()
        transpose_chunks(P, C, hb, rhb, 8, pt, rpt)
        cp(P, eng, dstT[:, :, col0:col0 + 128], pt[:].rearrange("p (k t) -> p k t", k=8), [rpt], [rdst])

    markA = nc.sbuf_base
    load_bcast(P, g_n, rgn, io["mix_norm"], D, None)
    wbg = P.sb([128, 8, 3072], BF16); rwA = P.res(); rwAp = P.res(); rwAo = P.res()
    load_w_bf16(P, wbg, rwA, io["w_bg"], 1024, 3072, None)
    wp = P.sb([128, 12, 1024], BF16)
    for i, nm in enumerate(("w_pa", "w_pb", "w_pc")):
        for k in range(4):
            P.dma("pool", wp[:, 4 * i + k, :], io[nm][k * 128:(k + 1) * 128, :], writes=[rwAp])
    wo = P.sb([128, 8, 1024], BF16)
    load_w_bf16(P, wo, rwAo, io["w_out"], 1024, 1024, None)
    hbR = Rot(P, 1, [128, D], BF16)
    hTR = Rot(P, 2, [128, 8, 128], BF16)
    gsb = P.sb([128, 3072], F32); rgsb = P.res()
    otR = Rot(P, 1, [128, 4, 384], BF16)
    oTR = Rot(P, 1, [128, 12, 128], BF16)
    mrg = P.sb([128, D], F32); rmrg = P.res()
    tmpm = P.sb([128, 512], F32); rtmpm = P.res()
    mbR = Rot(P, 1, [128, D], BF16)
    mTR = Rot(P, 1, [128, 8, 128], BF16)
    for t in range(NTILE):
        hb, rhb = hbR.next()
        hT, rhT = hTR.next()
        norm_T(t, hb, rhb, hT, 0, rhT)
        for c in range(6):
            pz, rpz = pzR.next()
            for k in range(8):
                MM(P, pz[:, :], hT[:, k, :], wbg[:, k, c * 512:(c + 1) * 512], k == 0, k == 7, [rhT, rwA], [rpz])
            ACT(P, gsb[:, c * 512:(c + 1) * 512], pz[:, :], AF.Sigmoid, [rpz], [rgsb])
        ot, rot = otR.next()
        if "o_tile3" in io:
            for m in range(3):
                P.dma("sp", ot[:, :, m * 128:(m + 1) * 128], io["o_tile3"](t, m).rearrange("s p c -> p s c"), writes=[rot], reads=list(io["o_dep"]))
        else:
            o_src = io["o_tile"](t) if "o_tile" in io else io["o"][:, t * 128:(t + 1) * 128, :]
            P.dma("sp", ot[:], o_src.rearrange("s p c -> p s c"), writes=[rot])
        oT, roT = oTR.next()
        for half, (a0, a1) in enumerate(((0, 8), (8, 12))):
            pt, rpt = ptR.next()
            for j in range(a0, a1):
                i, s = j // 4, j % 4
                TR(P, pt[:, (j - a0) * 128:(j - a0 + 1) * 128], ot[:, s, i * 128:(i + 1) * 128], C.ident[:], [rot, C.rid], [rpt])
            cp(P, "act" if half == 0 else "dve", oT[:, a0:a1, :], pt[:, 0:(a1 - a0) * 128].rearrange("p (k t) -> p k t", k=a1 - a0), [rpt], [roT])
        for c in range(2):
            cs = slice(c * 512, (c + 1) * 512)
            for i in range(3):
                pz, rpz = pzR.next()
                for k in range(4):
                    MM(P, pz[:, :], oT[:, 4 * i + k, :], wp[:, 4 * i + k, cs], k == 0, k == 3, [roT, rwAp], [rpz])
                gs = gsb[:, i * 1024 + c * 512:i * 1024 + (c + 1) * 512]
                if i == 0:
                    TT(P, "dve", mrg[:, cs], pz[:, :], gs, ALU.mult, [rpz, rgsb], [rmrg])
                else:
                    TT(P, "dve", tmpm[:], pz[:, :], gs, ALU.mult, [rpz, rgsb], [rtmpm])
                    TT(P, "dve", mrg[:, cs], mrg[:, cs], tmpm[:], ALU.add, [rtmpm], [rmrg])
        mb, rmb = mbR.next()
        cp(P, "act", mb[:], mrg[:], [rmrg], [rmb])
        pt, rpt = ptR.next()
        transpose_chunks(P, C, mb, rmb, 8, pt, rpt)
        mT, rmT = mTR.next()
        cp(P, "act", mT[:].rearrange("p k t -> p (k t)"), pt[:], [rpt], [rmT])
        for c in range(2):
            cs = slice(c * 512, (c + 1) * 512)
            pz, rpz = pzR.next()
            for k in range(8):
                MM(P, pz[:, :], mT[:, k, :], wo[:, k, cs], k == 0, k == 7, [rmT, rwAo], [rpz])
            TT(P, "dve", C.x[:, t, cs], C.x[:, t, cs], pz[:, :], ALU.add, [rpz], [C.rx[t]])
    P.barrier()
    nc.sbuf_base = markA

    load_bcast(P, g_n, rgn, io["ffn_norm"], D, None)
    h2T = P.sb([128, 8, NTOK], BF16); rh2T = [P.res() for _ in range(NSUP)]
    hbR = Rot(P, 2, [128, D], BF16)
    for t in range(NTILE):
        hb, rhb = hbR.next()
        norm_T(t, hb, rhb, h2T, t * 128, rh2T[t // 4], eng="act" if t % 2 == 0 else "dve")
    NF = 11
    wg = P.sb([128, 8, NF * 128], BF16); wu = P.sb([128, 8, NF * 128], BF16); wd = P.sb([128, NF, D], BF16); rwB = P.res()
    actT = P.sb([128, NF, 512], BF16); ractT = P.res()
    sgR = Rot(P, 2, [128, 512], F32)
    for grp in range(2):
        f0 = grp * NF * 128
        for k in range(8):
            P.dma("pool", wg[:, k, :], io["w_fg"][k * 128:(k + 1) * 128, f0:f0 + NF * 128], writes=[rwB])
            P.dma("pool", wu[:, k, :], io["w_fu"][k * 128:(k + 1) * 128, f0:f0 + NF * 128], writes=[rwB])
        for f in range(NF):
            P.dma("pool", wd[:, f, :], io["w_fd"][f0 + f * 128:f0 + (f + 1) * 128, :], writes=[rwB])
        for st in range(NSUP):
            ts_ = slice(st * 512, (st + 1) * 512)
            for f in range(NF):
                pg, rpg = pzR.next()
                for k in range(8):
                    MM(P, pg[:, :], wg[:, k, f * 128:(f + 1) * 128], h2T[:, k, ts_], k == 0, k == 7, [rwB, rh2T[st]], [rpg])
                pu, rpu = pzR.next()
                for k in range(8):
                    MM(P, pu[:, :], wu[:, k, f * 128:(f + 1) * 128], h2T[:, k, ts_], k == 0, k == 7, [rwB, rh2T[st]], [rpu])
                sg, rsg = sgR.next()
                ACT(P, sg[:], pg[:, :], AF.Silu, [rpg], [rsg])
                TT(P, "dve", actT[:, f, :], sg[:], pu[:, :], ALU.mult, [rsg, rpu], [ractT])
            for tt in range(4):
                t = st * 4 + tt
                for c in range(2):
                    cs = slice(c * 512, (c + 1) * 512)
                    pz, rpz = pzR.next()
                    for f in range(NF):
                        MM(P, pz[:, :], actT[:, f, tt * 128:(tt + 1) * 128], wd[:, f, cs], f == 0, f == NF - 1, [ractT, rwB], [rpz])
                    TT(P, "dve", C.x[:, t, cs], C.x[:, t, cs], pz[:, :], ALU.add, [rpz], [C.rx[t]])
    P.barrier()
    nc.sbuf_base = markA

    load_bcast(P, g_n, rgn, io["ple_norm"], D, None)
    wpg = P.sb([128, 8, D], BF16); wpp = P.sb([128, 2, D], BF16); rwC = P.res()
    load_w_bf16(P, wpg, rwC, io["w_pg"], 1024, 1024, None)
    load_w_bf16(P, wpp, rwC, io["w_pp"], 256, 1024, None)
    hbR = Rot(P, 2, [128, D], BF16)
    hTR = Rot(P, 2, [128, 8, 128], BF16)
    pfR = Rot(P, 2, [128, 256], BF16)
    pTR = Rot(P, 2, [128, 2, 128], BF16)
    sgR = Rot(P, 2, [128, 512], F32)
    tmpm = P.sb([128, 512], F32); rtmpm = P.res()
    if last:
        g_f = P.sb([128, D], F32); rgf = P.res()
        load_bcast(P, g_f, rgf, io["final_norm"], D, None)
        yR = Rot(P, 2, [128, D], F32)
    for t in range(NTILE):
        hb, rhb = hbR.next()
        hT, rhT = hTR.next()
        norm_T(t, hb, rhb, hT, 0, rhT)
        pf, rpf = pfR.next()
        P.dma("pool", pf[:], io["p"][t * 128:(t + 1) * 128, :], writes=[rpf])
        pt, rpt = ptR.next()
        transpose_chunks(P, C, pf, rpf, 2, pt, rpt)
        pT, rpT = pTR.next()
        cp(P, "dve", pT[:].rearrange("p k t -> p (k t)"), pt[:, 0:256], [rpt], [rpT])
        for c in range(2):
            cs = slice(c * 512, (c + 1) * 512)
            pz, rpz = pzR.next()
            for k in range(8):
                MM(P, pz[:, :], hT[:, k, :], wpg[:, k, cs], k == 0, k == 7, [rhT, rwC], [rpz])
            sg, rsg = sgR.next()
            ACT(P, sg[:], pz[:, :], AF.Sigmoid, [rpz], [rsg])
            pp, rpp = pzR.next()
            for k in range(2):
                MM(P, pp[:, :], pT[:, k, :], wpp[:, k, cs], k == 0, k == 1, [rpT, rwC], [rpp])
            TT(P, "dve", tmpm[:], sg[:], pp[:, :], ALU.mult, [rsg, rpp], [rtmpm])
            TT(P, "dve", C.x[:, t, cs], C.x[:, t, cs], tmpm[:], ALU.add, [rtmpm], [C.rx[t]])
        if last:
            ss, rss = ssR.next()
            y, ry = yR.next()
            MEMSET(P, "pool", ss[:, 0:1], 0.0, [], [rss])
            ACT(P, junk[:], C.x[:, t, :], AF.Square, [C.rx[t], rss], [rss], accum_out=ss[:, 0:1])
            TS(P, "dve", ss[:, 1:2], ss[:, 0:1], 1.0 / D, EPS, ALU.mult, ALU.add, [rss], [rss])
            ACT(P, ss[:, 2:3], ss[:, 1:2], AF.Sqrt, [rss], [rss])
            P.op("dve", lambda e, ss=ss: e.reciprocal(out=ss[:, 3:4], in_=ss[:, 2:3]), reads=[rss], writes=[rss])
            STT(P, y[:], C.x[:, t, :], ss[:, 3:4], g_f[:], ALU.mult, ALU.mult, [C.rx[t], rss, rgf], [ry])
            P.dma("sp", io["y"][t * 128:(t + 1) * 128, :], y[:], reads=[ry])
    P.barrier()
    nc.sbuf_base = base_mark
    nc.psum_base = pbase


P1_WNAMES = {"w_in": [D, 2616], "mix_norm": [1, D], "q_norm": [1, 256], "kv_norm": [1, 256], "w_q_up": [256, 768], "w_kv_up": [256, 1024]}
P3_WNAMES = {"mix_norm": [1, D], "w_bg": [D, 3072], "w_pa": [512, D], "w_pb": [512, D], "w_pc": [512, D], "w_out": [D, D],
             "ffn_norm": [1, D], "w_fg": [D, DFF], "w_fu": [D, DFF], "w_fd": [DFF, D], "ple_norm": [1, D], "w_pg": [D, D],
             "w_pp": [256, D], "p": [NTOK, 256]}


def build_tok_program(do_p3, do_p1, last):
    P = Prog()
    x_in = P.dram("x_in", [NTOK, D], F32, "ExternalInput").ap()
    pos_in = P.dram("pos_in", [128, NTILE], I32, "ExternalInput").ap()
    ident = P.dram("ident", [128, 128], F32, "ExternalInput").ap()
    invf = P.dram("invf", [128, 48], F32, "ExternalInput").ap()
    C = Common(P, x_in, pos_in, ident, invf)
    if do_p3:
        io = {}
        for k, shp in P3_WNAMES.items():
            io[k] = P.dram("p3_" + k, shp, F32, "ExternalInput").ap()
        io["o"] = P.dram("p3_o", [4, NTOK, 384], BF16, "ExternalInput").ap()
        if last:
            io["final_norm"] = P.dram("p3_final_norm", [1, D], F32, "ExternalInput").ap()
            io["y"] = P.dram("y", [NTOK, D], F32, "ExternalOutput").ap()
        emit_p3(P, C, io, last)
        if not last:
            x_out = P.dram("x_out", [NTOK, D], F32, "ExternalOutput").ap()
            for t in range(NTILE):
                P.dma("sp", x_out[t * 128:(t + 1) * 128, :], C.x[:, t, :], reads=[C.rx[t]])
    if do_p1:
        io = {}
        for k, shp in P1_WNAMES.items():
            io[k] = P.dram("p1_" + k, shp, F32, "ExternalInput").ap()
        for k, (shp, dt) in P1_X.items():
            io[k] = P.dram(k, [4] + shp, dt, "ExternalOutput").ap()
        emit_p1(P, C, io)
    return P.build()


def build_p2_program():
    P = Prog()
    io = {}
    for k, (shp, dt) in P2_IN.items():
        io[k] = P.dram(k, shp, dt, "ExternalInput").ap()
    for k, shp in P2_W.items():
        io[k] = P.dram(k, shp, F32, "ExternalInput").ap()
    identd = P.dram("ident", [128, 128], F32, "ExternalInput").ap()
    io["o"] = P.dram("o", [4, NTOK, 384], BF16, "ExternalOutput").ap()
    identf = P.sb([128, 128], F32)
    ident = P.sb([128, 128], BF16)
    rid = P.res()
    P.dma("sp", identf[:], identd, writes=[rid])
    cp(P, "dve", ident[:], identf[:], [rid], [rid])
    emit_p2(P, io, ident, rid)
    return P.build()


X1_TO_P2 = {"xqa": "qa", "xqg": "qg", "xka": "ka", "xva": "va", "xqb": "qb", "xkb": "kb", "xvb": "vb",
            "xqc": "qc", "xkc": "kc", "xvc": "vc", "xg": "g"}


def p1_weights(inp, l):
    m = {}
    m["p1_w_in"] = np.ascontiguousarray(inp["w_in"][l][:, W_IN_PERM])
    m["p1_mix_norm"] = np.ascontiguousarray(inp["mix_norm"][l][None, :])
    m["p1_q_norm"] = np.ascontiguousarray(inp["c_q_norm"][l][None, :])
    m["p1_kv_norm"] = np.ascontiguousarray(inp["c_kv_norm"][l][None, :])
    m["p1_w_q_up"] = np.ascontiguousarray(inp["c_w_q_up"][l])
    m["p1_w_kv_up"] = np.ascontiguousarray(inp["c_w_kv_up"][l])
    return m


def p2_weights(inp, l, r):
    m = {}
    for w in ("k", "v"):
        m["pos" + w] = np.ascontiguousarray(inp[f"a_cmp_pos_{w}"][l].reshape(16, 128).T)
        m["w1" + w] = np.ascontiguousarray(inp[f"a_cmp_w1_{w}"][l])
        m["w2" + w] = np.ascontiguousarray(inp[f"a_cmp_w2_{w}"][l])
    m["sinks"] = np.ascontiguousarray(inp["b_sinks"][l][2 * r:2 * r + 2][None, :])
    m["selmap"] = selmap_const()
    m["ident"] = np.eye(128, dtype=np.float32)
    return m


def p3_weights(inp, l, b, r, last):
    m = {}
    src = {"mix_norm": "mix_norm", "w_bg": "w_branch_gate", "w_pa": "w_branch_a", "w_pb": "w_branch_b", "w_pc": "w_branch_c",
           "w_out": "w_out", "ffn_norm": "ffn_norm", "w_fg": "w_ffn_gate", "w_fu": "w_ffn_up", "w_fd": "w_ffn_down",
           "ple_norm": "ple_norm", "w_pg": "w_ple_gate", "w_pp": "w_ple_proj"}
    for k, s in src.items():
        a = inp[s][l]
        m["p3_" + k] = np.ascontiguousarray(a[None, :] if a.ndim == 1 else a)
    m["p3_p"] = np.ascontiguousarray(inp["p"][l, b, r * NTOK:(r + 1) * NTOK])
    if last:
        m["p3_final_norm"] = np.ascontiguousarray(inp["final_norm"][None, :])
    return m


def all_to_all(outs, names):
    res = []
    for core in range(8):
        b, r = divmod(core, 4)
        res.append({nm: np.ascontiguousarray(np.stack([outs[4 * b + s][nm][r] for s in range(4)], axis=0)) for nm in names})
    return res


X1_LAYOUT = [("xqa", [2, 64, NTOK], 0, 0), ("xqg", [2, 64, NTOK], 0, 128), ("xka", [4, 64, NTOK], 1, 0),
             ("xva", [2, NTOK, 64], 2, 0), ("xqb", [2, 64, NTOK], 2, 128), ("xkb", [64, NTOK], 3, 0),
             ("xvb", [NTOK, 64], 3, 64), ("xqc", [2, 96, NTOK], 4, 0), ("xkc", [2, 96, NTOK], 5, 0),
             ("xvc", [2, NTOK, 64], 6, 0)]
X1_K = 7
X1_CR = 256


def x1_views(rows):
    views = {}
    for nm, shp, k, r0 in X1_LAYOUT:
        n = int(np.prod(shp)) // 2048
        v = rows(k, r0, n)
        if nm in ("xqa", "xqg", "xka", "xqb", "xqc", "xkc"):
            v = v.rearrange("e (h d) t -> e h d t", h=shp[0])
        elif nm in ("xva", "xvc"):
            v = v.rearrange("e r c -> e (r c)").rearrange("e (h t d) -> e h t d", h=shp[0], d=64)
        elif nm == "xvb":
            v = v.rearrange("e r c -> e (r c)").rearrange("e (t d) -> e t d", d=64)
        views[nm] = v
    return views


def build_fused_program():
    P = Prog()
    nc = P.nc
    x_in = P.dram("x_in", [NTOK, D], F32, "ExternalInput").ap()
    pos_in = P.dram("pos_in", [128, NTILE], I32, "ExternalInput").ap()
    ident = P.dram("ident", [128, 128], F32, "ExternalInput").ap()
    invf = P.dram("invf", [128, 48], F32, "ExternalInput").ap()
    y_out = P.dram("y", [NTOK, D], F32, "ExternalOutput").ap()
    RD = X1_K * X1_CR
    X1 = P.dram("ex_x1", [4 * RD, 2048], BF16, "Internal").ap()
    G1 = P.dram("ex_g1", [16 * RD, 2048], BF16, "Internal").ap()
    M1 = P.dram("ex_m1", [4 * RD, 2048], BF16, "Internal").ap()
    XG = P.dram("ex_xg", [4 * 16, 768], F32, "Internal").ap()
    GG = P.dram("ex_gg", [16 * 16, 768], F32, "Internal").ap()
    MG = P.dram("ex_mg", [4 * 16, 768], F32, "Internal").ap()
    O2 = P.dram("ex_o2", [12 * NTOK, 128], BF16, "Internal").ap()
    GO = P.dram("ex_go", [48 * NTOK, 128], BF16, "Internal").ap()
    MO = P.dram("ex_mo", [12 * NTOK, 128], BF16, "Internal").ap()
    C = Common(P, x_in, pos_in, ident, invf)
    mark, pmark = nc.sbuf_base, nc.psum_base

    def phase_end():
        P.barrier()
        nc.sbuf_base = mark
        nc.psum_base = pmark

    def exchange_chunked(src, gath, mine, nchunk, cr):
        P.barrier()
        rc_ = P.res()
        for j in range(4 * nchunk):
            P.allgather(src[j * cr:(j + 1) * cr, :], gath[j * 4 * cr:(j + 1) * 4 * cr, :], rc_)
        rm_ = P.res()
        g3 = gath.rearrange("(d x) c -> d x c", d=4)
        P.dma("pool", mine, (lambda: g3[bass.ds(P.rank(), 1), :, :]), reads=[rc_], writes=[rm_])
        P.barrier()

    def exchange_small(src, gath, mine):
        P.barrier()
        rc_ = P.res()
        P.allgather(src, gath, rc_)
        rm_ = P.res()
        g4 = gath.rearrange("(s d r) c -> s d r c", s=4, d=4)
        m3 = mine.rearrange("(s r) c -> s r c", s=4)
        P.dma("pool", m3, (lambda: g4[:, bass.ds(P.rank(), 1), :, :]), reads=[rc_], writes=[rm_])
        P.barrier()

    X1v = X1.rearrange("(e r) c -> e r c", e=4)
    M1v = M1.rearrange("(k s i) c -> k s i c", k=X1_K, s=4)
    MOv = MO.rearrange("(k s i) c -> k s i c", k=2, s=4)
    for l in range(2):
        last = l == 1
        io = {}
        for k, shp in P1_WNAMES.items():
            io[k] = P.dram(f"l{l}_p1_{k}", shp, F32, "ExternalInput").ap()
        io.update(x1_views(lambda k, r0, n: X1v[:, k * X1_CR + r0:k * X1_CR + r0 + n, :]))
        io["xg"] = XG.rearrange("(e a) (p c) -> e (a p) c", e=4, c=6)
        emit_p1(P, C, io)
        P.barrier()
        g3 = G1.rearrange("(d x) c -> d x c", d=4)
        groups = {"nsa": (0, 3), "swa": (3, 4), "mla": (4, 7)}
        rcg, rmg = {}, {}
        for gname, (k0, k1) in groups.items():
            rcg[gname] = P.res()
            rmg[gname] = P.res()
            for d in range(4):
                for k in range(k0, k1):
                    j = d * X1_K + k
                    P.allgather(X1[j * X1_CR:(j + 1) * X1_CR, :], G1[j * 4 * X1_CR:(j + 1) * 4 * X1_CR, :], rcg[gname], sem="cc_" + gname)
            if gname == "nsa":
                rcg["g"] = P.res()
                rmg["g"] = P.res()
                P.allgather(XG, GG, rcg["g"], sem="cc_g")

        def select(gname):
            k0, k1 = groups[gname]
            r0, r1 = k0 * 4 * X1_CR, k1 * 4 * X1_CR
            P.dma("sp", M1[r0:r1, :], (lambda: g3[bass.ds(P.rank("sp"), 1), r0:r1, :]), reads=[rcg[gname]], writes=[rmg[gname]])

        select("nsa")
        gg4 = GG.rearrange("(s d r) c -> s d r c", s=4, d=4)
        P.dma("sp", MG.rearrange("(s r) c -> s r c", s=4), (lambda: gg4[:, bass.ds(P.rank("sp"), 1), :, :]), reads=[rcg["g"]], writes=[rmg["g"]])
        nc.sbuf_base = mark
        nc.psum_base = pmark
        io = {}
        v = x1_views(lambda k, r0, n: M1v[k][:, r0:r0 + n, :])
        for k, vv in v.items():
            io[X1_TO_P2[k]] = vv
        io["g"] = MG.rearrange("(e a) (p c) -> e (a p) c", e=4, c=6)
        io["deps"] = {"nsa": [rmg["nsa"], rmg["g"]], "swa": [rmg["swa"]], "mla": [rmg["mla"]]}
        io["pre_swa"] = lambda: select("swa")
        io["pre_mla"] = lambda: select("mla")
        for k, shp in P2_W.items():
            if k == "selmap":
                if l == 0:
                    selmap_ap = P.dram("selmap", shp, F32, "ExternalInput").ap()
                io[k] = selmap_ap
            else:
                io[k] = P.dram(f"l{l}_p2_{k}", shp, F32, "ExternalInput").ap()
        O2v = O2.rearrange("(m e t) c -> m e t c", m=3, e=4)
        ro = [P.res(), P.res(), P.res()]
        rco = P.res()
        io["o_mix"] = lambda m: O2v[m]
        io["ro"] = ro

        def post_mix(m, ro=ro, rco=rco):
            for d in range(4):
                P.allgather(O2[(m * 4 + d) * NTOK:(m * 4 + d + 1) * NTOK, :], GO[((d * 3 + m) * 4) * NTOK:((d * 3 + m) * 4 + 4) * NTOK, :], rco,
                            reads=[ro[m]] if d == 0 else (), sem="cc_o")

        io["post_mix"] = post_mix
        emit_p2(P, io, C.ident, C.rid)
        P.barrier()
        nc.sbuf_base = mark
        nc.psum_base = pmark
        rmo = P.res()
        go3 = GO.rearrange("(d x) c -> d x c", d=4)
        P.dma("sp", MO, (lambda: go3[bass.ds(P.rank("sp"), 1), :, :]), reads=[rco], writes=[rmo])
        MOv = MO.rearrange("(m s t) c -> m s t c", m=3, s=4)
        io = {}
        for k, shp in P3_WNAMES.items():
            io[k] = P.dram(f"l{l}_p3_{k}", shp, F32, "ExternalInput").ap()
        io["o_tile3"] = lambda t, m: MOv[m][:, t * 128:(t + 1) * 128, :]
        io["o_dep"] = [rmo]
        if last:
            io["final_norm"] = P.dram("final_norm", [1, D], F32, "ExternalInput").ap()
            io["y"] = y_out
        emit_p3(P, C, io, last)
        phase_end()
    return P.build()


_PROGS = {}


def _prog(key):
    if key not in _PROGS:
        if key == "fused":
            _PROGS[key] = build_fused_program()
        elif key == "p2":
            _PROGS[key] = build_p2_program()
        else:
            _PROGS[key] = build_tok_program(*key)
    return _PROGS[key]


def kernel(**inp):
    inp = {k: np.asarray(v) for k, v in inp.items()}
    cst = const_inputs()
    cores = list(range(8))
    maps = []
    for c in cores:
        b, r = divmod(c, 4)
        m = dict(cst)
        m["x_in"] = np.ascontiguousarray(inp["x"][b, r * NTOK:(r + 1) * NTOK]).astype(np.float32)
        m["pos_in"] = np.ascontiguousarray(inp["positions"][b, r * NTOK:(r + 1) * NTOK].reshape(NTILE, 128).T.astype(np.int32))
        m["selmap"] = selmap_const()
        m["final_norm"] = np.ascontiguousarray(inp["final_norm"][None, :])
        for l in range(2):
            for k, v in p1_weights(inp, l).items():
                m[f"l{l}_{k}"] = v
            for k, v in p2_weights(inp, l, r).items():
                if k not in ("selmap", "ident"):
                    m[f"l{l}_p2_{k}"] = v
            for k, v in p3_weights(inp, l, b, r, False).items():
                m[f"l{l}_{k}"] = v
        maps.append(m)
    res = run_bass_kernel_spmd(_prog("fused"), maps, core_ids=cores).results
    y = np.stack([np.concatenate([np.asarray(res[4 * b + r]["y"]) for r in range(4)], axis=0) for b in range(2)], axis=0)
    return y.astype(np.float32)
```

```python
import numpy as np
import ml_dtypes
import concourse.bass as bass
import concourse.mybir as mybir
from concourse.bass_utils import run_bass_kernel_spmd

F32 = mybir.dt.float32
BF16 = mybir.dt.bfloat16
I32 = mybir.dt.int32
AF = mybir.ActivationFunctionType
ALU = mybir.AluOpType
AX = mybir.AxisListType

D = 1024
S = 8192
NTOK = 2048
NTILE = 16
NSUP = 4
EPS = 1e-6
DFF = 2816
NEG = -30000.0


class Res:
    __slots__ = ("name", "w", "r", "wdma")

    def __init__(self, name):
        self.name = name
        self.w = None
        self.r = {}


class Prog:
    ENG = ("pe", "act", "dve", "pool", "sp")

    def __init__(self):
        self.nc = bass.Bass("TRN2", target_bir_lowering=False)
        nc = self.nc
        self.cnt = {k: 0 for k in self.ENG}
        self.ops = {k: [] for k in self.ENG}
        self.seen = {k: {} for k in self.ENG}
        self.semobj = {}
        for k in self.ENG:
            self.semobj["s_" + k] = nc.alloc_semaphore("s_" + k)
        self.dsem = {}
        self.free_dsems = []
        self.nres = 0
        self.nname = 0
        self.ncc = 0
        self._rank = None

    def sb(self, shape, dt, name=None):
        self.nname += 1
        return self.nc.alloc_sbuf_tensor(name or f"t{self.nname}", list(shape), dt)

    def ps(self, shape, dt=F32, name=None):
        self.nname += 1
        return self.nc.alloc_psum_tensor(name or f"p{self.nname}", list(shape), dt)

    def res(self, name=None):
        self.nres += 1
        return Res(name or f"r{self.nres}")

    def dram(self, name, shape, dt, kind):
        return self.nc.dram_tensor(name, list(shape), dt, kind=kind)

    def _waits(self, eng, reads, writes, dma_key=None):
        waits = {}

        def need(tok):
            if tok is None:
                return
            s, v = tok
            if waits.get(s, 0) < v:
                waits[s] = v

        for r in reads:
            need(r.w)
        for w in writes:
            if not (dma_key is not None and w.w is not None and w.w[0] == dma_key):
                need(w.w)
            for tok in w.r.values():
                need(tok)
        wl = []
        for s, v in waits.items():
            if eng == "pe" and s == "s_pe":
                continue
            if self.seen[eng].get(s, 0) >= v:
                continue
            self.seen[eng][s] = v
            wl.append((s, v))
        return wl

    def op(self, eng, fn, reads=(), writes=()):
        wl = self._waits(eng, reads, writes)
        self.cnt[eng] += 1
        sname = "s_" + eng
        tok = (sname, self.cnt[eng])
        self.ops[eng].append((wl, fn, (sname, 1)))
        for r in reads:
            r.r[eng] = tok
        for w in writes:
            w.w = tok
            w.r = {}

    def dma(self, q, out, in_, reads=(), writes=(), chan=None, **kw):
        key = (list(writes) + list(reads))[0]
        if key.name not in self.dsem:
            if self.free_dsems:
                self.dsem[key.name] = self.free_dsems.pop()
            else:
                sname = "d_" + key.name
                self.semobj[sname] = self.nc.alloc_semaphore(sname)
                self.dsem[key.name] = [sname, 0]
        d = self.dsem[key.name]
        wl = self._waits(q, reads, writes, dma_key=d[0])
        d[1] += 16
        tok = (d[0], d[1])
        self.ops[q].append((wl, (lambda e: e.dma_start(out=out, in_=(in_() if callable(in_) else in_), **kw)), (d[0], 16)))
        for r in reads:
            r.r["dma:" + key.name] = tok
        for w in writes:
            w.w = tok
            w.wdma = True
            w.r = {}

    def barrier(self):
        allw = [(d[0], d[1]) for d in self.dsem.values() if d[1] > 0]
        for k in self.ENG:
            if self.cnt[k] > 0:
                allw.append(("s_" + k, self.cnt[k]))
        for k in self.ENG:
            wl = []
            for s, v in allw:
                if self.seen[k].get(s, 0) >= v:
                    continue
                self.seen[k][s] = v
                wl.append((s, v))
            if wl:
                self.ops[k].append((wl, None, None))
        self.free_dsems.extend(self.dsem.values())
        self.dsem = {}

    def freg(self, e, val):
        if not hasattr(self, "_fregs"):
            self._fregs = {}
        if val not in self._fregs:
            self._fregs[val] = e.to_reg(float(val))
        return self._fregs[val]

    def rank(self, eng="pool"):
        if self._rank is None:
            self._rank = {}
        if eng not in self._rank:
            et = {"pool": mybir.EngineType.Pool, "sp": mybir.EngineType.SP}[eng]
            self._rank[eng] = self.nc.partition_id([et]) % 4
        return self._rank[eng]

    def allgather(self, ins_ap, outs_ap, rres, reads=(), sem="cc"):
        sname = "s_" + sem
        if sname not in self.semobj:
            self.semobj[sname] = self.nc.alloc_semaphore(sname)
            self.cccnt = getattr(self, "cccnt", {})
            self.cccnt[sname] = 0
        self.cccnt[sname] += 1
        wl = self._waits("pool", list(reads), []) if reads else []
        self.ops["pool"].append((wl, (lambda e: e.collective_compute("AllGather", ALU.bypass, replica_groups=[[0, 1, 2, 3], [4, 5, 6, 7]],
                                                                    ins=[ins_ap], outs=[outs_ap])), (sname, 1)))
        rres.w = (sname, self.cccnt[sname])
        rres.r = {}

    def build(self):
        nc = self.nc
        self.barrier()
        with nc.Block() as block:
            def emit(k):
                def body(e):
                    for wl, fn, inc in self.ops[k]:
                        for s, v in wl:
                            e.wait_ge(self.semobj[s], v)
                        if fn is None:
                            continue
                        ins = fn(e)
                        ins.then_inc(self.semobj[inc[0]], inc[1])
                return body
            block.tensor(emit("pe"))
            block.scalar(emit("act"))
            block.vector(emit("dve"))
            block.gpsimd(emit("pool"))
            block.sync(emit("sp"))
        return nc


class Rot:
    def __init__(self, P, n, shape, dt, psum=False):
        self.t = [(P.ps(shape, dt) if psum else P.sb(shape, dt)) for _ in range(n)]
        self.r = [P.res() for _ in range(n)]
        self.i = -1
        self.n = n

    def next(self):
        self.i = (self.i + 1) % self.n
        return self.t[self.i], self.r[self.i]

    def cur(self):
        return self.t[self.i], self.r[self.i]


W_IN_PERM = np.concatenate([
    np.arange(0, 512), np.arange(512, 640), np.arange(768, 896), np.arange(1024, 1152),
    np.arange(1304, 1816), np.arange(1816, 1944),
    np.arange(640, 768), np.arange(896, 1024), np.arange(1152, 1280), np.arange(1944, 2072),
    np.arange(2072, 2328), np.arange(2328, 2584), np.arange(2584, 2616), np.arange(1280, 1304)])


def const_inputs():
    c = {}
    c["ident"] = np.eye(128, dtype=np.float32)
    f64 = (10000.0 ** (-np.arange(0, 64, 2, dtype=np.float32) / 64)).astype(np.float32)
    f32 = (10000.0 ** (-np.arange(0, 32, 2, dtype=np.float32) / 32)).astype(np.float32)
    c["invf"] = np.ascontiguousarray(np.broadcast_to(np.concatenate([f64, f32])[None, :], (128, 48))).astype(np.float32)
    return c


P1_X = {
    "xqa": ([2, 64, NTOK], BF16), "xqg": ([2, 64, NTOK], BF16), "xka": ([4, 64, NTOK], BF16),
    "xva": ([2, NTOK, 64], BF16), "xqb": ([2, 64, NTOK], BF16), "xkb": ([64, NTOK], BF16),
    "xvb": ([NTOK, 64], BF16), "xqc": ([2, 96, NTOK], BF16), "xkc": ([2, 96, NTOK], BF16),
    "xvc": ([2, NTOK, 64], BF16), "xg": ([NTOK, 6], F32),
}


class Common:
    def __init__(self, P, x_in, pos_in, ident_in, invf_in):
        self.P = P
        nc = P.nc
        self.x = P.sb([128, NTILE, D], F32, "xres")
        self.rx = [P.res() for _ in range(NTILE)]
        for t in range(NTILE):
            P.dma("sp", self.x[:, t, :], x_in[t * 128:(t + 1) * 128, :], writes=[self.rx[t]], chan="xin")
        self.ident = P.sb([128, 128], BF16)
        self.rid = P.res()
        self.cos = P.sb([128, NTILE, 48], F32)
        self.sin = P.sb([128, NTILE, 48], F32)
        self.rcs = P.res()
        mark = nc.sbuf_base
        self.identf = P.sb([128, 128], F32)
        P.dma("sp", self.identf[:], ident_in, writes=[self.rid], chan="c0")
        P.op("dve", lambda e: e.tensor_copy(out=self.ident[:], in_=self.identf[:]), reads=[self.rid], writes=[self.rid])
        pos_i = P.sb([128, NTILE], I32)
        pos_f = P.sb([128, NTILE], F32)
        invf = P.sb([128, 48], F32)
        rp = P.res()
        P.dma("sp", pos_i[:], pos_in, writes=[rp], chan="c0")
        P.dma("sp", invf[:], invf_in, writes=[rp], chan="c0")
        P.op("dve", lambda e: e.tensor_copy(out=pos_f[:], in_=pos_i[:]), reads=[rp], writes=[rp])
        ang = P.sb([128, NTILE, 48], F32)
        ra = P.res()
        for t in range(NTILE):
            P.op("dve", (lambda e, t=t: e.tensor_scalar(out=ang[:, t, :], in0=invf[:], scalar1=pos_f[:, t:t + 1], scalar2=None, op0=ALU.mult)),
                 reads=[rp], writes=[ra])
        tmp = P.sb([128, NTILE, 48], F32)
        ni = P.sb([128, NTILE, 48], I32)
        nf = P.sb([128, NTILE, 48], F32)
        msk = P.sb([128, NTILE, 48], F32)
        C1 = 6.28125
        C2 = 2.0 * np.pi - 6.28125
        PI = float(np.pi)
        TS = lambda **kw: (lambda e: e.tensor_scalar(**kw))
        STT = lambda **kw: (lambda e: e.scalar_tensor_tensor(**kw))
        A2 = lambda ap: ap.rearrange("p t c -> p (t c)")
        P.op("dve", TS(out=A2(ni[:]), in0=A2(ang[:]), scalar1=float(1.0 / (2.0 * np.pi)), scalar2=None, op0=ALU.mult), reads=[ra], writes=[ra])
        P.op("dve", lambda e: e.tensor_copy(out=A2(nf[:]), in_=A2(ni[:])), reads=[ra], writes=[ra])
        P.op("dve", STT(out=A2(tmp[:]), in0=A2(nf[:]), scalar=-C1, in1=A2(ang[:]), op0=ALU.mult, op1=ALU.add), reads=[ra], writes=[ra])
        P.op("dve", STT(out=A2(tmp[:]), in0=A2(nf[:]), scalar=-C2, in1=A2(tmp[:]), op0=ALU.mult, op1=ALU.add), reads=[ra], writes=[ra])
        P.op("dve", TS(out=A2(msk[:]), in0=A2(tmp[:]), scalar1=PI, scalar2=None, op0=ALU.is_gt), reads=[ra], writes=[ra])
        P.op("dve", STT(out=A2(tmp[:]), in0=A2(msk[:]), scalar=-2.0 * PI, in1=A2(tmp[:]), op0=ALU.mult, op1=ALU.add), reads=[ra], writes=[ra])
        P.op("dve", TS(out=A2(msk[:]), in0=A2(tmp[:]), scalar1=-PI, scalar2=None, op0=ALU.is_lt), reads=[ra], writes=[ra])
        P.op("dve", STT(out=A2(tmp[:]), in0=A2(msk[:]), scalar=2.0 * PI, in1=A2(tmp[:]), op0=ALU.mult, op1=ALU.add), reads=[ra], writes=[ra])
        P.op("act", lambda e: e.activation(out=self.sin[:], in_=tmp[:], func=AF.Sin), reads=[ra], writes=[ra, self.rcs])
        P.op("dve", TS(out=A2(tmp[:]), in0=A2(tmp[:]), scalar1=PI / 2.0, scalar2=None, op0=ALU.add), reads=[ra], writes=[ra])
        P.op("dve", TS(out=A2(msk[:]), in0=A2(tmp[:]), scalar1=PI, scalar2=None, op0=ALU.is_gt), reads=[ra], writes=[ra])
        P.op("dve", STT(out=A2(tmp[:]), in0=A2(msk[:]), scalar=-2.0 * PI, in1=A2(tmp[:]), op0=ALU.mult, op1=ALU.add), reads=[ra], writes=[ra])
        P.op("act", lambda e: e.activation(out=self.cos[:], in_=tmp[:], func=AF.Sin), reads=[ra], writes=[ra, self.rcs])
        P.barrier()
        nc.sbuf_base = mark

    def rms_to_T(self, src_fn, rsrc, g_tile, rg, hT, rhT, col0, ncols, scratch):
        pass


def load_w_bf16(P, dst, rdst, w_dram, rows, cols, chan):
    k = rows // 128
    for i in range(k):
        P.dma("pool", dst[:, i, :], w_dram[i * 128:(i + 1) * 128, :], writes=[rdst], chan=chan)


def load_bcast(P, dst, rdst, v_dram, n, chan):
    P.dma("sp", dst[:], v_dram.partition_broadcast(128), writes=[rdst], chan=chan)


def rmsnorm_tile(P, C, src, rsrc, n, g_tile, rg, out_bf, rout, tmp):
    junk, ss, rt = tmp
    P.op("pool", lambda e: e.memset(ss[:, 0:1], 0.0), writes=[rt])
    P.op("act", lambda e: e.activation(out=junk[:, 0:n], in_=src, func=AF.Square, accum_out=ss[:, 0:1]), reads=[rsrc, rt], writes=[rt])
    P.op("dve", lambda e: e.tensor_scalar(out=ss[:, 1:2], in0=ss[:, 0:1], scalar1=1.0 / n, scalar2=EPS, op0=ALU.mult, op1=ALU.add), reads=[rt], writes=[rt])
    P.op("act", lambda e: e.activation(out=ss[:, 2:3], in_=ss[:, 1:2], func=AF.Sqrt), reads=[rt], writes=[rt])
    P.op("dve", lambda e: e.reciprocal(out=ss[:, 3:4], in_=ss[:, 2:3]), reads=[rt], writes=[rt])
    P.op("dve", lambda e: e.scalar_tensor_tensor(out=out_bf, in0=src, scalar=ss[:, 3:4], in1=g_tile, op0=ALU.mult, op1=ALU.mult),
         reads=[rsrc, rt, rg], writes=[rout])


def transpose_chunks(P, C, src_bf, rsrc, nch, pt, rpt, width=128):
    for c in range(nch):
        P.op("pe", (lambda e, c=c: e.transpose(out=pt[0:width, c * 128:(c + 1) * 128], in_=src_bf[:, c * width:(c + 1) * width], identity=C.ident[:])),
             reads=[rsrc, C.rid], writes=[rpt])


def TT(P, eng, out, in0, in1, op, reads, writes):
    P.op(eng, lambda e: e.tensor_tensor(out=out, in0=in0, in1=in1, op=op), reads=reads, writes=writes)


def TS(P, eng, out, in0, s1, s2, op0, op1, reads, writes):
    if op1 is None:
        P.op(eng, lambda e: e.tensor_scalar(out=out, in0=in0, scalar1=s1, scalar2=None, op0=op0), reads=reads, writes=writes)
    else:
        P.op(eng, lambda e: e.tensor_scalar(out=out, in0=in0, scalar1=s1, scalar2=s2, op0=op0, op1=op1), reads=reads, writes=writes)


def STT(P, out, in0, scalar, in1, op0, op1, reads, writes):
    P.op("dve", lambda e: e.scalar_tensor_tensor(out=out, in0=in0, scalar=scalar, in1=in1, op0=op0, op1=op1), reads=reads, writes=writes)


def ACT(P, out, in_, func, reads, writes, **kw):
    P.op("act", lambda e: e.activation(out=out, in_=in_, func=func, **kw), reads=reads, writes=writes)


def MM(P, out, lhsT, rhs, start, stop, reads, writes, skip=False):
    if skip:
        P.op("pe", lambda e: e.matmul(out, lhsT=lhsT, rhs=rhs, start=start, stop=stop, skip_group_check=True), reads=reads, writes=writes)
    else:
        P.op("pe", lambda e: e.matmul(out, lhsT=lhsT, rhs=rhs, start=start, stop=stop), reads=reads, writes=writes)


def TR(P, out, in_, ident, reads, writes):
    P.op("pe", lambda e: e.transpose(out=out, in_=in_, identity=ident), reads=reads, writes=writes)


def MEMSET(P, eng, ap, val, reads, writes):
    P.op(eng, lambda e: e.memset(ap, val), reads=reads, writes=writes)


def cp(P, eng, out, in_, reads, writes):
    if eng == "act":
        P.op("act", lambda e: e.copy(out=out, in_=in_), reads=reads, writes=writes)
    else:
        P.op(eng, lambda e: e.tensor_copy(out=out, in_=in_), reads=reads, writes=writes)


def emit_p1(P, C, io):
    nc = P.nc
    w_in = P.sb([128, 8, 2616], BF16); rw = P.res()
    load_w_bf16(P, w_in, rw, io["w_in"], 1024, 2616, "w1")
    w_qu = P.sb([128, 2, 768], BF16); w_kvu = P.sb([128, 2, 1024], BF16); rwm = P.res()
    load_w_bf16(P, w_qu, rwm, io["w_q_up"], 256, 768, "w1")
    load_w_bf16(P, w_kvu, rwm, io["w_kv_up"], 256, 1024, "w1")
    g_mix = P.sb([128, D], F32); g_q = P.sb([128, 256], F32); g_kv = P.sb([128, 256], F32); rg = P.res()
    load_bcast(P, g_mix, rg, io["mix_norm"], D, "w2")
    load_bcast(P, g_q, rg, io["q_norm"], 256, "w2")
    load_bcast(P, g_kv, rg, io["kv_norm"], 256, "w2")

    junk = P.sb([128, D], BF16)
    ssR = Rot(P, 2, [128, 4], F32)
    hbR = Rot(P, 1, [128, D], BF16)
    hTR = Rot(P, 2, [128, 8, 128], BF16)
    ptR = Rot(P, 2, [128, 1024], BF16, psum=True)
    pzR = Rot(P, 3, [128, 512], F32, psum=True)
    zsR = Rot(P, 1, [128, 2616], F32)
    zs_rc = [[P.res() for _ in range(6)] for _ in range(zsR.n)]
    rqR = Rot(P, 2, [128, 26, 64], BF16)
    tmpA = P.sb([128, 12, 32], F32); tmpB = P.sb([128, 12, 32], F32); rtA = P.res()
    tmpC = P.sb([128, 12, 32], F32); tmpD = P.sb([128, 12, 32], F32); rtC = P.res()
    stq = P.sb([128, 13, 512], BF16); rstq = P.res()
    stv = P.sb([128, 4, 8, 64], BF16); rstv = P.res()
    stqc = P.sb([128, 8, 512], BF16); rstqc = P.res()
    stkc = P.sb([128, 8, 512], BF16); rstkc = P.res()
    stvc = P.sb([128, 4, 8, 64], BF16); rstvc = P.res()
    stg = P.sb([128, 4, 24], F32); rstg = P.res()
    cnR = Rot(P, 1, [128, 512], BF16)
    cnTR = Rot(P, 1, [128, 4, 128], BF16)
    qfR = Rot(P, 1, [128, 8, 96], BF16)
    kfR = Rot(P, 1, [128, 8, 96], BF16)
    kpe = P.sb([128, 32], F32); rkpe = P.res()
    qsb = P.sb([128, 768], F32); rqsb = P.res()
    t16 = [P.sb([128, 8, 16], F32) for _ in range(4)]; rt16 = P.res()
    ktmp = P.sb([128, 4, 16], F32)

    for t in range(NTILE):
        st, tt = t // 4, t % 4
        s0 = st * 512
        xt = C.x[:, t, :]
        ss, rss = ssR.next()
        hb, rhb = hbR.next()
        rmsnorm_tile(P, C, xt, C.rx[t], D, g_mix[:], rg, hb[:], rhb, (junk, ss, rss))
        pt, rpt = ptR.next()
        transpose_chunks(P, C, hb, rhb, 8, pt, rpt)
        hT, rhT = hTR.next()
        P.op("act", lambda e, hT=hT, pt=pt: e.copy(out=hT[:].rearrange("p k t -> p (k t)"), in_=pt[:]), reads=[rpt], writes=[rhT])
        zs, rzs = zsR.next()
        rzc = zs_rc[zsR.i]
        for c in range(6):
            c0 = c * 512
            n = min(512, 2616 - c0)
            pz, rpz = pzR.next()
            for k in range(8):
                P.op("pe", (lambda e, pz=pz, hT=hT, k=k, c0=c0, n=n: e.matmul(pz[:, 0:n], lhsT=hT[:, k, :], rhs=w_in[:, k, c0:c0 + n], start=(k == 0), stop=(k == 7))),
                     reads=[rhT, rw], writes=[rpz])
            P.op("act", (lambda e, pz=pz, zs=zs, c0=c0, n=n: e.copy(out=zs[:, c0:c0 + n], in_=pz[:, 0:n])), reads=[rpz], writes=[rzc[c]])
        if t == 0 and "dbg_zs" in io:
            P.dma("sp", io["dbg_zs"], zs[:], reads=rzc, chan="dbg")
            P.dma("sp", io["dbg_hb"], hb[:], reads=[rhb], chan="dbg")
            P.dma("sp", io["dbg_ss"], ss[:], reads=[rss], chan="dbg")
            P.dma("sp", io["dbg_hT"], hT[:].rearrange("p k t -> p (k t)"), reads=[rhT], chan="dbg")
        rq, rrq = rqR.next()
        zv = zs[:, 0:1536].rearrange("p (h two d) -> p h two d", h=24, two=2)
        rqv = rq[:].rearrange("p h (two d) -> p h two d", two=2)
        cb = C.cos[:, t, 0:32].unsqueeze(1).broadcast_to([128, 12, 32])
        sb_ = C.sin[:, t, 0:32].unsqueeze(1).broadcast_to([128, 12, 32])
        zr = rzc[0:3]
        for hh in range(2):
            hs = slice(hh * 12, hh * 12 + 12)
            x1, x2 = zv[:, hs, 0, :], zv[:, hs, 1, :]
            o1, o2 = rqv[:, hs, 0, :], rqv[:, hs, 1, :]
            P.op("dve", lambda e, x1=x1, cb=cb: e.tensor_tensor(out=tmpA[:], in0=x1, in1=cb, op=ALU.mult), reads=zr + [C.rcs], writes=[rtA])
            P.op("dve", lambda e, x2=x2, sb_=sb_: e.tensor_tensor(out=tmpB[:], in0=x2, in1=sb_, op=ALU.mult), reads=zr + [C.rcs], writes=[rtA])
            P.op("dve", lambda e, o1=o1: e.tensor_tensor(out=o1, in0=tmpA[:], in1=tmpB[:], op=ALU.subtract), reads=[rtA], writes=[rrq])
            P.op("pool", lambda e, x2=x2, cb=cb: e.tensor_tensor(out=tmpC[:], in0=x2, in1=cb, op=ALU.mult), reads=zr + [C.rcs], writes=[rtC])
            P.op("pool", lambda e, x1=x1, sb_=sb_: e.tensor_tensor(out=tmpD[:], in0=x1, in1=sb_, op=ALU.mult), reads=zr + [C.rcs], writes=[rtC])
            P.op("pool", lambda e, o2=o2: e.tensor_tensor(out=o2, in0=tmpC[:], in1=tmpD[:], op=ALU.add), reads=[rtC], writes=[rrq])
        if t == 0 and "dbg_rq" in io:
            P.dma("sp", io["dbg_rq"], rq[:].rearrange("p h d -> p (h d)"), reads=[rrq], chan="dbg")
            P.dma("sp", io["dbg_cs"], C.cos[:, 0, :], reads=[C.rcs], chan="dbg")
            P.dma("sp", io["dbg_sn"], C.sin[:, 0, :], reads=[C.rcs], chan="dbg")
        cp(P, "pool", rq[:, 24:26, :].rearrange("p h d -> p (h d)"), zs[:, 1536:1664], [rzc[3]], [rrq])
        rqf = rq[:].rearrange("p h d -> p (h d)")
        for half, (cs_, ce_) in enumerate(((0, 8), (8, 13))):
            pt2, rpt2 = ptR.next()
            for c in range(cs_, ce_):
                P.op("pe", (lambda e, c=c, pt2=pt2, cs_=cs_, rqf=rqf: e.transpose(out=pt2[:, (c - cs_) * 128:(c - cs_ + 1) * 128], in_=rqf[:, c * 128:(c + 1) * 128], identity=C.ident[:])),
                     reads=[rrq, C.rid], writes=[rpt2])
            nn = ce_ - cs_
            cp(P, "act" if half == 0 else "dve", stq[:, cs_:cs_ + nn, tt * 128:(tt + 1) * 128],
               pt2[:, 0:nn * 128].rearrange("p (c t) -> p c t", c=nn), [rpt2], [rstq])
        P.op("pool", lambda e, zs=zs, tt=tt: e.tensor_copy(out=stv[:, tt, :, :].rearrange("p s d -> p (s d)"), in_=zs[:, 1536:2048]), reads=[rzc[3]], writes=[rstv])
        P.op("act", lambda e, zs=zs, tt=tt: e.activation(out=stg[:, tt, :], in_=zs[:, 2592:2616], func=AF.Sigmoid), reads=[rzc[5]], writes=[rstg])
        cn, rcn = cnR.next()
        ss2, rss2 = ssR.next()
        rmsnorm_tile(P, C, zs[:, 2048:2304], rzc[4], 256, g_q[:], rg, cn[:, 0:256], rcn, (junk, ss2, rss2))
        ss3, rss3 = ssR.next()
        rmsnorm_tile(P, C, zs[:, 2304:2560], rzc[4], 256, g_kv[:], rg, cn[:, 256:512], rcn, (junk, ss3, rss3))
        pt3, rpt3 = ptR.next()
        transpose_chunks(P, C, cn, rcn, 4, pt3, rpt3)
        cnT, rcnT = cnTR.next()
        P.op("dve", lambda e, cnT=cnT, pt3=pt3: e.tensor_copy(out=cnT[:].rearrange("p k t -> p (k t)"), in_=pt3[:, 0:512]), reads=[rpt3], writes=[rcnT])
        qf, rqf_ = qfR.next()
        kf, rkf = kfR.next()
        for (c0, n) in ((0, 512), (512, 256)):
            pz, rpz = pzR.next()
            for k in range(2):
                P.op("pe", (lambda e, pz=pz, cnT=cnT, k=k, c0=c0, n=n: e.matmul(pz[:, 0:n], lhsT=cnT[:, k, :], rhs=w_qu[:, k, c0:c0 + n], start=(k == 0), stop=(k == 1))),
                     reads=[rcnT, rwm], writes=[rpz])
            P.op("act", (lambda e, pz=pz, c0=c0, n=n: e.copy(out=qsb[:, c0:c0 + n], in_=pz[:, 0:n])), reads=[rpz], writes=[rqsb])
        qv = qsb[:].rearrange("p (h d) -> p h d", h=8)
        P.op("pool", lambda e, qf=qf, qv=qv: e.tensor_copy(out=qf[:, :, 0:64], in_=qv[:, :, 0:64]), reads=[rqsb], writes=[rqf_])
        c32 = C.cos[:, t, 32:48].unsqueeze(1).broadcast_to([128, 8, 16])
        s32 = C.sin[:, t, 32:48].unsqueeze(1).broadcast_to([128, 8, 16])
        qx1, qx2 = qv[:, :, 64:80], qv[:, :, 80:96]
        P.op("dve", lambda e, qx1=qx1, c32=c32: e.tensor_tensor(out=t16[0][:], in0=qx1, in1=c32, op=ALU.mult), reads=[rqsb, C.rcs], writes=[rt16])
        P.op("dve", lambda e, qx2=qx2, s32=s32: e.tensor_tensor(out=t16[1][:], in0=qx2, in1=s32, op=ALU.mult), reads=[rqsb, C.rcs], writes=[rt16])
        P.op("dve", lambda e, qx2=qx2, c32=c32: e.tensor_tensor(out=t16[2][:], in0=qx2, in1=c32, op=ALU.mult), reads=[rqsb, C.rcs], writes=[rt16])
        P.op("dve", lambda e, qx1=qx1, s32=s32: e.tensor_tensor(out=t16[3][:], in0=qx1, in1=s32, op=ALU.mult), reads=[rqsb, C.rcs], writes=[rt16])
        P.op("dve", lambda e, qf=qf: e.tensor_tensor(out=qf[:, :, 64:80], in0=t16[0][:], in1=t16[1][:], op=ALU.subtract), reads=[rt16], writes=[rqf_])
        P.op("dve", lambda e, qf=qf: e.tensor_tensor(out=qf[:, :, 80:96], in0=t16[2][:], in1=t16[3][:], op=ALU.add), reads=[rt16], writes=[rqf_])
        kx1, kx2 = zs[:, 2560:2576], zs[:, 2576:2592]
        c16, s16 = C.cos[:, t, 32:48], C.sin[:, t, 32:48]
        P.op("pool", lambda e, kx1=kx1, c16=c16: e.tensor_tensor(out=ktmp[:, 0, :], in0=kx1, in1=c16, op=ALU.mult), reads=[rzc[5], C.rcs], writes=[rkpe])
        P.op("pool", lambda e, kx2=kx2, s16=s16: e.tensor_tensor(out=ktmp[:, 1, :], in0=kx2, in1=s16, op=ALU.mult), reads=[rzc[5], C.rcs], writes=[rkpe])
        P.op("pool", lambda e, kx2=kx2, c16=c16: e.tensor_tensor(out=ktmp[:, 2, :], in0=kx2, in1=c16, op=ALU.mult), reads=[rzc[5], C.rcs], writes=[rkpe])
        P.op("pool", lambda e, kx1=kx1, s16=s16: e.tensor_tensor(out=ktmp[:, 3, :], in0=kx1, in1=s16, op=ALU.mult), reads=[rzc[5], C.rcs], writes=[rkpe])
        P.op("pool", lambda e: e.tensor_tensor(out=kpe[:, 0:16], in0=ktmp[:, 0, :], in1=ktmp[:, 1, :], op=ALU.subtract), reads=[rkpe], writes=[rkpe])
        P.op("pool", lambda e: e.tensor_tensor(out=kpe[:, 16:32], in0=ktmp[:, 2, :], in1=ktmp[:, 3, :], op=ALU.add), reads=[rkpe], writes=[rkpe])
        P.op("pool", lambda e, kf=kf: e.tensor_copy(out=kf[:, :, 64:96], in_=kpe[:].unsqueeze(1).broadcast_to([128, 8, 32])), reads=[rkpe], writes=[rkf])
        for ci, c0 in enumerate((0, 512)):
            pz, rpz = pzR.next()
            for k in range(2):
                P.op("pe", (lambda e, pz=pz, cnT=cnT, k=k, c0=c0: e.matmul(pz[:, 0:512], lhsT=cnT[:, 2 + k, :], rhs=w_kvu[:, k, c0:c0 + 512], start=(k == 0), stop=(k == 1))),
                     reads=[rcnT, rwm], writes=[rpz])
            pv = pz[:, 0:512].rearrange("p (h d) -> p h d", h=4)
            P.op("act", (lambda e, pv=pv, kf=kf, ci=ci: e.copy(out=kf[:, ci * 4:(ci + 1) * 4, 0:64], in_=pv[:, :, 0:64])), reads=[rpz], writes=[rkf])
            P.op("dve", (lambda e, pv=pv, ci=ci, tt=tt: e.tensor_copy(out=stvc[:, tt, ci * 4:(ci + 1) * 4, :], in_=pv[:, :, 64:128])), reads=[rpz], writes=[rstvc])
        for src, rsrc, dst, rdst, eng in ((qf, rqf_, stqc, rstqc, "act"), (kf, rkf, stkc, rstkc, "dve")):
            pt4, rpt4 = ptR.next()
            for h in range(8):
                P.op("pe", (lambda e, h=h, pt4=pt4, src=src: e.transpose(out=pt4[0:96, h * 128:(h + 1) * 128], in_=src[:, h, :], identity=C.ident[:])),
                     reads=[rsrc, C.rid], writes=[rpt4])
            if eng == "act":
                P.op("act", (lambda e, pt4=pt4, dst=dst, tt=tt: e.copy(out=dst[0:96, :, tt * 128:(tt + 1) * 128], in_=pt4[0:96, :].rearrange("p (h t) -> p h t", h=8))), reads=[rpt4], writes=[rdst])
            else:
                P.op("dve", (lambda e, pt4=pt4, dst=dst, tt=tt: e.tensor_copy(out=dst[0:96, :, tt * 128:(tt + 1) * 128], in_=pt4[0:96, :].rearrange("p (h t) -> p h t", h=8))), reads=[rpt4], writes=[rdst])

        if tt == 3:
            sl = slice(s0, s0 + 512)
            for c in range(4):
                P.dma("sp", io["xqa"][c].rearrange("h d t -> (h d) t")[:, sl], stq[:, c, :], reads=[rstq])
                P.dma("sp", io["xqg"][c ^ 1].rearrange("h d t -> (h d) t")[:, sl], stq[:, c, :], reads=[rstq])
                P.dma("sp", io["xqb"][c].rearrange("h d t -> (h d) t")[:, sl], stq[:, 7 + c, :], reads=[rstq])
            for g in range(2):
                for dest in (2 * g, 2 * g + 1):
                    for ty, ch in enumerate((4, 5, 6, 12)):
                        P.dma("sp", io["xka"][dest, ty][:, sl], stq[g * 64:(g + 1) * 64, ch, :], reads=[rstq])
                    P.dma("sp", io["xkb"][dest][:, sl], stq[g * 64:(g + 1) * 64, 11, :], reads=[rstq])
                    for ty in range(2):
                        P.dma("sp", io["xva"][dest, ty][sl, :].rearrange("(tt p) d -> p tt d", p=128), stv[:, :, (ty + 1) * 2 + g, :], reads=[rstv])
                    P.dma("sp", io["xvb"][dest][sl, :].rearrange("(tt p) d -> p tt d", p=128), stv[:, :, 6 + g, :], reads=[rstv])
            for dest in range(4):
                P.dma("sp", io["xqc"][dest].rearrange("h d t -> d h t")[:, :, sl], stqc[0:96, 2 * dest:2 * dest + 2, :], reads=[rstqc], chan="x3")
                P.dma("sp", io["xkc"][dest].rearrange("h d t -> d h t")[:, :, sl], stkc[0:96, 2 * dest:2 * dest + 2, :], reads=[rstkc], chan="x3")
                for hh in range(2):
                    P.dma("sp", io["xvc"][dest, hh][sl, :].rearrange("(tt p) d -> p tt d", p=128), stvc[:, :, 2 * dest + hh, :], reads=[rstvc], chan="x4")
                P.dma("sp", io["xg"][dest][sl, :].rearrange("(tt p) c -> p tt c", p=128), stg[:, :, 6 * dest:6 * dest + 6], reads=[rstg], chan="x4")


P2_IN = {
    "qa": ([4, 2, 64, NTOK], BF16), "qg": ([4, 2, 64, NTOK], BF16), "ka": ([4, 4, 64, NTOK], BF16),
    "va": ([4, 2, NTOK, 64], BF16), "qb": ([4, 2, 64, NTOK], BF16), "kb": ([4, 64, NTOK], BF16),
    "vb": ([4, NTOK, 64], BF16), "qc": ([4, 2, 96, NTOK], BF16), "kc": ([4, 2, 96, NTOK], BF16),
    "vc": ([4, 2, NTOK, 64], BF16), "g": ([4, NTOK, 6], F32),
}
P2_W = {"posk": [128, 16], "w1k": [2048, 256], "w2k": [256, 64], "posv": [128, 16], "w1v": [2048, 256],
        "w2v": [256, 64], "sinks": [1, 2], "selmap": [128, 4, 128]}


def selmap_const():
    n_cmp = 511
    tok = np.arange(n_cmp)[:, None] * 16 + np.arange(32)[None, :]
    sm = np.zeros((512, 128), np.float32)
    np.add.at(sm, (np.repeat(np.arange(n_cmp), 32), (tok // 64).reshape(-1)), 1.0 / 32)
    return np.ascontiguousarray(sm.reshape(4, 128, 128).transpose(1, 0, 2))


class AttnCtx:
    def __init__(self, P, ident, rid):
        self.P = P
        self.ident = ident
        self.rid = rid
        self.S = Rot(P, 4, [128, 512], F32, psum=True)
        self.pT = Rot(P, 3, [128, 512], BF16)
        self.acc = Rot(P, 2, [128, 4, 128], F32, psum=True)


def attn_qgroup(P, A, kT, rkT, Vt, rV, nv, qT, rqT, kbs, scale, accv, racc, look=1):
    cover = {qb: [i for i, e in enumerate(kbs) if e[1] <= qb <= e[2]] for qb in range(4)}
    n_kb = len(kbs)
    tiles = [None] * n_kb

    def scores(i):
        kb, lo, hi, segs = kbs[i]
        ps, rps = A.S.next()
        for (q0, q1, extra) in segs:
            c0, c1 = q0 * 128, (q1 + 1) * 128
            n = len(extra)
            MM(P, ps[:, c0:c1], kT[:, kb * 128:(kb + 1) * 128], qT[:, c0:c1], True, n == 0, [rkT, rqT], [rps])
            for j, (l_, r_, rd) in enumerate(extra):
                MM(P, ps[:, c0:c1], l_, r_, False, j == n - 1, rd, [rps])
        tiles[i] = (ps, rps)

    def rest(i):
        kb, lo, hi, segs = kbs[i]
        ps, rps = tiles[i]
        pT, rpT = A.pT.next()
        c0, c1 = lo * 128, (hi + 1) * 128
        ACT(P, pT[:, c0:c1], ps[:, c0:c1], AF.Exp, [rps], [rpT], scale=scale)
        for qb in range(lo, hi + 1):
            MM(P, accv(qb), pT[:, qb * 128:(qb + 1) * 128], Vt(kb), i == 0 and qb == lo, cover[qb][-1] == i, [rpT, rV], [racc], skip=True)

    LOOK = look
    for i in range(n_kb + LOOK):
        if i < n_kb:
            scores(i)
        if i - LOOK >= 0:
            rest(i - LOOK)


def emit_p2(P, io, ident, rid):
    nc = P.nc
    deps = io.get("deps", {"nsa": [], "swa": [], "mla": []})
    dA, dB, dC = deps["nsa"], deps["swa"], deps["mla"]
    A = AttnCtx(P, ident, rid)
    rc = P.res()
    zero_bf = P.sb([128, 512], BF16)
    ones_bf = P.sb([128, 128], BF16)
    MEMSET(P, "pool", zero_bf[:], 0.0, [], [rc])
    MEMSET(P, "pool", ones_bf[:], 1.0, [], [rc])
    pen_diag = P.sb([128, 128], BF16)
    pen_far = P.sb([128, 128], BF16)
    P.op("pool", lambda e: e.affine_select(out=pen_diag[:], in_=zero_bf[:, 0:128], pattern=[[1, 128]], compare_op=ALU.is_ge, fill=P.freg(e, NEG), base=0, channel_multiplier=-1), reads=[rc], writes=[rc])
    P.op("pool", lambda e: e.affine_select(out=pen_far[:], in_=zero_bf[:, 0:128], pattern=[[-1, 128]], compare_op=ALU.is_gt, fill=P.freg(e, NEG), base=0, channel_multiplier=1), reads=[rc], writes=[rc])
    identf32 = P.sb([128, 128], F32)
    MEMSET(P, "pool", identf32[:], 1.0, [], [rc])
    P.op("pool", lambda e: e.affine_select(out=identf32[:], in_=identf32[:], pattern=[[-1, 128]], compare_op=ALU.is_equal, fill=P.freg(e, 0.0), base=0, channel_multiplier=1), reads=[rc], writes=[rc])
    E = P.sb([128, 64, 128], BF16)
    for j in range(64):
        P.op("pool", (lambda e, j=j: e.affine_select(out=E[:, j, :].rearrange("p (a b) -> p a b", a=2), in_=ones_bf[:].rearrange("p (a b) -> p a b", a=2),
                                                     pattern=[[-1, 2], [0, 64]], compare_op=ALU.is_equal, fill=P.freg(e, 0.0), base=-2 * j, channel_multiplier=1)), reads=[rc], writes=[rc])
    vcmp = P.sb([128, 4, 200], BF16); rvcmp = P.res()
    MEMSET(P, "pool", vcmp[:], 0.0, [], [rvcmp])
    MEMSET(P, "pool", vcmp[:, :, 64:65], 1.0, [], [rvcmp])
    P.dma("pool", vcmp[:, :, 65:193], io["selmap"], writes=[rvcmp])
    kcmpT = P.sb([64, 512], BF16); rkcmp = P.res()
    MEMSET(P, "pool", kcmpT[:], 0.0, [], [rkcmp])
    esink = P.sb([128, 2], F32); resink = P.res()
    P.dma("sp", esink[:], io["sinks"].partition_broadcast(128), writes=[resink])
    ACT(P, esink[:], esink[:], AF.Exp, [resink], [resink])
    mark = nc.sbuf_base
    kT2 = P.sb([128, S], BF16); rkT2 = P.res()
    w1 = P.sb([128, 16, 256], BF16); w2 = P.sb([128, 2, 64], BF16); posT = P.sb([128, 16], BF16); rwc = P.res()
    gT = P.sb([128, 2, 512], BF16); rgT = P.res()
    cb = P.sb([128, 2], F32); rcb = P.res()
    for which, ty in (("k", 0), ("v", 3)):
        for s in range(4):
            P.dma("sp", kT2[0:64, s * NTOK:(s + 1) * NTOK], io["ka"][s, ty], writes=[rkT2], reads=list(dA))
            P.dma("sp", kT2[64:128, s * NTOK:(s + 1) * NTOK - 1], io["ka"][s, ty][:, 1:NTOK], writes=[rkT2], reads=list(dA))
            if s < 3:
                P.dma("sp", kT2[64:128, (s + 1) * NTOK - 1:(s + 1) * NTOK], io["ka"][s + 1, ty][:, 0:1], writes=[rkT2], reads=list(dA), allow_slow_non_contiguous=True)
        load_w_bf16(P, w1, rwc, io["w1" + which], 2048, 256, None)
        load_w_bf16(P, w2, rwc, io["w2" + which], 256, 64, None)
        P.dma("pool", posT[:], io["pos" + which], writes=[rwc])
        kviews = [kT2[:, b0:b0 + 8176].rearrange("p (n s) -> p n s", s=16) for b0 in (0, 16)]
        for hc in range(2):
            ps, rps = A.S.next()
            for lp in range(16):
                MM(P, ps[:, 0:511], w1[:, lp, hc * 128:(hc + 1) * 128], kviews[(2 * lp) // 16][:, :, (2 * lp) % 16], lp == 0, lp == 15, [rwc, rkT2], [rps])
            pb, rpb = A.acc.next()
            for lp in range(16):
                MM(P, pb[:, 0, 0:1], w1[:, lp, hc * 128:(hc + 1) * 128], posT[:, lp:lp + 1], lp == 0, lp == 15, [rwc], [rpb])
            cp(P, "dve", cb[:, hc:hc + 1], pb[:, 0, 0:1], [rpb], [rcb])
            ACT(P, gT[:, hc, 0:511], ps[:, 0:511], AF.Gelu_apprx_tanh, [rps, rcb], [rgT], bias=cb[:, hc:hc + 1])
        if which == "k":
            ps, rps = A.S.next()
            for hc in range(2):
                MM(P, ps[0:64, 0:511], w2[:, hc, :], gT[:, hc, 0:511], hc == 0, hc == 1, [rwc, rgT], [rps])
            cp(P, "dve", kcmpT[:, 0:511], ps[0:64, 0:511], [rps], [rkcmp])
        else:
            for c in range(4):
                nn = 128 if c < 3 else 127
                ps, rps = A.S.next()
                for hc in range(2):
                    MM(P, ps[0:nn, 0:64], gT[:, hc, c * 128:c * 128 + nn], w2[:, hc, :], hc == 0, hc == 1, [rwc, rgT], [rps])
                cp(P, "dve", vcmp[0:nn, c, 0:64], ps[0:nn, 0:64], [rps], [rvcmp])
    P.barrier()
    nc.sbuf_base = mark

    kTa = P.sb([128, S], BF16); rkTa = P.res()
    kTb = P.sb([128, S], BF16); rkTb = P.res()
    Va = P.sb([128, 64, 65], BF16); rVa = P.res()
    Vb = P.sb([128, 64, 65], BF16); rVb = P.res()
    MEMSET(P, "pool", Va[:, :, 64:65], 1.0, [], [rVa])
    MEMSET(P, "pool", Vb[:, :, 64:65], 1.0, [], [rVb])

    def load_kT(dst, rdst, src_fn, dk, dep=()):
        for s in range(4):
            P.dma("sp", dst[0:dk, s * NTOK:(s + 1) * NTOK], src_fn(s), writes=[rdst], reads=list(dep))

    def load_V(dst, rdst, src_fn, dep=()):
        for s in range(4):
            P.dma("sp", dst[:, s * 16:(s + 1) * 16, 0:64], src_fn(s).rearrange("(blk p) d -> p blk d", p=128), writes=[rdst], reads=list(dep))

    qR = Rot(P, 2, [128, 4, 512], BF16)
    gR = Rot(P, 2, [128, 4, 6], F32)
    ostR = Rot(P, 2, [128, 4, 128], BF16)
    oacc = P.sb([128, 2, 4, 64], F32); roacc = [P.res(), P.res()]
    imp = P.sb([128, 4, 128], F32); rimp = P.res()
    rcp = P.sb([128, 8], F32); rrcp = P.res()
    fac = P.sb([128, 8], F32)
    tmpo = P.sb([128, 4, 64], F32); rtmpo = P.res()
    penR = Rot(P, 2, [128, 512], BF16)
    biasR = Rot(P, 2, [128, 128], F32)
    val = P.sb([128, 128], F32); rval = P.res()
    wk = P.sb([128, 128], F32)
    m16 = P.sb([128, 16], F32)
    penq = P.sb([128, 128], F32); rpenq = P.res()
    penT = P.sb([128, 512], BF16); rpenT = P.res()
    cmpacc = [P.ps([128, 2, 256], F32), P.ps([128, 2, 256], F32)]; rcmpacc = P.res()

    def out_dma(ost, rost, Gq, col0, ncol):
        src, off = Gq // 4, (Gq % 4) * 512
        for dest in range(1):
            pass
        d = Gq // 4
        if "o_mix" in io:
            m, c0 = col0 // 128, col0 % 128
            P.dma("sp", io["o_mix"](m)[d][off:off + 512, c0:c0 + ncol].rearrange("(qb p) c -> p qb c", p=128), ost[:, :, 0:ncol],
                  reads=[rost], writes=[io["ro"][m]])
        else:
            P.dma("sp", io["o"][d][off:off + 512, col0:col0 + ncol].rearrange("(qb p) c -> p qb c", p=128), ost[:, :, 0:ncol], reads=[rost])

    def finish_branch(accv_t, racc_, h, gcol, g, rg, first, extra_den=None):
        if extra_den is None:
            P.op("dve", lambda e: e.reciprocal(out=rcp[:, 0:4], in_=accv_t[:, :, 64]), reads=[racc_], writes=[rrcp])
        else:
            TS(P, "dve", rcp[:, 4:8], accv_t[:, :, 64], extra_den, None, ALU.add, None, [racc_, resink], [rrcp])
            P.op("dve", lambda e: e.reciprocal(out=rcp[:, 0:4], in_=rcp[:, 4:8]), reads=[rrcp], writes=[rrcp])
        if gcol is not None:
            TT(P, "dve", fac[:, 0:4], rcp[:, 0:4], g[:, :, gcol], ALU.mult, [rrcp, rg], [rrcp])
            f = fac[:, 0:4]
        else:
            f = rcp[:, 0:4]
        fb = f.unsqueeze(2).broadcast_to([128, 4, 64])
        if first:
            TT(P, "dve", oacc[:, h, :, :], accv_t[:, :, 0:64], fb, ALU.mult, [racc_, rrcp], [roacc[h]])
        else:
            TT(P, "dve", tmpo[:], accv_t[:, :, 0:64], fb, ALU.mult, [racc_, rrcp], [rtmpo])
            TT(P, "dve", oacc[:, h, :, :], oacc[:, h, :, :], tmpo[:], ALU.add, [rtmpo], [roacc[h]])

    load_kT(kTa, rkTa, lambda s: io["ka"][s, 1], 64, dA)
    load_kT(kTb, rkTb, lambda s: io["ka"][s, 2], 64, dA)
    load_V(Va, rVa, lambda s: io["va"][s, 0], dA)
    load_V(Vb, rVb, lambda s: io["va"][s, 1], dA)
    for Gq in range(16):
        src, off = Gq // 4, (Gq % 4) * 512
        q4, rq4 = qR.next()
        P.dma("sp", q4[0:64, 0:2, :], io["qa"][src].rearrange("h d t -> d h t")[:, :, off:off + 512], writes=[rq4], reads=list(dA))
        P.dma("sp", q4[0:64, 2:4, :], io["qg"][src].rearrange("h d t -> d h t")[:, :, off:off + 512], writes=[rq4], reads=list(dA))
        g, rg = gR.next()
        P.dma("sp", g[:], io["g"][src][off:off + 512, :].rearrange("(qb p) c -> p qb c", p=128), writes=[rg], reads=list(dA))
        cmax = (32 * Gq + 30) // 128
        pens = {}
        for c in range(cmax + 1):
            if Gq >= 4 * c + 5:
                continue
            pn, rpn = penR.next()
            P.op("pool", (lambda e, pn=pn, c=c, Gq=Gq: e.affine_select(out=pn[:], in_=zero_bf[:], pattern=[[1, 512]], compare_op=ALU.is_ge, fill=P.freg(e, NEG),
                                                                      base=512 * Gq - 2048 * c - 31, channel_multiplier=-16)), reads=[rc], writes=[rpn])
            pens[c] = (pn, rpn)
        for r4 in range(4):
            ctiles = {}

            def cscores(c, r4=r4):
                ps, rps = A.S.next()
                if c in pens:
                    MM(P, ps[:, :], kcmpT[:, c * 128:(c + 1) * 128], q4[0:64, r4, :], True, False, [rkcmp, rq4], [rps])
                    MM(P, ps[:, :], ident[:], pens[c][0][:], False, True, [rid, pens[c][1]], [rps])
                else:
                    MM(P, ps[:, :], kcmpT[:, c * 128:(c + 1) * 128], q4[0:64, r4, :], True, True, [rkcmp, rq4], [rps])
                ctiles[c] = (ps, rps)

            def crest(c):
                ps, rps = ctiles[c]
                pT, rpT = A.pT.next()
                ACT(P, pT[:, :], ps[:, :], AF.Exp, [rps], [rpT], scale=0.125)
                for qb in range(4):
                    MM(P, cmpacc[qb // 2][:, qb % 2, 0:193], pT[:, qb * 128:(qb + 1) * 128], vcmp[:, c, 0:193], c == 0 and qb % 2 == 0, c == cmax, [rpT, rvcmp], [rcmpacc], skip=True)

            for c in range(cmax + 2):
                if c <= cmax:
                    cscores(c)
                if c >= 1:
                    crest(c - 1)
            for half in range(2):
                TS(P, "dve", rcp[:, 4 + 2 * half:6 + 2 * half], cmpacc[half][:, :, 64], 1e-30, None, ALU.max, None, [rcmpacc], [rrcp])
            P.op("dve", lambda e: e.reciprocal(out=rcp[:, 0:4], in_=rcp[:, 4:8]), reads=[rrcp], writes=[rrcp])
            for qb in range(4):
                src_imp = cmpacc[qb // 2][:, qb % 2, 65:193]
                if r4 == 0:
                    TS(P, "dve", imp[:, qb, :], src_imp, rcp[:, qb:qb + 1], None, ALU.mult, None, [rcmpacc, rrcp], [rimp])
                else:
                    STT(P, imp[:, qb, :], src_imp, rcp[:, qb:qb + 1], imp[:, qb, :], ALU.mult, ALU.add, [rcmpacc, rrcp], [rimp])
            if r4 < 2:
                TT(P, "dve", fac[:, 0:4], rcp[:, 0:4], g[:, :, 3 * r4 + 0], ALU.mult, [rrcp, rg], [rrcp])
                for half in range(2):
                    fb = fac[:, 2 * half:2 * half + 2].unsqueeze(2).broadcast_to([128, 2, 64])
                    TT(P, "dve", oacc[:, r4, 2 * half:2 * half + 2, :], cmpacc[half][:, :, 0:64], fb, ALU.mult, [rcmpacc, rrcp], [roacc[r4]])
        trp, rtrp = A.S.next()
        for qb in range(4):
            j = 4 * Gq + qb
            bt, rbt = biasR.next()
            MEMSET(P, "pool", bt[:], 0.0, [], [rbt])
            MEMSET(P, "pool", bt[:, 0:1], 1e4, [], [rbt])
            if j >= 1:
                MEMSET(P, "pool", bt[0:64, 2 * j - 1:2 * j + 1], 1e4, [], [rbt])
            MEMSET(P, "pool", bt[64:128, 2 * j:2 * j + 2], 1e4, [], [rbt])
            if 2 * j + 1 < 128:
                MEMSET(P, "pool", bt[0:64, 2 * j + 1:128], -1e30, [], [rbt])
            if 2 * j + 2 < 128:
                MEMSET(P, "pool", bt[64:128, 2 * j + 2:128], -1e30, [], [rbt])
            TT(P, "dve", val[:], imp[:, qb, :], bt[:], ALU.add, [rimp, rbt], [rval])
            P.op("dve", lambda e: e.max(out=m16[:, 0:8], in_=val[:]), reads=[rval], writes=[rval])
            P.op("dve", lambda e: e.match_replace(out=wk[:], in_to_replace=m16[:, 0:8], in_values=val[:], imm_value=-3e38), reads=[rval], writes=[rval])
            P.op("dve", lambda e: e.max(out=m16[:, 8:16], in_=wk[:]), reads=[rval], writes=[rval])
            TS(P, "dve", penq[:], val[:], m16[:, 15:16], NEG, ALU.is_lt, ALU.mult, [rval], [rpenq])
            TR(P, trp[:, qb * 128:(qb + 1) * 128], penq[:], identf32[:], [rpenq, rc], [rtrp])
        cp(P, "dve", penT[:], trp[:, :], [rtrp], [rpenT])
        for h in range(2):
            acc, racc = A.acc.next()
            kbs = []
            for kb in range(4 * Gq + 4):
                ex_sel = lambda q0, q1, kb=kb: (E[:, kb // 1, :], penT[:, q0 * 128:(q1 + 1) * 128], [rc, rpenT])
                if kb < 4 * Gq:
                    kbs.append((kb, 0, 3, [(0, 3, [ex_sel(0, 3)])]))
                else:
                    i = kb - 4 * Gq
                    segs = [(i, i, [ex_sel(i, i), (ident[:], pen_diag[:], [rid, rc])])]
                    if i < 3:
                        segs.append((i + 1, 3, [ex_sel(i + 1, 3)]))
                    kbs.append((kb, i, 3, segs))
            attn_qgroup(P, A, kTa[0:64, :], rkTa, lambda kb: Va[:, kb, :], rVa, 65, q4[0:64, h, :], rq4, kbs, 0.125, lambda qb, acc=acc: acc[:, qb, 0:65], racc, look=2)
            finish_branch(acc, racc, h, 3 * h + 1, g, rg, False)
            acc, racc = A.acc.next()
            kbs = []
            for i in range(8):
                kb = 4 * Gq - 4 + i
                if kb < 0:
                    continue
                lo, hi = max(0, i - 4), min(3, i)
                segs = []
                if i <= 3:
                    if lo < i:
                        segs.append((lo, i - 1, []))
                    segs.append((i, i, [(ident[:], pen_far[:], [rid, rc])]))
                else:
                    segs.append((i - 4, i - 4, [(ident[:], pen_diag[:], [rid, rc])]))
                    if i - 4 < hi:
                        segs.append((i - 3, hi, []))
                kbs.append((kb, lo, hi, segs))
            attn_qgroup(P, A, kTb[0:64, :], rkTb, lambda kb: Vb[:, kb, :], rVb, 65, q4[0:64, h, :], rq4, kbs, 0.125, lambda qb, acc=acc: acc[:, qb, 0:65], racc, look=2)
            finish_branch(acc, racc, h, 3 * h + 2, g, rg, False)
        ost, rost = ostR.next()
        cp(P, "act", ost[:, :, 0:128].rearrange("p q (h d) -> p h q d", h=2), oacc[:], roacc, [rost])
        out_dma(ost, rost, Gq, 0, 128)

    if "post_mix" in io:
        io["post_mix"](0)
    P.barrier()
    S5 = Rot.__new__(Rot)
    S5.t = list(A.S.t) + [cmpacc[0][:].rearrange("p a b -> p (a b)"), cmpacc[1][:].rearrange("p a b -> p (a b)")]
    S5.r = list(A.S.r) + [P.res(), P.res()]
    S5.i = -1
    S5.n = len(S5.t)
    A.S = S5
    if "pre_swa" in io:
        io["pre_swa"]()
    load_kT(kTa, rkTa, lambda s: io["kb"][s], 64, dB)
    load_V(Va, rVa, lambda s: io["vb"][s], dB)
    for Gq in range(16):
        src, off = Gq // 4, (Gq % 4) * 512
        q4, rq4 = qR.next()
        P.dma("sp", q4[0:64, 0:2, :], io["qb"][src].rearrange("h d t -> d h t")[:, :, off:off + 512], writes=[rq4], reads=list(dB))
        for h in range(2):
            acc, racc = A.acc.next()
            kbs = []
            for i in range(5):
                kb = 4 * Gq - 1 + i
                if kb < 0:
                    continue
                segs = []
                lo, hi = max(0, i - 1), min(3, i)
                if i <= 3:
                    segs.append((i, i, [(ident[:], pen_far[:], [rid, rc])]))
                if i >= 1:
                    segs.append((i - 1, i - 1, [(ident[:], pen_diag[:], [rid, rc])]))
                segs.sort()
                kbs.append((kb, lo, hi, segs))
            attn_qgroup(P, A, kTa[0:64, :], rkTa, lambda kb: Va[:, kb, :], rVa, 65, q4[0:64, h, :], rq4, kbs, 0.125, lambda qb, acc=acc: acc[:, qb, 0:65], racc, look=2)
            finish_branch(acc, racc, h, None, None, None, True, extra_den=esink[:, h:h + 1])
        ost, rost = ostR.next()
        cp(P, "act", ost[:, :, 0:128].rearrange("p q (h d) -> p h q d", h=2), oacc[:], roacc, [rost])
        out_dma(ost, rost, Gq, 128, 128)

    if "post_mix" in io:
        io["post_mix"](1)
    if "pre_mla" in io:
        io["pre_mla"]()
    for h in range(2):
        kT, rkT = (kTa, rkTa) if h == 0 else (kTb, rkTb)
        Vx, rVx = (Va, rVa) if h == 0 else (Vb, rVb)
        load_kT(kT, rkT, lambda s, h=h: io["kc"][s, h], 96, dC)
        load_V(Vx, rVx, lambda s, h=h: io["vc"][s, h], dC)
        for Gq in range(16):
            src, off = Gq // 4, (Gq % 4) * 512
            q4, rq4 = qR.next()
            P.dma("sp", q4[0:96, 0, :], io["qc"][src, h][:, off:off + 512], writes=[rq4], reads=list(dC))
            acc, racc = A.acc.next()
            kbs = []
            for kb in range(4 * Gq + 4):
                if kb < 4 * Gq:
                    kbs.append((kb, 0, 3, [(0, 3, [])]))
                else:
                    i = kb - 4 * Gq
                    segs = [(i, i, [(ident[:], pen_diag[:], [rid, rc])])]
                    if i < 3:
                        segs.append((i + 1, 3, []))
                    kbs.append((kb, i, 3, segs))
            attn_qgroup(P, A, kT[0:96, :], rkT, lambda kb, Vx=Vx: Vx[:, kb, :], rVx, 65, q4[0:96, 0, :], rq4, kbs, 96 ** -0.5, lambda qb, acc=acc: acc[:, qb, 0:65], racc, look=2)
            finish_branch(acc, racc, 0, None, None, None, True)
            ost, rost = ostR.next()
            cp(P, "act", ost[:, :, 0:64], oacc[:, 0, :, :], roacc, [rost])
            out_dma(ost, rost, Gq, 256 + 64 * h, 64)
    if "post_mix" in io:
        io["post_mix"](2)


def emit_p3(P, C, io, last):
    nc = P.nc
    base_mark = nc.sbuf_base
    pbase = nc.psum_base
    junk = P.sb([128, D], BF16)
    ssR = Rot(P, 2, [128, 4], F32)
    ptR = Rot(P, 2, [128, 1024], BF16, psum=True)
    pzR = Rot(P, 4, [128, 512], F32, psum=True)
    g_n = P.sb([128, D], F32); rgn = P.res()

    def norm_T(t, hb, rhb, dstT, col0, rdst, eng="act"):
        ss, rss = ssR.next()
        rmsnorm_tile(P, C, C.x[:, t, :], C.rx[t], D, g_n[:], rgn, hb[:], rhb, (junk, ss, rss))
        pt, rpt = ptR.next()
        transpose_chunks(P, C, hb, rhb, 8, pt, rpt)
        cp(P, eng, dstT[:, :, col0:col0 + 128], pt[:].rearrange("p (k t) -> p k t", k=8), [rpt], [rdst])

    markA = nc.sbuf_base
    load_bcast(P, g_n, rgn, io["mix_norm"], D, None)
    wbg = P.sb([128, 8, 3072], BF16); rwA = P.res(); rwAp = P.res(); rwAo = P.res()
    load_w_bf16(P, wbg, rwA, io["w_bg"], 1024, 3072, None)
    wp = P.sb([128, 12, 1024], BF16)
    for i, nm in enumerate(("w_pa", "w_pb", "w_pc")):
        for k in range(4):
            P.dma("pool", wp[:, 4 * i + k, :], io[nm][k * 128:(k + 1) * 128, :], writes=[rwAp])
    wo = P.sb([128, 8, 1024], BF16)
    load_w_bf16(P, wo, rwAo, io["w_out"], 1024, 1024, None)
    hbR = Rot(P, 1, [128, D], BF16)
    hTR = Rot(P, 2, [128, 8, 128], BF16)
    gsb = P.sb([128, 3072], F32); rgsb = P.res()
    otR = Rot(P, 1, [128, 4, 384], BF16)
    oTR = Rot(P, 1, [128, 12, 128], BF16)
    mrg = P.sb([128, D], F32); rmrg = P.res()
    tmpm = P.sb([128, 512], F32); rtmpm = P.res()
    mbR = Rot(P, 1, [128, D], BF16)
    mTR = Rot(P, 1, [128, 8, 128], BF16)
    for t in range(NTILE):
        hb, rhb = hbR.next()
        hT, rhT = hTR.next()
        norm_T(t, hb, rhb, hT, 0, rhT)
        for c in range(6):
            pz, rpz = pzR.next()
            for k in range(8):
                MM(P, pz[:, :], hT[:, k, :], wbg[:, k, c * 512:(c + 1) * 512], k == 0, k == 7, [rhT, rwA], [rpz])
            ACT(P, gsb[:, c * 512:(c + 1) * 512], pz[:, :], AF.Sigmoid, [rpz], [rgsb])
        ot, rot = otR.next()
        if "o_tile3" in io:
            for m in range(3):
                P.dma("sp", ot[:, :, m * 128:(m + 1) * 128], io["o_tile3"](t, m).rearrange("s p c -> p s c"), writes=[rot], reads=list(io["o_dep"]))
        else:
            o_src = io["o_tile"](t) if "o_tile" in io else io["o"][:, t * 128:(t + 1) * 128, :]
            P.dma("sp", ot[:], o_src.rearrange("s p c -> p s c"), writes=[rot])
        oT, roT = oTR.next()
        for half, (a0, a1) in enumerate(((0, 8), (8, 12))):
            pt, rpt = ptR.next()
            for j in range(a0, a1):
                i, s = j // 4, j % 4
                TR(P, pt[:, (j - a0) * 128:(j - a0 + 1) * 128], ot[:, s, i * 128:(i + 1) * 128], C.ident[:], [rot, C.rid], [rpt])
            cp(P, "act" if half == 0 else "dve", oT[:, a0:a1, :], pt[:, 0:(a1 - a0) * 128].rearrange("p (k t) -> p k t", k=a1 - a0), [rpt], [roT])
        for c in range(2):
            cs = slice(c * 512, (c + 1) * 512)
            for i in range(3):
                pz, rpz = pzR.next()
                for k in range(4):
                    MM(P, pz[:, :], oT[:, 4 * i + k, :], wp[:, 4 * i + k, cs], k == 0, k == 3, [roT, rwAp], [rpz])
                gs = gsb[:, i * 1024 + c * 512:i * 1024 + (c + 1) * 512]
                if i == 0:
                    TT(P, "dve", mrg[:, cs], pz[:, :], gs, ALU.mult, [rpz, rgsb], [rmrg])
                else:
                    TT(P, "dve", tmpm[:], pz[:, :], gs, ALU.mult, [rpz, rgsb], [rtmpm])
                    TT(P, "dve", mrg[:, cs], mrg[:, cs], tmpm[:], ALU.add, [rtmpm], [rmrg])
        mb, rmb = mbR.next()
        cp(P, "act", mb[:], mrg[:], [rmrg], [rmb])
        pt, rpt = ptR.next()
        transpose_chunks(P, C, mb, rmb, 8, pt, rpt)
        mT, rmT = mTR.next()
        cp(P, "act", mT[:].rearrange("p k t -> p (k t)"), pt[:], [rpt], [rmT])
        for c in range(2):
            cs = slice(c * 512, (c + 1) * 512)
            pz, rpz = pzR.next()
            for k in range(8):
                MM(P, pz[:, :], mT[:, k, :], wo[:, k, cs], k == 0, k == 7, [rmT, rwAo], [rpz])
            TT(P, "dve", C.x[:, t, cs], C.x[:, t, cs], pz[:, :], ALU.add, [rpz], [C.rx[t]])
    P.barrier()
    nc.sbuf_base = markA

    load_bcast(P, g_n, rgn, io["ffn_norm"], D, None)
    h2T = P.sb([128, 8, NTOK], BF16); rh2T = [P.res() for _ in range(NSUP)]
    hbR = Rot(P, 2, [128, D], BF16)
    for t in range(NTILE):
        hb, rhb = hbR.next()
        norm_T(t, hb, rhb, h2T, t * 128, rh2T[t // 4], eng="act" if t % 2 == 0 else "dve")
    NF = 11
    wg = P.sb([128, 8, NF * 128], BF16); wu = P.sb([128, 8, NF * 128], BF16); wd = P.sb([128, NF, D], BF16); rwB = P.res()
    actT = P.sb([128, NF, 512], BF16); ractT = P.res()
    sgR = Rot(P, 2, [128, 512], F32)
    for grp in range(2):
        f0 = grp * NF * 128
        for k in range(8):
            P.dma("pool", wg[:, k, :], io["w_fg"][k * 128:(k + 1) * 128, f0:f0 + NF * 128], writes=[rwB])
            P.dma("pool", wu[:, k, :], io["w_fu"][k * 128:(k + 1) * 128, f0:f0 + NF * 128], writes=[rwB])
        for f in range(NF):
            P.dma("pool", wd[:, f, :], io["w_fd"][f0 + f * 128:f0 + (f + 1) * 128, :], writes=[rwB])
        for st in range(NSUP):
            ts_ = slice(st * 512, (st + 1) * 512)
            for f in range(NF):
                pg, rpg = pzR.next()
                for k in range(8):
                    MM(P, pg[:, :], wg[:, k, f * 128:(f + 1) * 128], h2T[:, k, ts_], k == 0, k == 7, [rwB, rh2T[st]], [rpg])
                pu, rpu = pzR.next()
                for k in range(8):
                    MM(P, pu[:, :], wu[:, k, f * 128:(f + 1) * 128], h2T[:, k, ts_], k == 0, k == 7, [rwB, rh2T[st]], [rpu])
                sg, rsg = sgR.next()
                ACT(P, sg[:], pg[:, :], AF.Silu, [rpg], [rsg])
                TT(P, "dve", actT[:, f, :], sg[:], pu[:, :], ALU.mult, [rsg, rpu], [ractT])
            for tt in range(4):
                t = st * 4 + tt
                for c in range(2):
                    cs = slice(c * 512, (c + 1) * 512)
                    pz, rpz = pzR.next()
                    for f in range(NF):
                        MM(P, pz[:, :], actT[:, f, tt * 128:(tt + 1) * 128], wd[:, f, cs], f == 0, f == NF - 1, [ractT, rwB], [rpz])
                    TT(P, "dve", C.x[:, t, cs], C.x[:, t, cs], pz[:, :], ALU.add, [rpz], [C.rx[t]])
    P.barrier()
    nc.sbuf_base = markA

    load_bcast(P, g_n, rgn, io["ple_norm"], D, None)
    wpg = P.sb([128, 8, D], BF16); wpp = P.sb([128, 2, D], BF16); rwC = P.res()
    load_w_bf16(P, wpg, rwC, io["w_pg"], 1024, 1024, None)
    load_w_bf16(P, wpp, rwC, io["w_pp"], 256, 1024, None)
    hbR = Rot(P, 2, [128, D], BF16)
    hTR = Rot(P, 2, [128, 8, 128], BF16)
    pfR = Rot(P, 2, [128, 256], BF16)
    pTR = Rot(P, 2, [128, 2, 128], BF16)
    sgR = Rot(P, 2, [128, 512], F32)
    tmpm = P.sb([128, 512], F32); rtmpm = P.res()
    if last:
        g_f = P.sb([128, D], F32); rgf = P.res()
        load_bcast(P, g_f, rgf, io["final_norm"], D, None)
        yR = Rot(P, 2, [128, D], F32)
    for t in range(NTILE):
        hb, rhb = hbR.next()
        hT, rhT = hTR.next()
        norm_T(t, hb, rhb, hT, 0, rhT)
        pf, rpf = pfR.next()
        P.dma("pool", pf[:], io["p"][t * 128:(t + 1) * 128, :], writes=[rpf])
        pt, rpt = ptR.next()
        transpose_chunks(P, C, pf, rpf, 2, pt, rpt)
        pT, rpT = pTR.next()
        cp(P, "dve", pT[:].rearrange("p k t -> p (k t)"), pt[:, 0:256], [rpt], [rpT])
        for c in range(2):
            cs = slice(c * 512, (c + 1) * 512)
            pz, rpz = pzR.next()
            for k in range(8):
                MM(P, pz[:, :], hT[:, k, :], wpg[:, k, cs], k == 0, k == 7, [rhT, rwC], [rpz])
            sg, rsg = sgR.next()
            ACT(P, sg[:], pz[:, :], AF.Sigmoid, [rpz], [rsg])
            pp, rpp = pzR.next()
            for k in range(2):
                MM(P, pp[:, :], pT[:, k, :], wpp[:, k, cs], k == 0, k == 1, [rpT, rwC], [rpp])
            TT(P, "dve", tmpm[:], sg[:], pp[:, :], ALU.mult, [rsg, rpp], [rtmpm])
            TT(P, "dve", C.x[:, t, cs], C.x[:, t, cs], tmpm[:], ALU.add, [rtmpm], [C.rx[t]])
        if last:
            ss, rss = ssR.next()
            y, ry = yR.next()
            MEMSET(P, "pool", ss[:, 0:1], 0.0, [], [rss])
            ACT(P, junk[:], C.x[:, t, :], AF.Square, [C.rx[t], rss], [rss], accum_out=ss[:, 0:1])
            TS(P, "dve", ss[:, 1:2], ss[:, 0:1], 1.0 / D, EPS, ALU.mult, ALU.add, [rss], [rss])
            ACT(P, ss[:, 2:3], ss[:, 1:2], AF.Sqrt, [rss], [rss])
            P.op("dve", lambda e, ss=ss: e.reciprocal(out=ss[:, 3:4], in_=ss[:, 2:3]), reads=[rss], writes=[rss])
            STT(P, y[:], C.x[:, t, :], ss[:, 3:4], g_f[:], ALU.mult, ALU.mult, [C.rx[t], rss, rgf], [ry])
            P.dma("sp", io["y"][t * 128:(t + 1) * 128, :], y[:], reads=[ry])
    P.barrier()
    nc.sbuf_base = base_mark
    nc.psum_base = pbase


P1_WNAMES = {"w_in": [D, 2616], "mix_norm": [1, D], "q_norm": [1, 256], "kv_norm": [1, 256], "w_q_up": [256, 768], "w_kv_up": [256, 1024]}
P3_WNAMES = {"mix_norm": [1, D], "w_bg": [D, 3072], "w_pa": [512, D], "w_pb": [512, D], "w_pc": [512, D], "w_out": [D, D],
             "ffn_norm": [1, D], "w_fg": [D, DFF], "w_fu": [D, DFF], "w_fd": [DFF, D], "ple_norm": [1, D], "w_pg": [D, D],
             "w_pp": [256, D], "p": [NTOK, 256]}


def build_tok_program(do_p3, do_p1, last):
    P = Prog()
    x_in = P.dram("x_in", [NTOK, D], F32, "ExternalInput").ap()
    pos_in = P.dram("pos_in", [128, NTILE], I32, "ExternalInput").ap()
    ident = P.dram("ident", [128, 128], F32, "ExternalInput").ap()
    invf = P.dram("invf", [128, 48], F32, "ExternalInput").ap()
    C = Common(P, x_in, pos_in, ident, invf)
    if do_p3:
        io = {}
        for k, shp in P3_WNAMES.items():
            io[k] = P.dram("p3_" + k, shp, F32, "ExternalInput").ap()
        io["o"] = P.dram("p3_o", [4, NTOK, 384], BF16, "ExternalInput").ap()
        if last:
            io["final_norm"] = P.dram("p3_final_norm", [1, D], F32, "ExternalInput").ap()
            io["y"] = P.dram("y", [NTOK, D], F32, "ExternalOutput").ap()
        emit_p3(P, C, io, last)
        if not last:
            x_out = P.dram("x_out", [NTOK, D], F32, "ExternalOutput").ap()
            for t in range(NTILE):
                P.dma("sp", x_out[t * 128:(t + 1) * 128, :], C.x[:, t, :], reads=[C.rx[t]])
    if do_p1:
        io = {}
        for k, shp in P1_WNAMES.items():
            io[k] = P.dram("p1_" + k, shp, F32, "ExternalInput").ap()
        for k, (shp, dt) in P1_X.items():
            io[k] = P.dram(k, [4] + shp, dt, "ExternalOutput").ap()
        emit_p1(P, C, io)
    return P.build()


def build_p2_program():
    P = Prog()
    io = {}
    for k, (shp, dt) in P2_IN.items():
        io[k] = P.dram(k, shp, dt, "ExternalInput").ap()
    for k, shp in P2_W.items():
        io[k] = P.dram(k, shp, F32, "ExternalInput").ap()
    identd = P.dram("ident", [128, 128], F32, "ExternalInput").ap()
    io["o"] = P.dram("o", [4, NTOK, 384], BF16, "ExternalOutput").ap()
    identf = P.sb([128, 128], F32)
    ident = P.sb([128, 128], BF16)
    rid = P.res()
    P.dma("sp", identf[:], identd, writes=[rid])
    cp(P, "dve", ident[:], identf[:], [rid], [rid])
    emit_p2(P, io, ident, rid)
    return P.build()


X1_TO_P2 = {"xqa": "qa", "xqg": "qg", "xka": "ka", "xva": "va", "xqb": "qb", "xkb": "kb", "xvb": "vb",
            "xqc": "qc", "xkc": "kc", "xvc": "vc", "xg": "g"}


def p1_weights(inp, l):
    m = {}
    m["p1_w_in"] = np.ascontiguousarray(inp["w_in"][l][:, W_IN_PERM])
    m["p1_mix_norm"] = np.ascontiguousarray(inp["mix_norm"][l][None, :])
    m["p1_q_norm"] = np.ascontiguousarray(inp["c_q_norm"][l][None, :])
    m["p1_kv_norm"] = np.ascontiguousarray(inp["c_kv_norm"][l][None, :])
    m["p1_w_q_up"] = np.ascontiguousarray(inp["c_w_q_up"][l])
    m["p1_w_kv_up"] = np.ascontiguousarray(inp["c_w_kv_up"][l])
    return m


def p2_weights(inp, l, r):
    m = {}
    for w in ("k", "v"):
        m["pos" + w] = np.ascontiguousarray(inp[f"a_cmp_pos_{w}"][l].reshape(16, 128).T)
        m["w1" + w] = np.ascontiguousarray(inp[f"a_cmp_w1_{w}"][l])
        m["w2" + w] = np.ascontiguousarray(inp[f"a_cmp_w2_{w}"][l])
    m["sinks"] = np.ascontiguousarray(inp["b_sinks"][l][2 * r:2 * r + 2][None, :])
    m["selmap"] = selmap_const()
    m["ident"] = np.eye(128, dtype=np.float32)
    return m


def p3_weights(inp, l, b, r, last):
    m = {}
    src = {"mix_norm": "mix_norm", "w_bg": "w_branch_gate", "w_pa": "w_branch_a", "w_pb": "w_branch_b", "w_pc": "w_branch_c",
           "w_out": "w_out", "ffn_norm": "ffn_norm", "w_fg": "w_ffn_gate", "w_fu": "w_ffn_up", "w_fd": "w_ffn_down",
           "ple_norm": "ple_norm", "w_pg": "w_ple_gate", "w_pp": "w_ple_proj"}
    for k, s in src.items():
        a = inp[s][l]
        m["p3_" + k] = np.ascontiguousarray(a[None, :] if a.ndim == 1 else a)
    m["p3_p"] = np.ascontiguousarray(inp["p"][l, b, r * NTOK:(r + 1) * NTOK])
    if last:
        m["p3_final_norm"] = np.ascontiguousarray(inp["final_norm"][None, :])
    return m


def all_to_all(outs, names):
    res = []
    for core in range(8):
        b, r = divmod(core, 4)
        res.append({nm: np.ascontiguousarray(np.stack([outs[4 * b + s][nm][r] for s in range(4)], axis=0)) for nm in names})
    return res


X1_LAYOUT = [("xqa", [2, 64, NTOK], 0, 0), ("xqg", [2, 64, NTOK], 0, 128), ("xka", [4, 64, NTOK], 1, 0),
             ("xva", [2, NTOK, 64], 2, 0), ("xqb", [2, 64, NTOK], 2, 128), ("xkb", [64, NTOK], 3, 0),
             ("xvb", [NTOK, 64], 3, 64), ("xqc", [2, 96, NTOK], 4, 0), ("xkc", [2, 96, NTOK], 5, 0),
             ("xvc", [2, NTOK, 64], 6, 0)]
X1_K = 7
X1_CR = 256


def x1_views(rows):
    views = {}
    for nm, shp, k, r0 in X1_LAYOUT:
        n = int(np.prod(shp)) // 2048
        v = rows(k, r0, n)
        if nm in ("xqa", "xqg", "xka", "xqb", "xqc", "xkc"):
            v = v.rearrange("e (h d) t -> e h d t", h=shp[0])
        elif nm in ("xva", "xvc"):
            v = v.rearrange("e r c -> e (r c)").rearrange("e (h t d) -> e h t d", h=shp[0], d=64)
        elif nm == "xvb":
            v = v.rearrange("e r c -> e (r c)").rearrange("e (t d) -> e t d", d=64)
        views[nm] = v
    return views


def build_fused_program():
    P = Prog()
    nc = P.nc
    x_in = P.dram("x_in", [NTOK, D], F32, "ExternalInput").ap()
    pos_in = P.dram("pos_in", [128, NTILE], I32, "ExternalInput").ap()
    ident = P.dram("ident", [128, 128], F32, "ExternalInput").ap()
    invf = P.dram("invf", [128, 48], F32, "ExternalInput").ap()
    y_out = P.dram("y", [NTOK, D], F32, "ExternalOutput").ap()
    RD = X1_K * X1_CR
    X1 = P.dram("ex_x1", [4 * RD, 2048], BF16, "Internal").ap()
    G1 = P.dram("ex_g1", [16 * RD, 2048], BF16, "Internal").ap()
    M1 = P.dram("ex_m1", [4 * RD, 2048], BF16, "Internal").ap()
    XG = P.dram("ex_xg", [4 * 16, 768], F32, "Internal").ap()
    GG = P.dram("ex_gg", [16 * 16, 768], F32, "Internal").ap()
    MG = P.dram("ex_mg", [4 * 16, 768], F32, "Internal").ap()
    O2 = P.dram("ex_o2", [12 * NTOK, 128], BF16, "Internal").ap()
    GO = P.dram("ex_go", [48 * NTOK, 128], BF16, "Internal").ap()
    MO = P.dram("ex_mo", [12 * NTOK, 128], BF16, "Internal").ap()
    C = Common(P, x_in, pos_in, ident, invf)
    mark, pmark = nc.sbuf_base, nc.psum_base

    def phase_end():
        P.barrier()
        nc.sbuf_base = mark
        nc.psum_base = pmark

    def exchange_chunked(src, gath, mine, nchunk, cr):
        P.barrier()
        rc_ = P.res()
        for j in range(4 * nchunk):
            P.allgather(src[j * cr:(j + 1) * cr, :], gath[j * 4 * cr:(j + 1) * 4 * cr, :], rc_)
        rm_ = P.res()
        g3 = gath.rearrange("(d x) c -> d x c", d=4)
        P.dma("pool", mine, (lambda: g3[bass.ds(P.rank(), 1), :, :]), reads=[rc_], writes=[rm_])
        P.barrier()

    def exchange_small(src, gath, mine):
        P.barrier()
        rc_ = P.res()
        P.allgather(src, gath, rc_)
        rm_ = P.res()
        g4 = gath.rearrange("(s d r) c -> s d r c", s=4, d=4)
        m3 = mine.rearrange("(s r) c -> s r c", s=4)
        P.dma("pool", m3, (lambda: g4[:, bass.ds(P.rank(), 1), :, :]), reads=[rc_], writes=[rm_])
        P.barrier()

    X1v = X1.rearrange("(e r) c -> e r c", e=4)
    M1v = M1.rearrange("(k s i) c -> k s i c", k=X1_K, s=4)
    MOv = MO.rearrange("(k s i) c -> k s i c", k=2, s=4)
    for l in range(2):
        last = l == 1
        io = {}
        for k, shp in P1_WNAMES.items():
            io[k] = P.dram(f"l{l}_p1_{k}", shp, F32, "ExternalInput").ap()
        io.update(x1_views(lambda k, r0, n: X1v[:, k * X1_CR + r0:k * X1_CR + r0 + n, :]))
        io["xg"] = XG.rearrange("(e a) (p c) -> e (a p) c", e=4, c=6)
        emit_p1(P, C, io)
        P.barrier()
        g3 = G1.rearrange("(d x) c -> d x c", d=4)
        groups = {"nsa": (0, 3), "swa": (3, 4), "mla": (4, 7)}
        rcg, rmg = {}, {}
        for gname, (k0, k1) in groups.items():
            rcg[gname] = P.res()
            rmg[gname] = P.res()
            for d in range(4):
                for k in range(k0, k1):
                    j = d * X1_K + k
                    P.allgather(X1[j * X1_CR:(j + 1) * X1_CR, :], G1[j * 4 * X1_CR:(j + 1) * 4 * X1_CR, :], rcg[gname], sem="cc_" + gname)
            if gname == "nsa":
                rcg["g"] = P.res()
                rmg["g"] = P.res()
                P.allgather(XG, GG, rcg["g"], sem="cc_g")

        def select(gname):
            k0, k1 = groups[gname]
            r0, r1 = k0 * 4 * X1_CR, k1 * 4 * X1_CR
            P.dma("sp", M1[r0:r1, :], (lambda: g3[bass.ds(P.rank("sp"), 1), r0:r1, :]), reads=[rcg[gname]], writes=[rmg[gname]])

        select("nsa")
        gg4 = GG.rearrange("(s d r) c -> s d r c", s=4, d=4)
        P.dma("sp", MG.rearrange("(s r) c -> s r c", s=4), (lambda: gg4[:, bass.ds(P.rank("sp"), 1), :, :]), reads=[rcg["g"]], writes=[rmg["g"]])
        nc.sbuf_base = mark
        nc.psum_base = pmark
        io = {}
        v = x1_views(lambda k, r0, n: M1v[k][:, r0:r0 + n, :])
        for k, vv in v.items():
            io[X1_TO_P2[k]] = vv
        io["g"] = MG.rearrange("(e a) (p c) -> e (a p) c", e=4, c=6)
        io["deps"] = {"nsa": [rmg["nsa"], rmg["g"]], "swa": [rmg["swa"]], "mla": [rmg["mla"]]}
        io["pre_swa"] = lambda: select("swa")
        io["pre_mla"] = lambda: select("mla")
        for k, shp in P2_W.items():
            if k == "selmap":
                if l == 0:
                    selmap_ap = P.dram("selmap", shp, F32, "ExternalInput").ap()
                io[k] = selmap_ap
            else:
                io[k] = P.dram(f"l{l}_p2_{k}", shp, F32, "ExternalInput").ap()
        O2v = O2.rearrange("(m e t) c -> m e t c", m=3, e=4)
        ro = [P.res(), P.res(), P.res()]
        rco = P.res()
        io["o_mix"] = lambda m: O2v[m]
        io["ro"] = ro

        def post_mix(m, ro=ro, rco=rco):
            for d in range(4):
                P.allgather(O2[(m * 4 + d) * NTOK:(m * 4 + d + 1) * NTOK, :], GO[((d * 3 + m) * 4) * NTOK:((d * 3 + m) * 4 + 4) * NTOK, :], rco,
                            reads=[ro[m]] if d == 0 else (), sem="cc_o")

        io["post_mix"] = post_mix
        emit_p2(P, io, C.ident, C.rid)
        P.barrier()
        nc.sbuf_base = mark
        nc.psum_base = pmark
        rmo = P.res()
        go3 = GO.rearrange("(d x) c -> d x c", d=4)
        P.dma("sp", MO, (lambda: go3[bass.ds(P.rank("sp"), 1), :, :]), reads=[rco], writes=[rmo])
        MOv = MO.rearrange("(m s t) c -> m s t c", m=3, s=4)
        io = {}
        for k, shp in P3_WNAMES.items():
            io[k] = P.dram(f"l{l}_p3_{k}", shp, F32, "ExternalInput").ap()
        io["o_tile3"] = lambda t, m: MOv[m][:, t * 128:(t + 1) * 128, :]
        io["o_dep"] = [rmo]
        if last:
            io["final_norm"] = P.dram("final_norm", [1, D], F32, "ExternalInput").ap()
            io["y"] = y_out
        emit_p3(P, C, io, last)
        phase_end()
    return P.build()


_PROGS = {}


def _prog(key):
    if key not in _PROGS:
        if key == "fused":
            _PROGS[key] = build_fused_program()
        elif key == "p2":
            _PROGS[key] = build_p2_program()
        else:
            _PROGS[key] = build_tok_program(*key)
    return _PROGS[key]


def kernel(**inp):
    inp = {k: np.asarray(v) for k, v in inp.items()}
    cst = const_inputs()
    cores = list(range(8))
    maps = []
    for c in cores:
        b, r = divmod(c, 4)
        m = dict(cst)
        m["x_in"] = np.ascontiguousarray(inp["x"][b, r * NTOK:(r + 1) * NTOK]).astype(np.float32)
        m["pos_in"] = np.ascontiguousarray(inp["positions"][b, r * NTOK:(r + 1) * NTOK].reshape(NTILE, 128).T.astype(np.int32))
        m["selmap"] = selmap_const()
        m["final_norm"] = np.ascontiguousarray(inp["final_norm"][None, :])
        for l in range(2):
            for k, v in p1_weights(inp, l).items():
                m[f"l{l}_{k}"] = v
            for k, v in p2_weights(inp, l, r).items():
                if k not in ("selmap", "ident"):
                    m[f"l{l}_p2_{k}"] = v
            for k, v in p3_weights(inp, l, b, r, False).items():
                m[f"l{l}_{k}"] = v
        maps.append(m)
    res = run_bass_kernel_spmd(_prog("fused"), maps, core_ids=cores).results
    y = np.stack([np.concatenate([np.asarray(res[4 * b + r]["y"]) for r in range(4)], axis=0) for b in range(2)], axis=0)
    return y.astype(np.float32)
```

```python
import numpy as np
import ml_dtypes
import concourse.bass as bass
import concourse.mybir as mybir
from concourse.bass_utils import run_bass_kernel_spmd

F32 = mybir.dt.float32
BF16 = mybir.dt.bfloat16
I32 = mybir.dt.int32
AF = mybir.ActivationFunctionType
ALU = mybir.AluOpType
AX = mybir.AxisListType

D = 1024
S = 8192
NTOK = 2048
NTILE = 16
NSUP = 4
EPS = 1e-6
DFF = 2816
NEG = -30000.0


class Res:
    __slots__ = ("name", "w", "r", "wdma")

    def __init__(self, name):
        self.name = name
        self.w = None
        self.r = {}


class Prog:
    ENG = ("pe", "act", "dve", "pool", "sp")

    def __init__(self):
        self.nc = bass.Bass("TRN2", target_bir_lowering=False)
        nc = self.nc
        self.cnt = {k: 0 for k in self.ENG}
        self.ops = {k: [] for k in self.ENG}
        self.seen = {k: {} for k in self.ENG}
        self.semobj = {}
        for k in self.ENG:
            self.semobj["s_" + k] = nc.alloc_semaphore("s_" + k)
        self.dsem = {}
        self.free_dsems = []
        self.nres = 0
        self.nname = 0
        self.ncc = 0
        self._rank = None

    def sb(self, shape, dt, name=None):
        self.nname += 1
        return self.nc.alloc_sbuf_tensor(name or f"t{self.nname}", list(shape), dt)

    def ps(self, shape, dt=F32, name=None):
        self.nname += 1
        return self.nc.alloc_psum_tensor(name or f"p{self.nname}", list(shape), dt)

    def res(self, name=None):
        self.nres += 1
        return Res(name or f"r{self.nres}")

    def dram(self, name, shape, dt, kind):
        return self.nc.dram_tensor(name, list(shape), dt, kind=kind)

    def _waits(self, eng, reads, writes, dma_key=None):
        waits = {}

        def need(tok):
            if tok is None:
                return
            s, v = tok
            if waits.get(s, 0) < v:
                waits[s] = v

        for r in reads:
            need(r.w)
        for w in writes:
            if not (dma_key is not None and w.w is not None and w.w[0] == dma_key):
                need(w.w)
            for tok in w.r.values():
                need(tok)
        wl = []
        for s, v in waits.items():
            if eng == "pe" and s == "s_pe":
                continue
            if self.seen[eng].get(s, 0) >= v:
                continue
            self.seen[eng][s] = v
            wl.append((s, v))
        return wl

    def op(self, eng, fn, reads=(), writes=()):
        wl = self._waits(eng, reads, writes)
        self.cnt[eng] += 1
        sname = "s_" + eng
        tok = (sname, self.cnt[eng])
        self.ops[eng].append((wl, fn, (sname, 1)))
        for r in reads:
            r.r[eng] = tok
        for w in writes:
            w.w = tok
            w.r = {}

    def dma(self, q, out, in_, reads=(), writes=(), chan=None, **kw):
        key = (list(writes) + list(reads))[0]
        if key.name not in self.dsem:
            if self.free_dsems:
                self.dsem[key.name] = self.free_dsems.pop()
            else:
                sname = "d_" + key.name
                self.semobj[sname] = self.nc.alloc_semaphore(sname)
                self.dsem[key.name] = [sname, 0]
        d = self.dsem[key.name]
        wl = self._waits(q, reads, writes, dma_key=d[0])
        d[1] += 16
        tok = (d[0], d[1])
        self.ops[q].append((wl, (lambda e: e.dma_start(out=out, in_=(in_() if callable(in_) else in_), **kw)), (d[0], 16)))
        for r in reads:
            r.r["dma:" + key.name] = tok
        for w in writes:
            w.w = tok
            w.wdma = True
            w.r = {}

    def barrier(self):
        allw = [(d[0], d[1]) for d in self.dsem.values() if d[1] > 0]
        for k in self.ENG:
            if self.cnt[k] > 0:
                allw.append(("s_" + k, self.cnt[k]))
        for k in self.ENG:
            wl = []
            for s, v in allw:
                if self.seen[k].get(s, 0) >= v:
                    continue
                self.seen[k][s] = v
                wl.append((s, v))
            if wl:
                self.ops[k].append((wl, None, None))
        self.free_dsems.extend(self.dsem.values())
        self.dsem = {}

    def freg(self, e, val):
        if not hasattr(self, "_fregs"):
            self._fregs = {}
        if val not in self._fregs:
            self._fregs[val] = e.to_reg(float(val))
        return self._fregs[val]

    def rank(self, eng="pool"):
        if self._rank is None:
            self._rank = {}
        if eng not in self._rank:
            et = {"pool": mybir.EngineType.Pool, "sp": mybir.EngineType.SP}[eng]
            self._rank[eng] = self.nc.partition_id([et]) % 4
        return self._rank[eng]

    def allgather(self, ins_ap, outs_ap, rres, reads=(), sem="cc"):
        sname = "s_" + sem
        if sname not in self.semobj:
            self.semobj[sname] = self.nc.alloc_semaphore(sname)
            self.cccnt = getattr(self, "cccnt", {})
            self.cccnt[sname] = 0
        self.cccnt[sname] += 1
        wl = self._waits("pool", list(reads), []) if reads else []
        self.ops["pool"].append((wl, (lambda e: e.collective_compute("AllGather", ALU.bypass, replica_groups=[[0, 1, 2, 3], [4, 5, 6, 7]],
                                                                    ins=[ins_ap], outs=[outs_ap])), (sname, 1)))
        rres.w = (sname, self.cccnt[sname])
        rres.r = {}

    def build(self):
        nc = self.nc
        self.barrier()
        with nc.Block() as block:
            def emit(k):
                def body(e):
                    for wl, fn, inc in self.ops[k]:
                        for s, v in wl:
                            e.wait_ge(self.semobj[s], v)
                        if fn is None:
                            continue
                        ins = fn(e)
                        ins.then_inc(self.semobj[inc[0]], inc[1])
                return body
            block.tensor(emit("pe"))
            block.scalar(emit("act"))
            block.vector(emit("dve"))
            block.gpsimd(emit("pool"))
            block.sync(emit("sp"))
        return nc


class Rot:
    def __init__(self, P, n, shape, dt, psum=False):
        self.t = [(P.ps(shape, dt) if psum else P.sb(shape, dt)) for _ in range(n)]
        self.r = [P.res() for _ in range(n)]
        self.i = -1
        self.n = n

    def next(self):
        self.i = (self.i + 1) % self.n
        return self.t[self.i], self.r[self.i]

    def cur(self):
        return self.t[self.i], self.r[self.i]


W_IN_PERM = np.concatenate([
    np.arange(0, 512), np.arange(512, 640), np.arange(768, 896), np.arange(1024, 1152),
    np.arange(1304, 1816), np.arange(1816, 1944),
    np.arange(640, 768), np.arange(896, 1024), np.arange(1152, 1280), np.arange(1944, 2072),
    np.arange(2072, 2328), np.arange(2328, 2584), np.arange(2584, 2616), np.arange(1280, 1304)])


def const_inputs():
    c = {}
    c["ident"] = np.eye(128, dtype=np.float32)
    f64 = (10000.0 ** (-np.arange(0, 64, 2, dtype=np.float32) / 64)).astype(np.float32)
    f32 = (10000.0 ** (-np.arange(0, 32, 2, dtype=np.float32) / 32)).astype(np.float32)
    c["invf"] = np.ascontiguousarray(np.broadcast_to(np.concatenate([f64, f32])[None, :], (128, 48))).astype(np.float32)
    return c


P1_X = {
    "xqa": ([2, 64, NTOK], BF16), "xqg": ([2, 64, NTOK], BF16), "xka": ([4, 64, NTOK], BF16),
    "xva": ([2, NTOK, 64], BF16), "xqb": ([2, 64, NTOK], BF16), "xkb": ([64, NTOK], BF16),
    "xvb": ([NTOK, 64], BF16), "xqc": ([2, 96, NTOK], BF16), "xkc": ([2, 96, NTOK], BF16),
    "xvc": ([2, NTOK, 64], BF16), "xg": ([NTOK, 6], F32),
}


class Common:
    def __init__(self, P, x_in, pos_in, ident_in, invf_in):
        self.P = P
        nc = P.nc
        self.x = P.sb([128, NTILE, D], F32, "xres")
        self.rx = [P.res() for _ in range(NTILE)]
        for t in range(NTILE):
            P.dma("sp", self.x[:, t, :], x_in[t * 128:(t + 1) * 128, :], writes=[self.rx[t]], chan="xin")
        self.ident = P.sb([128, 128], BF16)
        self.rid = P.res()
        self.cos = P.sb([128, NTILE, 48], F32)
        self.sin = P.sb([128, NTILE, 48], F32)
        self.rcs = P.res()
        mark = nc.sbuf_base
        self.identf = P.sb([128, 128], F32)
        P.dma("sp", self.identf[:], ident_in, writes=[self.rid], chan="c0")
        P.op("dve", lambda e: e.tensor_copy(out=self.ident[:], in_=self.identf[:]), reads=[self.rid], writes=[self.rid])
        pos_i = P.sb([128, NTILE], I32)
        pos_f = P.sb([128, NTILE], F32)
        invf = P.sb([128, 48], F32)
        rp = P.res()
        P.dma("sp", pos_i[:], pos_in, writes=[rp], chan="c0")
        P.dma("sp", invf[:], invf_in, writes=[rp], chan="c0")
        P.op("dve", lambda e: e.tensor_copy(out=pos_f[:], in_=pos_i[:]), reads=[rp], writes=[rp])
        ang = P.sb([128, NTILE, 48], F32)
        ra = P.res()
        for t in range(NTILE):
            P.op("dve", (lambda e, t=t: e.tensor_scalar(out=ang[:, t, :], in0=invf[:], scalar1=pos_f[:, t:t + 1], scalar2=None, op0=ALU.mult)),
                 reads=[rp], writes=[ra])
        tmp = P.sb([128, NTILE, 48], F32)
        ni = P.sb([128, NTILE, 48], I32)
        nf = P.sb([128, NTILE, 48], F32)
        msk = P.sb([128, NTILE, 48], F32)
        C1 = 6.28125
        C2 = 2.0 * np.pi - 6.28125
        PI = float(np.pi)
        TS = lambda **kw: (lambda e: e.tensor_scalar(**kw))
        STT = lambda **kw: (lambda e: e.scalar_tensor_tensor(**kw))
        A2 = lambda ap: ap.rearrange("p t c -> p (t c)")
        P.op("dve", TS(out=A2(ni[:]), in0=A2(ang[:]), scalar1=float(1.0 / (2.0 * np.pi)), scalar2=None, op0=ALU.mult), reads=[ra], writes=[ra])
        P.op("dve", lambda e: e.tensor_copy(out=A2(nf[:]), in_=A2(ni[:])), reads=[ra], writes=[ra])
        P.op("dve", STT(out=A2(tmp[:]), in0=A2(nf[:]), scalar=-C1, in1=A2(ang[:]), op0=ALU.mult, op1=ALU.add), reads=[ra], writes=[ra])
        P.op("dve", STT(out=A2(tmp[:]), in0=A2(nf[:]), scalar=-C2, in1=A2(tmp[:]), op0=ALU.mult, op1=ALU.add), reads=[ra], writes=[ra])
        P.op("dve", TS(out=A2(msk[:]), in0=A2(tmp[:]), scalar1=PI, scalar2=None, op0=ALU.is_gt), reads=[ra], writes=[ra])
        P.op("dve", STT(out=A2(tmp[:]), in0=A2(msk[:]), scalar=-2.0 * PI, in1=A2(tmp[:]), op0=ALU.mult, op1=ALU.add), reads=[ra], writes=[ra])
        P.op("dve", TS(out=A2(msk[:]), in0=A2(tmp[:]), scalar1=-PI, scalar2=None, op0=ALU.is_lt), reads=[ra], writes=[ra])
        P.op("dve", STT(out=A2(tmp[:]), in0=A2(msk[:]), scalar=2.0 * PI, in1=A2(tmp[:]), op0=ALU.mult, op1=ALU.add), reads=[ra], writes=[ra])
        P.op("act", lambda e: e.activation(out=self.sin[:], in_=tmp[:], func=AF.Sin), reads=[ra], writes=[ra, self.rcs])
        P.op("dve", TS(out=A2(tmp[:]), in0=A2(tmp[:]), scalar1=PI / 2.0, scalar2=None, op0=ALU.add), reads=[ra], writes=[ra])
        P.op("dve", TS(out=A2(msk[:]), in0=A2(tmp[:]), scalar1=PI, scalar2=None, op0=ALU.is_gt), reads=[ra], writes=[ra])
        P.op("dve", STT(out=A2(tmp[:]), in0=A2(msk[:]), scalar=-2.0 * PI, in1=A2(tmp[:]), op0=ALU.mult, op1=ALU.add), reads=[ra], writes=[ra])
        P.op("act", lambda e: e.activation(out=self.cos[:], in_=tmp[:], func=AF.Sin), reads=[ra], writes=[ra, self.rcs])
        P.barrier()
        nc.sbuf_base = mark

    def rms_to_T(self, src_fn, rsrc, g_tile, rg, hT, rhT, col0, ncols, scratch):
        pass


def load_w_bf16(P, dst, rdst, w_dram, rows, cols, chan):
    k = rows // 128
    for i in range(k):
        P.dma("pool", dst[:, i, :], w_dram[i * 128:(i + 1) * 128, :], writes=[rdst], chan=chan)


def load_bcast(P, dst, rdst, v_dram, n, chan):
    P.dma("sp", dst[:], v_dram.partition_broadcast(128), writes=[rdst], chan=chan)


def rmsnorm_tile(P, C, src, rsrc, n, g_tile, rg, out_bf, rout, tmp):
    junk, ss, rt = tmp
    P.op("pool", lambda e: e.memset(ss[:, 0:1], 0.0), writes=[rt])
    P.op("act", lambda e: e.activation(out=junk[:, 0:n], in_=src, func=AF.Square, accum_out=ss[:, 0:1]), reads=[rsrc, rt], writes=[rt])
    P.op("dve", lambda e: e.tensor_scalar(out=ss[:, 1:2], in0=ss[:, 0:1], scalar1=1.0 / n, scalar2=EPS, op0=ALU.mult, op1=ALU.add), reads=[rt], writes=[rt])
    P.op("act", lambda e: e.activation(out=ss[:, 2:3], in_=ss[:, 1:2], func=AF.Sqrt), reads=[rt], writes=[rt])
    P.op("dve", lambda e: e.reciprocal(out=ss[:, 3:4], in_=ss[:, 2:3]), reads=[rt], writes=[rt])
    P.op("dve", lambda e: e.scalar_tensor_tensor(out=out_bf, in0=src, scalar=ss[:, 3:4], in1=g_tile, op0=ALU.mult, op1=ALU.mult),
         reads=[rsrc, rt, rg], writes=[rout])


def transpose_chunks(P, C, src_bf, rsrc, nch, pt, rpt, width=128):
    for c in range(nch):
        P.op("pe", (lambda e, c=c: e.transpose(out=pt[0:width, c * 128:(c + 1) * 128], in_=src_bf[:, c * width:(c + 1) * width], identity=C.ident[:])),
             reads=[rsrc, C.rid], writes=[rpt])


def TT(P, eng, out, in0, in1, op, reads, writes):
    P.op(eng, lambda e: e.tensor_tensor(out=out, in0=in0, in1=in1, op=op), reads=reads, writes=writes)


def TS(P, eng, out, in0, s1, s2, op0, op1, reads, writes):
    if op1 is None:
        P.op(eng, lambda e: e.tensor_scalar(out=out, in0=in0, scalar1=s1, scalar2=None, op0=op0), reads=reads, writes=writes)
    else:
        P.op(eng, lambda e: e.tensor_scalar(out=out, in0=in0, scalar1=s1, scalar2=s2, op0=op0, op1=op1), reads=reads, writes=writes)


def STT(P, out, in0, scalar, in1, op0, op1, reads, writes):
    P.op("dve", lambda e: e.scalar_tensor_tensor(out=out, in0=in0, scalar=scalar, in1=in1, op0=op0, op1=op1), reads=reads, writes=writes)


def ACT(P, out, in_, func, reads, writes, **kw):
    P.op("act", lambda e: e.activation(out=out, in_=in_, func=func, **kw), reads=reads, writes=writes)


def MM(P, out, lhsT, rhs, start, stop, reads, writes, skip=False):
    if skip:
        P.op("pe", lambda e: e.matmul(out, lhsT=lhsT, rhs=rhs, start=start, stop=stop, skip_group_check=True), reads=reads, writes=writes)
    else:
        P.op("pe", lambda e: e.matmul(out, lhsT=lhsT, rhs=rhs, start=start, stop=stop), reads=reads, writes=writes)


def TR(P, out, in_, ident, reads, writes):
    P.op("pe", lambda e: e.transpose(out=out, in_=in_, identity=ident), reads=reads, writes=writes)


def MEMSET(P, eng, ap, val, reads, writes):
    P.op(eng, lambda e: e.memset(ap, val), reads=reads, writes=writes)


def cp(P, eng, out, in_, reads, writes):
    if eng == "act":
        P.op("act", lambda e: e.copy(out=out, in_=in_), reads=reads, writes=writes)
    else:
        P.op(eng, lambda e: e.tensor_copy(out=out, in_=in_), reads=reads, writes=writes)


def emit_p1(P, C, io):
    nc = P.nc
    w_in = P.sb([128, 8, 2616], BF16); rw = P.res()
    load_w_bf16(P, w_in, rw, io["w_in"], 1024, 2616, "w1")
    w_qu = P.sb([128, 2, 768], BF16); w_kvu = P.sb([128, 2, 1024], BF16); rwm = P.res()
    load_w_bf16(P, w_qu, rwm, io["w_q_up"], 256, 768, "w1")
    load_w_bf16(P, w_kvu, rwm, io["w_kv_up"], 256, 1024, "w1")
    g_mix = P.sb([128, D], F32); g_q = P.sb([128, 256], F32); g_kv = P.sb([128, 256], F32); rg = P.res()
    load_bcast(P, g_mix, rg, io["mix_norm"], D, "w2")
    load_bcast(P, g_q, rg, io["q_norm"], 256, "w2")
    load_bcast(P, g_kv, rg, io["kv_norm"], 256, "w2")

    junk = P.sb([128, D], BF16)
    ssR = Rot(P, 2, [128, 4], F32)
    hbR = Rot(P, 1, [128, D], BF16)
    hTR = Rot(P, 2, [128, 8, 128], BF16)
    ptR = Rot(P, 2, [128, 1024], BF16, psum=True)
    pzR = Rot(P, 3, [128, 512], F32, psum=True)
    zsR = Rot(P, 1, [128, 2616], F32)
    zs_rc = [[P.res() for _ in range(6)] for _ in range(zsR.n)]
    rqR = Rot(P, 2, [128, 26, 64], BF16)
    tmpA = P.sb([128, 12, 32], F32); tmpB = P.sb([128, 12, 32], F32); rtA = P.res()
    tmpC = P.sb([128, 12, 32], F32); tmpD = P.sb([128, 12, 32], F32); rtC = P.res()
    stq = P.sb([128, 13, 512], BF16); rstq = P.res()
    stv = P.sb([128, 4, 8, 64], BF16); rstv = P.res()
    stqc = P.sb([128, 8, 512], BF16); rstqc = P.res()
    stkc = P.sb([128, 8, 512], BF16); rstkc = P.res()
    stvc = P.sb([128, 4, 8, 64], BF16); rstvc = P.res()
    stg = P.sb([128, 4, 24], F32); rstg = P.res()
    cnR = Rot(P, 1, [128, 512], BF16)
    cnTR = Rot(P, 1, [128, 4, 128], BF16)
    qfR = Rot(P, 1, [128, 8, 96], BF16)
    kfR = Rot(P, 1, [128, 8, 96], BF16)
    kpe = P.sb([128, 32], F32); rkpe = P.res()
    qsb = P.sb([128, 768], F32); rqsb = P.res()
    t16 = [P.sb([128, 8, 16], F32) for _ in range(4)]; rt16 = P.res()
    ktmp = P.sb([128, 4, 16], F32)

    for t in range(NTILE):
        st, tt = t // 4, t % 4
        s0 = st * 512
        xt = C.x[:, t, :]
        ss, rss = ssR.next()
        hb, rhb = hbR.next()
        rmsnorm_tile(P, C, xt, C.rx[t], D, g_mix[:], rg, hb[:], rhb, (junk, ss, rss))
        pt, rpt = ptR.next()
        transpose_chunks(P, C, hb, rhb, 8, pt, rpt)
        hT, rhT = hTR.next()
        P.op("act", lambda e, hT=hT, pt=pt: e.copy(out=hT[:].rearrange("p k t -> p (k t)"), in_=pt[:]), reads=[rpt], writes=[rhT])
        zs, rzs = zsR.next()
        rzc = zs_rc[zsR.i]
        for c in range(6):
            c0 = c * 512
            n = min(512, 2616 - c0)
            pz, rpz = pzR.next()
            for k in range(8):
                P.op("pe", (lambda e, pz=pz, hT=hT, k=k, c0=c0, n=n: e.matmul(pz[:, 0:n], lhsT=hT[:, k, :], rhs=w_in[:, k, c0:c0 + n], start=(k == 0), stop=(k == 7))),
                     reads=[rhT, rw], writes=[rpz])
            P.op("act", (lambda e, pz=pz, zs=zs, c0=c0, n=n: e.copy(out=zs[:, c0:c0 + n], in_=pz[:, 0:n])), reads=[rpz], writes=[rzc[c]])
        if t == 0 and "dbg_zs" in io:
            P.dma("sp", io["dbg_zs"], zs[:], reads=rzc, chan="dbg")
            P.dma("sp", io["dbg_hb"], hb[:], reads=[rhb], chan="dbg")
            P.dma("sp", io["dbg_ss"], ss[:], reads=[rss], chan="dbg")
            P.dma("sp", io["dbg_hT"], hT[:].rearrange("p k t -> p (k t)"), reads=[rhT], chan="dbg")
        rq, rrq = rqR.next()
        zv = zs[:, 0:1536].rearrange("p (h two d) -> p h two d", h=24, two=2)
        rqv = rq[:].rearrange("p h (two d) -> p h two d", two=2)
        cb = C.cos[:, t, 0:32].unsqueeze(1).broadcast_to([128, 12, 32])
        sb_ = C.sin[:, t, 0:32].unsqueeze(1).broadcast_to([128, 12, 32])
        zr = rzc[0:3]
        for hh in range(2):
            hs = slice(hh * 12, hh * 12 + 12)
            x1, x2 = zv[:, hs, 0, :], zv[:, hs, 1, :]
            o1, o2 = rqv[:, hs, 0, :], rqv[:, hs, 1, :]
            P.op("dve", lambda e, x1=x1, cb=cb: e.tensor_tensor(out=tmpA[:], in0=x1, in1=cb, op=ALU.mult), reads=zr + [C.rcs], writes=[rtA])
            P.op("dve", lambda e, x2=x2, sb_=sb_: e.tensor_tensor(out=tmpB[:], in0=x2, in1=sb_, op=ALU.mult), reads=zr + [C.rcs], writes=[rtA])
            P.op("dve", lambda e, o1=o1: e.tensor_tensor(out=o1, in0=tmpA[:], in1=tmpB[:], op=ALU.subtract), reads=[rtA], writes=[rrq])
            P.op("pool", lambda e, x2=x2, cb=cb: e.tensor_tensor(out=tmpC[:], in0=x2, in1=cb, op=ALU.mult), reads=zr + [C.rcs], writes=[rtC])
            P.op("pool", lambda e, x1=x1, sb_=sb_: e.tensor_tensor(out=tmpD[:], in0=x1, in1=sb_, op=ALU.mult), reads=zr + [C.rcs], writes=[rtC])
            P.op("pool", lambda e, o2=o2: e.tensor_tensor(out=o2, in0=tmpC[:], in1=tmpD[:], op=ALU.add), reads=[rtC], writes=[rrq])
        if t == 0 and "dbg_rq" in io:
            P.dma("sp", io["dbg_rq"], rq[:].rearrange("p h d -> p (h d)"), reads=[rrq], chan="dbg")
            P.dma("sp", io["dbg_cs"], C.cos[:, 0, :], reads=[C.rcs], chan="dbg")
            P.dma("sp", io["dbg_sn"], C.sin[:, 0, :], reads=[C.rcs], chan="dbg")
        cp(P, "pool", rq[:, 24:26, :].rearrange("p h d -> p (h d)"), zs[:, 1536:1664], [rzc[3]], [rrq])
        rqf = rq[:].rearrange("p h d -> p (h d)")
        for half, (cs_, ce_) in enumerate(((0, 8), (8, 13))):
            pt2, rpt2 = ptR.next()
            for c in range(cs_, ce_):
                P.op("pe", (lambda e, c=c, pt2=pt2, cs_=cs_, rqf=rqf: e.transpose(out=pt2[:, (c - cs_) * 128:(c - cs_ + 1) * 128], in_=rqf[:, c * 128:(c + 1) * 128], identity=C.ident[:])),
                     reads=[rrq, C.rid], writes=[rpt2])
            nn = ce_ - cs_
            cp(P, "act" if half == 0 else "dve", stq[:, cs_:cs_ + nn, tt * 128:(tt + 1) * 128],
               pt2[:, 0:nn * 128].rearrange("p (c t) -> p c t", c=nn), [rpt2], [rstq])
        P.op("pool", lambda e, zs=zs, tt=tt: e.tensor_copy(out=stv[:, tt, :, :].rearrange("p s d -> p (s d)"), in_=zs[:, 1536:2048]), reads=[rzc[3]], writes=[rstv])
        P.op("act", lambda e, zs=zs, tt=tt: e.activation(out=stg[:, tt, :], in_=zs[:, 2592:2616], func=AF.Sigmoid), reads=[rzc[5]], writes=[rstg])
        cn, rcn = cnR.next()
        ss2, rss2 = ssR.next()
        rmsnorm_tile(P, C, zs[:, 2048:2304], rzc[4], 256, g_q[:], rg, cn[:, 0:256], rcn, (junk, ss2, rss2))
        ss3, rss3 = ssR.next()
        rmsnorm_tile(P, C, zs[:, 2304:2560], rzc[4], 256, g_kv[:], rg, cn[:, 256:512], rcn, (junk, ss3, rss3))
        pt3, rpt3 = ptR.next()
        transpose_chunks(P, C, cn, rcn, 4, pt3, rpt3)
        cnT, rcnT = cnTR.next()
        P.op("dve", lambda e, cnT=cnT, pt3=pt3: e.tensor_copy(out=cnT[:].rearrange("p k t -> p (k t)"), in_=pt3[:, 0:512]), reads=[rpt3], writes=[rcnT])
        qf, rqf_ = qfR.next()
        kf, rkf = kfR.next()
        for (c0, n) in ((0, 512), (512, 256)):
            pz, rpz = pzR.next()
            for k in range(2):
                P.op("pe", (lambda e, pz=pz, cnT=cnT, k=k, c0=c0, n=n: e.matmul(pz[:, 0:n], lhsT=cnT[:, k, :], rhs=w_qu[:, k, c0:c0 + n], start=(k == 0), stop=(k == 1))),
                     reads=[rcnT, rwm], writes=[rpz])
            P.op("act", (lambda e, pz=pz, c0=c0, n=n: e.copy(out=qsb[:, c0:c0 + n], in_=pz[:, 0:n])), reads=[rpz], writes=[rqsb])
        qv = qsb[:].rearrange("p (h d) -> p h d", h=8)
        P.op("pool", lambda e, qf=qf, qv=qv: e.tensor_copy(out=qf[:, :, 0:64], in_=qv[:, :, 0:64]), reads=[rqsb], writes=[rqf_])
        c32 = C.cos[:, t, 32:48].unsqueeze(1).broadcast_to([128, 8, 16])
        s32 = C.sin[:, t, 32:48].unsqueeze(1).broadcast_to([128, 8, 16])
        qx1, qx2 = qv[:, :, 64:80], qv[:, :, 80:96]
        P.op("dve", lambda e, qx1=qx1, c32=c32: e.tensor_tensor(out=t16[0][:], in0=qx1, in1=c32, op=ALU.mult), reads=[rqsb, C.rcs], writes=[rt16])
        P.op("dve", lambda e, qx2=qx2, s32=s32: e.tensor_tensor(out=t16[1][:], in0=qx2, in1=s32, op=ALU.mult), reads=[rqsb, C.rcs], writes=[rt16])
        P.op("dve", lambda e, qx2=qx2, c32=c32: e.tensor_tensor(out=t16[2][:], in0=qx2, in1=c32, op=ALU.mult), reads=[rqsb, C.rcs], writes=[rt16])
        P.op("dve", lambda e, qx1=qx1, s32=s32: e.tensor_tensor(out=t16[3][:], in0=qx1, in1=s32, op=ALU.mult), reads=[rqsb, C.rcs], writes=[rt16])
        P.op("dve", lambda e, qf=qf: e.tensor_tensor(out=qf[:, :, 64:80], in0=t16[0][:], in1=t16[1][:], op=ALU.subtract), reads=[rt16], writes=[rqf_])
        P.op("dve", lambda e, qf=qf: e.tensor_tensor(out=qf[:, :, 80:96], in0=t16[2][:], in1=t16[3][:], op=ALU.add), reads=[rt16], writes=[rqf_])
        kx1, kx2 = zs[:, 2560:2576], zs[:, 2576:2592]
        c16, s16 = C.cos[:, t, 32:48], C.sin[:, t, 32:48]
        P.op("pool", lambda e, kx1=kx1, c16=c16: e.tensor_tensor(out=ktmp[:, 0, :], in0=kx1, in1=c16, op=ALU.mult), reads=[rzc[5], C.rcs], writes=[rkpe])
        P.op("pool", lambda e, kx2=kx2, s16=s16: e.tensor_tensor(out=ktmp[:, 1, :], in0=kx2, in1=s16, op=ALU.mult), reads=[rzc[5], C.rcs], writes=[rkpe])
        P.op("pool", lambda e, kx2=kx2, c16=c16: e.tensor_tensor(out=ktmp[:, 2, :], in0=kx2, in1=c16, op=ALU.mult), reads=[rzc[5], C.rcs], writes=[rkpe])
        P.op("pool", lambda e, kx1=kx1, s16=s16: e.tensor_tensor(out=ktmp[:, 3, :], in0=kx1, in1=s16, op=ALU.mult), reads=[rzc[5], C.rcs], writes=[rkpe])
        P.op("pool", lambda e: e.tensor_tensor(out=kpe[:, 0:16], in0=ktmp[:, 0, :], in1=ktmp[:, 1, :], op=ALU.subtract), reads=[rkpe], writes=[rkpe])
        P.op("pool", lambda e: e.tensor_tensor(out=kpe[:, 16:32], in0=ktmp[:, 2, :], in1=ktmp[:, 3, :], op=ALU.add), reads=[rkpe], writes=[rkpe])
        P.op("pool", lambda e, kf=kf: e.tensor_copy(out=kf[:, :, 64:96], in_=kpe[:].unsqueeze(1).broadcast_to([128, 8, 32])), reads=[rkpe], writes=[rkf])
        for ci, c0 in enumerate((0, 512)):
            pz, rpz = pzR.next()
            for k in range(2):
                P.op("pe", (lambda e, pz=pz, cnT=cnT, k=k, c0=c0: e.matmul(pz[:, 0:512], lhsT=cnT[:, 2 + k, :], rhs=w_kvu[:, k, c0:c0 + 512], start=(k == 0), stop=(k == 1))),
                     reads=[rcnT, rwm], writes=[rpz])
            pv = pz[:, 0:512].rearrange("p (h d) -> p h d", h=4)
            P.op("act", (lambda e, pv=pv, kf=kf, ci=ci: e.copy(out=kf[:, ci * 4:(ci + 1) * 4, 0:64], in_=pv[:, :, 0:64])), reads=[rpz], writes=[rkf])
            P.op("dve", (lambda e, pv=pv, ci=ci, tt=tt: e.tensor_copy(out=stvc[:, tt, ci * 4:(ci + 1) * 4, :], in_=pv[:, :, 64:128])), reads=[rpz], writes=[rstvc])
        for src, rsrc, dst, rdst, eng in ((qf, rqf_, stqc, rstqc, "act"), (kf, rkf, stkc, rstkc, "dve")):
            pt4, rpt4 = ptR.next()
            for h in range(8):
                P.op("pe", (lambda e, h=h, pt4=pt4, src=src: e.transpose(out=pt4[0:96, h * 128:(h + 1) * 128], in_=src[:, h, :], identity=C.ident[:])),
                     reads=[rsrc, C.rid], writes=[rpt4])
            if eng == "act":
                P.op("act", (lambda e, pt4=pt4, dst=dst, tt=tt: e.copy(out=dst[0:96, :, tt * 128:(tt + 1) * 128], in_=pt4[0:96, :].rearrange("p (h t) -> p h t", h=8))), reads=[rpt4], writes=[rdst])
            else:
                P.op("dve", (lambda e, pt4=pt4, dst=dst, tt=tt: e.tensor_copy(out=dst[0:96, :, tt * 128:(tt + 1) * 128], in_=pt4[0:96, :].rearrange("p (h t) -> p h t", h=8))), reads=[rpt4], writes=[rdst])

        if tt == 3:
            sl = slice(s0, s0 + 512)
            for c in range(4):
                P.dma("sp", io["xqa"][c].rearrange("h d t -> (h d) t")[:, sl], stq[:, c, :], reads=[rstq])
                P.dma("sp", io["xqg"][c ^ 1].rearrange("h d t -> (h d) t")[:, sl], stq[:, c, :], reads=[rstq])
                P.dma("sp", io["xqb"][c].rearrange("h d t -> (h d) t")[:, sl], stq[:, 7 + c, :], reads=[rstq])
            for g in range(2):
                for dest in (2 * g, 2 * g + 1):
                    for ty, ch in enumerate((4, 5, 6, 12)):
                        P.dma("sp", io["xka"][dest, ty][:, sl], stq[g * 64:(g + 1) * 64, ch, :], reads=[rstq])
                    P.dma("sp", io["xkb"][dest][:, sl], stq[g * 64:(g + 1) * 64, 11, :], reads=[rstq])
                    for ty in range(2):
                        P.dma("sp", io["xva"][dest, ty][sl, :].rearrange("(tt p) d -> p tt d", p=128), stv[:, :, (ty + 1) * 2 + g, :], reads=[rstv])
                    P.dma("sp", io["xvb"][dest][sl, :].rearrange("(tt p) d -> p tt d", p=128), stv[:, :, 6 + g, :], reads=[rstv])
            for dest in range(4):
                P.dma("sp", io["xqc"][dest].rearrange("h d t -> d h t")[:, :, sl], stqc[0:96, 2 * dest:2 * dest + 2, :], reads=[rstqc], chan="x3")
                P.dma("sp", io["xkc"][dest].rearrange("h d t -> d h t")[:, :, sl], stkc[0:96, 2 * dest:2 * dest + 2, :], reads=[rstkc], chan="x3")
                for hh in range(2):
                    P.dma("sp", io["xvc"][dest, hh][sl, :].rearrange("(tt p) d -> p tt d", p=128), stvc[:, :, 2 * dest + hh, :], reads=[rstvc], chan="x4")
                P.dma("sp", io["xg"][dest][sl, :].rearrange("(tt p) c -> p tt c", p=128), stg[:, :, 6 * dest:6 * dest + 6], reads=[rstg], chan="x4")


P2_IN = {
    "qa": ([4, 2, 64, NTOK], BF16), "qg": ([4, 2, 64, NTOK], BF16), "ka": ([4, 4, 64, NTOK], BF16),
    "va": ([4, 2, NTOK, 64], BF16), "qb": ([4, 2, 64, NTOK], BF16), "kb": ([4, 64, NTOK], BF16),
    "vb": ([4, NTOK, 64], BF16), "qc": ([4, 2, 96, NTOK], BF16), "kc": ([4, 2, 96, NTOK], BF16),
    "vc": ([4, 2, NTOK, 64], BF16), "g": ([4, NTOK, 6], F32),
}
P2_W = {"posk": [128, 16], "w1k": [2048, 256], "w2k": [256, 64], "posv": [128, 16], "w1v": [2048, 256],
        "w2v": [256, 64], "sinks": [1, 2], "selmap": [128, 4, 128]}


def selmap_const():
    n_cmp = 511
    tok = np.arange(n_cmp)[:, None] * 16 + np.arange(32)[None, :]
    sm = np.zeros((512, 128), np.float32)
    np.add.at(sm, (np.repeat(np.arange(n_cmp), 32), (tok // 64).reshape(-1)), 1.0 / 32)
    return np.ascontiguousarray(sm.reshape(4, 128, 128).transpose(1, 0, 2))


class AttnCtx:
    def __init__(self, P, ident, rid):
        self.P = P
        self.ident = ident
        self.rid = rid
        self.S = Rot(P, 4, [128, 512], F32, psum=True)
        self.pT = Rot(P, 3, [128, 512], BF16)
        self.acc = Rot(P, 2, [128, 4, 128], F32, psum=True)


def attn_qgroup(P, A, kT, rkT, Vt, rV, nv, qT, rqT, kbs, scale, accv, racc, look=1):
    cover = {qb: [i for i, e in enumerate(kbs) if e[1] <= qb <= e[2]] for qb in range(4)}
    n_kb = len(kbs)
    tiles = [None] * n_kb

    def scores(i):
        kb, lo, hi, segs = kbs[i]
        ps, rps = A.S.next()
        for (q0, q1, extra) in segs:
            c0, c1 = q0 * 128, (q1 + 1) * 128
            n = len(extra)
            MM(P, ps[:, c0:c1], kT[:, kb * 128:(kb + 1) * 128], qT[:, c0:c1], True, n == 0, [rkT, rqT], [rps])
            for j, (l_, r_, rd) in enumerate(extra):
                MM(P, ps[:, c0:c1], l_, r_, False, j == n - 1, rd, [rps])
        tiles[i] = (ps, rps)

    def rest(i):
        kb, lo, hi, segs = kbs[i]
        ps, rps = tiles[i]
        pT, rpT = A.pT.next()
        c0, c1 = lo * 128, (hi + 1) * 128
        ACT(P, pT[:, c0:c1], ps[:, c0:c1], AF.Exp, [rps], [rpT], scale=scale)
        for qb in range(lo, hi + 1):
            MM(P, accv(qb), pT[:, qb * 128:(qb + 1) * 128], Vt(kb), i == 0 and qb == lo, cover[qb][-1] == i, [rpT, rV], [racc], skip=True)

    LOOK = look
    for i in range(n_kb + LOOK):
        if i < n_kb:
            scores(i)
        if i - LOOK >= 0:
            rest(i - LOOK)


def emit_p2(P, io, ident, rid):
    nc = P.nc
    deps = io.get("deps", {"nsa": [], "swa": [], "mla": []})
    dA, dB, dC = deps["nsa"], deps["swa"], deps["mla"]
    A = AttnCtx(P, ident, rid)
    rc = P.res()
    zero_bf = P.sb([128, 512], BF16)
    ones_bf = P.sb([128, 128], BF16)
    MEMSET(P, "pool", zero_bf[:], 0.0, [], [rc])
    MEMSET(P, "pool", ones_bf[:], 1.0, [], [rc])
    pen_diag = P.sb([128, 128], BF16)
    pen_far = P.sb([128, 128], BF16)
    P.op("pool", lambda e: e.affine_select(out=pen_diag[:], in_=zero_bf[:, 0:128], pattern=[[1, 128]], compare_op=ALU.is_ge, fill=P.freg(e, NEG), base=0, channel_multiplier=-1), reads=[rc], writes=[rc])
    P.op("pool", lambda e: e.affine_select(out=pen_far[:], in_=zero_bf[:, 0:128], pattern=[[-1, 128]], compare_op=ALU.is_gt, fill=P.freg(e, NEG), base=0, channel_multiplier=1), reads=[rc], writes=[rc])
    identf32 = P.sb([128, 128], F32)
    MEMSET(P, "pool", identf32[:], 1.0, [], [rc])
    P.op("pool", lambda e: e.affine_select(out=identf32[:], in_=identf32[:], pattern=[[-1, 128]], compare_op=ALU.is_equal, fill=P.freg(e, 0.0), base=0, channel_multiplier=1), reads=[rc], writes=[rc])
    E = P.sb([128, 64, 128], BF16)
    for j in range(64):
        P.op("pool", (lambda e, j=j: e.affine_select(out=E[:, j, :].rearrange("p (a b) -> p a b", a=2), in_=ones_bf[:].rearrange("p (a b) -> p a b", a=2),
                                                     pattern=[[-1, 2], [0, 64]], compare_op=ALU.is_equal, fill=P.freg(e, 0.0), base=-2 * j, channel_multiplier=1)), reads=[rc], writes=[rc])
    vcmp = P.sb([128, 4, 200], BF16); rvcmp = P.res()
    MEMSET(P, "pool", vcmp[:], 0.0, [], [rvcmp])
    MEMSET(P, "pool", vcmp[:, :, 64:65], 1.0, [], [rvcmp])
    P.dma("pool", vcmp[:, :, 65:193], io["selmap"], writes=[rvcmp])
    kcmpT = P.sb([64, 512], BF16); rkcmp = P.res()
    MEMSET(P, "pool", kcmpT[:], 0.0, [], [rkcmp])
    esink = P.sb([128, 2], F32); resink = P.res()
    P.dma("sp", esink[:], io["sinks"].partition_broadcast(128), writes=[resink])
    ACT(P, esink[:], esink[:], AF.Exp, [resink], [resink])
    mark = nc.sbuf_base
    kT2 = P.sb([128, S], BF16); rkT2 = P.res()
    w1 = P.sb([128, 16, 256], BF16); w2 = P.sb([128, 2, 64], BF16); posT = P.sb([128, 16], BF16); rwc = P.res()
    gT = P.sb([128, 2, 512], BF16); rgT = P.res()
    cb = P.sb([128, 2], F32); rcb = P.res()
    for which, ty in (("k", 0), ("v", 3)):
        for s in range(4):
            P.dma("sp", kT2[0:64, s * NTOK:(s + 1) * NTOK], io["ka"][s, ty], writes=[rkT2], reads=list(dA))
            P.dma("sp", kT2[64:128, s * NTOK:(s + 1) * NTOK - 1], io["ka"][s, ty][:, 1:NTOK], writes=[rkT2], reads=list(dA))
            if s < 3:
                P.dma("sp", kT2[64:128, (s + 1) * NTOK - 1:(s + 1) * NTOK], io["ka"][s + 1, ty][:, 0:1], writes=[rkT2], reads=list(dA), allow_slow_non_contiguous=True)
        load_w_bf16(P, w1, rwc, io["w1" + which], 2048, 256, None)
        load_w_bf16(P, w2, rwc, io["w2" + which], 256, 64, None)
        P.dma("pool", posT[:], io["pos" + which], writes=[rwc])
        kviews = [kT2[:, b0:b0 + 8176].rearrange("p (n s) -> p n s", s=16) for b0 in (0, 16)]
        for hc in range(2):
            ps, rps = A.S.next()
            for lp in range(16):
                MM(P, ps[:, 0:511], w1[:, lp, hc * 128:(hc + 1) * 128], kviews[(2 * lp) // 16][:, :, (2 * lp) % 16], lp == 0, lp == 15, [rwc, rkT2], [rps])
            pb, rpb = A.acc.next()
            for lp in range(16):
                MM(P, pb[:, 0, 0:1], w1[:, lp, hc * 128:(hc + 1) * 128], posT[:, lp:lp + 1], lp == 0, lp == 15, [rwc], [rpb])
            cp(P, "dve", cb[:, hc:hc + 1], pb[:, 0, 0:1], [rpb], [rcb])
            ACT(P, gT[:, hc, 0:511], ps[:, 0:511], AF.Gelu_apprx_tanh, [rps, rcb], [rgT], bias=cb[:, hc:hc + 1])
        if which == "k":
            ps, rps = A.S.next()
            for hc in range(2):
                MM(P, ps[0:64, 0:511], w2[:, hc, :], gT[:, hc, 0:511], hc == 0, hc == 1, [rwc, rgT], [rps])
            cp(P, "dve", kcmpT[:, 0:511], ps[0:64, 0:511], [rps], [rkcmp])
        else:
            for c in range(4):
                nn = 128 if c < 3 else 127
                ps, rps = A.S.next()
                for hc in range(2):
                    MM(P, ps[0:nn, 0:64], gT[:, hc, c * 128:c * 128 + nn], w2[:, hc, :], hc == 0, hc == 1, [rwc, rgT], [rps])
                cp(P, "dve", vcmp[0:nn, c, 0:64], ps[0:nn, 0:64], [rps], [rvcmp])
    P.barrier()
    nc.sbuf_base = mark

    kTa = P.sb([128, S], BF16); rkTa = P.res()
    kTb = P.sb([128, S], BF16); rkTb = P.res()
    Va = P.sb([128, 64, 65], BF16); rVa = P.res()
    Vb = P.sb([128, 64, 65], BF16); rVb = P.res()
    MEMSET(P, "pool", Va[:, :, 64:65], 1.0, [], [rVa])
    MEMSET(P, "pool", Vb[:, :, 64:65], 1.0, [], [rVb])

    def load_kT(dst, rdst, src_fn, dk, dep=()):
        for s in range(4):
            P.dma("sp", dst[0:dk, s * NTOK:(s + 1) * NTOK], src_fn(s), writes=[rdst], reads=list(dep))

    def load_V(dst, rdst, src_fn, dep=()):
        for s in range(4):
            P.dma("sp", dst[:, s * 16:(s + 1) * 16, 0:64], src_fn(s).rearrange("(blk p) d -> p blk d", p=128), writes=[rdst], reads=list(dep))

    qR = Rot(P, 2, [128, 4, 512], BF16)
    gR = Rot(P, 2, [128, 4, 6], F32)
    ostR = Rot(P, 2, [128, 4, 128], BF16)
    oacc = P.sb([128, 2, 4, 64], F32); roacc = [P.res(), P.res()]
    imp = P.sb([128, 4, 128], F32); rimp = P.res()
    rcp = P.sb([128, 8], F32); rrcp = P.res()
    fac = P.sb([128, 8], F32)
    tmpo = P.sb([128, 4, 64], F32); rtmpo = P.res()
    penR = Rot(P, 2, [128, 512], BF16)
    biasR = Rot(P, 2, [128, 128], F32)
    val = P.sb([128, 128], F32); rval = P.res()
    wk = P.sb([128, 128], F32)
    m16 = P.sb([128, 16], F32)
    penq = P.sb([128, 128], F32); rpenq = P.res()
    penT = P.sb([128, 512], BF16); rpenT = P.res()
    cmpacc = [P.ps([128, 2, 256], F32), P.ps([128, 2, 256], F32)]; rcmpacc = P.res()

    def out_dma(ost, rost, Gq, col0, ncol):
        src, off = Gq // 4, (Gq % 4) * 512
        for dest in range(1):
            pass
        d = Gq // 4
        if "o_mix" in io:
            m, c0 = col0 // 128, col0 % 128
            P.dma("sp", io["o_mix"](m)[d][off:off + 512, c0:c0 + ncol].rearrange("(qb p) c -> p qb c", p=128), ost[:, :, 0:ncol],
                  reads=[rost], writes=[io["ro"][m]])
        else:
            P.dma("sp", io["o"][d][off:off + 512, col0:col0 + ncol].rearrange("(qb p) c -> p qb c", p=128), ost[:, :, 0:ncol], reads=[rost])

    def finish_branch(accv_t, racc_, h, gcol, g, rg, first, extra_den=None):
        if extra_den is None:
            P.op("dve", lambda e: e.reciprocal(out=rcp[:, 0:4], in_=accv_t[:, :, 64]), reads=[racc_], writes=[rrcp])
        else:
            TS(P, "dve", rcp[:, 4:8], accv_t[:, :, 64], extra_den, None, ALU.add, None, [racc_, resink], [rrcp])
            P.op("dve", lambda e: e.reciprocal(out=rcp[:, 0:4], in_=rcp[:, 4:8]), reads=[rrcp], writes=[rrcp])
        if gcol is not None:
            TT(P, "dve", fac[:, 0:4], rcp[:, 0:4], g[:, :, gcol], ALU.mult, [rrcp, rg], [rrcp])
            f = fac[:, 0:4]
        else:
            f = rcp[:, 0:4]
        fb = f.unsqueeze(2).broadcast_to([128, 4, 64])
        if first:
            TT(P, "dve", oacc[:, h, :, :], accv_t[:, :, 0:64], fb, ALU.mult, [racc_, rrcp], [roacc[h]])
        else:
            TT(P, "dve", tmpo[:], accv_t[:, :, 0:64], fb, ALU.mult, [racc_, rrcp], [rtmpo])
            TT(P, "dve", oacc[:, h, :, :], oacc[:, h, :, :], tmpo[:], ALU.add, [rtmpo], [roacc[h]])

    load_kT(kTa, rkTa, lambda s: io["ka"][s, 1], 64, dA)
    load_kT(kTb, rkTb, lambda s: io["ka"][s, 2], 64, dA)
    load_V(Va, rVa, lambda s: io["va"][s, 0], dA)
    load_V(Vb, rVb, lambda s: io["va"][s, 1], dA)
    for Gq in range(16):
        src, off = Gq // 4, (Gq % 4) * 512
        q4, rq4 = qR.next()
        P.dma("sp", q4[0:64, 0:2, :], io["qa"][src].rearrange("h d t -> d h t")[:, :, off:off + 512], writes=[rq4], reads=list(dA))
        P.dma("sp", q4[0:64, 2:4, :], io["qg"][src].rearrange("h d t -> d h t")[:, :, off:off + 512], writes=[rq4], reads=list(dA))
        g, rg = gR.next()
        P.dma("sp", g[:], io["g"][src][off:off + 512, :].rearrange("(qb p) c -> p qb c", p=128), writes=[rg], reads=list(dA))
        cmax = (32 * Gq + 30) // 128
        pens = {}
        for c in range(cmax + 1):
            if Gq >= 4 * c + 5:
                continue
            pn, rpn = penR.next()
            P.op("pool", (lambda e, pn=pn, c=c, Gq=Gq: e.affine_select(out=pn[:], in_=zero_bf[:], pattern=[[1, 512]], compare_op=ALU.is_ge, fill=P.freg(e, NEG),
                                                                      base=512 * Gq - 2048 * c - 31, channel_multiplier=-16)), reads=[rc], writes=[rpn])
            pens[c] = (pn, rpn)
        for r4 in range(4):
            ctiles = {}

            def cscores(c, r4=r4):
                ps, rps = A.S.next()
                if c in pens:
                    MM(P, ps[:, :], kcmpT[:, c * 128:(c + 1) * 128], q4[0:64, r4, :], True, False, [rkcmp, rq4], [rps])
                    MM(P, ps[:, :], ident[:], pens[c][0][:], False, True, [rid, pens[c][1]], [rps])
                else:
                    MM(P, ps[:, :], kcmpT[:, c * 128:(c + 1) * 128], q4[0:64, r4, :], True, True, [rkcmp, rq4], [rps])
                ctiles[c] = (ps, rps)

            def crest(c):
                ps, rps = ctiles[c]
                pT, rpT = A.pT.next()
                ACT(P, pT[:, :], ps[:, :], AF.Exp, [rps], [rpT], scale=0.125)
                for qb in range(4):
                    MM(P, cmpacc[qb // 2][:, qb % 2, 0:193], pT[:, qb * 128:(qb + 1) * 128], vcmp[:, c, 0:193], c == 0 and qb % 2 == 0, c == cmax, [rpT, rvcmp], [rcmpacc], skip=True)

            for c in range(cmax + 2):
                if c <= cmax:
                    cscores(c)
                if c >= 1:
                    crest(c - 1)
            for half in range(2):
                TS(P, "dve", rcp[:, 4 + 2 * half:6 + 2 * half], cmpacc[half][:, :, 64], 1e-30, None, ALU.max, None, [rcmpacc], [rrcp])
            P.op("dve", lambda e: e.reciprocal(out=rcp[:, 0:4], in_=rcp[:, 4:8]), reads=[rrcp], writes=[rrcp])
            for qb in range(4):
                src_imp = cmpacc[qb // 2][:, qb % 2, 65:193]
                if r4 == 0:
                    TS(P, "dve", imp[:, qb, :], src_imp, rcp[:, qb:qb + 1], None, ALU.mult, None, [rcmpacc, rrcp], [rimp])
                else:
                    STT(P, imp[:, qb, :], src_imp, rcp[:, qb:qb + 1], imp[:, qb, :], ALU.mult, ALU.add, [rcmpacc, rrcp], [rimp])
            if r4 < 2:
                TT(P, "dve", fac[:, 0:4], rcp[:, 0:4], g[:, :, 3 * r4 + 0], ALU.mult, [rrcp, rg], [rrcp])
                for half in range(2):
                    fb = fac[:, 2 * half:2 * half + 2].unsqueeze(2).broadcast_to([128, 2, 64])
                    TT(P, "dve", oacc[:, r4, 2 * half:2 * half + 2, :], cmpacc[half][:, :, 0:64], fb, ALU.mult, [rcmpacc, rrcp], [roacc[r4]])
        trp, rtrp = A.S.next()
        for qb in range(4):
            j = 4 * Gq + qb
            bt, rbt = biasR.next()
            MEMSET(P, "pool", bt[:], 0.0, [], [rbt])
            MEMSET(P, "pool", bt[:, 0:1], 1e4, [], [rbt])
            if j >= 1:
                MEMSET(P, "pool", bt[0:64, 2 * j - 1:2 * j + 1], 1e4, [], [rbt])
            MEMSET(P, "pool", bt[64:128, 2 * j:2 * j + 2], 1e4, [], [rbt])
            if 2 * j + 1 < 128:
                MEMSET(P, "pool", bt[0:64, 2 * j + 1:128], -1e30, [], [rbt])
            if 2 * j + 2 < 128:
                MEMSET(P, "pool", bt[64:128, 2 * j + 2:128], -1e30, [], [rbt])
            TT(P, "dve", val[:], imp[:, qb, :], bt[:], ALU.add, [rimp, rbt], [rval])
            P.op("dve", lambda e: e.max(out=m16[:, 0:8], in_=val[:]), reads=[rval], writes=[rval])
            P.op("dve", lambda e: e.match_replace(out=wk[:], in_to_replace=m16[:, 0:8], in_values=val[:], imm_value=-3e38), reads=[rval], writes=[rval])
            P.op("dve", lambda e: e.max(out=m16[:, 8:16], in_=wk[:]), reads=[rval], writes=[rval])
            TS(P, "dve", penq[:], val[:], m16[:, 15:16], NEG, ALU.is_lt, ALU.mult, [rval], [rpenq])
            TR(P, trp[:, qb * 128:(qb + 1) * 128], penq[:], identf32[:], [rpenq, rc], [rtrp])
        cp(P, "dve", penT[:], trp[:, :], [rtrp], [rpenT])
        for h in range(2):
            acc, racc = A.acc.next()
            kbs = []
            for kb in range(4 * Gq + 4):
                ex_sel = lambda q0, q1, kb=kb: (E[:, kb // 1, :], penT[:, q0 * 128:(q1 + 1) * 128], [rc, rpenT])
                if kb < 4 * Gq:
                    kbs.append((kb, 0, 3, [(0, 3, [ex_sel(0, 3)])]))
                else:
                    i = kb - 4 * Gq
                    segs = [(i, i, [ex_sel(i, i), (ident[:], pen_diag[:], [rid, rc])])]
                    if i < 3:
                        segs.append((i + 1, 3, [ex_sel(i + 1, 3)]))
                    kbs.append((kb, i, 3, segs))
            attn_qgroup(P, A, kTa[0:64, :], rkTa, lambda kb: Va[:, kb, :], rVa, 65, q4[0:64, h, :], rq4, kbs, 0.125, lambda qb, acc=acc: acc[:, qb, 0:65], racc, look=2)
            finish_branch(acc, racc, h, 3 * h + 1, g, rg, False)
            acc, racc = A.acc.next()
            kbs = []
            for i in range(8):
                kb = 4 * Gq - 4 + i
                if kb < 0:
                    continue
                lo, hi = max(0, i - 4), min(3, i)
                segs = []
                if i <= 3:
                    if lo < i:
                        segs.append((lo, i - 1, []))
                    segs.append((i, i, [(ident[:], pen_far[:], [rid, rc])]))
                else:
                    segs.append((i - 4, i - 4, [(ident[:], pen_diag[:], [rid, rc])]))
                    if i - 4 < hi:
                        segs.append((i - 3, hi, []))
                kbs.append((kb, lo, hi, segs))
            attn_qgroup(P, A, kTb[0:64, :], rkTb, lambda kb: Vb[:, kb, :], rVb, 65, q4[0:64, h, :], rq4, kbs, 0.125, lambda qb, acc=acc: acc[:, qb, 0:65], racc, look=2)
            finish_branch(acc, racc, h, 3 * h + 2, g, rg, False)
        ost, rost = ostR.next()
        cp(P, "act", ost[:, :, 0:128].rearrange("p q (h d) -> p h q d", h=2), oacc[:], roacc, [rost])
        out_dma(ost, rost, Gq, 0, 128)

    if "post_mix" in io:
        io["post_mix"](0)
    P.barrier()
    S5 = Rot.__new__(Rot)
    S5.t = list(A.S.t) + [cmpacc[0][:].rearrange("p a b -> p (a b)"), cmpacc[1][:].rearrange("p a b -> p (a b)")]
    S5.r = list(A.S.r) + [P.res(), P.res()]
    S5.i = -1
    S5.n = len(S5.t)
    A.S = S5
    if "pre_swa" in io:
        io["pre_swa"]()
    load_kT(kTa, rkTa, lambda s: io["kb"][s], 64, dB)
    load_V(Va, rVa, lambda s: io["vb"][s], dB)
    for Gq in range(16):
        src, off = Gq // 4, (Gq % 4) * 512
        q4, rq4 = qR.next()
        P.dma("sp", q4[0:64, 0:2, :], io["qb"][src].rearrange("h d t -> d h t")[:, :, off:off + 512], writes=[rq4], reads=list(dB))
        for h in range(2):
            acc, racc = A.acc.next()
            kbs = []
            for i in range(5):
                kb = 4 * Gq - 1 + i
                if kb < 0:
                    continue
                segs = []
                lo, hi = max(0, i - 1), min(3, i)
                if i <= 3:
                    segs.append((i, i, [(ident[:], pen_far[:], [rid, rc])]))
                if i >= 1:
                    segs.append((i - 1, i - 1, [(ident[:], pen_diag[:], [rid, rc])]))
                segs.sort()
                kbs.append((kb, lo, hi, segs))
            attn_qgroup(P, A, kTa[0:64, :], rkTa, lambda kb: Va[:, kb, :], rVa, 65, q4[0:64, h, :], rq4, kbs, 0.125, lambda qb, acc=acc: acc[:, qb, 0:65], racc, look=2)
            finish_branch(acc, racc, h, None, None, None, True, extra_den=esink[:, h:h + 1])
        ost, rost = ostR.next()
        cp(P, "act", ost[:, :, 0:128].rearrange("p q (h d) -> p h q d", h=2), oacc[:], roacc, [rost])
        out_dma(ost, rost, Gq, 128, 128)

    if "post_mix" in io:
        io["post_mix"](1)
    if "pre_mla" in io:
        io["pre_mla"]()
    for h in range(2):
        kT, rkT = (kTa, rkTa) if h == 0 else (kTb, rkTb)
        Vx, rVx = (Va, rVa) if h == 0 else (Vb, rVb)
        load_kT(kT, rkT, lambda s, h=h: io["kc"][s, h], 96, dC)
        load_V(Vx, rVx, lambda s, h=h: io["vc"][s, h], dC)
        for Gq in range(16):
            src, off = Gq // 4, (Gq % 4) * 512
            q4, rq4 = qR.next()
            P.dma("sp", q4[0:96, 0, :], io["qc"][src, h][:, off:off + 512], writes=[rq4], reads=list(dC))
            acc, racc = A.acc.next()
            kbs = []
            for kb in range(4 * Gq + 4):
                if kb < 4 * Gq:
                    kbs.append((kb, 0, 3, [(0, 3, [])]))
                else:
                    i = kb - 4 * Gq
                    segs = [(i, i, [(ident[:], pen_diag[:], [rid, rc])])]
                    if i < 3:
                        segs.append((i + 1, 3, []))
                    kbs.append((kb, i, 3, segs))
            attn_qgroup(P, A, kT[0:96, :], rkT, lambda kb, Vx=Vx: Vx[:, kb, :], rVx, 65, q4[0:96, 0, :], rq4, kbs, 96 ** -0.5, lambda qb, acc=acc: acc[:, qb, 0:65], racc, look=2)
            finish_branch(acc, racc, 0, None, None, None, True)
            ost, rost = ostR.next()
            cp(P, "act", ost[:, :, 0:64], oacc[:, 0, :, :], roacc, [rost])
            out_dma(ost, rost, Gq, 256 + 64 * h, 64)
    if "post_mix" in io:
        io["post_mix"](2)


def emit_p3(P, C, io, last):
    nc = P.nc
    base_mark = nc.sbuf_base
    pbase = nc.psum_base
    junk = P.sb([128, D], BF16)
    ssR = Rot(P, 2, [128, 4], F32)
    ptR = Rot(P, 2, [128, 1024], BF16, psum=True)
    pzR = Rot(P, 4, [128, 512], F32, psum=True)
    g_n = P.sb([128, D], F32); rgn = P.res()

    def norm_T(t, hb, rhb, dstT, col0, rdst, eng="act"):
        ss, rss = ssR.next()
        rmsnorm_tile(P, C, C.x[:, t, :], C.rx[t], D, g_n[:], rgn, hb[:], rhb, (junk, ss, rss))
        pt, rpt = ptR.next()
        transpose_chunks(P, C, hb, rhb, 8, pt, rpt)
        cp(P, eng, dstT[:, :, col0:col0 + 128], pt[:].rearrange("p (k t) -> p k t", k=8), [rpt], [rdst])

    markA = nc.sbuf_base
    load_bcast(P, g_n, rgn, io["mix_norm"], D, None)
    wbg = P.sb([128, 8, 3072], BF16); rwA = P.res(); rwAp = P.res(); rwAo = P.res()
    load_w_bf16(P, wbg, rwA, io["w_bg"], 1024, 3072, None)
    wp = P.sb([128, 12, 1024], BF16)
    for i, nm in enumerate(("w_pa", "w_pb", "w_pc")):
        for k in range(4):
            P.dma("pool", wp[:, 4 * i + k, :], io[nm][k * 128:(k + 1) * 128, :], writes=[rwAp])
    wo = P.sb([128, 8, 1024], BF16)
    load_w_bf16(P, wo, rwAo, io["w_out"], 1024, 1024, None)
    hbR = Rot(P, 1, [128, D], BF16)
    hTR = Rot(P, 2, [128, 8, 128], BF16)
    gsb = P.sb([128, 3072], F32); rgsb = P.res()
    otR = Rot(P, 1, [128, 4, 384], BF16)
    oTR = Rot(P, 1, [128, 12, 128], BF16)
    mrg = P.sb([128, D], F32); rmrg = P.res()
    tmpm = P.sb([128, 512], F32); rtmpm = P.res()
    mbR = Rot(P, 1, [128, D], BF16)
    mTR = Rot(P, 1, [128, 8, 128], BF16)
    for t in range(NTILE):
        hb, rhb = hbR.next()
        hT, rhT = hTR.next()
        norm_T(t, hb, rhb, hT, 0, rhT)
        for c in range(6):
            pz, rpz = pzR.next()
            for k in range(8):
                MM(P, pz[:, :], hT[:, k, :], wbg[:, k, c * 512:(c + 1) * 512], k == 0, k == 7, [rhT, rwA], [rpz])
            ACT(P, gsb[:, c * 512:(c + 1) * 512], pz[:, :], AF.Sigmoid, [rpz], [rgsb])
        ot, rot = otR.next()
        if "o_tile3" in io:
            for m in range(3):
                P.dma("sp", ot[:, :, m * 128:(m + 1) * 128], io["o_tile3"](t, m).rearrange("s p c -> p s c"), writes=[rot], reads=list(io["o_dep"]))
        else:
            o_src = io["o_tile"](t) if "o_tile" in io else io["o"][:, t * 128:(t + 1) * 128, :]
            P.dma("sp", ot[:], o_src.rearrange("s p c -> p s c"), writes=[rot])
        oT, roT = oTR.next()
        for half, (a0, a1) in enumerate(((0, 8), (8, 12))):
            pt, rpt = ptR.next()
            for j in range(a0, a1):
                i, s = j // 4, j % 4
                TR(P, pt[:, (j - a0) * 128:(j - a0 + 1) * 128], ot[:, s, i * 128:(i + 1) * 128], C.ident[:], [rot, C.rid], [rpt])
            cp(P, "act" if half == 0 else "dve", oT[:, a0:a1, :], pt[:, 0:(a1 - a0) * 128].rearrange("p (k t) -> p k t", k=a1 - a0), [rpt], [roT])
        for c in range(2):
            cs = slice(c * 512, (c + 1) * 512)
            for i in range(3):
                pz, rpz = pzR.next()
                for k in range(4):
                    MM(P, pz[:, :], oT[:, 4 * i + k, :], wp[:, 4 * i + k, cs], k == 0, k == 3, [roT, rwAp], [rpz])
                gs = gsb[:, i * 1024 + c * 512:i * 1024 + (c + 1) * 512]
                if i == 0:
                    TT(P, "dve", mrg[:, cs], pz[:, :], gs, ALU.mult, [rpz, rgsb], [rmrg])
                else:
                    TT(P, "dve", tmpm[:], pz[:, :], gs, ALU.mult, [rpz, rgsb], [rtmpm])
                    TT(P, "dve", mrg[:, cs], mrg[:, cs], tmpm[:], ALU.add, [rtmpm], [rmrg])
        mb, rmb = mbR.next()
        cp(P, "act", mb[:], mrg[:], [rmrg], [rmb])
        pt, rpt = ptR.next()
        transpose_chunks(P, C, mb, rmb, 8, pt, rpt)
        mT, rmT = mTR.next()
        cp(P, "act", mT[:].rearrange("p k t -> p (k t)"), pt[:], [rpt], [rmT])
        for c in range(2):
            cs = slice(c * 512, (c + 1) * 512)
            pz, rpz = pzR.next()
            for k in range(8):
                MM(P, pz[:, :], mT[:, k, :], wo[:, k, cs], k == 0, k == 7, [rmT, rwAo], [rpz])
            TT(P, "dve", C.x[:, t, cs], C.x[:, t, cs], pz[:, :], ALU.add, [rpz], [C.rx[t]])
    P.barrier()
    nc.sbuf_base = markA

    load_bcast(P, g_n, rgn, io["ffn_norm"], D, None)
    h2T = P.sb([128, 8, NTOK], BF16); rh2T = [P.res() for _ in range(NSUP)]
    hbR = Rot(P, 2, [128, D], BF16)
    for t in range(NTILE):
        hb, rhb = hbR.next()
        norm_T(t, hb, rhb, h2T, t * 128, rh2T[t // 4], eng="act" if t % 2 == 0 else "dve")
    GR = [(0, 6), (6, 6), (12, 5), (17, 5)]
    NFM = 6
    wsets = []
    for _ in range(2):
        wsets.append((P.sb([128, 8, NFM * 128], BF16), P.sb([128, 8, NFM * 128], BF16), P.sb([128, NFM, D], BF16), P.res()))
    actT = P.sb([128, NFM, 512], BF16); ractT = P.res()
    sgR = Rot(P, 2, [128, 512], F32)

    def load_group(gi, after=()):
        wg, wu, wd, rwB = wsets[gi % 2]
        c0, NF = GR[gi]
        f0 = c0 * 128
        dep = list(after)
        for k in range(8):
            P.dma("pool", wg[:, k, 0:NF * 128], io["w_fg"][k * 128:(k + 1) * 128, f0:f0 + NF * 128], writes=[rwB], reads=dep)
            P.dma("pool", wu[:, k, 0:NF * 128], io["w_fu"][k * 128:(k + 1) * 128, f0:f0 + NF * 128], writes=[rwB], reads=dep)
        for f in range(NF):
            P.dma("pool", wd[:, f, :], io["w_fd"][f0 + f * 128:f0 + (f + 1) * 128, :], writes=[rwB], reads=dep)

    load_group(0)
    for gi in range(4):
        wg, wu, wd, rwB = wsets[gi % 2]
        NF = GR[gi][1]
        for st in range(NSUP):
            if st == 1 and gi + 1 < 4:
                load_group(gi + 1, after=[rwB])
            ts_ = slice(st * 512, (st + 1) * 512)
            for f in range(NF):
                pg, rpg = pzR.next()
                for k in range(8):
                    MM(P, pg[:, :], wg[:, k, f * 128:(f + 1) * 128], h2T[:, k, ts_], k == 0, k == 7, [rwB, rh2T[st]], [rpg])
                pu, rpu = pzR.next()
                for k in range(8):
                    MM(P, pu[:, :], wu[:, k, f * 128:(f + 1) * 128], h2T[:, k, ts_], k == 0, k == 7, [rwB, rh2T[st]], [rpu])
                sg, rsg = sgR.next()
                ACT(P, sg[:], pg[:, :], AF.Silu, [rpg], [rsg])
                TT(P, "dve", actT[:, f, :], sg[:], pu[:, :], ALU.mult, [rsg, rpu], [ractT])
            for tt in range(4):
                t = st * 4 + tt
                for c in range(2):
                    cs = slice(c * 512, (c + 1) * 512)
                    pz, rpz = pzR.next()
                    for f in range(NF):
                        MM(P, pz[:, :], actT[:, f, tt * 128:(tt + 1) * 128], wd[:, f, cs], f == 0, f == NF - 1, [ractT, rwB], [rpz])
                    TT(P, "dve", C.x[:, t, cs], C.x[:, t, cs], pz[:, :], ALU.add, [rpz], [C.rx[t]])
    P.barrier()
    nc.sbuf_base = markA

    load_bcast(P, g_n, rgn, io["ple_norm"], D, None)
    wpg = P.sb([128, 8, D], BF16); wpp = P.sb([128, 2, D], BF16); rwC = P.res()
    load_w_bf16(P, wpg, rwC, io["w_pg"], 1024, 1024, None)
    load_w_bf16(P, wpp, rwC, io["w_pp"], 256, 1024, None)
    hbR = Rot(P, 2, [128, D], BF16)
    hTR = Rot(P, 2, [128, 8, 128], BF16)
    pfR = Rot(P, 2, [128, 256], BF16)
    pTR = Rot(P, 2, [128, 2, 128], BF16)
    sgR = Rot(P, 2, [128, 512], F32)
    tmpm = P.sb([128, 512], F32); rtmpm = P.res()
    if last:
        g_f = P.sb([128, D], F32); rgf = P.res()
        load_bcast(P, g_f, rgf, io["final_norm"], D, None)
        yR = Rot(P, 2, [128, D], F32)
    for t in range(NTILE):
        hb, rhb = hbR.next()
        hT, rhT = hTR.next()
        norm_T(t, hb, rhb, hT, 0, rhT)
        pf, rpf = pfR.next()
        P.dma("pool", pf[:], io["p"][t * 128:(t + 1) * 128, :], writes=[rpf])
        pt, rpt = ptR.next()
        transpose_chunks(P, C, pf, rpf, 2, pt, rpt)
        pT, rpT = pTR.next()
        cp(P, "dve", pT[:].rearrange("p k t -> p (k t)"), pt[:, 0:256], [rpt], [rpT])
        for c in range(2):
            cs = slice(c * 512, (c + 1) * 512)
            pz, rpz = pzR.next()
            for k in range(8):
                MM(P, pz[:, :], hT[:, k, :], wpg[:, k, cs], k == 0, k == 7, [rhT, rwC], [rpz])
            sg, rsg = sgR.next()
            ACT(P, sg[:], pz[:, :], AF.Sigmoid, [rpz], [rsg])
            pp, rpp = pzR.next()
            for k in range(2):
                MM(P, pp[:, :], pT[:, k, :], wpp[:, k, cs], k == 0, k == 1, [rpT, rwC], [rpp])
            TT(P, "dve", tmpm[:], sg[:], pp[:, :], ALU.mult, [rsg, rpp], [rtmpm])
            TT(P, "dve", C.x[:, t, cs], C.x[:, t, cs], tmpm[:], ALU.add, [rtmpm], [C.rx[t]])
        if last:
            ss, rss = ssR.next()
            y, ry = yR.next()
            MEMSET(P, "pool", ss[:, 0:1], 0.0, [], [rss])
            ACT(P, junk[:], C.x[:, t, :], AF.Square, [C.rx[t], rss], [rss], accum_out=ss[:, 0:1])
            TS(P, "dve", ss[:, 1:2], ss[:, 0:1], 1.0 / D, EPS, ALU.mult, ALU.add, [rss], [rss])
            ACT(P, ss[:, 2:3], ss[:, 1:2], AF.Sqrt, [rss], [rss])
            P.op("dve", lambda e, ss=ss: e.reciprocal(out=ss[:, 3:4], in_=ss[:, 2:3]), reads=[rss], writes=[rss])
            STT(P, y[:], C.x[:, t, :], ss[:, 3:4], g_f[:], ALU.mult, ALU.mult, [C.rx[t], rss, rgf], [ry])
            P.dma("sp", io["y"][t * 128:(t + 1) * 128, :], y[:], reads=[ry])
    P.barrier()
    nc.sbuf_base = base_mark
    nc.psum_base = pbase


P1_WNAMES = {"w_in": [D, 2616], "mix_norm": [1, D], "q_norm": [1, 256], "kv_norm": [1, 256], "w_q_up": [256, 768], "w_kv_up": [256, 1024]}
P3_WNAMES = {"mix_norm": [1, D], "w_bg": [D, 3072], "w_pa": [512, D], "w_pb": [512, D], "w_pc": [512, D], "w_out": [D, D],
             "ffn_norm": [1, D], "w_fg": [D, DFF], "w_fu": [D, DFF], "w_fd": [DFF, D], "ple_norm": [1, D], "w_pg": [D, D],
             "w_pp": [256, D], "p": [NTOK, 256]}


def build_tok_program(do_p3, do_p1, last):
    P = Prog()
    x_in = P.dram("x_in", [NTOK, D], F32, "ExternalInput").ap()
    pos_in = P.dram("pos_in", [128, NTILE], I32, "ExternalInput").ap()
    ident = P.dram("ident", [128, 128], F32, "ExternalInput").ap()
    invf = P.dram("invf", [128, 48], F32, "ExternalInput").ap()
    C = Common(P, x_in, pos_in, ident, invf)
    if do_p3:
        io = {}
        for k, shp in P3_WNAMES.items():
            io[k] = P.dram("p3_" + k, shp, F32, "ExternalInput").ap()
        io["o"] = P.dram("p3_o", [4, NTOK, 384], BF16, "ExternalInput").ap()
        if last:
            io["final_norm"] = P.dram("p3_final_norm", [1, D], F32, "ExternalInput").ap()
            io["y"] = P.dram("y", [NTOK, D], F32, "ExternalOutput").ap()
        emit_p3(P, C, io, last)
        if not last:
            x_out = P.dram("x_out", [NTOK, D], F32, "ExternalOutput").ap()
            for t in range(NTILE):
                P.dma("sp", x_out[t * 128:(t + 1) * 128, :], C.x[:, t, :], reads=[C.rx[t]])
    if do_p1:
        io = {}
        for k, shp in P1_WNAMES.items():
            io[k] = P.dram("p1_" + k, shp, F32, "ExternalInput").ap()
        for k, (shp, dt) in P1_X.items():
            io[k] = P.dram(k, [4] + shp, dt, "ExternalOutput").ap()
        emit_p1(P, C, io)
    return P.build()


def build_p2_program():
    P = Prog()
    io = {}
    for k, (shp, dt) in P2_IN.items():
        io[k] = P.dram(k, shp, dt, "ExternalInput").ap()
    for k, shp in P2_W.items():
        io[k] = P.dram(k, shp, F32, "ExternalInput").ap()
    identd = P.dram("ident", [128, 128], F32, "ExternalInput").ap()
    io["o"] = P.dram("o", [4, NTOK, 384], BF16, "ExternalOutput").ap()
    identf = P.sb([128, 128], F32)
    ident = P.sb([128, 128], BF16)
    rid = P.res()
    P.dma("sp", identf[:], identd, writes=[rid])
    cp(P, "dve", ident[:], identf[:], [rid], [rid])
    emit_p2(P, io, ident, rid)
    return P.build()


X1_TO_P2 = {"xqa": "qa", "xqg": "qg", "xka": "ka", "xva": "va", "xqb": "qb", "xkb": "kb", "xvb": "vb",
            "xqc": "qc", "xkc": "kc", "xvc": "vc", "xg": "g"}


def p1_weights(inp, l):
    m = {}
    m["p1_w_in"] = np.ascontiguousarray(inp["w_in"][l][:, W_IN_PERM])
    m["p1_mix_norm"] = np.ascontiguousarray(inp["mix_norm"][l][None, :])
    m["p1_q_norm"] = np.ascontiguousarray(inp["c_q_norm"][l][None, :])
    m["p1_kv_norm"] = np.ascontiguousarray(inp["c_kv_norm"][l][None, :])
    m["p1_w_q_up"] = np.ascontiguousarray(inp["c_w_q_up"][l])
    m["p1_w_kv_up"] = np.ascontiguousarray(inp["c_w_kv_up"][l])
    return m


def p2_weights(inp, l, r):
    m = {}
    for w in ("k", "v"):
        m["pos" + w] = np.ascontiguousarray(inp[f"a_cmp_pos_{w}"][l].reshape(16, 128).T)
        m["w1" + w] = np.ascontiguousarray(inp[f"a_cmp_w1_{w}"][l])
        m["w2" + w] = np.ascontiguousarray(inp[f"a_cmp_w2_{w}"][l])
    m["sinks"] = np.ascontiguousarray(inp["b_sinks"][l][2 * r:2 * r + 2][None, :])
    m["selmap"] = selmap_const()
    m["ident"] = np.eye(128, dtype=np.float32)
    return m


def p3_weights(inp, l, b, r, last):
    m = {}
    src = {"mix_norm": "mix_norm", "w_bg": "w_branch_gate", "w_pa": "w_branch_a", "w_pb": "w_branch_b", "w_pc": "w_branch_c",
           "w_out": "w_out", "ffn_norm": "ffn_norm", "w_fg": "w_ffn_gate", "w_fu": "w_ffn_up", "w_fd": "w_ffn_down",
           "ple_norm": "ple_norm", "w_pg": "w_ple_gate", "w_pp": "w_ple_proj"}
    for k, s in src.items():
        a = inp[s][l]
        m["p3_" + k] = np.ascontiguousarray(a[None, :] if a.ndim == 1 else a)
    m["p3_p"] = np.ascontiguousarray(inp["p"][l, b, r * NTOK:(r + 1) * NTOK])
    if last:
        m["p3_final_norm"] = np.ascontiguousarray(inp["final_norm"][None, :])
    return m


def all_to_all(outs, names):
    res = []
    for core in range(8):
        b, r = divmod(core, 4)
        res.append({nm: np.ascontiguousarray(np.stack([outs[4 * b + s][nm][r] for s in range(4)], axis=0)) for nm in names})
    return res


X1_LAYOUT = [("xqa", [2, 64, NTOK], 0, 0), ("xqg", [2, 64, NTOK], 0, 128), ("xka", [4, 64, NTOK], 1, 0),
             ("xva", [2, NTOK, 64], 2, 0), ("xqb", [2, 64, NTOK], 2, 128), ("xkb", [64, NTOK], 3, 0),
             ("xvb", [NTOK, 64], 3, 64), ("xqc", [2, 96, NTOK], 4, 0), ("xkc", [2, 96, NTOK], 5, 0),
             ("xvc", [2, NTOK, 64], 6, 0)]
X1_K = 7
X1_CR = 256


def x1_views(rows):
    views = {}
    for nm, shp, k, r0 in X1_LAYOUT:
        n = int(np.prod(shp)) // 2048
        v = rows(k, r0, n)
        if nm in ("xqa", "xqg", "xka", "xqb", "xqc", "xkc"):
            v = v.rearrange("e (h d) t -> e h d t", h=shp[0])
        elif nm in ("xva", "xvc"):
            v = v.rearrange("e r c -> e (r c)").rearrange("e (h t d) -> e h t d", h=shp[0], d=64)
        elif nm == "xvb":
            v = v.rearrange("e r c -> e (r c)").rearrange("e (t d) -> e t d", d=64)
        views[nm] = v
    return views


def build_fused_program():
    P = Prog()
    nc = P.nc
    x_in = P.dram("x_in", [NTOK, D], F32, "ExternalInput").ap()
    pos_in = P.dram("pos_in", [128, NTILE], I32, "ExternalInput").ap()
    ident = P.dram("ident", [128, 128], F32, "ExternalInput").ap()
    invf = P.dram("invf", [128, 48], F32, "ExternalInput").ap()
    y_out = P.dram("y", [NTOK, D], F32, "ExternalOutput").ap()
    RD = X1_K * X1_CR
    X1 = P.dram("ex_x1", [4 * RD, 2048], BF16, "Internal").ap()
    G1 = P.dram("ex_g1", [16 * RD, 2048], BF16, "Internal").ap()
    M1 = P.dram("ex_m1", [4 * RD, 2048], BF16, "Internal").ap()
    XG = P.dram("ex_xg", [4 * 16, 768], F32, "Internal").ap()
    GG = P.dram("ex_gg", [16 * 16, 768], F32, "Internal").ap()
    MG = P.dram("ex_mg", [4 * 16, 768], F32, "Internal").ap()
    O2 = P.dram("ex_o2", [12 * NTOK, 128], BF16, "Internal").ap()
    GO = P.dram("ex_go", [48 * NTOK, 128], BF16, "Internal").ap()
    MO = P.dram("ex_mo", [12 * NTOK, 128], BF16, "Internal").ap()
    C = Common(P, x_in, pos_in, ident, invf)
    mark, pmark = nc.sbuf_base, nc.psum_base

    def phase_end():
        P.barrier()
        nc.sbuf_base = mark
        nc.psum_base = pmark

    def exchange_chunked(src, gath, mine, nchunk, cr):
        P.barrier()
        rc_ = P.res()
        for j in range(4 * nchunk):
            P.allgather(src[j * cr:(j + 1) * cr, :], gath[j * 4 * cr:(j + 1) * 4 * cr, :], rc_)
        rm_ = P.res()
        g3 = gath.rearrange("(d x) c -> d x c", d=4)
        P.dma("pool", mine, (lambda: g3[bass.ds(P.rank(), 1), :, :]), reads=[rc_], writes=[rm_])
        P.barrier()

    def exchange_small(src, gath, mine):
        P.barrier()
        rc_ = P.res()
        P.allgather(src, gath, rc_)
        rm_ = P.res()
        g4 = gath.rearrange("(s d r) c -> s d r c", s=4, d=4)
        m3 = mine.rearrange("(s r) c -> s r c", s=4)
        P.dma("pool", m3, (lambda: g4[:, bass.ds(P.rank(), 1), :, :]), reads=[rc_], writes=[rm_])
        P.barrier()

    X1v = X1.rearrange("(e r) c -> e r c", e=4)
    M1v = M1.rearrange("(k s i) c -> k s i c", k=X1_K, s=4)
    MOv = MO.rearrange("(k s i) c -> k s i c", k=2, s=4)
    for l in range(2):
        last = l == 1
        io = {}
        for k, shp in P1_WNAMES.items():
            io[k] = P.dram(f"l{l}_p1_{k}", shp, F32, "ExternalInput").ap()
        io.update(x1_views(lambda k, r0, n: X1v[:, k * X1_CR + r0:k * X1_CR + r0 + n, :]))
        io["xg"] = XG.rearrange("(e a) (p c) -> e (a p) c", e=4, c=6)
        emit_p1(P, C, io)
        P.barrier()
        g3 = G1.rearrange("(d x) c -> d x c", d=4)
        groups = {"nsa": (0, 3), "swa": (3, 4), "mla": (4, 7)}
        rcg, rmg = {}, {}
        for gname, (k0, k1) in groups.items():
            rcg[gname] = P.res()
            rmg[gname] = P.res()
            for d in range(4):
                for k in range(k0, k1):
                    j = d * X1_K + k
                    P.allgather(X1[j * X1_CR:(j + 1) * X1_CR, :], G1[j * 4 * X1_CR:(j + 1) * 4 * X1_CR, :], rcg[gname], sem="cc_" + gname)
            if gname == "nsa":
                rcg["g"] = P.res()
                rmg["g"] = P.res()
                P.allgather(XG, GG, rcg["g"], sem="cc_g")

        def select(gname):
            k0, k1 = groups[gname]
            r0, r1 = k0 * 4 * X1_CR, k1 * 4 * X1_CR
            P.dma("sp", M1[r0:r1, :], (lambda: g3[bass.ds(P.rank("sp"), 1), r0:r1, :]), reads=[rcg[gname]], writes=[rmg[gname]])

        select("nsa")
        gg4 = GG.rearrange("(s d r) c -> s d r c", s=4, d=4)
        P.dma("sp", MG.rearrange("(s r) c -> s r c", s=4), (lambda: gg4[:, bass.ds(P.rank("sp"), 1), :, :]), reads=[rcg["g"]], writes=[rmg["g"]])
        nc.sbuf_base = mark
        nc.psum_base = pmark
        io = {}
        v = x1_views(lambda k, r0, n: M1v[k][:, r0:r0 + n, :])
        for k, vv in v.items():
            io[X1_TO_P2[k]] = vv
        io["g"] = MG.rearrange("(e a) (p c) -> e (a p) c", e=4, c=6)
        io["deps"] = {"nsa": [rmg["nsa"], rmg["g"]], "swa": [rmg["swa"]], "mla": [rmg["mla"]]}
        io["pre_swa"] = lambda: select("swa")
        io["pre_mla"] = lambda: select("mla")
        for k, shp in P2_W.items():
            if k == "selmap":
                if l == 0:
                    selmap_ap = P.dram("selmap", shp, F32, "ExternalInput").ap()
                io[k] = selmap_ap
            else:
                io[k] = P.dram(f"l{l}_p2_{k}", shp, F32, "ExternalInput").ap()
        O2v = O2.rearrange("(m e t) c -> m e t c", m=3, e=4)
        ro = [P.res(), P.res(), P.res()]
        rco = P.res()
        io["o_mix"] = lambda m: O2v[m]
        io["ro"] = ro

        def post_mix(m, ro=ro, rco=rco):
            for d in range(4):
                P.allgather(O2[(m * 4 + d) * NTOK:(m * 4 + d + 1) * NTOK, :], GO[((d * 3 + m) * 4) * NTOK:((d * 3 + m) * 4 + 4) * NTOK, :], rco,
                            reads=[ro[m]] if d == 0 else (), sem="cc_o")

        io["post_mix"] = post_mix
        emit_p2(P, io, C.ident, C.rid)
        P.barrier()
        nc.sbuf_base = mark
        nc.psum_base = pmark
        rmo = P.res()
        go3 = GO.rearrange("(d x) c -> d x c", d=4)
        P.dma("sp", MO, (lambda: go3[bass.ds(P.rank("sp"), 1), :, :]), reads=[rco], writes=[rmo])
        MOv = MO.rearrange("(m s t) c -> m s t c", m=3, s=4)
        io = {}
        for k, shp in P3_WNAMES.items():
            io[k] = P.dram(f"l{l}_p3_{k}", shp, F32, "ExternalInput").ap()
        io["o_tile3"] = lambda t, m: MOv[m][:, t * 128:(t + 1) * 128, :]
        io["o_dep"] = [rmo]
        if last:
            io["final_norm"] = P.dram("final_norm", [1, D], F32, "ExternalInput").ap()
            io["y"] = y_out
        emit_p3(P, C, io, last)
        phase_end()
    return P.build()


_PROGS = {}


def _prog(key):
    if key not in _PROGS:
        if key == "fused":
            _PROGS[key] = build_fused_program()
        elif key == "p2":
            _PROGS[key] = build_p2_program()
        else:
            _PROGS[key] = build_tok_program(*key)
    return _PROGS[key]


def kernel(**inp):
    inp = {k: np.asarray(v) for k, v in inp.items()}
    cst = const_inputs()
    cores = list(range(8))
    maps = []
    for c in cores:
        b, r = divmod(c, 4)
        m = dict(cst)
        m["x_in"] = np.ascontiguousarray(inp["x"][b, r * NTOK:(r + 1) * NTOK]).astype(np.float32)
        m["pos_in"] = np.ascontiguousarray(inp["positions"][b, r * NTOK:(r + 1) * NTOK].reshape(NTILE, 128).T.astype(np.int32))
        m["selmap"] = selmap_const()
        m["final_norm"] = np.ascontiguousarray(inp["final_norm"][None, :])
        for l in range(2):
            for k, v in p1_weights(inp, l).items():
                m[f"l{l}_{k}"] = v
            for k, v in p2_weights(inp, l, r).items():
                if k not in ("selmap", "ident"):
                    m[f"l{l}_p2_{k}"] = v
            for k, v in p3_weights(inp, l, b, r, False).items():
                m[f"l{l}_{k}"] = v
        maps.append(m)
    res = run_bass_kernel_spmd(_prog("fused"), maps, core_ids=cores).results
    y = np.stack([np.concatenate([np.asarray(res[4 * b + r]["y"]) for r in range(4)], axis=0) for b in range(2)], axis=0)
    return y.astype(np.float32)
```

```python
import numpy as np
import ml_dtypes
import concourse.bass as bass
import concourse.mybir as mybir
from concourse.bass_utils import run_bass_kernel_spmd

F32 = mybir.dt.float32
BF16 = mybir.dt.bfloat16
I32 = mybir.dt.int32
AF = mybir.ActivationFunctionType
ALU = mybir.AluOpType
AX = mybir.AxisListType

D = 1024
S = 8192
NTOK = 2048
NTILE = 16
NSUP = 4
EPS = 1e-6
DFF = 2816
NEG = -30000.0


class Res:
    __slots__ = ("name", "w", "r", "wdma")

    def __init__(self, name):
        self.name = name
        self.w = None
        self.r = {}


class Prog:
    ENG = ("pe", "act", "dve", "pool", "sp")

    def __init__(self):
        self.nc = bass.Bass("TRN2", target_bir_lowering=False)
        nc = self.nc
        self.cnt = {k: 0 for k in self.ENG}
        self.ops = {k: [] for k in self.ENG}
        self.seen = {k: {} for k in self.ENG}
        self.semobj = {}
        for k in self.ENG:
            self.semobj["s_" + k] = nc.alloc_semaphore("s_" + k)
        self.dsem = {}
        self.free_dsems = []
        self.nres = 0
        self.nname = 0
        self.ncc = 0
        self._rank = None

    def sb(self, shape, dt, name=None):
        self.nname += 1
        return self.nc.alloc_sbuf_tensor(name or f"t{self.nname}", list(shape), dt)

    def ps(self, shape, dt=F32, name=None):
        self.nname += 1
        return self.nc.alloc_psum_tensor(name or f"p{self.nname}", list(shape), dt)

    def res(self, name=None):
        self.nres += 1
        return Res(name or f"r{self.nres}")

    def dram(self, name, shape, dt, kind):
        return self.nc.dram_tensor(name, list(shape), dt, kind=kind)

    def _waits(self, eng, reads, writes, dma_key=None):
        waits = {}

        def need(tok):
            if tok is None:
                return
            s, v = tok
            if waits.get(s, 0) < v:
                waits[s] = v

        for r in reads:
            need(r.w)
        for w in writes:
            if not (dma_key is not None and w.w is not None and w.w[0] == dma_key):
                need(w.w)
            for tok in w.r.values():
                need(tok)
        wl = []
        for s, v in waits.items():
            if eng == "pe" and s == "s_pe":
                continue
            if self.seen[eng].get(s, 0) >= v:
                continue
            self.seen[eng][s] = v
            wl.append((s, v))
        return wl

    def op(self, eng, fn, reads=(), writes=()):
        wl = self._waits(eng, reads, writes)
        self.cnt[eng] += 1
        sname = "s_" + eng
        tok = (sname, self.cnt[eng])
        self.ops[eng].append((wl, fn, (sname, 1)))
        for r in reads:
            r.r[eng] = tok
        for w in writes:
            w.w = tok
            w.r = {}

    def dma(self, q, out, in_, reads=(), writes=(), chan=None, **kw):
        key = (list(writes) + list(reads))[0]
        if key.name not in self.dsem:
            if self.free_dsems:
                self.dsem[key.name] = self.free_dsems.pop()
            else:
                sname = "d_" + key.name
                self.semobj[sname] = self.nc.alloc_semaphore(sname)
                self.dsem[key.name] = [sname, 0]
        d = self.dsem[key.name]
        wl = self._waits(q, reads, writes, dma_key=d[0])
        d[1] += 16
        tok = (d[0], d[1])
        self.ops[q].append((wl, (lambda e: e.dma_start(out=out, in_=(in_() if callable(in_) else in_), **kw)), (d[0], 16)))
        for r in reads:
            r.r["dma:" + key.name] = tok
        for w in writes:
            w.w = tok
            w.wdma = True
            w.r = {}

    def barrier(self):
        allw = [(d[0], d[1]) for d in self.dsem.values() if d[1] > 0]
        for k in self.ENG:
            if self.cnt[k] > 0:
                allw.append(("s_" + k, self.cnt[k]))
        for k in self.ENG:
            wl = []
            for s, v in allw:
                if self.seen[k].get(s, 0) >= v:
                    continue
                self.seen[k][s] = v
                wl.append((s, v))
            if wl:
                self.ops[k].append((wl, None, None))
        self.free_dsems.extend(self.dsem.values())
        self.dsem = {}

    def freg(self, e, val):
        if not hasattr(self, "_fregs"):
            self._fregs = {}
        if val not in self._fregs:
            self._fregs[val] = e.to_reg(float(val))
        return self._fregs[val]

    def rank(self, eng="pool"):
        if self._rank is None:
            self._rank = {}
        if eng not in self._rank:
            et = {"pool": mybir.EngineType.Pool, "sp": mybir.EngineType.SP}[eng]
            self._rank[eng] = self.nc.partition_id([et]) % 4
        return self._rank[eng]

    def allgather(self, ins_ap, outs_ap, rres, reads=(), sem="cc"):
        sname = "s_" + sem
        if sname not in self.semobj:
            self.semobj[sname] = self.nc.alloc_semaphore(sname)
            self.cccnt = getattr(self, "cccnt", {})
            self.cccnt[sname] = 0
        self.cccnt[sname] += 1
        wl = self._waits("pool", list(reads), []) if reads else []
        self.ops["pool"].append((wl, (lambda e: e.collective_compute("AllGather", ALU.bypass, replica_groups=[[0, 1, 2, 3], [4, 5, 6, 7]],
                                                                    ins=[ins_ap], outs=[outs_ap])), (sname, 1)))
        rres.w = (sname, self.cccnt[sname])
        rres.r = {}

    def build(self):
        nc = self.nc
        self.barrier()
        with nc.Block() as block:
            def emit(k):
                def body(e):
                    for wl, fn, inc in self.ops[k]:
                        for s, v in wl:
                            e.wait_ge(self.semobj[s], v)
                        if fn is None:
                            continue
                        ins = fn(e)
                        ins.then_inc(self.semobj[inc[0]], inc[1])
                return body
            block.tensor(emit("pe"))
            block.scalar(emit("act"))
            block.vector(emit("dve"))
            block.gpsimd(emit("pool"))
            block.sync(emit("sp"))
        return nc


class Rot:
    def __init__(self, P, n, shape, dt, psum=False):
        self.t = [(P.ps(shape, dt) if psum else P.sb(shape, dt)) for _ in range(n)]
        self.r = [P.res() for _ in range(n)]
        self.i = -1
        self.n = n

    def next(self):
        self.i = (self.i + 1) % self.n
        return self.t[self.i], self.r[self.i]

    def cur(self):
        return self.t[self.i], self.r[self.i]


W_IN_PERM = np.concatenate([
    np.arange(0, 512), np.arange(512, 640), np.arange(768, 896), np.arange(1024, 1152),
    np.arange(1304, 1816), np.arange(1816, 1944),
    np.arange(640, 768), np.arange(896, 1024), np.arange(1152, 1280), np.arange(1944, 2072),
    np.arange(2072, 2328), np.arange(2328, 2584), np.arange(2584, 2616), np.arange(1280, 1304)])


def const_inputs():
    c = {}
    c["ident"] = np.eye(128, dtype=np.float32)
    f64 = (10000.0 ** (-np.arange(0, 64, 2, dtype=np.float32) / 64)).astype(np.float32)
    f32 = (10000.0 ** (-np.arange(0, 32, 2, dtype=np.float32) / 32)).astype(np.float32)
    c["invf"] = np.ascontiguousarray(np.broadcast_to(np.concatenate([f64, f32])[None, :], (128, 48))).astype(np.float32)
    return c


P1_X = {
    "xqa": ([2, 64, NTOK], BF16), "xqg": ([2, 64, NTOK], BF16), "xka": ([4, 64, NTOK], BF16),
    "xva": ([2, NTOK, 64], BF16), "xqb": ([2, 64, NTOK], BF16), "xkb": ([64, NTOK], BF16),
    "xvb": ([NTOK, 64], BF16), "xqc": ([2, 96, NTOK], BF16), "xkc": ([2, 96, NTOK], BF16),
    "xvc": ([2, NTOK, 64], BF16), "xg": ([NTOK, 6], F32),
}


class Common:
    def __init__(self, P, x_in, pos_in, ident_in, invf_in):
        self.P = P
        nc = P.nc
        self.x = P.sb([128, NTILE, D], F32, "xres")
        self.rx = [P.res() for _ in range(NTILE)]
        for t in range(NTILE):
            P.dma("sp", self.x[:, t, :], x_in[t * 128:(t + 1) * 128, :], writes=[self.rx[t]], chan="xin")
        self.ident = P.sb([128, 128], BF16)
        self.rid = P.res()
        self.cos = P.sb([128, NTILE, 48], F32)
        self.sin = P.sb([128, NTILE, 48], F32)
        self.rcs = P.res()
        mark = nc.sbuf_base
        self.identf = P.sb([128, 128], F32)
        P.dma("sp", self.identf[:], ident_in, writes=[self.rid], chan="c0")
        P.op("dve", lambda e: e.tensor_copy(out=self.ident[:], in_=self.identf[:]), reads=[self.rid], writes=[self.rid])
        pos_i = P.sb([128, NTILE], I32)
        pos_f = P.sb([128, NTILE], F32)
        invf = P.sb([128, 48], F32)
        rp = P.res()
        P.dma("sp", pos_i[:], pos_in, writes=[rp], chan="c0")
        P.dma("sp", invf[:], invf_in, writes=[rp], chan="c0")
        P.op("dve", lambda e: e.tensor_copy(out=pos_f[:], in_=pos_i[:]), reads=[rp], writes=[rp])
        ang = P.sb([128, NTILE, 48], F32)
        ra = P.res()
        for t in range(NTILE):
            P.op("dve", (lambda e, t=t: e.tensor_scalar(out=ang[:, t, :], in0=invf[:], scalar1=pos_f[:, t:t + 1], scalar2=None, op0=ALU.mult)),
                 reads=[rp], writes=[ra])
        tmp = P.sb([128, NTILE, 48], F32)
        ni = P.sb([128, NTILE, 48], I32)
        nf = P.sb([128, NTILE, 48], F32)
        msk = P.sb([128, NTILE, 48], F32)
        C1 = 6.28125
        C2 = 2.0 * np.pi - 6.28125
        PI = float(np.pi)
        TS = lambda **kw: (lambda e: e.tensor_scalar(**kw))
        STT = lambda **kw: (lambda e: e.scalar_tensor_tensor(**kw))
        A2 = lambda ap: ap.rearrange("p t c -> p (t c)")
        P.op("dve", TS(out=A2(ni[:]), in0=A2(ang[:]), scalar1=float(1.0 / (2.0 * np.pi)), scalar2=None, op0=ALU.mult), reads=[ra], writes=[ra])
        P.op("dve", lambda e: e.tensor_copy(out=A2(nf[:]), in_=A2(ni[:])), reads=[ra], writes=[ra])
        P.op("dve", STT(out=A2(tmp[:]), in0=A2(nf[:]), scalar=-C1, in1=A2(ang[:]), op0=ALU.mult, op1=ALU.add), reads=[ra], writes=[ra])
        P.op("dve", STT(out=A2(tmp[:]), in0=A2(nf[:]), scalar=-C2, in1=A2(tmp[:]), op0=ALU.mult, op1=ALU.add), reads=[ra], writes=[ra])
        P.op("dve", TS(out=A2(msk[:]), in0=A2(tmp[:]), scalar1=PI, scalar2=None, op0=ALU.is_gt), reads=[ra], writes=[ra])
        P.op("dve", STT(out=A2(tmp[:]), in0=A2(msk[:]), scalar=-2.0 * PI, in1=A2(tmp[:]), op0=ALU.mult, op1=ALU.add), reads=[ra], writes=[ra])
        P.op("dve", TS(out=A2(msk[:]), in0=A2(tmp[:]), scalar1=-PI, scalar2=None, op0=ALU.is_lt), reads=[ra], writes=[ra])
        P.op("dve", STT(out=A2(tmp[:]), in0=A2(msk[:]), scalar=2.0 * PI, in1=A2(tmp[:]), op0=ALU.mult, op1=ALU.add), reads=[ra], writes=[ra])
        P.op("act", lambda e: e.activation(out=self.sin[:], in_=tmp[:], func=AF.Sin), reads=[ra], writes=[ra, self.rcs])
        P.op("dve", TS(out=A2(tmp[:]), in0=A2(tmp[:]), scalar1=PI / 2.0, scalar2=None, op0=ALU.add), reads=[ra], writes=[ra])
        P.op("dve", TS(out=A2(msk[:]), in0=A2(tmp[:]), scalar1=PI, scalar2=None, op0=ALU.is_gt), reads=[ra], writes=[ra])
        P.op("dve", STT(out=A2(tmp[:]), in0=A2(msk[:]), scalar=-2.0 * PI, in1=A2(tmp[:]), op0=ALU.mult, op1=ALU.add), reads=[ra], writes=[ra])
        P.op("act", lambda e: e.activation(out=self.cos[:], in_=tmp[:], func=AF.Sin), reads=[ra], writes=[ra, self.rcs])
        P.barrier()
        nc.sbuf_base = mark

    def rms_to_T(self, src_fn, rsrc, g_tile, rg, hT, rhT, col0, ncols, scratch):
        pass


def load_w_bf16(P, dst, rdst, w_dram, rows, cols, chan):
    k = rows // 128
    for i in range(k):
        P.dma("pool", dst[:, i, :], w_dram[i * 128:(i + 1) * 128, :], writes=[rdst], chan=chan)


def load_bcast(P, dst, rdst, v_dram, n, chan):
    P.dma("sp", dst[:], v_dram.partition_broadcast(128), writes=[rdst], chan=chan)


def rmsnorm_tile(P, C, src, rsrc, n, g_tile, rg, out_bf, rout, tmp):
    junk, ss, rt = tmp
    P.op("pool", lambda e: e.memset(ss[:, 0:1], 0.0), writes=[rt])
    P.op("act", lambda e: e.activation(out=junk[:, 0:n], in_=src, func=AF.Square, accum_out=ss[:, 0:1]), reads=[rsrc, rt], writes=[rt])
    P.op("dve", lambda e: e.tensor_scalar(out=ss[:, 1:2], in0=ss[:, 0:1], scalar1=1.0 / n, scalar2=EPS, op0=ALU.mult, op1=ALU.add), reads=[rt], writes=[rt])
    P.op("act", lambda e: e.activation(out=ss[:, 2:3], in_=ss[:, 1:2], func=AF.Sqrt), reads=[rt], writes=[rt])
    P.op("dve", lambda e: e.reciprocal(out=ss[:, 3:4], in_=ss[:, 2:3]), reads=[rt], writes=[rt])
    P.op("dve", lambda e: e.scalar_tensor_tensor(out=out_bf, in0=src, scalar=ss[:, 3:4], in1=g_tile, op0=ALU.mult, op1=ALU.mult),
         reads=[rsrc, rt, rg], writes=[rout])


def transpose_chunks(P, C, src_bf, rsrc, nch, pt, rpt, width=128):
    for c in range(nch):
        P.op("pe", (lambda e, c=c: e.transpose(out=pt[0:width, c * 128:(c + 1) * 128], in_=src_bf[:, c * width:(c + 1) * width], identity=C.ident[:])),
             reads=[rsrc, C.rid], writes=[rpt])


def TT(P, eng, out, in0, in1, op, reads, writes):
    P.op(eng, lambda e: e.tensor_tensor(out=out, in0=in0, in1=in1, op=op), reads=reads, writes=writes)


def TS(P, eng, out, in0, s1, s2, op0, op1, reads, writes):
    if op1 is None:
        P.op(eng, lambda e: e.tensor_scalar(out=out, in0=in0, scalar1=s1, scalar2=None, op0=op0), reads=reads, writes=writes)
    else:
        P.op(eng, lambda e: e.tensor_scalar(out=out, in0=in0, scalar1=s1, scalar2=s2, op0=op0, op1=op1), reads=reads, writes=writes)


def STT(P, out, in0, scalar, in1, op0, op1, reads, writes):
    P.op("dve", lambda e: e.scalar_tensor_tensor(out=out, in0=in0, scalar=scalar, in1=in1, op0=op0, op1=op1), reads=reads, writes=writes)


def ACT(P, out, in_, func, reads, writes, **kw):
    P.op("act", lambda e: e.activation(out=out, in_=in_, func=func, **kw), reads=reads, writes=writes)


def MM(P, out, lhsT, rhs, start, stop, reads, writes, skip=False):
    if skip:
        P.op("pe", lambda e: e.matmul(out, lhsT=lhsT, rhs=rhs, start=start, stop=stop, skip_group_check=True), reads=reads, writes=writes)
    else:
        P.op("pe", lambda e: e.matmul(out, lhsT=lhsT, rhs=rhs, start=start, stop=stop), reads=reads, writes=writes)


def TR(P, out, in_, ident, reads, writes):
    P.op("pe", lambda e: e.transpose(out=out, in_=in_, identity=ident), reads=reads, writes=writes)


def MEMSET(P, eng, ap, val, reads, writes):
    P.op(eng, lambda e: e.memset(ap, val), reads=reads, writes=writes)


def cp(P, eng, out, in_, reads, writes):
    if eng == "act":
        P.op("act", lambda e: e.copy(out=out, in_=in_), reads=reads, writes=writes)
    else:
        P.op(eng, lambda e: e.tensor_copy(out=out, in_=in_), reads=reads, writes=writes)


def emit_p1(P, C, io):
    nc = P.nc
    w_in = P.sb([128, 8, 2616], BF16); rw = P.res()
    load_w_bf16(P, w_in, rw, io["w_in"], 1024, 2616, "w1")
    w_qu = P.sb([128, 2, 768], BF16); w_kvu = P.sb([128, 2, 1024], BF16); rwm = P.res()
    load_w_bf16(P, w_qu, rwm, io["w_q_up"], 256, 768, "w1")
    load_w_bf16(P, w_kvu, rwm, io["w_kv_up"], 256, 1024, "w1")
    g_mix = P.sb([128, D], F32); g_q = P.sb([128, 256], F32); g_kv = P.sb([128, 256], F32); rg = P.res()
    load_bcast(P, g_mix, rg, io["mix_norm"], D, "w2")
    load_bcast(P, g_q, rg, io["q_norm"], 256, "w2")
    load_bcast(P, g_kv, rg, io["kv_norm"], 256, "w2")

    junk = P.sb([128, D], BF16)
    ssR = Rot(P, 2, [128, 4], F32)
    hbR = Rot(P, 1, [128, D], BF16)
    hTR = Rot(P, 2, [128, 8, 128], BF16)
    ptR = Rot(P, 2, [128, 1024], BF16, psum=True)
    pzR = Rot(P, 3, [128, 512], F32, psum=True)
    zsR = Rot(P, 1, [128, 2616], F32)
    zs_rc = [[P.res() for _ in range(6)] for _ in range(zsR.n)]
    rqR = Rot(P, 2, [128, 26, 64], BF16)
    tmpA = P.sb([128, 12, 32], F32); tmpB = P.sb([128, 12, 32], F32); rtA = P.res()
    tmpC = P.sb([128, 12, 32], F32); tmpD = P.sb([128, 12, 32], F32); rtC = P.res()
    stq = P.sb([128, 13, 512], BF16); rstq = P.res()
    stv = P.sb([128, 4, 8, 64], BF16); rstv = P.res()
    stqc = P.sb([128, 8, 512], BF16); rstqc = P.res()
    stkc = P.sb([128, 8, 512], BF16); rstkc = P.res()
    stvc = P.sb([128, 4, 8, 64], BF16); rstvc = P.res()
    stg = P.sb([128, 4, 24], F32); rstg = P.res()
    cnR = Rot(P, 1, [128, 512], BF16)
    cnTR = Rot(P, 1, [128, 4, 128], BF16)
    qfR = Rot(P, 1, [128, 8, 96], BF16)
    kfR = Rot(P, 1, [128, 8, 96], BF16)
    kpe = P.sb([128, 32], F32); rkpe = P.res()
    qsb = P.sb([128, 768], F32); rqsb = P.res()
    t16 = [P.sb([128, 8, 16], F32) for _ in range(4)]; rt16 = P.res()
    ktmp = P.sb([128, 4, 16], F32)

    for t in range(NTILE):
        st, tt = t // 4, t % 4
        s0 = st * 512
        xt = C.x[:, t, :]
        ss, rss = ssR.next()
        hb, rhb = hbR.next()
        rmsnorm_tile(P, C, xt, C.rx[t], D, g_mix[:], rg, hb[:], rhb, (junk, ss, rss))
        pt, rpt = ptR.next()
        transpose_chunks(P, C, hb, rhb, 8, pt, rpt)
        hT, rhT = hTR.next()
        P.op("act", lambda e, hT=hT, pt=pt: e.copy(out=hT[:].rearrange("p k t -> p (k t)"), in_=pt[:]), reads=[rpt], writes=[rhT])
        zs, rzs = zsR.next()
        rzc = zs_rc[zsR.i]
        for c in range(6):
            c0 = c * 512
            n = min(512, 2616 - c0)
            pz, rpz = pzR.next()
            for k in range(8):
                P.op("pe", (lambda e, pz=pz, hT=hT, k=k, c0=c0, n=n: e.matmul(pz[:, 0:n], lhsT=hT[:, k, :], rhs=w_in[:, k, c0:c0 + n], start=(k == 0), stop=(k == 7))),
                     reads=[rhT, rw], writes=[rpz])
            P.op("act", (lambda e, pz=pz, zs=zs, c0=c0, n=n: e.copy(out=zs[:, c0:c0 + n], in_=pz[:, 0:n])), reads=[rpz], writes=[rzc[c]])
        if t == 0 and "dbg_zs" in io:
            P.dma("sp", io["dbg_zs"], zs[:], reads=rzc, chan="dbg")
            P.dma("sp", io["dbg_hb"], hb[:], reads=[rhb], chan="dbg")
            P.dma("sp", io["dbg_ss"], ss[:], reads=[rss], chan="dbg")
            P.dma("sp", io["dbg_hT"], hT[:].rearrange("p k t -> p (k t)"), reads=[rhT], chan="dbg")
        rq, rrq = rqR.next()
        zv = zs[:, 0:1536].rearrange("p (h two d) -> p h two d", h=24, two=2)
        rqv = rq[:].rearrange("p h (two d) -> p h two d", two=2)
        cb = C.cos[:, t, 0:32].unsqueeze(1).broadcast_to([128, 12, 32])
        sb_ = C.sin[:, t, 0:32].unsqueeze(1).broadcast_to([128, 12, 32])
        zr = rzc[0:3]
        for hh in range(2):
            hs = slice(hh * 12, hh * 12 + 12)
            x1, x2 = zv[:, hs, 0, :], zv[:, hs, 1, :]
            o1, o2 = rqv[:, hs, 0, :], rqv[:, hs, 1, :]
            P.op("dve", lambda e, x1=x1, cb=cb: e.tensor_tensor(out=tmpA[:], in0=x1, in1=cb, op=ALU.mult), reads=zr + [C.rcs], writes=[rtA])
            P.op("dve", lambda e, x2=x2, sb_=sb_: e.tensor_tensor(out=tmpB[:], in0=x2, in1=sb_, op=ALU.mult), reads=zr + [C.rcs], writes=[rtA])
            P.op("dve", lambda e, o1=o1: e.tensor_tensor(out=o1, in0=tmpA[:], in1=tmpB[:], op=ALU.subtract), reads=[rtA], writes=[rrq])
            P.op("pool", lambda e, x2=x2, cb=cb: e.tensor_tensor(out=tmpC[:], in0=x2, in1=cb, op=ALU.mult), reads=zr + [C.rcs], writes=[rtC])
            P.op("pool", lambda e, x1=x1, sb_=sb_: e.tensor_tensor(out=tmpD[:], in0=x1, in1=sb_, op=ALU.mult), reads=zr + [C.rcs], writes=[rtC])
            P.op("pool", lambda e, o2=o2: e.tensor_tensor(out=o2, in0=tmpC[:], in1=tmpD[:], op=ALU.add), reads=[rtC], writes=[rrq])
        if t == 0 and "dbg_rq" in io:
            P.dma("sp", io["dbg_rq"], rq[:].rearrange("p h d -> p (h d)"), reads=[rrq], chan="dbg")
            P.dma("sp", io["dbg_cs"], C.cos[:, 0, :], reads=[C.rcs], chan="dbg")
            P.dma("sp", io["dbg_sn"], C.sin[:, 0, :], reads=[C.rcs], chan="dbg")
        cp(P, "pool", rq[:, 24:26, :].rearrange("p h d -> p (h d)"), zs[:, 1536:1664], [rzc[3]], [rrq])
        rqf = rq[:].rearrange("p h d -> p (h d)")
        for half, (cs_, ce_) in enumerate(((0, 8), (8, 13))):
            pt2, rpt2 = ptR.next()
            for c in range(cs_, ce_):
                P.op("pe", (lambda e, c=c, pt2=pt2, cs_=cs_, rqf=rqf: e.transpose(out=pt2[:, (c - cs_) * 128:(c - cs_ + 1) * 128], in_=rqf[:, c * 128:(c + 1) * 128], identity=C.ident[:])),
                     reads=[rrq, C.rid], writes=[rpt2])
            nn = ce_ - cs_
            cp(P, "act" if half == 0 else "dve", stq[:, cs_:cs_ + nn, tt * 128:(tt + 1) * 128],
               pt2[:, 0:nn * 128].rearrange("p (c t) -> p c t", c=nn), [rpt2], [rstq])
        P.op("pool", lambda e, zs=zs, tt=tt: e.tensor_copy(out=stv[:, tt, :, :].rearrange("p s d -> p (s d)"), in_=zs[:, 1536:2048]), reads=[rzc[3]], writes=[rstv])
        P.op("act", lambda e, zs=zs, tt=tt: e.activation(out=stg[:, tt, :], in_=zs[:, 2592:2616], func=AF.Sigmoid), reads=[rzc[5]], writes=[rstg])
        cn, rcn = cnR.next()
        ss2, rss2 = ssR.next()
        rmsnorm_tile(P, C, zs[:, 2048:2304], rzc[4], 256, g_q[:], rg, cn[:, 0:256], rcn, (junk, ss2, rss2))
        ss3, rss3 = ssR.next()
        rmsnorm_tile(P, C, zs[:, 2304:2560], rzc[4], 256, g_kv[:], rg, cn[:, 256:512], rcn, (junk, ss3, rss3))
        pt3, rpt3 = ptR.next()
        transpose_chunks(P, C, cn, rcn, 4, pt3, rpt3)
        cnT, rcnT = cnTR.next()
        P.op("dve", lambda e, cnT=cnT, pt3=pt3: e.tensor_copy(out=cnT[:].rearrange("p k t -> p (k t)"), in_=pt3[:, 0:512]), reads=[rpt3], writes=[rcnT])
        qf, rqf_ = qfR.next()
        kf, rkf = kfR.next()
        for (c0, n) in ((0, 512), (512, 256)):
            pz, rpz = pzR.next()
            for k in range(2):
                P.op("pe", (lambda e, pz=pz, cnT=cnT, k=k, c0=c0, n=n: e.matmul(pz[:, 0:n], lhsT=cnT[:, k, :], rhs=w_qu[:, k, c0:c0 + n], start=(k == 0), stop=(k == 1))),
                     reads=[rcnT, rwm], writes=[rpz])
            P.op("act", (lambda e, pz=pz, c0=c0, n=n: e.copy(out=qsb[:, c0:c0 + n], in_=pz[:, 0:n])), reads=[rpz], writes=[rqsb])
        qv = qsb[:].rearrange("p (h d) -> p h d", h=8)
        P.op("pool", lambda e, qf=qf, qv=qv: e.tensor_copy(out=qf[:, :, 0:64], in_=qv[:, :, 0:64]), reads=[rqsb], writes=[rqf_])
        c32 = C.cos[:, t, 32:48].unsqueeze(1).broadcast_to([128, 8, 16])
        s32 = C.sin[:, t, 32:48].unsqueeze(1).broadcast_to([128, 8, 16])
        qx1, qx2 = qv[:, :, 64:80], qv[:, :, 80:96]
        P.op("dve", lambda e, qx1=qx1, c32=c32: e.tensor_tensor(out=t16[0][:], in0=qx1, in1=c32, op=ALU.mult), reads=[rqsb, C.rcs], writes=[rt16])
        P.op("dve", lambda e, qx2=qx2, s32=s32: e.tensor_tensor(out=t16[1][:], in0=qx2, in1=s32, op=ALU.mult), reads=[rqsb, C.rcs], writes=[rt16])
        P.op("dve", lambda e, qx2=qx2, c32=c32: e.tensor_tensor(out=t16[2][:], in0=qx2, in1=c32, op=ALU.mult), reads=[rqsb, C.rcs], writes=[rt16])
        P.op("dve", lambda e, qx1=qx1, s32=s32: e.tensor_tensor(out=t16[3][:], in0=qx1, in1=s32, op=ALU.mult), reads=[rqsb, C.rcs], writes=[rt16])
        P.op("dve", lambda e, qf=qf: e.tensor_tensor(out=qf[:, :, 64:80], in0=t16[0][:], in1=t16[1][:], op=ALU.subtract), reads=[rt16], writes=[rqf_])
        P.op("dve", lambda e, qf=qf: e.tensor_tensor(out=qf[:, :, 80:96], in0=t16[2][:], in1=t16[3][:], op=ALU.add), reads=[rt16], writes=[rqf_])
        kx1, kx2 = zs[:, 2560:2576], zs[:, 2576:2592]
        c16, s16 = C.cos[:, t, 32:48], C.sin[:, t, 32:48]
        P.op("pool", lambda e, kx1=kx1, c16=c16: e.tensor_tensor(out=ktmp[:, 0, :], in0=kx1, in1=c16, op=ALU.mult), reads=[rzc[5], C.rcs], writes=[rkpe])
        P.op("pool", lambda e, kx2=kx2, s16=s16: e.tensor_tensor(out=ktmp[:, 1, :], in0=kx2, in1=s16, op=ALU.mult), reads=[rzc[5], C.rcs], writes=[rkpe])
        P.op("pool", lambda e, kx2=kx2, c16=c16: e.tensor_tensor(out=ktmp[:, 2, :], in0=kx2, in1=c16, op=ALU.mult), reads=[rzc[5], C.rcs], writes=[rkpe])
        P.op("pool", lambda e, kx1=kx1, s16=s16: e.tensor_tensor(out=ktmp[:, 3, :], in0=kx1, in1=s16, op=ALU.mult), reads=[rzc[5], C.rcs], writes=[rkpe])
        P.op("pool", lambda e: e.tensor_tensor(out=kpe[:, 0:16], in0=ktmp[:, 0, :], in1=ktmp[:, 1, :], op=ALU.subtract), reads=[rkpe], writes=[rkpe])
        P.op("pool", lambda e: e.tensor_tensor(out=kpe[:, 16:32], in0=ktmp[:, 2, :], in1=ktmp[:, 3, :], op=ALU.add), reads=[rkpe], writes=[rkpe])
        P.op("pool", lambda e, kf=kf: e.tensor_copy(out=kf[:, :, 64:96], in_=kpe[:].unsqueeze(1).broadcast_to([128, 8, 32])), reads=[rkpe], writes=[rkf])
        for ci, c0 in enumerate((0, 512)):
            pz, rpz = pzR.next()
            for k in range(2):
                P.op("pe", (lambda e, pz=pz, cnT=cnT, k=k, c0=c0: e.matmul(pz[:, 0:512], lhsT=cnT[:, 2 + k, :], rhs=w_kvu[:, k, c0:c0 + 512], start=(k == 0), stop=(k == 1))),
                     reads=[rcnT, rwm], writes=[rpz])
            pv = pz[:, 0:512].rearrange("p (h d) -> p h d", h=4)
            P.op("act", (lambda e, pv=pv, kf=kf, ci=ci: e.copy(out=kf[:, ci * 4:(ci + 1) * 4, 0:64], in_=pv[:, :, 0:64])), reads=[rpz], writes=[rkf])
            P.op("dve", (lambda e, pv=pv, ci=ci, tt=tt: e.tensor_copy(out=stvc[:, tt, ci * 4:(ci + 1) * 4, :], in_=pv[:, :, 64:128])), reads=[rpz], writes=[rstvc])
        for src, rsrc, dst, rdst, eng in ((qf, rqf_, stqc, rstqc, "act"), (kf, rkf, stkc, rstkc, "dve")):
            pt4, rpt4 = ptR.next()
            for h in range(8):
                P.op("pe", (lambda e, h=h, pt4=pt4, src=src: e.transpose(out=pt4[0:96, h * 128:(h + 1) * 128], in_=src[:, h, :], identity=C.ident[:])),
                     reads=[rsrc, C.rid], writes=[rpt4])
            if eng == "act":
                P.op("act", (lambda e, pt4=pt4, dst=dst, tt=tt: e.copy(out=dst[0:96, :, tt * 128:(tt + 1) * 128], in_=pt4[0:96, :].rearrange("p (h t) -> p h t", h=8))), reads=[rpt4], writes=[rdst])
            else:
                P.op("dve", (lambda e, pt4=pt4, dst=dst, tt=tt: e.tensor_copy(out=dst[0:96, :, tt * 128:(tt + 1) * 128], in_=pt4[0:96, :].rearrange("p (h t) -> p h t", h=8))), reads=[rpt4], writes=[rdst])

        if tt == 3:
            sl = slice(s0, s0 + 512)
            for c in range(4):
                P.dma("sp", io["xqa"][c].rearrange("h d t -> (h d) t")[:, sl], stq[:, c, :], reads=[rstq])
                P.dma("sp", io["xqg"][c ^ 1].rearrange("h d t -> (h d) t")[:, sl], stq[:, c, :], reads=[rstq])
                P.dma("sp", io["xqb"][c].rearrange("h d t -> (h d) t")[:, sl], stq[:, 7 + c, :], reads=[rstq])
            for g in range(2):
                for dest in (2 * g, 2 * g + 1):
                    for ty, ch in enumerate((4, 5, 6, 12)):
                        P.dma("sp", io["xka"][dest, ty][:, sl], stq[g * 64:(g + 1) * 64, ch, :], reads=[rstq])
                    P.dma("sp", io["xkb"][dest][:, sl], stq[g * 64:(g + 1) * 64, 11, :], reads=[rstq])
                    for ty in range(2):
                        P.dma("sp", io["xva"][dest, ty][sl, :].rearrange("(tt p) d -> p tt d", p=128), stv[:, :, (ty + 1) * 2 + g, :], reads=[rstv])
                    P.dma("sp", io["xvb"][dest][sl, :].rearrange("(tt p) d -> p tt d", p=128), stv[:, :, 6 + g, :], reads=[rstv])
            for dest in range(4):
                P.dma("sp", io["xqc"][dest].rearrange("h d t -> d h t")[:, :, sl], stqc[0:96, 2 * dest:2 * dest + 2, :], reads=[rstqc], chan="x3")
                P.dma("sp", io["xkc"][dest].rearrange("h d t -> d h t")[:, :, sl], stkc[0:96, 2 * dest:2 * dest + 2, :], reads=[rstkc], chan="x3")
                for hh in range(2):
                    P.dma("sp", io["xvc"][dest, hh][sl, :].rearrange("(tt p) d -> p tt d", p=128), stvc[:, :, 2 * dest + hh, :], reads=[rstvc], chan="x4")
                P.dma("sp", io["xg"][dest][sl, :].rearrange("(tt p) c -> p tt c", p=128), stg[:, :, 6 * dest:6 * dest + 6], reads=[rstg], chan="x4")


P2_IN = {
    "qa": ([4, 2, 64, NTOK], BF16), "qg": ([4, 2, 64, NTOK], BF16), "ka": ([4, 4, 64, NTOK], BF16),
    "va": ([4, 2, NTOK, 64], BF16), "qb": ([4, 2, 64, NTOK], BF16), "kb": ([4, 64, NTOK], BF16),
    "vb": ([4, NTOK, 64], BF16), "qc": ([4, 2, 96, NTOK], BF16), "kc": ([4, 2, 96, NTOK], BF16),
    "vc": ([4, 2, NTOK, 64], BF16), "g": ([4, NTOK, 6], F32),
}
P2_W = {"posk": [128, 16], "w1k": [2048, 256], "w2k": [256, 64], "posv": [128, 16], "w1v": [2048, 256],
        "w2v": [256, 64], "sinks": [1, 2], "selmap": [128, 4, 128]}


def selmap_const():
    n_cmp = 511
    tok = np.arange(n_cmp)[:, None] * 16 + np.arange(32)[None, :]
    sm = np.zeros((512, 128), np.float32)
    np.add.at(sm, (np.repeat(np.arange(n_cmp), 32), (tok // 64).reshape(-1)), 1.0 / 32)
    return np.ascontiguousarray(sm.reshape(4, 128, 128).transpose(1, 0, 2))


class AttnCtx:
    def __init__(self, P, ident, rid):
        self.P = P
        self.ident = ident
        self.rid = rid
        self.S = Rot(P, 4, [128, 512], F32, psum=True)
        self.pT = Rot(P, 3, [128, 512], BF16)
        self.acc = Rot(P, 2, [128, 4, 128], F32, psum=True)


def attn_qgroup(P, A, kT, rkT, Vt, rV, nv, qT, rqT, kbs, scale, accv, racc, look=1):
    cover = {qb: [i for i, e in enumerate(kbs) if e[1] <= qb <= e[2]] for qb in range(4)}
    n_kb = len(kbs)
    tiles = [None] * n_kb

    def scores(i):
        kb, lo, hi, segs = kbs[i]
        ps, rps = A.S.next()
        for (q0, q1, extra) in segs:
            c0, c1 = q0 * 128, (q1 + 1) * 128
            n = len(extra)
            MM(P, ps[:, c0:c1], kT[:, kb * 128:(kb + 1) * 128], qT[:, c0:c1], True, n == 0, [rkT, rqT], [rps])
            for j, (l_, r_, rd) in enumerate(extra):
                MM(P, ps[:, c0:c1], l_, r_, False, j == n - 1, rd, [rps])
        tiles[i] = (ps, rps)

    def rest(i):
        kb, lo, hi, segs = kbs[i]
        ps, rps = tiles[i]
        pT, rpT = A.pT.next()
        c0, c1 = lo * 128, (hi + 1) * 128
        ACT(P, pT[:, c0:c1], ps[:, c0:c1], AF.Exp, [rps], [rpT], scale=scale)
        for qb in range(lo, hi + 1):
            MM(P, accv(qb), pT[:, qb * 128:(qb + 1) * 128], Vt(kb), i == 0 and qb == lo, cover[qb][-1] == i, [rpT, rV], [racc], skip=True)

    LOOK = look
    for i in range(n_kb + LOOK):
        if i < n_kb:
            scores(i)
        if i - LOOK >= 0:
            rest(i - LOOK)


def emit_p2(P, io, ident, rid):
    nc = P.nc
    deps = io.get("deps", {"nsa": [], "swa": [], "mla": []})
    dA, dB, dC = deps["nsa"], deps["swa"], deps["mla"]
    A = AttnCtx(P, ident, rid)
    rc = P.res()
    zero_bf = P.sb([128, 512], BF16)
    ones_bf = P.sb([128, 128], BF16)
    MEMSET(P, "pool", zero_bf[:], 0.0, [], [rc])
    MEMSET(P, "pool", ones_bf[:], 1.0, [], [rc])
    pen_diag = P.sb([128, 128], BF16)
    pen_far = P.sb([128, 128], BF16)
    P.op("pool", lambda e: e.affine_select(out=pen_diag[:], in_=zero_bf[:, 0:128], pattern=[[1, 128]], compare_op=ALU.is_ge, fill=P.freg(e, NEG), base=0, channel_multiplier=-1), reads=[rc], writes=[rc])
    P.op("pool", lambda e: e.affine_select(out=pen_far[:], in_=zero_bf[:, 0:128], pattern=[[-1, 128]], compare_op=ALU.is_gt, fill=P.freg(e, NEG), base=0, channel_multiplier=1), reads=[rc], writes=[rc])
    identf32 = P.sb([128, 128], F32)
    MEMSET(P, "pool", identf32[:], 1.0, [], [rc])
    P.op("pool", lambda e: e.affine_select(out=identf32[:], in_=identf32[:], pattern=[[-1, 128]], compare_op=ALU.is_equal, fill=P.freg(e, 0.0), base=0, channel_multiplier=1), reads=[rc], writes=[rc])
    E = P.sb([128, 64, 128], BF16)
    for j in range(64):
        P.op("pool", (lambda e, j=j: e.affine_select(out=E[:, j, :].rearrange("p (a b) -> p a b", a=2), in_=ones_bf[:].rearrange("p (a b) -> p a b", a=2),
                                                     pattern=[[-1, 2], [0, 64]], compare_op=ALU.is_equal, fill=P.freg(e, 0.0), base=-2 * j, channel_multiplier=1)), reads=[rc], writes=[rc])
    vcmp = P.sb([128, 4, 200], BF16); rvcmp = P.res()
    MEMSET(P, "pool", vcmp[:], 0.0, [], [rvcmp])
    MEMSET(P, "pool", vcmp[:, :, 64:65], 1.0, [], [rvcmp])
    P.dma("pool", vcmp[:, :, 65:193], io["selmap"], writes=[rvcmp])
    kcmpT = P.sb([64, 512], BF16); rkcmp = P.res()
    MEMSET(P, "pool", kcmpT[:], 0.0, [], [rkcmp])
    esink = P.sb([128, 2], F32); resink = P.res()
    P.dma("sp", esink[:], io["sinks"].partition_broadcast(128), writes=[resink])
    ACT(P, esink[:], esink[:], AF.Exp, [resink], [resink])
    mark = nc.sbuf_base
    kT2 = P.sb([128, S], BF16); rkT2 = P.res()
    w1 = P.sb([128, 16, 256], BF16); w2 = P.sb([128, 2, 64], BF16); posT = P.sb([128, 16], BF16); rwc = P.res()
    gT = P.sb([128, 2, 512], BF16); rgT = P.res()
    cb = P.sb([128, 2], F32); rcb = P.res()
    for which, ty in (("k", 0), ("v", 3)):
        for s in range(4):
            P.dma("sp", kT2[0:64, s * NTOK:(s + 1) * NTOK], io["ka"][s, ty], writes=[rkT2], reads=list(dA))
            P.dma("sp", kT2[64:128, s * NTOK:(s + 1) * NTOK - 1], io["ka"][s, ty][:, 1:NTOK], writes=[rkT2], reads=list(dA))
            if s < 3:
                P.dma("sp", kT2[64:128, (s + 1) * NTOK - 1:(s + 1) * NTOK], io["ka"][s + 1, ty][:, 0:1], writes=[rkT2], reads=list(dA), allow_slow_non_contiguous=True)
        load_w_bf16(P, w1, rwc, io["w1" + which], 2048, 256, None)
        load_w_bf16(P, w2, rwc, io["w2" + which], 256, 64, None)
        P.dma("pool", posT[:], io["pos" + which], writes=[rwc])
        kviews = [kT2[:, b0:b0 + 8176].rearrange("p (n s) -> p n s", s=16) for b0 in (0, 16)]
        for hc in range(2):
            ps, rps = A.S.next()
            for lp in range(16):
                MM(P, ps[:, 0:511], w1[:, lp, hc * 128:(hc + 1) * 128], kviews[(2 * lp) // 16][:, :, (2 * lp) % 16], lp == 0, lp == 15, [rwc, rkT2], [rps])
            pb, rpb = A.acc.next()
            for lp in range(16):
                MM(P, pb[:, 0, 0:1], w1[:, lp, hc * 128:(hc + 1) * 128], posT[:, lp:lp + 1], lp == 0, lp == 15, [rwc], [rpb])
            cp(P, "dve", cb[:, hc:hc + 1], pb[:, 0, 0:1], [rpb], [rcb])
            ACT(P, gT[:, hc, 0:511], ps[:, 0:511], AF.Gelu_apprx_tanh, [rps, rcb], [rgT], bias=cb[:, hc:hc + 1])
        if which == "k":
            ps, rps = A.S.next()
            for hc in range(2):
                MM(P, ps[0:64, 0:511], w2[:, hc, :], gT[:, hc, 0:511], hc == 0, hc == 1, [rwc, rgT], [rps])
            cp(P, "dve", kcmpT[:, 0:511], ps[0:64, 0:511], [rps], [rkcmp])
        else:
            for c in range(4):
                nn = 128 if c < 3 else 127
                ps, rps = A.S.next()
                for hc in range(2):
                    MM(P, ps[0:nn, 0:64], gT[:, hc, c * 128:c * 128 + nn], w2[:, hc, :], hc == 0, hc == 1, [rwc, rgT], [rps])
                cp(P, "dve", vcmp[0:nn, c, 0:64], ps[0:nn, 0:64], [rps], [rvcmp])
    P.barrier()
    nc.sbuf_base = mark

    kTa = P.sb([128, S], BF16); rkTa = P.res()
    kTb = P.sb([128, S], BF16); rkTb = P.res()
    Va = P.sb([128, 64, 65], BF16); rVa = P.res()
    Vb = P.sb([128, 64, 65], BF16); rVb = P.res()
    MEMSET(P, "pool", Va[:, :, 64:65], 1.0, [], [rVa])
    MEMSET(P, "pool", Vb[:, :, 64:65], 1.0, [], [rVb])

    def load_kT(dst, rdst, src_fn, dk, dep=()):
        for s in range(4):
            P.dma("sp", dst[0:dk, s * NTOK:(s + 1) * NTOK], src_fn(s), writes=[rdst], reads=list(dep))

    def load_V(dst, rdst, src_fn, dep=()):
        for s in range(4):
            P.dma("sp", dst[:, s * 16:(s + 1) * 16, 0:64], src_fn(s).rearrange("(blk p) d -> p blk d", p=128), writes=[rdst], reads=list(dep))

    qR = Rot(P, 2, [128, 4, 512], BF16)
    gR = Rot(P, 2, [128, 4, 6], F32)
    ostR = Rot(P, 2, [128, 4, 128], BF16)
    oacc = P.sb([128, 2, 4, 64], F32); roacc = [P.res(), P.res()]
    imp = P.sb([128, 4, 128], F32); rimp = P.res()
    rcp = P.sb([128, 8], F32); rrcp = P.res()
    fac = P.sb([128, 8], F32)
    tmpo = P.sb([128, 4, 64], F32); rtmpo = P.res()
    penR = Rot(P, 2, [128, 512], BF16)
    biasR = Rot(P, 2, [128, 128], F32)
    val = P.sb([128, 128], F32); rval = P.res()
    wk = P.sb([128, 128], F32)
    m16 = P.sb([128, 16], F32)
    penq = P.sb([128, 128], F32); rpenq = P.res()
    penT = P.sb([128, 512], BF16); rpenT = P.res()
    cmpacc = [P.ps([128, 2, 256], F32), P.ps([128, 2, 256], F32)]; rcmpacc = P.res()

    def out_dma(ost, rost, Gq, col0, ncol):
        src, off = Gq // 4, (Gq % 4) * 512
        for dest in range(1):
            pass
        d = Gq // 4
        if "o_mix" in io:
            m, c0 = col0 // 128, col0 % 128
            P.dma("sp", io["o_mix"](m)[d][off:off + 512, c0:c0 + ncol].rearrange("(qb p) c -> p qb c", p=128), ost[:, :, 0:ncol],
                  reads=[rost], writes=[io["ro"][m]])
        else:
            P.dma("sp", io["o"][d][off:off + 512, col0:col0 + ncol].rearrange("(qb p) c -> p qb c", p=128), ost[:, :, 0:ncol], reads=[rost])

    def finish_branch(accv_t, racc_, h, gcol, g, rg, first, extra_den=None):
        if extra_den is None:
            P.op("dve", lambda e: e.reciprocal(out=rcp[:, 0:4], in_=accv_t[:, :, 64]), reads=[racc_], writes=[rrcp])
        else:
            TS(P, "dve", rcp[:, 4:8], accv_t[:, :, 64], extra_den, None, ALU.add, None, [racc_, resink], [rrcp])
            P.op("dve", lambda e: e.reciprocal(out=rcp[:, 0:4], in_=rcp[:, 4:8]), reads=[rrcp], writes=[rrcp])
        if gcol is not None:
            TT(P, "dve", fac[:, 0:4], rcp[:, 0:4], g[:, :, gcol], ALU.mult, [rrcp, rg], [rrcp])
            f = fac[:, 0:4]
        else:
            f = rcp[:, 0:4]
        fb = f.unsqueeze(2).broadcast_to([128, 4, 64])
        if first:
            TT(P, "dve", oacc[:, h, :, :], accv_t[:, :, 0:64], fb, ALU.mult, [racc_, rrcp], [roacc[h]])
        else:
            TT(P, "dve", tmpo[:], accv_t[:, :, 0:64], fb, ALU.mult, [racc_, rrcp], [rtmpo])
            TT(P, "dve", oacc[:, h, :, :], oacc[:, h, :, :], tmpo[:], ALU.add, [rtmpo], [roacc[h]])

    load_kT(kTa, rkTa, lambda s: io["ka"][s, 1], 64, dA)
    load_kT(kTb, rkTb, lambda s: io["ka"][s, 2], 64, dA)
    load_V(Va, rVa, lambda s: io["va"][s, 0], dA)
    load_V(Vb, rVb, lambda s: io["va"][s, 1], dA)
    for Gq in range(16):
        src, off = Gq // 4, (Gq % 4) * 512
        q4, rq4 = qR.next()
        P.dma("sp", q4[0:64, 0:2, :], io["qa"][src].rearrange("h d t -> d h t")[:, :, off:off + 512], writes=[rq4], reads=list(dA))
        P.dma("sp", q4[0:64, 2:4, :], io["qg"][src].rearrange("h d t -> d h t")[:, :, off:off + 512], writes=[rq4], reads=list(dA))
        g, rg = gR.next()
        P.dma("sp", g[:], io["g"][src][off:off + 512, :].rearrange("(qb p) c -> p qb c", p=128), writes=[rg], reads=list(dA))
        cmax = (32 * Gq + 30) // 128
        pens = {}
        for c in range(cmax + 1):
            if Gq >= 4 * c + 5:
                continue
            pn, rpn = penR.next()
            P.op("pool", (lambda e, pn=pn, c=c, Gq=Gq: e.affine_select(out=pn[:], in_=zero_bf[:], pattern=[[1, 512]], compare_op=ALU.is_ge, fill=P.freg(e, NEG),
                                                                      base=512 * Gq - 2048 * c - 31, channel_multiplier=-16)), reads=[rc], writes=[rpn])
            pens[c] = (pn, rpn)
        for r4 in range(4):
            ctiles = {}

            def cscores(c, r4=r4):
                ps, rps = A.S.next()
                if c in pens:
                    MM(P, ps[:, :], kcmpT[:, c * 128:(c + 1) * 128], q4[0:64, r4, :], True, False, [rkcmp, rq4], [rps])
                    MM(P, ps[:, :], ident[:], pens[c][0][:], False, True, [rid, pens[c][1]], [rps])
                else:
                    MM(P, ps[:, :], kcmpT[:, c * 128:(c + 1) * 128], q4[0:64, r4, :], True, True, [rkcmp, rq4], [rps])
                ctiles[c] = (ps, rps)

            def crest(c):
                ps, rps = ctiles[c]
                pT, rpT = A.pT.next()
                ACT(P, pT[:, :], ps[:, :], AF.Exp, [rps], [rpT], scale=0.125)
                for qb in range(4):
                    MM(P, cmpacc[qb // 2][:, qb % 2, 0:193], pT[:, qb * 128:(qb + 1) * 128], vcmp[:, c, 0:193], c == 0 and qb % 2 == 0, c == cmax, [rpT, rvcmp], [rcmpacc], skip=True)

            for c in range(cmax + 2):
                if c <= cmax:
                    cscores(c)
                if c >= 1:
                    crest(c - 1)
            for half in range(2):
                TS(P, "dve", rcp[:, 4 + 2 * half:6 + 2 * half], cmpacc[half][:, :, 64], 1e-30, None, ALU.max, None, [rcmpacc], [rrcp])
            P.op("dve", lambda e: e.reciprocal(out=rcp[:, 0:4], in_=rcp[:, 4:8]), reads=[rrcp], writes=[rrcp])
            for qb in range(4):
                src_imp = cmpacc[qb // 2][:, qb % 2, 65:193]
                if r4 == 0:
                    TS(P, "dve", imp[:, qb, :], src_imp, rcp[:, qb:qb + 1], None, ALU.mult, None, [rcmpacc, rrcp], [rimp])
                else:
                    STT(P, imp[:, qb, :], src_imp, rcp[:, qb:qb + 1], imp[:, qb, :], ALU.mult, ALU.add, [rcmpacc, rrcp], [rimp])
            if r4 < 2:
                TT(P, "dve", fac[:, 0:4], rcp[:, 0:4], g[:, :, 3 * r4 + 0], ALU.mult, [rrcp, rg], [rrcp])
                for half in range(2):
                    fb = fac[:, 2 * half:2 * half + 2].unsqueeze(2).broadcast_to([128, 2, 64])
                    TT(P, "dve", oacc[:, r4, 2 * half:2 * half + 2, :], cmpacc[half][:, :, 0:64], fb, ALU.mult, [rcmpacc, rrcp], [roacc[r4]])
        trp, rtrp = A.S.next()
        for qb in range(4):
            j = 4 * Gq + qb
            bt, rbt = biasR.next()
            MEMSET(P, "pool", bt[:], 0.0, [], [rbt])
            MEMSET(P, "pool", bt[:, 0:1], 1e4, [], [rbt])
            if j >= 1:
                MEMSET(P, "pool", bt[0:64, 2 * j - 1:2 * j + 1], 1e4, [], [rbt])
            MEMSET(P, "pool", bt[64:128, 2 * j:2 * j + 2], 1e4, [], [rbt])
            if 2 * j + 1 < 128:
                MEMSET(P, "pool", bt[0:64, 2 * j + 1:128], -1e30, [], [rbt])
            if 2 * j + 2 < 128:
                MEMSET(P, "pool", bt[64:128, 2 * j + 2:128], -1e30, [], [rbt])
            TT(P, "dve", val[:], imp[:, qb, :], bt[:], ALU.add, [rimp, rbt], [rval])
            P.op("dve", lambda e: e.max(out=m16[:, 0:8], in_=val[:]), reads=[rval], writes=[rval])
            P.op("dve", lambda e: e.match_replace(out=wk[:], in_to_replace=m16[:, 0:8], in_values=val[:], imm_value=-3e38), reads=[rval], writes=[rval])
            P.op("dve", lambda e: e.max(out=m16[:, 8:16], in_=wk[:]), reads=[rval], writes=[rval])
            TS(P, "dve", penq[:], val[:], m16[:, 15:16], NEG, ALU.is_lt, ALU.mult, [rval], [rpenq])
            TR(P, trp[:, qb * 128:(qb + 1) * 128], penq[:], identf32[:], [rpenq, rc], [rtrp])
        cp(P, "dve", penT[:], trp[:, :], [rtrp], [rpenT])
        for h in range(2):
            acc, racc = A.acc.next()
            kbs = []
            for kb in range(4 * Gq + 4):
                ex_sel = lambda q0, q1, kb=kb: (E[:, kb // 1, :], penT[:, q0 * 128:(q1 + 1) * 128], [rc, rpenT])
                if kb < 4 * Gq:
                    kbs.append((kb, 0, 3, [(0, 3, [ex_sel(0, 3)])]))
                else:
                    i = kb - 4 * Gq
                    segs = [(i, i, [ex_sel(i, i), (ident[:], pen_diag[:], [rid, rc])])]
                    if i < 3:
                        segs.append((i + 1, 3, [ex_sel(i + 1, 3)]))
                    kbs.append((kb, i, 3, segs))
            attn_qgroup(P, A, kTa[0:64, :], rkTa, lambda kb: Va[:, kb, :], rVa, 65, q4[0:64, h, :], rq4, kbs, 0.125, lambda qb, acc=acc: acc[:, qb, 0:65], racc, look=2)
            finish_branch(acc, racc, h, 3 * h + 1, g, rg, False)
            acc, racc = A.acc.next()
            kbs = []
            for i in range(8):
                kb = 4 * Gq - 4 + i
                if kb < 0:
                    continue
                lo, hi = max(0, i - 4), min(3, i)
                segs = []
                if i <= 3:
                    if lo < i:
                        segs.append((lo, i - 1, []))
                    segs.append((i, i, [(ident[:], pen_far[:], [rid, rc])]))
                else:
                    segs.append((i - 4, i - 4, [(ident[:], pen_diag[:], [rid, rc])]))
                    if i - 4 < hi:
                        segs.append((i - 3, hi, []))
                kbs.append((kb, lo, hi, segs))
            attn_qgroup(P, A, kTb[0:64, :], rkTb, lambda kb: Vb[:, kb, :], rVb, 65, q4[0:64, h, :], rq4, kbs, 0.125, lambda qb, acc=acc: acc[:, qb, 0:65], racc, look=2)
            finish_branch(acc, racc, h, 3 * h + 2, g, rg, False)
        ost, rost = ostR.next()
        cp(P, "act", ost[:, :, 0:128].rearrange("p q (h d) -> p h q d", h=2), oacc[:], roacc, [rost])
        out_dma(ost, rost, Gq, 0, 128)

    if "post_mix" in io:
        io["post_mix"](0)
    P.barrier()
    S5 = Rot.__new__(Rot)
    S5.t = list(A.S.t) + [cmpacc[0][:].rearrange("p a b -> p (a b)"), cmpacc[1][:].rearrange("p a b -> p (a b)")]
    S5.r = list(A.S.r) + [P.res(), P.res()]
    S5.i = -1
    S5.n = len(S5.t)
    A.S = S5
    if "pre_swa" in io:
        io["pre_swa"]()
    load_kT(kTa, rkTa, lambda s: io["kb"][s], 64, dB)
    load_V(Va, rVa, lambda s: io["vb"][s], dB)
    for Gq in range(16):
        src, off = Gq // 4, (Gq % 4) * 512
        q4, rq4 = qR.next()
        P.dma("sp", q4[0:64, 0:2, :], io["qb"][src].rearrange("h d t -> d h t")[:, :, off:off + 512], writes=[rq4], reads=list(dB))
        for h in range(2):
            acc, racc = A.acc.next()
            kbs = []
            for i in range(5):
                kb = 4 * Gq - 1 + i
                if kb < 0:
                    continue
                segs = []
                lo, hi = max(0, i - 1), min(3, i)
                if i <= 3:
                    segs.append((i, i, [(ident[:], pen_far[:], [rid, rc])]))
                if i >= 1:
                    segs.append((i - 1, i - 1, [(ident[:], pen_diag[:], [rid, rc])]))
                segs.sort()
                kbs.append((kb, lo, hi, segs))
            attn_qgroup(P, A, kTa[0:64, :], rkTa, lambda kb: Va[:, kb, :], rVa, 65, q4[0:64, h, :], rq4, kbs, 0.125, lambda qb, acc=acc: acc[:, qb, 0:65], racc, look=2)
            finish_branch(acc, racc, h, None, None, None, True, extra_den=esink[:, h:h + 1])
        ost, rost = ostR.next()
        cp(P, "act", ost[:, :, 0:128].rearrange("p q (h d) -> p h q d", h=2), oacc[:], roacc, [rost])
        out_dma(ost, rost, Gq, 128, 128)

    if "post_mix" in io:
        io["post_mix"](1)
    if "pre_mla" in io:
        io["pre_mla"]()
    for h in range(2):
        kT, rkT = (kTa, rkTa) if h == 0 else (kTb, rkTb)
        Vx, rVx = (Va, rVa) if h == 0 else (Vb, rVb)
        load_kT(kT, rkT, lambda s, h=h: io["kc"][s, h], 96, dC)
        load_V(Vx, rVx, lambda s, h=h: io["vc"][s, h], dC)
        for Gq in range(16):
            src, off = Gq // 4, (Gq % 4) * 512
            q4, rq4 = qR.next()
            P.dma("sp", q4[0:96, 0, :], io["qc"][src, h][:, off:off + 512], writes=[rq4], reads=list(dC))
            acc, racc = A.acc.next()
            kbs = []
            for kb in range(4 * Gq + 4):
                if kb < 4 * Gq:
                    kbs.append((kb, 0, 3, [(0, 3, [])]))
                else:
                    i = kb - 4 * Gq
                    segs = [(i, i, [(ident[:], pen_diag[:], [rid, rc])])]
                    if i < 3:
                        segs.append((i + 1, 3, []))
                    kbs.append((kb, i, 3, segs))
            attn_qgroup(P, A, kT[0:96, :], rkT, lambda kb, Vx=Vx: Vx[:, kb, :], rVx, 65, q4[0:96, 0, :], rq4, kbs, 96 ** -0.5, lambda qb, acc=acc: acc[:, qb, 0:65], racc, look=3)
            finish_branch(acc, racc, 0, None, None, None, True)
            ost, rost = ostR.next()
            cp(P, "act", ost[:, :, 0:64], oacc[:, 0, :, :], roacc, [rost])
            out_dma(ost, rost, Gq, 256 + 64 * h, 64)
    if "post_mix" in io:
        io["post_mix"](2)


def emit_p3(P, C, io, last):
    nc = P.nc
    base_mark = nc.sbuf_base
    pbase = nc.psum_base
    junk = P.sb([128, D], BF16)
    ssR = Rot(P, 2, [128, 4], F32)
    ptR = Rot(P, 2, [128, 1024], BF16, psum=True)
    pzR = Rot(P, 4, [128, 512], F32, psum=True)
    g_n = P.sb([128, D], F32); rgn = P.res()

    def norm_T(t, hb, rhb, dstT, col0, rdst, eng="act"):
        ss, rss = ssR.next()
        rmsnorm_tile(P, C, C.x[:, t, :], C.rx[t], D, g_n[:], rgn, hb[:], rhb, (junk, ss, rss))
        pt, rpt = ptR.next()
        transpose_chunks(P, C, hb, rhb, 8, pt, rpt)
        cp(P, eng, dstT[:, :, col0:col0 + 128], pt[:].rearrange("p (k t) -> p k t", k=8), [rpt], [rdst])

    markA = nc.sbuf_base
    load_bcast(P, g_n, rgn, io["mix_norm"], D, None)
    wbg = P.sb([128, 8, 3072], BF16); rwA = P.res(); rwAp = P.res(); rwAo = P.res()
    load_w_bf16(P, wbg, rwA, io["w_bg"], 1024, 3072, None)
    wp = P.sb([128, 12, 1024], BF16)
    for i, nm in enumerate(("w_pa", "w_pb", "w_pc")):
        for k in range(4):
            P.dma("pool", wp[:, 4 * i + k, :], io[nm][k * 128:(k + 1) * 128, :], writes=[rwAp])
    wo = P.sb([128, 8, 1024], BF16)
    load_w_bf16(P, wo, rwAo, io["w_out"], 1024, 1024, None)
    hbR = Rot(P, 1, [128, D], BF16)
    hTR = Rot(P, 2, [128, 8, 128], BF16)
    gsb = P.sb([128, 3072], F32); rgsb = P.res()
    otR = Rot(P, 2, [128, 4, 384], BF16)
    oTR = Rot(P, 2, [128, 12, 128], BF16)
    mrg = P.sb([128, D], F32); rmrg = P.res()
    tmpm = P.sb([128, 512], F32); rtmpm = P.res()
    mbR = Rot(P, 1, [128, D], BF16)
    mTR = Rot(P, 1, [128, 8, 128], BF16)
    for t in range(NTILE):
        hb, rhb = hbR.next()
        hT, rhT = hTR.next()
        norm_T(t, hb, rhb, hT, 0, rhT)
        for c in range(6):
            pz, rpz = pzR.next()
            for k in range(8):
                MM(P, pz[:, :], hT[:, k, :], wbg[:, k, c * 512:(c + 1) * 512], k == 0, k == 7, [rhT, rwA], [rpz])
            ACT(P, gsb[:, c * 512:(c + 1) * 512], pz[:, :], AF.Sigmoid, [rpz], [rgsb])
        ot, rot = otR.next()
        if "o_tile3" in io:
            for m in range(3):
                P.dma("sp", ot[:, :, m * 128:(m + 1) * 128], io["o_tile3"](t, m).rearrange("s p c -> p s c"), writes=[rot], reads=list(io["o_dep"]))
        else:
            o_src = io["o_tile"](t) if "o_tile" in io else io["o"][:, t * 128:(t + 1) * 128, :]
            P.dma("sp", ot[:], o_src.rearrange("s p c -> p s c"), writes=[rot])
        oT, roT = oTR.next()
        for half, (a0, a1) in enumerate(((0, 8), (8, 12))):
            pt, rpt = ptR.next()
            for j in range(a0, a1):
                i, s = j // 4, j % 4
                TR(P, pt[:, (j - a0) * 128:(j - a0 + 1) * 128], ot[:, s, i * 128:(i + 1) * 128], C.ident[:], [rot, C.rid], [rpt])
            cp(P, "act" if half == 0 else "dve", oT[:, a0:a1, :], pt[:, 0:(a1 - a0) * 128].rearrange("p (k t) -> p k t", k=a1 - a0), [rpt], [roT])
        for c in range(2):
            cs = slice(c * 512, (c + 1) * 512)
            for i in range(3):
                pz, rpz = pzR.next()
                for k in range(4):
                    MM(P, pz[:, :], oT[:, 4 * i + k, :], wp[:, 4 * i + k, cs], k == 0, k == 3, [roT, rwAp], [rpz])
                gs = gsb[:, i * 1024 + c * 512:i * 1024 + (c + 1) * 512]
                if i == 0:
                    TT(P, "dve", mrg[:, cs], pz[:, :], gs, ALU.mult, [rpz, rgsb], [rmrg])
                else:
                    TT(P, "dve", tmpm[:], pz[:, :], gs, ALU.mult, [rpz, rgsb], [rtmpm])
                    TT(P, "dve", mrg[:, cs], mrg[:, cs], tmpm[:], ALU.add, [rtmpm], [rmrg])
        mb, rmb = mbR.next()
        cp(P, "act", mb[:], mrg[:], [rmrg], [rmb])
        pt, rpt = ptR.next()
        transpose_chunks(P, C, mb, rmb, 8, pt, rpt)
        mT, rmT = mTR.next()
        cp(P, "act", mT[:].rearrange("p k t -> p (k t)"), pt[:], [rpt], [rmT])
        for c in range(2):
            cs = slice(c * 512, (c + 1) * 512)
            pz, rpz = pzR.next()
            for k in range(8):
                MM(P, pz[:, :], mT[:, k, :], wo[:, k, cs], k == 0, k == 7, [rmT, rwAo], [rpz])
            TT(P, "dve", C.x[:, t, cs], C.x[:, t, cs], pz[:, :], ALU.add, [rpz], [C.rx[t]])
    P.barrier()
    nc.sbuf_base = markA

    load_bcast(P, g_n, rgn, io["ffn_norm"], D, None)
    h2T = P.sb([128, 8, NTOK], BF16); rh2T = [P.res() for _ in range(NSUP)]
    hbR = Rot(P, 2, [128, D], BF16)
    for t in range(NTILE):
        hb, rhb = hbR.next()
        norm_T(t, hb, rhb, h2T, t * 128, rh2T[t // 4], eng="act" if t % 2 == 0 else "dve")
    GR = [(0, 6), (6, 6), (12, 5), (17, 5)]
    NFM = 6
    wsets = []
    for _ in range(2):
        wsets.append((P.sb([128, 8, NFM * 128], BF16), P.sb([128, 8, NFM * 128], BF16), P.sb([128, NFM, D], BF16), P.res()))
    actT = P.sb([128, NFM, 512], BF16); ractT = P.res()
    sgR = Rot(P, 2, [128, 512], F32)

    def load_group(gi, after=()):
        wg, wu, wd, rwB = wsets[gi % 2]
        c0, NF = GR[gi]
        f0 = c0 * 128
        dep = list(after)
        for k in range(8):
            P.dma("pool", wg[:, k, 0:NF * 128], io["w_fg"][k * 128:(k + 1) * 128, f0:f0 + NF * 128], writes=[rwB], reads=dep)
            P.dma("pool", wu[:, k, 0:NF * 128], io["w_fu"][k * 128:(k + 1) * 128, f0:f0 + NF * 128], writes=[rwB], reads=dep)
        for f in range(NF):
            P.dma("pool", wd[:, f, :], io["w_fd"][f0 + f * 128:f0 + (f + 1) * 128, :], writes=[rwB], reads=dep)

    load_group(0)
    for gi in range(4):
        wg, wu, wd, rwB = wsets[gi % 2]
        NF = GR[gi][1]
        for st in range(NSUP):
            if st == 1 and gi + 1 < 4:
                load_group(gi + 1, after=[rwB])
            ts_ = slice(st * 512, (st + 1) * 512)
            for f in range(NF):
                pg, rpg = pzR.next()
                for k in range(8):
                    MM(P, pg[:, :], wg[:, k, f * 128:(f + 1) * 128], h2T[:, k, ts_], k == 0, k == 7, [rwB, rh2T[st]], [rpg])
                pu, rpu = pzR.next()
                for k in range(8):
                    MM(P, pu[:, :], wu[:, k, f * 128:(f + 1) * 128], h2T[:, k, ts_], k == 0, k == 7, [rwB, rh2T[st]], [rpu])
                sg, rsg = sgR.next()
                ACT(P, sg[:], pg[:, :], AF.Silu, [rpg], [rsg])
                TT(P, "dve", actT[:, f, :], sg[:], pu[:, :], ALU.mult, [rsg, rpu], [ractT])
            for tt in range(4):
                t = st * 4 + tt
                for c in range(2):
                    cs = slice(c * 512, (c + 1) * 512)
                    pz, rpz = pzR.next()
                    for f in range(NF):
                        MM(P, pz[:, :], actT[:, f, tt * 128:(tt + 1) * 128], wd[:, f, cs], f == 0, f == NF - 1, [ractT, rwB], [rpz])
                    TT(P, "dve", C.x[:, t, cs], C.x[:, t, cs], pz[:, :], ALU.add, [rpz], [C.rx[t]])
    P.barrier()
    nc.sbuf_base = markA

    load_bcast(P, g_n, rgn, io["ple_norm"], D, None)
    wpg = P.sb([128, 8, D], BF16); wpp = P.sb([128, 2, D], BF16); rwC = P.res()
    load_w_bf16(P, wpg, rwC, io["w_pg"], 1024, 1024, None)
    load_w_bf16(P, wpp, rwC, io["w_pp"], 256, 1024, None)
    hbR = Rot(P, 2, [128, D], BF16)
    hTR = Rot(P, 2, [128, 8, 128], BF16)
    pfR = Rot(P, 2, [128, 256], BF16)
    pTR = Rot(P, 2, [128, 2, 128], BF16)
    sgR = Rot(P, 2, [128, 512], F32)
    tmpm = P.sb([128, 512], F32); rtmpm = P.res()
    if last:
        g_f = P.sb([128, D], F32); rgf = P.res()
        load_bcast(P, g_f, rgf, io["final_norm"], D, None)
        yR = Rot(P, 2, [128, D], F32)
    for t in range(NTILE):
        hb, rhb = hbR.next()
        hT, rhT = hTR.next()
        norm_T(t, hb, rhb, hT, 0, rhT)
        pf, rpf = pfR.next()
        P.dma("pool", pf[:], io["p"][t * 128:(t + 1) * 128, :], writes=[rpf])
        pt, rpt = ptR.next()
        transpose_chunks(P, C, pf, rpf, 2, pt, rpt)
        pT, rpT = pTR.next()
        cp(P, "dve", pT[:].rearrange("p k t -> p (k t)"), pt[:, 0:256], [rpt], [rpT])
        for c in range(2):
            cs = slice(c * 512, (c + 1) * 512)
            pz, rpz = pzR.next()
            for k in range(8):
                MM(P, pz[:, :], hT[:, k, :], wpg[:, k, cs], k == 0, k == 7, [rhT, rwC], [rpz])
            sg, rsg = sgR.next()
            ACT(P, sg[:], pz[:, :], AF.Sigmoid, [rpz], [rsg])
            pp, rpp = pzR.next()
            for k in range(2):
                MM(P, pp[:, :], pT[:, k, :], wpp[:, k, cs], k == 0, k == 1, [rpT, rwC], [rpp])
            TT(P, "dve", tmpm[:], sg[:], pp[:, :], ALU.mult, [rsg, rpp], [rtmpm])
            TT(P, "dve", C.x[:, t, cs], C.x[:, t, cs], tmpm[:], ALU.add, [rtmpm], [C.rx[t]])
        if last:
            ss, rss = ssR.next()
            y, ry = yR.next()
            MEMSET(P, "pool", ss[:, 0:1], 0.0, [], [rss])
            ACT(P, junk[:], C.x[:, t, :], AF.Square, [C.rx[t], rss], [rss], accum_out=ss[:, 0:1])
            TS(P, "dve", ss[:, 1:2], ss[:, 0:1], 1.0 / D, EPS, ALU.mult, ALU.add, [rss], [rss])
            ACT(P, ss[:, 2:3], ss[:, 1:2], AF.Sqrt, [rss], [rss])
            P.op("dve", lambda e, ss=ss: e.reciprocal(out=ss[:, 3:4], in_=ss[:, 2:3]), reads=[rss], writes=[rss])
            STT(P, y[:], C.x[:, t, :], ss[:, 3:4], g_f[:], ALU.mult, ALU.mult, [C.rx[t], rss, rgf], [ry])
            P.dma("sp", io["y"][t * 128:(t + 1) * 128, :], y[:], reads=[ry])
    P.barrier()
    nc.sbuf_base = base_mark
    nc.psum_base = pbase


P1_WNAMES = {"w_in": [D, 2616], "mix_norm": [1, D], "q_norm": [1, 256], "kv_norm": [1, 256], "w_q_up": [256, 768], "w_kv_up": [256, 1024]}
P3_WNAMES = {"mix_norm": [1, D], "w_bg": [D, 3072], "w_pa": [512, D], "w_pb": [512, D], "w_pc": [512, D], "w_out": [D, D],
             "ffn_norm": [1, D], "w_fg": [D, DFF], "w_fu": [D, DFF], "w_fd": [DFF, D], "ple_norm": [1, D], "w_pg": [D, D],
             "w_pp": [256, D], "p": [NTOK, 256]}


def build_tok_program(do_p3, do_p1, last):
    P = Prog()
    x_in = P.dram("x_in", [NTOK, D], F32, "ExternalInput").ap()
    pos_in = P.dram("pos_in", [128, NTILE], I32, "ExternalInput").ap()
    ident = P.dram("ident", [128, 128], F32, "ExternalInput").ap()
    invf = P.dram("invf", [128, 48], F32, "ExternalInput").ap()
    C = Common(P, x_in, pos_in, ident, invf)
    if do_p3:
        io = {}
        for k, shp in P3_WNAMES.items():
            io[k] = P.dram("p3_" + k, shp, F32, "ExternalInput").ap()
        io["o"] = P.dram("p3_o", [4, NTOK, 384], BF16, "ExternalInput").ap()
        if last:
            io["final_norm"] = P.dram("p3_final_norm", [1, D], F32, "ExternalInput").ap()
            io["y"] = P.dram("y", [NTOK, D], F32, "ExternalOutput").ap()
        emit_p3(P, C, io, last)
        if not last:
            x_out = P.dram("x_out", [NTOK, D], F32, "ExternalOutput").ap()
            for t in range(NTILE):
                P.dma("sp", x_out[t * 128:(t + 1) * 128, :], C.x[:, t, :], reads=[C.rx[t]])
    if do_p1:
        io = {}
        for k, shp in P1_WNAMES.items():
            io[k] = P.dram("p1_" + k, shp, F32, "ExternalInput").ap()
        for k, (shp, dt) in P1_X.items():
            io[k] = P.dram(k, [4] + shp, dt, "ExternalOutput").ap()
        emit_p1(P, C, io)
    return P.build()


def build_p2_program():
    P = Prog()
    io = {}
    for k, (shp, dt) in P2_IN.items():
        io[k] = P.dram(k, shp, dt, "ExternalInput").ap()
    for k, shp in P2_W.items():
        io[k] = P.dram(k, shp, F32, "ExternalInput").ap()
    identd = P.dram("ident", [128, 128], F32, "ExternalInput").ap()
    io["o"] = P.dram("o", [4, NTOK, 384], BF16, "ExternalOutput").ap()
    identf = P.sb([128, 128], F32)
    ident = P.sb([128, 128], BF16)
    rid = P.res()
    P.dma("sp", identf[:], identd, writes=[rid])
    cp(P, "dve", ident[:], identf[:], [rid], [rid])
    emit_p2(P, io, ident, rid)
    return P.build()


X1_TO_P2 = {"xqa": "qa", "xqg": "qg", "xka": "ka", "xva": "va", "xqb": "qb", "xkb": "kb", "xvb": "vb",
            "xqc": "qc", "xkc": "kc", "xvc": "vc", "xg": "g"}


def p1_weights(inp, l):
    m = {}
    m["p1_w_in"] = np.ascontiguousarray(inp["w_in"][l][:, W_IN_PERM])
    m["p1_mix_norm"] = np.ascontiguousarray(inp["mix_norm"][l][None, :])
    m["p1_q_norm"] = np.ascontiguousarray(inp["c_q_norm"][l][None, :])
    m["p1_kv_norm"] = np.ascontiguousarray(inp["c_kv_norm"][l][None, :])
    m["p1_w_q_up"] = np.ascontiguousarray(inp["c_w_q_up"][l])
    m["p1_w_kv_up"] = np.ascontiguousarray(inp["c_w_kv_up"][l])
    return m


def p2_weights(inp, l, r):
    m = {}
    for w in ("k", "v"):
        m["pos" + w] = np.ascontiguousarray(inp[f"a_cmp_pos_{w}"][l].reshape(16, 128).T)
        m["w1" + w] = np.ascontiguousarray(inp[f"a_cmp_w1_{w}"][l])
        m["w2" + w] = np.ascontiguousarray(inp[f"a_cmp_w2_{w}"][l])
    m["sinks"] = np.ascontiguousarray(inp["b_sinks"][l][2 * r:2 * r + 2][None, :])
    m["selmap"] = selmap_const()
    m["ident"] = np.eye(128, dtype=np.float32)
    return m


def p3_weights(inp, l, b, r, last):
    m = {}
    src = {"mix_norm": "mix_norm", "w_bg": "w_branch_gate", "w_pa": "w_branch_a", "w_pb": "w_branch_b", "w_pc": "w_branch_c",
           "w_out": "w_out", "ffn_norm": "ffn_norm", "w_fg": "w_ffn_gate", "w_fu": "w_ffn_up", "w_fd": "w_ffn_down",
           "ple_norm": "ple_norm", "w_pg": "w_ple_gate", "w_pp": "w_ple_proj"}
    for k, s in src.items():
        a = inp[s][l]
        m["p3_" + k] = np.ascontiguousarray(a[None, :] if a.ndim == 1 else a)
    m["p3_p"] = np.ascontiguousarray(inp["p"][l, b, r * NTOK:(r + 1) * NTOK])
    if last:
        m["p3_final_norm"] = np.ascontiguousarray(inp["final_norm"][None, :])
    return m


def all_to_all(outs, names):
    res = []
    for core in range(8):
        b, r = divmod(core, 4)
        res.append({nm: np.ascontiguousarray(np.stack([outs[4 * b + s][nm][r] for s in range(4)], axis=0)) for nm in names})
    return res


X1_LAYOUT = [("xqa", [2, 64, NTOK], 0, 0), ("xqg", [2, 64, NTOK], 0, 128), ("xka", [4, 64, NTOK], 1, 0),
             ("xva", [2, NTOK, 64], 2, 0), ("xqb", [2, 64, NTOK], 2, 128), ("xkb", [64, NTOK], 3, 0),
             ("xvb", [NTOK, 64], 3, 64), ("xqc", [2, 96, NTOK], 4, 0), ("xkc", [2, 96, NTOK], 5, 0),
             ("xvc", [2, NTOK, 64], 6, 0)]
X1_K = 7
X1_CR = 256


def x1_views(rows):
    views = {}
    for nm, shp, k, r0 in X1_LAYOUT:
        n = int(np.prod(shp)) // 2048
        v = rows(k, r0, n)
        if nm in ("xqa", "xqg", "xka", "xqb", "xqc", "xkc"):
            v = v.rearrange("e (h d) t -> e h d t", h=shp[0])
        elif nm in ("xva", "xvc"):
            v = v.rearrange("e r c -> e (r c)").rearrange("e (h t d) -> e h t d", h=shp[0], d=64)
        elif nm == "xvb":
            v = v.rearrange("e r c -> e (r c)").rearrange("e (t d) -> e t d", d=64)
        views[nm] = v
    return views


def build_fused_program():
    P = Prog()
    nc = P.nc
    x_in = P.dram("x_in", [NTOK, D], F32, "ExternalInput").ap()
    pos_in = P.dram("pos_in", [128, NTILE], I32, "ExternalInput").ap()
    ident = P.dram("ident", [128, 128], F32, "ExternalInput").ap()
    invf = P.dram("invf", [128, 48], F32, "ExternalInput").ap()
    y_out = P.dram("y", [NTOK, D], F32, "ExternalOutput").ap()
    RD = X1_K * X1_CR
    X1 = P.dram("ex_x1", [4 * RD, 2048], BF16, "Internal").ap()
    G1 = P.dram("ex_g1", [16 * RD, 2048], BF16, "Internal").ap()
    M1 = P.dram("ex_m1", [4 * RD, 2048], BF16, "Internal").ap()
    XG = P.dram("ex_xg", [4 * 16, 768], F32, "Internal").ap()
    GG = P.dram("ex_gg", [16 * 16, 768], F32, "Internal").ap()
    MG = P.dram("ex_mg", [4 * 16, 768], F32, "Internal").ap()
    O2 = P.dram("ex_o2", [12 * NTOK, 128], BF16, "Internal").ap()
    GO = P.dram("ex_go", [48 * NTOK, 128], BF16, "Internal").ap()
    MO = P.dram("ex_mo", [12 * NTOK, 128], BF16, "Internal").ap()
    C = Common(P, x_in, pos_in, ident, invf)
    mark, pmark = nc.sbuf_base, nc.psum_base

    def phase_end():
        P.barrier()
        nc.sbuf_base = mark
        nc.psum_base = pmark

    def exchange_chunked(src, gath, mine, nchunk, cr):
        P.barrier()
        rc_ = P.res()
        for j in range(4 * nchunk):
            P.allgather(src[j * cr:(j + 1) * cr, :], gath[j * 4 * cr:(j + 1) * 4 * cr, :], rc_)
        rm_ = P.res()
        g3 = gath.rearrange("(d x) c -> d x c", d=4)
        P.dma("pool", mine, (lambda: g3[bass.ds(P.rank(), 1), :, :]), reads=[rc_], writes=[rm_])
        P.barrier()

    def exchange_small(src, gath, mine):
        P.barrier()
        rc_ = P.res()
        P.allgather(src, gath, rc_)
        rm_ = P.res()
        g4 = gath.rearrange("(s d r) c -> s d r c", s=4, d=4)
        m3 = mine.rearrange("(s r) c -> s r c", s=4)
        P.dma("pool", m3, (lambda: g4[:, bass.ds(P.rank(), 1), :, :]), reads=[rc_], writes=[rm_])
        P.barrier()

    X1v = X1.rearrange("(e r) c -> e r c", e=4)
    M1v = M1.rearrange("(k s i) c -> k s i c", k=X1_K, s=4)
    MOv = MO.rearrange("(k s i) c -> k s i c", k=2, s=4)
    for l in range(2):
        last = l == 1
        io = {}
        for k, shp in P1_WNAMES.items():
            io[k] = P.dram(f"l{l}_p1_{k}", shp, F32, "ExternalInput").ap()
        io.update(x1_views(lambda k, r0, n: X1v[:, k * X1_CR + r0:k * X1_CR + r0 + n, :]))
        io["xg"] = XG.rearrange("(e a) (p c) -> e (a p) c", e=4, c=6)
        emit_p1(P, C, io)
        P.barrier()
        g3 = G1.rearrange("(d x) c -> d x c", d=4)
        groups = {"nsa": (0, 3), "swa": (3, 4), "mla": (4, 7)}
        rcg, rmg = {}, {}
        for gname, (k0, k1) in groups.items():
            rcg[gname] = P.res()
            rmg[gname] = P.res()
            for d in range(4):
                for k in range(k0, k1):
                    j = d * X1_K + k
                    P.allgather(X1[j * X1_CR:(j + 1) * X1_CR, :], G1[j * 4 * X1_CR:(j + 1) * 4 * X1_CR, :], rcg[gname], sem="cc_" + gname)
            if gname == "nsa":
                rcg["g"] = P.res()
                rmg["g"] = P.res()
                P.allgather(XG, GG, rcg["g"], sem="cc_g")

        def select(gname):
            k0, k1 = groups[gname]
            r0, r1 = k0 * 4 * X1_CR, k1 * 4 * X1_CR
            P.dma("sp", M1[r0:r1, :], (lambda: g3[bass.ds(P.rank("sp"), 1), r0:r1, :]), reads=[rcg[gname]], writes=[rmg[gname]])

        select("nsa")
        gg4 = GG.rearrange("(s d r) c -> s d r c", s=4, d=4)
        P.dma("sp", MG.rearrange("(s r) c -> s r c", s=4), (lambda: gg4[:, bass.ds(P.rank("sp"), 1), :, :]), reads=[rcg["g"]], writes=[rmg["g"]])
        nc.sbuf_base = mark
        nc.psum_base = pmark
        io = {}
        v = x1_views(lambda k, r0, n: M1v[k][:, r0:r0 + n, :])
        for k, vv in v.items():
            io[X1_TO_P2[k]] = vv
        io["g"] = MG.rearrange("(e a) (p c) -> e (a p) c", e=4, c=6)
        io["deps"] = {"nsa": [rmg["nsa"], rmg["g"]], "swa": [rmg["swa"]], "mla": [rmg["mla"]]}
        io["pre_swa"] = lambda: select("swa")
        io["pre_mla"] = lambda: select("mla")
        for k, shp in P2_W.items():
            if k == "selmap":
                if l == 0:
                    selmap_ap = P.dram("selmap", shp, F32, "ExternalInput").ap()
                io[k] = selmap_ap
            else:
                io[k] = P.dram(f"l{l}_p2_{k}", shp, F32, "ExternalInput").ap()
        O2v = O2.rearrange("(m e t) c -> m e t c", m=3, e=4)
        ro = [P.res(), P.res(), P.res()]
        rco = P.res()
        io["o_mix"] = lambda m: O2v[m]
        io["ro"] = ro

        def post_mix(m, ro=ro, rco=rco):
            for d in range(4):
                P.allgather(O2[(m * 4 + d) * NTOK:(m * 4 + d + 1) * NTOK, :], GO[((d * 3 + m) * 4) * NTOK:((d * 3 + m) * 4 + 4) * NTOK, :], rco,
                            reads=[ro[m]] if d == 0 else (), sem="cc_o")

        io["post_mix"] = post_mix
        emit_p2(P, io, C.ident, C.rid)
        P.barrier()
        nc.sbuf_base = mark
        nc.psum_base = pmark
        rmo = P.res()
        go3 = GO.rearrange("(d x) c -> d x c", d=4)
        P.dma("sp", MO, (lambda: go3[bass.ds(P.rank("sp"), 1), :, :]), reads=[rco], writes=[rmo])
        MOv = MO.rearrange("(m s t) c -> m s t c", m=3, s=4)
        io = {}
        for k, shp in P3_WNAMES.items():
            io[k] = P.dram(f"l{l}_p3_{k}", shp, F32, "ExternalInput").ap()
        io["o_tile3"] = lambda t, m: MOv[m][:, t * 128:(t + 1) * 128, :]
        io["o_dep"] = [rmo]
        if last:
            io["final_norm"] = P.dram("final_norm", [1, D], F32, "ExternalInput").ap()
            io["y"] = y_out
        emit_p3(P, C, io, last)
        phase_end()
    return P.build()


_PROGS = {}


def _prog(key):
    if key not in _PROGS:
        if key == "fused":
            _PROGS[key] = build_fused_program()
        elif key == "p2":
            _PROGS[key] = build_p2_program()
        else:
            _PROGS[key] = build_tok_program(*key)
    return _PROGS[key]


def kernel(**inp):
    inp = {k: np.asarray(v) for k, v in inp.items()}
    cst = const_inputs()
    cores = list(range(8))
    maps = []
    for c in cores:
        b, r = divmod(c, 4)
        m = dict(cst)
        m["x_in"] = np.ascontiguousarray(inp["x"][b, r * NTOK:(r + 1) * NTOK]).astype(np.float32)
        m["pos_in"] = np.ascontiguousarray(inp["positions"][b, r * NTOK:(r + 1) * NTOK].reshape(NTILE, 128).T.astype(np.int32))
        m["selmap"] = selmap_const()
        m["final_norm"] = np.ascontiguousarray(inp["final_norm"][None, :])
        for l in range(2):
            for k, v in p1_weights(inp, l).items():
                m[f"l{l}_{k}"] = v
            for k, v in p2_weights(inp, l, r).items():
                if k not in ("selmap", "ident"):
                    m[f"l{l}_p2_{k}"] = v
            for k, v in p3_weights(inp, l, b, r, False).items():
                m[f"l{l}_{k}"] = v
        maps.append(m)
    res = run_bass_kernel_spmd(_prog("fused"), maps, core_ids=cores).results
    y = np.stack([np.concatenate([np.asarray(res[4 * b + r]["y"]) for r in range(4)], axis=0) for b in range(2)], axis=0)
    return y.astype(np.float32)
```
